# Optimizing a Trainium2 kernel written in Bass

```python
import jax, jax.numpy as jnp
from jax import lax
import numpy as np

D_MODEL = 1024
BATCH = 4
SEQ = 4096
DEPTH = 1
DEC_BATCH = 32
DEC_SEQ = 2048
PAST_LEN = 128

N_META = 16
MLA_HEADS = 8
QK_NOPE = 64
QK_ROPE = 32
V_HEAD = 64
Q_LORA = 384
KV_LORA = 256
ROPE_THETA = 10000.0
Q_BLOCK = 128
GDN_HEADS = 8
GDN_DK = 64
GDN_DV = 64
CONV_K = 4
CHUNK = 64
MLA_W = MLA_HEADS * V_HEAD
GDN_W = GDN_HEADS * GDN_DV
MIX_W = MLA_W + GDN_W
QKV_W = GDN_HEADS * (2 * GDN_DK + GDN_DV)
D_FF = 4 * D_MODEL
IN_SPLITS = (Q_LORA, KV_LORA, QK_ROPE, QKV_W, GDN_W, GDN_HEADS, GDN_HEADS, GDN_HEADS, GDN_HEADS)
N_IN = Q_LORA + KV_LORA + QK_ROPE + QKV_W + GDN_W + 4 * GDN_HEADS
DN_ALPHA = (2 * DEPTH) ** 0.25
DN_BETA = (8 * DEPTH) ** -0.25
LN_EPS = 1e-5
RMS_EPS = 1e-6

kernel_name = 'hymba_mla_bigdn_deepnorm_encoder'


def _layer_norm(x, g, b):
    xf = x.astype(jnp.float32)
    mu = jnp.mean(xf, -1, keepdims=True)
    var = jnp.mean(jnp.square(xf - mu), -1, keepdims=True)
    return ((xf - mu) * lax.rsqrt(var + LN_EPS) * g.astype(jnp.float32) + b.astype(jnp.float32)).astype(x.dtype)


def _rms_norm(x, g):
    xf = x.astype(jnp.float32)
    return (xf * lax.rsqrt(jnp.mean(xf * xf, -1, keepdims=True) + RMS_EPS) * g.astype(jnp.float32)).astype(x.dtype)


def _l2norm(x):
    return x * lax.rsqrt(jnp.sum(x * x, -1, keepdims=True) + 1e-6)


def _rope_tables(L, dtype):
    inv = ROPE_THETA ** (-jnp.arange(0, QK_ROPE, 2, dtype=jnp.float32) / QK_ROPE)
    ang = jnp.arange(L, dtype=jnp.float32)[:, None] * inv[None, :]
    return jnp.cos(ang)[:, None, :].astype(dtype), jnp.sin(ang)[:, None, :].astype(dtype)


def _rope(x, cos, sin):
    x1, x2 = jnp.split(x, 2, axis=-1)
    return jnp.concatenate([x1 * cos - x2 * sin, x2 * cos + x1 * sin], axis=-1)


def _block_attention(q, k, v):
    B, L, H, Dq = q.shape
    nblk = -(-L // Q_BLOCK)
    qp = jnp.pad(q, ((0, 0), (0, nblk * Q_BLOCK - L), (0, 0), (0, 0)))
    qb = jnp.moveaxis(qp.reshape(B, nblk, Q_BLOCK, H, Dq), 1, 0)
    scale = Dq ** -0.5

    def one(qblk):
        s = jnp.einsum('bqhd,bkhd->bhqk', qblk, k, preferred_element_type=jnp.float32) * scale
        p = jax.nn.softmax(s, axis=-1).astype(v.dtype)
        return jnp.einsum('bhqk,bkhd->bqhd', p, v)

    o = lax.map(one, qb)
    return jnp.moveaxis(o, 0, 1).reshape(B, nblk * Q_BLOCK, H, v.shape[-1])[:, :L]


def _centred_conv(x, w):
    left = (CONV_K - 1) // 2
    return lax.conv_general_dilated(x, w[:, None, :].astype(x.dtype), window_strides=(1,),
                                    padding=[(left, CONV_K - 1 - left)],
                                    dimension_numbers=('NWC', 'WIO', 'NWC'),
                                    feature_group_count=x.shape[-1])


def _gated_delta_chunked(q, k, v, g, beta):
    Bt, Lp, H, DK = q.shape
    DV = v.shape[-1]
    N = Lp // CHUNK

    def blk(t):
        return jnp.moveaxis(t.reshape((Bt, N, CHUNK, H) + t.shape[3:]), 3, 1)

    q, k, v, g, beta = blk(q) * (DK ** -0.5), blk(k), blk(v), blk(g), blk(beta)
    gc = jnp.cumsum(g, axis=-1)
    tri = jnp.tril(jnp.ones((CHUNK, CHUNK), bool))
    strict = jnp.tril(jnp.ones((CHUNK, CHUNK), bool), -1)
    decay = jnp.exp(jnp.where(tri, gc[..., :, None] - gc[..., None, :], -jnp.inf))
    kb = k * beta[..., None]
    m = jnp.where(strict, jnp.einsum('bhncd,bhnsd->bhncs', kb, k) * decay, 0.0)
    rhs = jnp.concatenate([v * beta[..., None], kb * jnp.exp(gc)[..., None]], axis=-1)
    sol = lax.linalg.triangular_solve(m + jnp.eye(CHUNK, dtype=jnp.float32), rhs, left_side=True,
                                      lower=True, unit_diagonal=True)
    u, w = sol[..., :DV], sol[..., DV:]
    attn = jnp.einsum('bhncd,bhnsd->bhncs', q, k) * decay
    qg = q * jnp.exp(gc)[..., None]
    glast = gc[..., -1]
    kdec = k * jnp.exp(glast[..., None] - gc)[..., None]

    def step(S, xs):
        qg_i, w_i, u_i, kd_i, at_i, gl_i = xs
        v_new = u_i - jnp.einsum('bhcd,bhde->bhce', w_i, S)
        o_i = jnp.einsum('bhcd,bhde->bhce', qg_i, S) + jnp.einsum('bhcs,bhse->bhce', at_i, v_new)
        S = S * jnp.exp(gl_i)[..., None, None] + jnp.einsum('bhcd,bhce->bhde', kd_i, v_new)
        return S, o_i

    xs = tuple(jnp.moveaxis(t, 2, 0) for t in (qg, w, u, kdec, attn, glast))
    S0 = jnp.zeros((Bt, H, DK, DV), jnp.float32)
    _, o = lax.scan(step, S0, xs)
    return jnp.transpose(o, (1, 0, 3, 2, 4)).reshape(Bt, Lp, H, DV)


def _gdn_branch(qkv, z, a_f, a_b, b_f, b_b, conv_w, a_log_f, a_log_b, dt_bias_f, dt_bias_b, gdn_norm_g):
    B, L, _ = qkv.shape
    f32 = jnp.float32
    qkv = jax.nn.silu(_centred_conv(qkv, conv_w)).astype(f32)
    q, k, v = jnp.split(qkv, [GDN_HEADS * GDN_DK, 2 * GDN_HEADS * GDN_DK], axis=-1)
    q = _l2norm(q.reshape(B, L, GDN_HEADS, GDN_DK))
    k = _l2norm(k.reshape(B, L, GDN_HEADS, GDN_DK))
    v = v.reshape(B, L, GDN_HEADS, GDN_DV)

    def gate(a, b, a_log, dt_bias):
        gl = -jnp.exp(a_log.astype(f32)) * jax.nn.softplus(a.astype(f32) + dt_bias.astype(f32))
        return gl, jax.nn.sigmoid(b.astype(f32))

    g_f, beta_f = gate(a_f, b_f, a_log_f, dt_bias_f)
    g_b, beta_b = gate(a_b, b_b, a_log_b, dt_bias_b)
    pad = (-N_META) % CHUNK

    def both(tf, tb):
        widths = ((0, 0), (pad, 0)) + ((0, 0),) * (tf.ndim - 2)
        return jnp.concatenate([jnp.pad(tf, widths), jnp.flip(jnp.pad(tb, widths), axis=1)], axis=0)

    o = _gated_delta_chunked(both(q, q), both(k, k), both(v, v), both(g_f, g_b), both(beta_f, beta_b))
    o = (o[:B] + jnp.flip(o[B:], axis=1))[:, pad:]
    o = o * lax.rsqrt(jnp.mean(o * o, -1, keepdims=True) + RMS_EPS) * gdn_norm_g.astype(f32)
    o = o * jax.nn.silu(z.astype(f32).reshape(B, L, GDN_HEADS, GDN_DV))
    return o.reshape(B, L, GDN_W)


def _mixer(h, cos, sin, w_in, g_cq, g_ckv, w_uq, w_uk, w_uv, conv_w, a_log_f, a_log_b,
           dt_bias_f, dt_bias_b, gdn_norm_g, w_out):
    B, L, _ = h.shape
    proj = h @ w_in
    idx = np.cumsum(IN_SPLITS)[:-1].tolist()
    c_q, c_kv, k_r, qkv, z, a_f, a_b, b_f, b_b = jnp.split(proj, idx, axis=-1)
    q = (_rms_norm(c_q, g_cq) @ w_uq).reshape(B, L, MLA_HEADS, QK_NOPE + QK_ROPE)
    q = jnp.concatenate([q[..., :QK_NOPE], _rope(q[..., QK_NOPE:], cos, sin)], axis=-1)
    c_kv = _rms_norm(c_kv, g_ckv)
    k_nope = (c_kv @ w_uk).reshape(B, L, MLA_HEADS, QK_NOPE)
    v = (c_kv @ w_uv).reshape(B, L, MLA_HEADS, V_HEAD)
    k_r = jnp.broadcast_to(_rope(k_r[:, :, None, :], cos, sin), (B, L, MLA_HEADS, QK_ROPE))
    k = jnp.concatenate([k_nope, k_r], axis=-1)
    mla_out = _block_attention(q, k, v).reshape(B, L, MLA_W)
    gdn_out = _gdn_branch(qkv, z, a_f, a_b, b_f, b_b, conv_w, a_log_f, a_log_b, dt_bias_f, dt_bias_b,
                          gdn_norm_g).astype(h.dtype)
    return jnp.concatenate([mla_out, gdn_out], axis=-1) @ w_out


def _encode(x, meta_tokens, ln_in_g, ln_in_b, w_in, g_cq, g_ckv, w_uq, w_uk, w_uv, conv_w, a_log_f, a_log_b,
            dt_bias_f, dt_bias_b, gdn_norm_g, w_out, ln1_g, ln1_b, w_ff1, w_ff2, ln2_g, ln2_b):
    B = x.shape[0]
    h = jnp.concatenate([jnp.broadcast_to(meta_tokens.astype(x.dtype)[None], (B, N_META, D_MODEL)), x], axis=1)
    h = _layer_norm(h, ln_in_g, ln_in_b)
    cos, sin = _rope_tables(h.shape[1], h.dtype)
    for l in range(DEPTH):
        mix = _mixer(h, cos, sin, w_in[l], g_cq[l], g_ckv[l], w_uq[l], w_uk[l], w_uv[l], conv_w[l],
                     a_log_f[l], a_log_b[l], dt_bias_f[l], dt_bias_b[l], gdn_norm_g[l], w_out[l])
        h = _layer_norm(DN_ALPHA * h + mix, ln1_g[l], ln1_b[l])
        ff = jnp.square(jax.nn.relu(h @ w_ff1[l])) @ w_ff2[l]
        h = _layer_norm(DN_ALPHA * h + ff, ln2_g[l], ln2_b[l])
    return h[:, N_META:]


def setup_inputs(seed: int = 0) -> dict:
    key = jax.random.key(seed)
    ks = jax.random.split(key, 32)
    f32 = jnp.float32

    def nrm(i, shape, scale):
        return jax.random.normal(ks[i], shape, f32) * scale

    def gain(i, shape):
        return 1.0 + 0.02 * jax.random.normal(ks[i], shape, f32)

    def dt_bias(i):
        dt = jnp.exp(jax.random.uniform(ks[i], (DEPTH, GDN_HEADS), f32, np.log(1e-3), np.log(1e-1)))
        return dt + jnp.log(-jnp.expm1(-dt))

    return {
        'x_prompt': nrm(0, (BATCH, SEQ, D_MODEL), 1.0),
        'x_sample': nrm(1, (DEC_BATCH, DEC_SEQ, D_MODEL), 1.0),
        'meta_tokens': nrm(2, (N_META, D_MODEL), 1.0),
        'ln_in_g': gain(3, (D_MODEL,)),
        'ln_in_b': nrm(4, (D_MODEL,), 0.02),
        'w_in': nrm(5, (DEPTH, D_MODEL, N_IN), D_MODEL ** -0.5),
        'g_cq': gain(6, (DEPTH, Q_LORA)),
        'g_ckv': gain(7, (DEPTH, KV_LORA)),
        'w_uq': nrm(8, (DEPTH, Q_LORA, MLA_HEADS * (QK_NOPE + QK_ROPE)), Q_LORA ** -0.5),
        'w_uk': nrm(9, (DEPTH, KV_LORA, MLA_HEADS * QK_NOPE), KV_LORA ** -0.5),
        'w_uv': nrm(10, (DEPTH, KV_LORA, MLA_HEADS * V_HEAD), KV_LORA ** -0.5),
        'conv_w': nrm(11, (DEPTH, CONV_K, QKV_W), CONV_K ** -0.5),
        'a_log_f': jnp.log(jax.random.uniform(ks[12], (DEPTH, GDN_HEADS), f32, 1.0, 16.0)),
        'a_log_b': jnp.log(jax.random.uniform(ks[13], (DEPTH, GDN_HEADS), f32, 1.0, 16.0)),
        'dt_bias_f': dt_bias(14),
        'dt_bias_b': dt_bias(15),
        'gdn_norm_g': gain(16, (DEPTH, GDN_DV)),
        'w_out': nrm(17, (DEPTH, MIX_W, D_MODEL), MIX_W ** -0.5 * DN_BETA),
        'ln1_g': gain(18, (DEPTH, D_MODEL)),
        'ln1_b': nrm(19, (DEPTH, D_MODEL), 0.02),
        'w_ff1': nrm(20, (DEPTH, D_MODEL, D_FF), D_MODEL ** -0.5),
        'w_ff2': nrm(21, (DEPTH, D_FF, D_MODEL), D_FF ** -0.5 * DN_BETA),
        'ln2_g': gain(22, (DEPTH, D_MODEL)),
        'ln2_b': nrm(23, (DEPTH, D_MODEL), 0.02),
    }


def reference(x_prompt, x_sample, meta_tokens, ln_in_g, ln_in_b, w_in, g_cq, g_ckv, w_uq, w_uk, w_uv, conv_w,
              a_log_f, a_log_b, dt_bias_f, dt_bias_b, gdn_norm_g, w_out, ln1_g, ln1_b, w_ff1, w_ff2, ln2_g, ln2_b):
    y_prompt = _encode(x_prompt, meta_tokens, ln_in_g, ln_in_b, w_in, g_cq, g_ckv, w_uq, w_uk, w_uv, conv_w,
                       a_log_f, a_log_b, dt_bias_f, dt_bias_b, gdn_norm_g, w_out, ln1_g, ln1_b, w_ff1, w_ff2,
                       ln2_g, ln2_b)
    y_sample = _encode(x_sample, meta_tokens, ln_in_g, ln_in_b, w_in, g_cq, g_ckv, w_uq, w_uk, w_uv, conv_w,
                       a_log_f, a_log_b, dt_bias_f, dt_bias_b, gdn_norm_g, w_out, ln1_g, ln1_b, w_ff1, w_ff2,
                       ln2_g, ln2_b)
    return (y_prompt, y_sample)
```

```python
import numpy as np
from contextlib import ExitStack
import concourse.bass as bass
import concourse.mybir as mybir
from concourse.bass_utils import run_bass_kernel_spmd

F32 = mybir.dt.float32
BF16 = mybir.dt.bfloat16
I32 = mybir.dt.int32
AF = mybir.ActivationFunctionType
ALU = mybir.AluOpType
AX = mybir.AxisListType

D = 1024
NIN = 2752
H = 8
DFF = 4096
C_Q0, C_KV0, C_KR0, C_QKV0, C_Z0, C_G0 = 0, 384, 640, 672, 2208, 2720
DN_ALPHA = 2.0 ** 0.25
LN_EPS = 1e-5
RMS_EPS = 1e-6
QSCALE = 96.0 ** -0.5
NEG = -30000.0


class Reg:
    __slots__ = ("name", "w", "r", "sem", "cnt")

    def __init__(self, name):
        self.name = name
        self.w = {}
        self.r = {}
        self.sem = None
        self.cnt = 0


class Buf:
    def __init__(self, t, reg):
        self.t = t
        self.reg = reg

    def __getitem__(self, k):
        return self.t[k]


class _PEProxy:
    def __init__(self, kb):
        self.kb = kb
        self.e = kb.engs["pe"]

    def matmul(self, out, lhsT, rhs, **kw):
        self.kb._pe_pos(lhsT, out)
        return self.e.matmul(out, lhsT, rhs, **kw)

    def transpose(self, out, in_, ident):
        self.kb._pe_pos(in_, out)
        return self.e.transpose(out, in_, ident)


class KB:
    def __init__(self, nc, es):
        self.nc = nc
        self.es = es
        self.engs = {"pe": nc.tensor, "act": nc.scalar, "dve": nc.vector, "pool": nc.gpsimd, "sp": nc.sync}
        self.sem = {}
        self.cnt = {}
        self.waited = {}
        self.semname = {}
        for n in self.engs:
            self.sem[n] = es.enter_context(nc.semaphore("e_" + n))
            self.cnt[n] = 0
            self.waited[n] = {}
        self.nreg = 0
        self.all_dma_regs = []
        self.dpool = []
        self.dfree = {"sw": [], "hw": []}
        self.dkind = {}
        self.ninst = 0
        self.pe_live = set()
        self.nfence = 0
        self.pend = {n: False for n in self.engs}
        self.last_pe_w = None
        self.last_ins = {n: None for n in self.engs}
        self.pe_proxy = _PEProxy(self)

    def reg(self, name):
        self.nreg += 1
        return Reg(f"{name}_{self.nreg}")

    def sb(self, stack, name, shape, dt):
        self.nreg += 1
        nm = f"{name}_{self.nreg}"
        t = stack.enter_context(self.nc.sbuf_tensor(nm, list(shape), dt))
        return Buf(t, Reg(nm))

    def ps(self, stack, name, shape, dt):
        self.nreg += 1
        nm = f"{name}_{self.nreg}"
        t = stack.enter_context(self.nc.psum_tensor(nm, list(shape), dt))
        return Buf(t, Reg(nm))

    def _regs(self, xs):
        out = []
        for x in xs:
            if x is None:
                continue
            out.append(x.reg if isinstance(x, Buf) else x)
        return out

    def _flush(self, en):
        if self.pend[en]:
            self.last_ins[en].then_inc(self.sem[en], 1)
            self.cnt[en] += 1
            self.pend[en] = False

    def _wait(self, en, key, sem, val):
        w = self.waited[en]
        if w.get(key, 0) >= val:
            return
        if key in self.engs and val > self.cnt[key]:
            assert self.pend[key] and val == self.cnt[key] + 1
            self._flush(key)
        self.engs[en].wait_ge(sem, val)
        w[key] = val

    def op(self, en, fn, reads=(), writes=(), dma=None):
        reads = self._regs(reads)
        writes = self._regs(writes)
        deps = {}
        for R in reads:
            for k, v in R.w.items():
                if deps.get(k, (None, 0))[1] < v[1]:
                    deps[k] = v
        for R in writes:
            for dd in (R.w, R.r):
                for k, v in dd.items():
                    if deps.get(k, (None, 0))[1] < v[1]:
                        deps[k] = v
        for k, (sem, val) in deps.items():
            if k == "pe" and en == "pe" and dma is None:
                continue
            self._wait(en, k, sem, val)
        if dma is None and en == "pe":
            wkey = tuple(id(R) for R in writes)
            if self.pend["pe"] and wkey != self.last_pe_w:
                self._flush("pe")
            self.last_pe_w = wkey
        ins = fn(self.pe_proxy if en == "pe" else self.engs[en])
        self.ninst += 1
        if dma is not None:
            R = dma.reg if isinstance(dma, Buf) else dma
            if R.sem is None:
                kind = "sw" if en == "pool" else "hw"
                if self.dfree[kind]:
                    R.sem = self.dfree[kind].pop()
                else:
                    R.sem = len(self.dpool)
                    self.dpool.append([self.es.enter_context(self.nc.semaphore(f"ds{R.sem}")), 0])
                    self.dkind[R.sem] = kind
                self.all_dma_regs.append(R)
            assert self.dkind[R.sem] == ("sw" if en == "pool" else "hw"), "mixed DMA queues on one region semaphore"
            ent = self.dpool[R.sem]
            ent[1] += 16
            ins.then_inc(ent[0], 16)
            key, tok = f"ds{R.sem}", (ent[0], ent[1])
        else:
            if en == "pe":
                self.last_ins[en] = ins
                self.pend[en] = True
                key, tok = en, (self.sem[en], self.cnt[en] + 1)
            else:
                self.cnt[en] += 1
                ins.then_inc(self.sem[en], 1)
                key, tok = en, (self.sem[en], self.cnt[en])
        for R in reads:
            R.r[key] = tok
        for R in writes:
            R.w[key] = tok
        return tok

    def _pe_pos(self, kap, oap):
        k0, kn = kap.base_partition(), kap.partition_size()
        kq = 32 if kn <= 32 else (64 if kn <= 64 else 128)
        m0, mn = oap.base_partition(), oap.partition_size()
        key = (k0, kq, m0, mn)
        if key in self.pe_live:
            return
        conflict = False
        for (a0, aq, b0, bn) in self.pe_live:
            if (a0, aq) != (k0, kq) and not (m0 + mn <= b0 or b0 + bn <= m0):
                conflict = True
                break
        if conflict:
            self._flush("pe")
            if self.cnt["pe"] > 0:
                self.engs["pe"].wait_ge(self.sem["pe"], self.cnt["pe"])
                self.waited["pe"]["pe"] = self.cnt["pe"]
            self.pe_live = set()
            self.nfence += 1
        self.pe_live.add(key)

    def barrier(self):
        for en in self.engs:
            self._flush(en)
        for en in self.engs:
            for o in self.engs:
                if o != en and self.cnt[o] > 0:
                    self._wait(en, o, self.sem[o], self.cnt[o])
            for i, ent in enumerate(self.dpool):
                if ent[1] > 0:
                    self._wait(en, f"ds{i}", ent[0], ent[1])
        for R in self.all_dma_regs:
            self.dfree[self.dkind[R.sem]].append(R.sem)
            R.sem = None
        self.all_dma_regs = []


def bc(ap, shape):
    return ap.to_broadcast(list(shape))


class _Stop(Exception):
    pass


class Prog:
    stop_at = None

    def _ck(self, n):
        return self.stop_at is not None and n == self.stop_at

    def __init__(self, seq_lens, debug=False):
        self.seq_lens = list(seq_lens)
        self.debug = debug
        self.nc = bass.Bass("TRN2", target_bir_lowering=False)
        self.es = ExitStack()
        self.kb = KB(self.nc, self.es)
        self.maxLx = max(self.seq_lens)
        self.maxLp = self.maxLx + 64

    def dram_in(self, name, shape, dt=F32):
        return self.nc.dram_tensor(name, list(shape), dt, kind="ExternalInput").ap()

    def dram_out(self, name, shape, dt=F32):
        return self.nc.dram_tensor(name, list(shape), dt, kind="ExternalOutput").ap()

    def dram_scr(self, name, shape, dt):
        kind = "ExternalOutput" if self.debug else "Internal"
        ap = self.nc.dram_tensor(name, list(shape), dt, kind=kind).ap()
        return Buf(ap, self.kb.reg(name))

    def declare(self):
        nseq = len(self.seq_lens)
        self.x_in = [self.dram_in(f"x{i}", [L, D]) for i, L in enumerate(self.seq_lens)]
        self.y_out = [self.dram_out(f"y{i}", [L, D]) for i, L in enumerate(self.seq_lens)]
        self.xin_reg = self.kb.reg("xin")
        self.yout_reg = self.kb.reg("yout")
        self.w = {}
        for name, shape in [
            ("meta_tokens", [16, D]), ("ln_in_g", [D]), ("ln_in_b", [D]), ("w_in", [D, NIN]),
            ("g_cq", [384]), ("g_ckv", [256]), ("w_uq", [384, 768]), ("w_uk", [256, 512]), ("w_uv", [256, 512]),
            ("conv_w", [4, 1536]), ("a_log_f", [8]), ("a_log_b", [8]), ("dt_bias_f", [8]), ("dt_bias_b", [8]),
            ("gdn_norm_g", [64]), ("w_out", [D, D]), ("ln1_g", [D]), ("ln1_b", [D]),
            ("w_ff1", [D, DFF]), ("w_ff2", [DFF, D]), ("ln2_g", [D]), ("ln2_b", [D]),
        ]:
            self.w[name] = self.dram_in(name, shape)
        self.w_reg = self.kb.reg("weights")
        self.Wb = {}
        for name, shape in [("w_in", [D, NIN]), ("w_uq", [384, 768]), ("w_uk", [256, 512]), ("w_uv", [256, 512]),
                            ("w_out", [D, D]), ("w_ff1", [D, DFF]), ("w_ff2", [DFF, D])]:
            self.Wb[name] = self.dram_scr("wb_" + name, shape, BF16)
        Lp, Lx = self.maxLp, self.maxLx
        self.QT = self.dram_scr("s_qt", [H, 96, Lx], BF16)
        self.KT = self.dram_scr("s_kt", [H, 96, Lp], BF16)
        self.V = self.dram_scr("s_v", [Lp, 512], BF16)
        self.QKVT = self.dram_scr("s_qkvt", [12, 128, Lp + 4], F32)
        self.Z = self.dram_scr("s_z", [Lp, 512], F32)
        self.GT = self.dram_scr("s_gt", [4, 8, Lp], F32)
        self.GQT = self.dram_scr("s_gqt", [4, 128, Lp], BF16)
        self.GKT = self.dram_scr("s_gkt", [4, 128, Lp], BF16)
        self.GKTOK = self.dram_scr("s_gktok", [Lp, 512], BF16)
        self.GVTOK = self.dram_scr("s_gvtok", [Lp, 512], BF16)
        self.OF = self.dram_scr("s_of", [Lp, 512], F32)
        self.OB = self.dram_scr("s_ob", [Lp, 512], F32)
        self.CATT = self.dram_scr("s_catt", [8, 128, Lx], BF16)
        self.H1 = self.dram_scr("s_h1", [Lx, D], F32)
        self.H1T = self.dram_scr("s_h1t", [8, 128, Lx], BF16)

    def dma(self, en, out_ap, in_ap, reads, writes, sem):
        return self.kb.op(en, lambda e: e.dma_start(out=out_ap, in_=in_ap), reads=reads, writes=writes, dma=sem)

    def rsqrt(self, out_buf, out_ap, in_buf, in_ap, scale, eps_ap, tmp_buf, tmp_ap):
        kb = self.kb
        kb.op("act", lambda e: e.activation(out=tmp_ap, in_=in_ap, func=AF.Ln, bias=eps_ap, scale=scale),
              reads=[in_buf, self.cst], writes=[tmp_buf])
        kb.op("act", lambda e: e.activation(out=out_ap, in_=tmp_ap, func=AF.Exp, scale=-0.5),
              reads=[tmp_buf], writes=[out_buf])

    def setup_consts(self):
        kb, nc = self.kb, self.nc
        es = self.es
        self.cst = kb.sb(es, "cst", [128, 16], F32)
        kb.op("dve", lambda e: e.memset(self.cst[:, 0:1], LN_EPS), writes=[self.cst])
        kb.op("dve", lambda e: e.memset(self.cst[:, 1:2], RMS_EPS), writes=[self.cst])
        kb.op("dve", lambda e: e.memset(self.cst[:, 2:3], 1.0), writes=[self.cst])
        kb.op("dve", lambda e: e.memset(self.cst[:, 3:4], 0.0), writes=[self.cst])
        self.io_i = kb.sb(es, "io_i", [128, 128], I32)
        self.io_f = kb.sb(es, "io_f", [128, 128], F32)
        kb.op("pool", lambda e: e.iota(self.io_i[:], [[1, 128]], base=0, channel_multiplier=-1), writes=[self.io_i])
        kb.op("dve", lambda e: e.tensor_copy(self.io_f[:], self.io_i[:]), reads=[self.io_i], writes=[self.io_f])
        self.ident_f = kb.sb(es, "ident_f", [128, 128], F32)
        self.ident_b = kb.sb(es, "ident_b", [128, 128], BF16)
        kb.op("dve", lambda e: e.tensor_single_scalar(self.ident_f[:], self.io_f[:], 0.0, ALU.is_equal),
              reads=[self.io_f], writes=[self.ident_f])
        kb.op("dve", lambda e: e.tensor_copy(self.ident_b[:], self.ident_f[:]), reads=[self.ident_f], writes=[self.ident_b])
        self.ones_f = kb.sb(es, "ones_f", [128, 128], F32)
        kb.op("dve", lambda e: e.memset(self.ones_f[:], 1.0), writes=[self.ones_f])
        for name, wb in self.Wb.items():
            src = self.w[name]
            nrow = src.shape[0]
            step = 256
            for r0 in range(0, nrow, step):
                r1 = min(nrow, r0 + step)
                self.dma("pool", wb[r0:r1, :], src[r0:r1, :], [self.w_reg], [wb], wb)
        kb.barrier()


    def finish(self):
        self.kb.barrier()
        self.es.close()

    def load_bcast_vec(self, st, name, vec_ap, n):
        b = self.kb.sb(st, name, [128, n], F32)
        self.dma("sp", b[:, :], vec_ap.partition_broadcast(128), reads=[self.w_reg], writes=[b], sem=b)
        return b

    def load_x_tile(self, si, t, xt):
        kb = self.kb
        Lx = self.seq_lens[si]
        x = self.x_in[si]
        if t == 0:
            kb.op("dve", lambda e: e.memset(xt[0:48, :], 0.0), writes=[xt])
            self.dma("sp", xt[48:64, :], self.w["meta_tokens"][:, :], reads=[self.w_reg], writes=[xt], sem=xt)
            self.dma("sp", xt[64:128, :], x[0:64, :], reads=[self.xin_reg], writes=[xt], sem=xt)
            return 128
        r0 = 128 * t - 64
        nt = min(128, Lx - r0)
        self.dma("sp", xt[0:nt, :], x[r0:r0 + nt, :], reads=[self.xin_reg], writes=[xt], sem=xt)
        return nt

    def layer_norm_gen(self, xin, xin_ap, g_bc, b_bc, out_buf, out_ap, tmp, stats):
        kb = self.kb
        for c in range(2):
            kb.op("dve", lambda e, c=c: e.bn_stats(stats[:, 6 * c:6 * c + 6], xin_ap[:, 512 * c:512 * c + 512]),
                  reads=[xin], writes=[stats])
        yield
        kb.op("dve", lambda e: e.bn_aggr(stats[:, 16:18], stats[:, 0:12].rearrange("p (a b) -> p a b", a=2)),
              reads=[stats], writes=[stats])
        yield
        kb.op("act", lambda e: e.activation(out=stats[:, 20:21], in_=stats[:, 17:18], func=AF.Ln,
                                            bias=self.cst[:, 0:1], scale=1.0), reads=[stats, self.cst], writes=[stats])
        yield
        kb.op("act", lambda e: e.activation(out=stats[:, 21:22], in_=stats[:, 20:21], func=AF.Exp, scale=-0.5),
              reads=[stats], writes=[stats])
        yield
        kb.op("dve", lambda e: e.scalar_tensor_tensor(stats[:, 22:23], stats[:, 16:17], -1.0, stats[:, 21:22], ALU.mult, ALU.mult),
              reads=[stats], writes=[stats])
        yield
        kb.op("act", lambda e: e.activation(out=tmp[:, :], in_=xin_ap, func=AF.Identity, bias=stats[:, 22:23], scale=stats[:, 21:22]),
              reads=[xin, stats], writes=[tmp])
        yield
        kb.op("dve", lambda e: e.tensor_tensor(tmp[:, :], tmp[:, :], g_bc[:, :], ALU.mult),
              reads=[tmp, g_bc], writes=[tmp])
        yield
        kb.op("dve", lambda e: e.tensor_tensor(out_ap, tmp[:, :], b_bc[:, :], ALU.add),
              reads=[tmp, b_bc], writes=[out_buf])

    @staticmethod
    def run_gens(*gens):
        gens = [g for g in gens if g is not None]
        while gens:
            for g in list(gens):
                try:
                    next(g)
                except StopIteration:
                    gens.remove(g)

    def layer_norm(self, *a):
        self.run_gens(self.layer_norm_gen(*a))

    def transpose_tile(self, src, src_ap_fn, pT, dst, dst_ap_fn, nk=8):
        kb = self.kb
        for k in range(nk):
            kb.op("pe", lambda e, k=k: e.transpose(pT[:, k, :], src_ap_fn(k), self.ident_b[:, :]),
                  reads=[src, self.ident_b], writes=[pT])
        h = nk // 2
        kb.op("act", lambda e: e.copy(dst_ap_fn(0, h), pT[:, 0:h, :]), reads=[pT], writes=[dst])
        kb.op("dve", lambda e: e.tensor_copy(dst_ap_fn(h, nk), pT[:, h:nk, :]), reads=[pT], writes=[dst])

    def phase1(self, si):
        kb, nc = self.kb, self.nc
        Lx = self.seq_lens[si]
        Lp = Lx + 64
        W = self.w
        with ExitStack() as st:
            w_in = kb.sb(st, "w_in", [128, 8, NIN], BF16)
            wqA = kb.sb(st, "wqA", [128, 3, 8, 96], BF16)
            wqB = kb.sb(st, "wqB", [128, 3, 8, 96], BF16)
            wkrA = kb.sb(st, "wkrA", [128, 8, 96], BF16)
            wkrB = kb.sb(st, "wkrB", [128, 8, 96], BF16)
            w_uk = kb.sb(st, "w_uk", [128, 2, 512], BF16)
            w_uv = kb.sb(st, "w_uv", [128, 2, 512], BF16)
            kb.op("dve", lambda e: e.memset(wqB[:], 0.0), writes=[wqB])
            kb.op("dve", lambda e: e.memset(wkrA[:], 0.0), writes=[wkrA])
            kb.op("dve", lambda e: e.memset(wkrB[:], 0.0), writes=[wkrB])
            for kc in range(8):
                rows = slice(kc * 128, kc * 128 + 128)
                self.dma("sp", w_in[:, kc, :], self.Wb["w_in"][rows, :], [self.Wb["w_in"]], [w_in], w_in)
                self.dma("sp", wkrA[:, kc, 64:96], self.Wb["w_in"][rows, C_KR0:C_KR0 + 32], [self.Wb["w_in"]], [wkrA], wkrA)
                self.dma("sp", wkrB[:, kc, 64:80], self.Wb["w_in"][rows, C_KR0 + 16:C_KR0 + 32], [self.Wb["w_in"]], [wkrB], wkrB)
                self.dma("sp", wkrB[:, kc, 80:96], self.Wb["w_in"][rows, C_KR0:C_KR0 + 16], [self.Wb["w_in"]], [wkrB], wkrB)
            for kc in range(3):
                rows = slice(kc * 128, kc * 128 + 128)
                wq3 = self.Wb["w_uq"][rows, :].rearrange("p (h c) -> p h c", c=96)
                self.dma("sp", wqA[:, kc, :, :], wq3, [self.Wb["w_uq"]], [wqA], wqA)
                self.dma("sp", wqB[:, kc, :, 64:80], wq3[:, :, 80:96], [self.Wb["w_uq"]], [wqB], wqB)
                self.dma("sp", wqB[:, kc, :, 80:96], wq3[:, :, 64:80], [self.Wb["w_uq"]], [wqB], wqB)
            for kc in range(2):
                rows = slice(kc * 128, kc * 128 + 128)
                self.dma("sp", w_uk[:, kc, :], self.Wb["w_uk"][rows, :], [self.Wb["w_uk"]], [w_uk], w_uk)
                self.dma("sp", w_uv[:, kc, :], self.Wb["w_uv"][rows, :], [self.Wb["w_uv"]], [w_uv], w_uv)
            lng = self.load_bcast_vec(st, "lng", W["ln_in_g"], D)
            lnb = self.load_bcast_vec(st, "lnb", W["ln_in_b"], D)
            gcq = kb.sb(st, "gcq", [128, 3], F32)
            gckv = kb.sb(st, "gckv", [128, 2], F32)
            for kc in range(3):
                self.dma("sp", gcq[:, kc:kc + 1], W["g_cq"][kc * 128:kc * 128 + 128].rearrange("(p o) -> p o", o=1),
                         [self.w_reg], [gcq], gcq)
            for kc in range(2):
                self.dma("sp", gckv[:, kc:kc + 1], W["g_ckv"][kc * 128:kc * 128 + 128].rearrange("(p o) -> p o", o=1),
                         [self.w_reg], [gckv], gckv)
            gpar = kb.sb(st, "gpar", [128, 4], F32)
            kb.op("dve", lambda e: e.memset(gpar[:], 0.0), writes=[gpar])
            for off, sfx in ((0, "f"), (32, "b")):
                self.dma("sp", gpar[off:off + 8, 0:1], W["dt_bias_" + sfx].rearrange("(p o) -> p o", o=1),
                         [self.w_reg], [gpar], gpar)
                self.dma("sp", gpar[off:off + 8, 1:2], W["a_log_" + sfx].rearrange("(p o) -> p o", o=1),
                         [self.w_reg], [gpar], gpar)
            kb.op("act", lambda e: e.activation(out=gpar[0:40, 2:3], in_=gpar[0:40, 1:2], func=AF.Exp),
                  reads=[gpar], writes=[gpar])
            kb.op("dve", lambda e: e.tensor_scalar_mul(gpar[0:40, 2:3], gpar[0:40, 2:3], -1.0), reads=[gpar], writes=[gpar])
            ropeC = kb.sb(st, "ropeC", [128, 512], F32)
            ropeS = kb.sb(st, "ropeS", [128, 512], F32)
            rp = kb.sb(st, "rp", [128, 8], F32)
            rpi = kb.sb(st, "rpi", [128, 2], I32)
            R = slice(64, 96)
            kb.op("pool", lambda e: e.iota(rpi[R, 0:1], [[0, 1]], base=0, channel_multiplier=1), writes=[rpi])
            kb.op("dve", lambda e: e.tensor_copy(rp[R, 0:1], rpi[R, 0:1]), reads=[rpi], writes=[rp])
            kb.op("dve", lambda e: e.tensor_single_scalar(rp[R, 1:2], rp[R, 0:1], 16.0, ALU.is_ge), reads=[rp], writes=[rp])
            kb.op("dve", lambda e: e.scalar_tensor_tensor(rp[R, 2:3], rp[R, 1:2], -16.0, rp[R, 0:1], ALU.mult, ALU.add),
                  reads=[rp], writes=[rp])
            kb.op("act", lambda e: e.activation(out=rp[R, 3:4], in_=rp[R, 2:3], func=AF.Exp,
                                                scale=-float(np.log(10000.0)) / 16.0), reads=[rp], writes=[rp])
            kb.op("dve", lambda e: e.tensor_scalar_mul(rp[R, 3:4], rp[R, 3:4], float(1.0 / (2.0 * np.pi))),
                  reads=[rp], writes=[rp])
            kb.op("dve", lambda e: e.tensor_scalar(rp[R, 4:5], rp[R, 1:2], float(4.0 * np.pi), float(-2.0 * np.pi),
                                                   ALU.mult, ALU.add), reads=[rp], writes=[rp])
            kb.op("dve", lambda e: e.memset(rp[R, 5:6], float(2.0 * np.pi)), writes=[rp])
            posi = kb.sb(st, "posi", [128, 512], I32)
            posf = kb.sb(st, "posf", [128, 512], F32)
            ru = kb.sb(st, "ru", [128, 512], F32)
            rui = kb.sb(st, "rui", [128, 512], I32)
            ruf = kb.sb(st, "ruf", [128, 512], F32)

            def rope_tables(c):
                kb.op("pool", lambda e: e.iota(posi[R, :], [[1, 512]], base=512 * c - 48, channel_multiplier=0),
                      writes=[posi])
                kb.op("dve", lambda e: e.tensor_copy(posf[R, :], posi[R, :]), reads=[posi], writes=[posf])
                for tab, off, sc in ((ropeC, 0.25, 5), (ropeS, 0.0, 4)):
                    kb.op("dve", lambda e, off=off: e.tensor_scalar(ru[R, :], posf[R, :], rp[R, 3:4], off, ALU.mult, ALU.add),
                          reads=[posf, rp], writes=[ru])
                    kb.op("dve", lambda e: e.tensor_copy(rui[R, :], ru[R, :]), reads=[ru], writes=[rui])
                    kb.op("dve", lambda e: e.tensor_copy(ruf[R, :], rui[R, :]), reads=[rui], writes=[ruf])
                    kb.op("dve", lambda e: e.tensor_tensor(ru[R, :], ru[R, :], ruf[R, :], ALU.subtract),
                          reads=[ru, ruf], writes=[ru])
                    kb.op("act", lambda e, tab=tab, sc=sc: e.activation(
                        out=tab[R, :], in_=ru[R, :], func=AF.Sin, scale=rp[R, sc:sc + 1]),
                        reads=[ru, rp], writes=[tab])
            xts = [kb.sb(st, f"xt{i}", [128, D], F32) for i in range(4)]
            for xt in xts:
                kb.op("pool", lambda e, xt=xt: e.memset(xt[:], 0.0), writes=[xt])
            statss = [kb.sb(st, f"stats{i}", [128, 32], F32) for i in range(4)]
            hbs = [kb.sb(st, f"hb{i}", [128, D], BF16) for i in range(4)]
            hTs = [kb.sb(st, f"hT{i}", [128, 8, 512], BF16) for i in range(2)]
            stage = kb.sb(st, "stage", [128, 12, 512], F32)
            cq = kb.sb(st, "cq", [128, 3, 512], F32)
            sq = kb.sb(st, "sq", [128, 3, 512], F32)
            rstd = kb.sb(st, "rstd", [128, 512], F32)
            rtmp = kb.sb(st, "rtmp", [128, 512], F32)
            cqn = kb.sb(st, "cqn", [128, 3, 512], BF16)
            ckvn = kb.sb(st, "ckvn", [128, 2, 512], BF16)
            qs = kb.sb(st, "qs", [128, 8, 512], BF16)
            ks = kb.sb(st, "ks", [128, 8, 512], BF16)
            t1 = kb.sb(st, "t1", [128, 512], F32)
            t2 = kb.sb(st, "t2", [128, 512], F32)
            vs = kb.sb(st, "vs", [128, 512], BF16)
            zs = kb.sb(st, "zs", [128, 512], F32)
            ze = kb.sb(st, "ze", [128, 512], F32)
            gs = kb.sb(st, "gs", [128, 512], F32)
            ga = kb.sb(st, "ga", [128, 512], F32)
            gb = kb.sb(st, "gb", [128, 512], F32)
            lnq = kb.sb(st, "lnq", [128, 1], F32)
            kb.op("dve", lambda e: e.memset(lnq[:], float(np.log(QSCALE))), writes=[lnq])
            kb.op("dve", lambda e: e.memset(gs[:], 0.0), writes=[gs])
            gs2 = kb.sb(st, "gs2", [128, 512], F32)
            kb.op("dve", lambda e: e.memset(gs2[:], 0.0), writes=[gs2])
            pT = kb.ps(st, "pT", [128, 8, 128], BF16)
            pb = [kb.ps(st, f"pb{i}", [128, 512], F32) for i in range(7)]
            self._pbi = 0

            def bank():
                self._pbi = (self._pbi + 1) % 7
                return pb[self._pbi]

            nblk = (Lp + 511) // 512

            def ln_block(b):
                p0_ = 512 * b
                ntok_ = min(512, Lp - p0_)
                hT_ = hTs[b % 2]
                ntl_ = (ntok_ + 127) // 128
                gens = []
                for tl in range(ntl_):
                    xt = xts[tl]
                    self.load_x_tile(si, 4 * b + tl, xt)
                    gens.append(self.layer_norm_gen(xt, xt[:, :], lng, lnb, hbs[tl], hbs[tl][:, :], xt, statss[tl]))
                self.run_gens(*gens)
                for tl in range(ntl_):
                    hb = hbs[tl]
                    self.transpose_tile(hb, lambda k, hb=hb: hb[:, k * 128:(k + 1) * 128], pT, hT_,
                                        lambda k0, k1, tl=tl: hT_[:, k0:k1, tl * 128:(tl + 1) * 128])

            ln_block(0)
            for b in range(nblk):
                p0 = 512 * b
                ntok = min(512, Lp - p0)
                ntl = (ntok + 127) // 128
                hT = hTs[b % 2]
                if b + 1 < nblk:
                    ln_block(b + 1)
                N = slice(0, ntok)

                def proj(col0, m, out_ap_fn=None):
                    bk = bank()
                    o = bk[0:m, N] if out_ap_fn is None else out_ap_fn(bk)
                    for kc in range(8):
                        kb.op("pe", lambda e, kc=kc: e.matmul(o, w_in[:, kc, col0:col0 + m], hT[:, kc, N],
                                                               start=(kc == 0), stop=(kc == 7)),
                              reads=[w_in, hT], writes=[bk])
                    return bk

                for mc in range(12):
                    bk = proj(C_QKV0 + mc * 128, 128)
                    if mc % 2 == 0:
                        kb.op("act", lambda e, bk=bk, mc=mc: e.copy(stage[:, mc, N], bk[:, N]), reads=[bk], writes=[stage])
                    else:
                        kb.op("dve", lambda e, bk=bk, mc=mc: e.tensor_copy(stage[:, mc, N], bk[:, N]), reads=[bk], writes=[stage])
                self.dma("sp", self.QKVT[:, :, 1 + p0:1 + p0 + ntok].rearrange("c p t -> p c t"), stage[:, :, N],
                         [stage], [self.QKVT], stage)

                def rms_fm(col0, nch, gpp, outn, lnbias):
                    banks = [proj(col0 + mc * 128, 128) for mc in range(nch)]
                    for mc, bk in enumerate(banks):
                        kb.op("act", lambda e, bk=bk, mc=mc: e.copy(cq[:, mc, N], bk[:, N]), reads=[bk], writes=[cq])
                        kb.op("act", lambda e, bk=bk, mc=mc: e.activation(out=sq[:, mc, N], in_=bk[:, N], func=AF.Square),
                              reads=[bk], writes=[sq])
                    bs = bank()
                    for mc in range(nch):
                        kb.op("pe", lambda e, mc=mc: e.matmul(bs[:, N], self.ones_f[:, :], sq[:, mc, N],
                                                               start=(mc == 0), stop=(mc == nch - 1)),
                              reads=[self.ones_f, sq], writes=[bs])
                    kb.op("act", lambda e: e.activation(out=rtmp[:, N], in_=bs[:, N], func=AF.Ln, bias=self.cst[:, 1:2],
                                                        scale=1.0 / (nch * 128)), reads=[bs, self.cst], writes=[rtmp])
                    if lnbias is None:
                        kb.op("act", lambda e: e.activation(out=rstd[:, N], in_=rtmp[:, N], func=AF.Exp, scale=-0.5),
                              reads=[rtmp], writes=[rstd])
                    else:
                        kb.op("act", lambda e: e.activation(out=rstd[:, N], in_=rtmp[:, N], func=AF.Exp, scale=-0.5,
                                                            bias=lnbias[:, 0:1]), reads=[rtmp, lnbias], writes=[rstd])
                    for mc in range(nch):
                        kb.op("dve", lambda e, mc=mc: e.scalar_tensor_tensor(outn[:, mc, N], cq[:, mc, N], gpp[:, mc:mc + 1],
                                                                             rstd[:, N], ALU.mult, ALU.mult),
                              reads=[cq, gpp, rstd], writes=[outn])

                rms_fm(C_KV0, 2, gckv, ckvn, None)
                for h in range(H):
                    bk = bank()
                    for kc in range(2):
                        kb.op("pe", lambda e, kc=kc, h=h, bk=bk: e.matmul(bk[0:64, N], w_uk[:, kc, h * 64:(h + 1) * 64],
                                                                          ckvn[:, kc, N], start=(kc == 0), stop=(kc == 1)),
                              reads=[w_uk, ckvn], writes=[bk])
                    if h % 2 == 0:
                        kb.op("act", lambda e, bk=bk, h=h: e.copy(ks[0:64, h, N], bk[0:64, N]), reads=[bk], writes=[ks])
                    else:
                        kb.op("dve", lambda e, bk=bk, h=h: e.tensor_copy(ks[0:64, h, N], bk[0:64, N]), reads=[bk], writes=[ks])
                RR = slice(64, 96)
                P = slice(p0, p0 + ntok)
                rope_tables(b)

                def rope_pair(bkA, bkB, dst_fn):
                    kb.op("dve", lambda e: e.tensor_tensor(t1[RR, N], bkA[RR, N], ropeC[RR, N], ALU.mult),
                          reads=[bkA, ropeC], writes=[t1])
                    kb.op("dve", lambda e: e.tensor_tensor(t2[RR, N], bkB[RR, N], ropeS[RR, N], ALU.mult),
                          reads=[bkB, ropeS], writes=[t2])
                    dst_fn()

                bkA = bank()
                bkB = bank()
                for kc in range(8):
                    kb.op("pe", lambda e, kc=kc: e.matmul(bkA[0:96, N], wkrA[:, kc, :], hT[:, kc, N], start=(kc == 0), stop=(kc == 7)),
                          reads=[wkrA, hT], writes=[bkA])
                for kc in range(8):
                    kb.op("pe", lambda e, kc=kc: e.matmul(bkB[0:96, N], wkrB[:, kc, :], hT[:, kc, N], start=(kc == 0), stop=(kc == 7)),
                          reads=[wkrB, hT], writes=[bkB])

                def kdst():
                    kb.op("dve", lambda e: e.tensor_tensor(t1[RR, N], t1[RR, N], t2[RR, N], ALU.add), reads=[t1, t2], writes=[t1])
                    kb.op("act", lambda e: e.copy(ks[RR, 0:4, N], t1[RR, N].unsqueeze(1).to_broadcast([32, 4, ntok])),
                          reads=[t1], writes=[ks])
                    kb.op("dve", lambda e: e.tensor_copy(ks[RR, 4:8, N], t1[RR, N].unsqueeze(1).to_broadcast([32, 4, ntok])),
                          reads=[t1], writes=[ks])
                rope_pair(bkA, bkB, kdst)
                self.dma("sp", self.KT[:, :, P].rearrange("h r t -> r h t"), ks[0:96, :, N], [ks], [self.KT], ks)
                for tl in range(ntl):
                    nt = min(128, ntok - tl * 128)
                    bk = bank()
                    for kc in range(2):
                        kb.op("pe", lambda e, kc=kc, tl=tl, nt=nt, bk=bk: e.matmul(
                            bk[0:nt, :], ckvn[:, kc, tl * 128:tl * 128 + nt], w_uv[:, kc, :], start=(kc == 0), stop=(kc == 1)),
                            reads=[ckvn, w_uv], writes=[bk])
                    kb.op("act", lambda e, bk=bk, nt=nt: e.copy(vs[0:nt, :], bk[0:nt, :]), reads=[bk], writes=[vs])
                    self.dma("sp", self.V[p0 + tl * 128:p0 + tl * 128 + nt, :], vs[0:nt, :], [vs], [self.V], vs)

                c0 = 64 if b == 0 else 0
                if ntok > c0:
                    rms_fm(C_Q0, 3, gcq, cqn, lnq)
                    for h in range(H):
                        bkA = bank()
                        bkB = bank()
                        for kc in range(3):
                            kb.op("pe", lambda e, kc=kc, h=h, bkA=bkA: e.matmul(bkA[0:96, N], wqA[:, kc, h, :], cqn[:, kc, N],
                                                                                 start=(kc == 0), stop=(kc == 2)),
                                  reads=[wqA, cqn], writes=[bkA])
                        for kc in range(3):
                            kb.op("pe", lambda e, kc=kc, h=h, bkB=bkB: e.matmul(bkB[0:96, N], wqB[:, kc, h, :], cqn[:, kc, N],
                                                                                 start=(kc == 0), stop=(kc == 2)),
                                  reads=[wqB, cqn], writes=[bkB])
                        kb.op("act", lambda e, bkA=bkA, h=h: e.copy(qs[0:64, h, N], bkA[0:64, N]), reads=[bkA], writes=[qs])

                        def qdst(h=h):
                            kb.op("dve", lambda e: e.tensor_tensor(qs[RR, h, N], t1[RR, N], t2[RR, N], ALU.add),
                                  reads=[t1, t2], writes=[qs])
                        rope_pair(bkA, bkB, qdst)
                    self.dma("sp", self.QT[:, :, p0 + c0 - 64:p0 + ntok - 64].rearrange("h r t -> r h t"),
                             qs[0:96, :, c0:ntok], [qs], [self.QT], qs)

                for tl in range(ntl):
                    nt = min(128, ntok - tl * 128)
                    bk = bank()
                    for kc in range(8):
                        kb.op("pe", lambda e, kc=kc, tl=tl, nt=nt, bk=bk: e.matmul(
                            bk[0:nt, :], hT[:, kc, tl * 128:tl * 128 + nt], w_in[:, kc, C_Z0:C_Z0 + 512],
                            start=(kc == 0), stop=(kc == 7)), reads=[hT, w_in], writes=[bk])
                    kb.op("act", lambda e, bk=bk, nt=nt: e.activation(out=ze[0:nt, :], in_=bk[0:nt, :], func=AF.Exp, scale=-1.0),
                          reads=[bk], writes=[ze])
                    kb.op("act", lambda e, nt=nt: e.activation(out=ze[0:nt, :], in_=ze[0:nt, :], func=AF.Ln, bias=self.cst[0:nt, 2:3], scale=1.0),
                          reads=[ze, self.cst], writes=[ze])
                    kb.op("act", lambda e, nt=nt: e.activation(out=ze[0:nt, :], in_=ze[0:nt, :], func=AF.Exp, scale=-1.0), reads=[ze], writes=[ze])
                    kb.op("dve", lambda e, bk=bk, nt=nt: e.tensor_tensor(zs[0:nt, :], bk[0:nt, :], ze[0:nt, :], ALU.mult),
                          reads=[bk, ze], writes=[zs])
                    self.dma("sp", self.Z[p0 + tl * 128:p0 + tl * 128 + nt, :], zs[0:nt, :], [zs], [self.Z], zs)

                bk = bank()
                bk2 = bank()
                for g in range(4):
                    bb = bk if g < 2 else bk2
                    ro = 32 * (g % 2)
                    for kc in range(8):
                        kb.op("pe", lambda e, kc=kc, g=g, bb=bb, ro=ro: e.matmul(
                            bb[ro:ro + 8, N], w_in[:, kc, C_G0 + 8 * g:C_G0 + 8 * g + 8], hT[:, kc, N],
                            start=(kc == 0), stop=(kc == 7)), reads=[w_in, hT], writes=[bb])
                A = slice(0, 40)
                kb.op("dve", lambda e: e.tensor_scalar(ga[A, N], bk[A, N], gpar[A, 0:1], None, ALU.add), reads=[bk, gpar], writes=[ga])
                kb.op("act", lambda e: e.activation(out=gb[A, N], in_=ga[A, N], func=AF.Abs), reads=[ga], writes=[gb])
                kb.op("act", lambda e: e.activation(out=gb[A, N], in_=gb[A, N], func=AF.Exp, scale=-1.0), reads=[gb], writes=[gb])
                kb.op("act", lambda e: e.activation(out=gb[A, N], in_=gb[A, N], func=AF.Ln, bias=self.cst[A, 2:3], scale=1.0),
                      reads=[gb, self.cst], writes=[gb])
                kb.op("dve", lambda e: e.scalar_tensor_tensor(ga[A, N], ga[A, N], 0.0, gb[A, N], ALU.max, ALU.add),
                      reads=[ga, gb], writes=[ga])
                kb.op("dve", lambda e: e.tensor_scalar(gs[A, N], ga[A, N], gpar[A, 2:3], None, ALU.mult), reads=[ga, gpar], writes=[gs])
                kb.op("act", lambda e: e.activation(out=gb[A, N], in_=bk2[A, N], func=AF.Exp, scale=-1.0), reads=[bk2], writes=[gb])
                kb.op("act", lambda e: e.activation(out=gb[A, N], in_=gb[A, N], func=AF.Ln, bias=self.cst[A, 2:3], scale=1.0),
                      reads=[gb, self.cst], writes=[gb])
                kb.op("act", lambda e: e.activation(out=gs2[A, N], in_=gb[A, N], func=AF.Exp, scale=-1.0), reads=[gb], writes=[gs2])
                if b == 0:
                    kb.op("dve", lambda e: e.memset(gs[A, 0:48], 0.0), writes=[gs])
                    kb.op("dve", lambda e: e.memset(gs2[A, 0:48], 0.0), writes=[gs2])
                for g in range(4):
                    src = gs if g < 2 else gs2
                    ro = 32 * (g % 2)
                    self.dma("sp", self.GT[g, :, P], src[ro:ro + 8, N], [src], [self.GT], src)
        kb.barrier()

    def phase2a(self, si):
        kb = self.kb
        Lx = self.seq_lens[si]
        Lp = Lx + 64
        W = self.w
        with ExitStack() as st:
            cw = kb.sb(st, "cw", [128, 12, 4], F32)
            for k in range(4):
                self.kb.op("sp", lambda e, k=k: e.dma_start(out=cw[:, :, k:k + 1],
                                                            in_=W["conv_w"][k, :].rearrange("(c p o) -> p c o", p=128, o=1),
                                                            allow_slow_non_contiguous=True),
                           reads=[self.w_reg], writes=[cw], dma=cw)
            blk = kb.sb(st, "blk", [128, 128], F32)
            kb.op("dve", lambda e: e.memset(blk[:], 0.0), writes=[blk])
            kb.op("dve", lambda e: e.memset(blk[0:64, 0:64], 1.0), writes=[blk])
            kb.op("dve", lambda e: e.memset(blk[64:128, 64:128], 1.0), writes=[blk])
            raw = kb.sb(st, "raw", [128, 12, 516], F32)
            accs = [kb.sb(st, f"acc{i}", [128, 512], F32) for i in range(3)]
            ss = [kb.sb(st, f"s{i}", [128, 512], F32) for i in range(3)]
            sqs = [kb.sb(st, f"sq{i}", [128, 512], F32) for i in range(3)]
            rns = [kb.sb(st, f"rn{i}", [128, 512], F32) for i in range(3)]
            rtmps = [kb.sb(st, f"rtmp{i}", [128, 512], F32) for i in range(3)]
            qn = kb.sb(st, "qn", [128, 4, 512], BF16)
            kn = kb.sb(st, "kn", [128, 4, 512], BF16)
            vn = kb.sb(st, "vn", [128, 4, 512], BF16)
            ktok = kb.sb(st, "ktok", [128, 512], BF16)
            vtok = kb.sb(st, "vtok", [128, 512], BF16)
            pT = kb.ps(st, "pT", [128, 4, 128], BF16)
            pb = [kb.ps(st, f"pb{i}", [128, 512], F32) for i in range(3)]
            nblk = (Lp + 511) // 512
            for b in range(nblk):
                p0 = 512 * b
                ntok = min(512, Lp - p0)
                N = slice(0, ntok)
                P = slice(p0, p0 + ntok)
                j0 = 1 if b == 0 else 0
                j1 = min(ntok + 3, Lp + 1 - p0)
                self.dma("sp", raw[:, :, j0:j1], self.QKVT[:, :, p0 + j0:p0 + j1].rearrange("c p t -> p c t"),
                         [self.QKVT], [raw], raw)
                if b == 0:
                    kb.op("dve", lambda e: e.memset(raw[:, :, 0:49], 0.0), writes=[raw])
                if b == nblk - 1:
                    kb.op("dve", lambda e: e.memset(raw[:, :, ntok + 1:ntok + 3], 0.0), writes=[raw])
                def stage1(ci):
                    acc = accs[ci % 3]
                    s_ = ss[ci % 3]
                    kb.op("dve", lambda e: e.tensor_scalar(acc[:, N], raw[:, ci, 0:ntok], cw[:, ci, 0:1], None, ALU.mult),
                          reads=[raw, cw], writes=[acc])
                    for k in range(1, 4):
                        kb.op("dve", lambda e, k=k: e.scalar_tensor_tensor(
                            acc[:, N], raw[:, ci, k:k + ntok], cw[:, ci, k:k + 1], acc[:, N], ALU.mult, ALU.add),
                            reads=[raw, cw, acc], writes=[acc])
                    kb.op("act", lambda e: e.activation(out=s_[:, N], in_=acc[:, N], func=AF.Silu), reads=[acc], writes=[s_])
                    if b == 0:
                        kb.op("pool", lambda e: e.memset(s_[:, 0:48], 0.0), writes=[s_])
                    if ci < 8:
                        sq_, rt_, rn_ = sqs[ci % 3], rtmps[ci % 3], rns[ci % 3]
                        kb.op("act", lambda e: e.activation(out=sq_[:, N], in_=s_[:, N], func=AF.Square), reads=[s_], writes=[sq_])
                        bk = pb[ci % 3]
                        kb.op("pe", lambda e: e.matmul(bk[:, N], blk[:, :], sq_[:, N], start=True, stop=True),
                              reads=[blk, sq_], writes=[bk])
                        kb.op("act", lambda e: e.activation(out=rt_[:, N], in_=bk[:, N], func=AF.Ln, bias=self.cst[:, 1:2], scale=1.0),
                              reads=[bk, self.cst], writes=[rt_])
                        kb.op("act", lambda e: e.activation(out=rn_[:, N], in_=rt_[:, N], func=AF.Exp, scale=-0.5), reads=[rt_], writes=[rn_])

                def stage2(ci):
                    s_ = ss[ci % 3]
                    if ci < 4:
                        rn_ = rns[ci % 3]
                        kb.op("dve", lambda e: e.scalar_tensor_tensor(qn[:, ci, N], s_[:, N], 0.125, rn_[:, N], ALU.mult, ALU.mult),
                              reads=[s_, rn_], writes=[qn])
                    elif ci < 8:
                        rn_ = rns[ci % 3]
                        kb.op("dve", lambda e: e.tensor_tensor(kn[:, ci - 4, N], s_[:, N], rn_[:, N], ALU.mult),
                              reads=[s_, rn_], writes=[kn])
                    else:
                        kb.op("act", lambda e: e.copy(vn[:, ci - 8, N], s_[:, N]), reads=[s_], writes=[vn])

                stage1(0)
                for ci in range(12):
                    if ci + 1 < 12:
                        stage1(ci + 1)
                    stage2(ci)
                self.dma("sp", self.GQT[:, :, P].rearrange("c p t -> p c t"), qn[:, :, N], [qn], [self.GQT], qn)
                self.dma("sp", self.GKT[:, :, P].rearrange("c p t -> p c t"), kn[:, :, N], [kn], [self.GKT], kn)
                for tl in range((ntok + 127) // 128):
                    nt = min(128, ntok - tl * 128)
                    rows = slice(p0 + tl * 128, p0 + tl * 128 + nt)
                    for src, dstb, dram in ((kn, ktok, self.GKTOK), (vn, vtok, self.GVTOK)):
                        for j in range(4):
                            kb.op("pe", lambda e, j=j, src=src, tl=tl, nt=nt: e.transpose(
                                pT[0:nt, j, :], src[:, j, tl * 128:tl * 128 + nt], self.ident_b[:, :]),
                                reads=[src, self.ident_b], writes=[pT])
                        kb.op("act" if src is kn else "dve",
                              (lambda e, dstb=dstb, nt=nt: e.copy(dstb[0:nt, :], pT[0:nt, :, :])) if src is kn else
                              (lambda e, dstb=dstb, nt=nt: e.tensor_copy(dstb[0:nt, :], pT[0:nt, :, :])),
                              reads=[pT], writes=[dstb])
                        self.dma("sp", dram[rows, :], dstb[0:nt, :], [dstb], [dram], dstb)
        kb.barrier()

    def phase2b(self, si):
        kb = self.kb
        Lx = self.seq_lens[si]
        Lp = Lx + 64
        W = self.w
        with ExitStack() as st:
            dm_i = kb.sb(st, "dm_i", [128, 512], I32)
            dmask = kb.sb(st, "dmask", [128, 512], F32)
            dtmp = kb.sb(st, "dtmp", [128, 512], F32)
            kb.op("pool", lambda e: e.iota(dm_i[0:8, :], [[1, 512]], base=0, channel_multiplier=-64), writes=[dm_i])
            kb.op("dve", lambda e: e.tensor_copy(dtmp[0:8, :], dm_i[0:8, :]), reads=[dm_i], writes=[dtmp])
            kb.op("dve", lambda e: e.tensor_single_scalar(dmask[0:8, :], dtmp[0:8, :], 0.0, ALU.is_ge), reads=[dtmp], writes=[dmask])
            kb.op("dve", lambda e: e.tensor_single_scalar(dtmp[0:8, :], dtmp[0:8, :], 64.0, ALU.is_lt), reads=[dtmp], writes=[dtmp])
            kb.op("dve", lambda e: e.tensor_tensor(dmask[0:8, :], dmask[0:8, :], dtmp[0:8, :], ALU.mult), reads=[dmask, dtmp], writes=[dmask])
            csm = kb.sb(st, "csm", [128, 64], F32)
            kb.op("dve", lambda e: e.tensor_copy(csm[0:64, :], self.io_f[0:64, 0:64]), reads=[self.io_f], writes=[csm])
            kb.op("dve", lambda e: e.tensor_copy(csm[64:128, :], self.io_f[64:128, 64:128]), reads=[self.io_f], writes=[csm])
            identp = kb.sb(st, "identp", [128, 64], BF16)
            identpf = kb.sb(st, "identpf", [128, 64], F32)
            negoff = kb.sb(st, "negoff", [128, 64], F32)
            kb.op("dve", lambda e: e.tensor_single_scalar(identpf[:, :], csm[:, :], 0.0, ALU.is_equal), reads=[csm], writes=[identpf])
            kb.op("dve", lambda e: e.tensor_copy(identp[:, :], identpf[:, :]), reads=[identpf], writes=[identp])
            kb.op("dve", lambda e: e.tensor_scalar_add(negoff[:, :], identpf[:, :], -1.0), reads=[identpf], writes=[negoff])
            masks = []
            for x, cmpop in ((0, ALU.is_ge), (1, ALU.is_le)):
                mk = kb.sb(st, f"mask{x}", [128, 64], BF16)
                kb.op("dve", lambda e, cmpop=cmpop: e.tensor_single_scalar(dtmp[:, 0:64], csm[:, :], 0.0, cmpop), reads=[csm], writes=[dtmp])
                kb.op("dve", lambda e, mk=mk: e.tensor_scalar(mk[:, :], dtmp[:, 0:64], -NEG, NEG, ALU.mult, ALU.add), reads=[dtmp], writes=[mk])
                masks.append(mk)
            cmask = kb.sb(st, "cmask", [128, 512], F32)
            kb.op("dve", lambda e: e.memset(cmask[0:8, :], 1.0), writes=[cmask])
            kb.op("dve", lambda e: e.memset(cmask[0:8, :].rearrange("p (n c) -> p n c", c=64)[:, :, 0:1], 0.0), writes=[cmask])
            gnorm = self.load_bcast_vec(st, "gnorm", W["gdn_norm_g"], 64)
            pT = kb.ps(st, "pT", [128, 4, 128], BF16)
            pb = [kb.ps(st, f"pb{i}", [128, 512], F32) for i in range(7)]
            self._pbi = 0

            def bank():
                self._pbi = (self._pbi + 1) % 7
                return pb[self._pbi]

            def v3(ap):
                return ap.rearrange("p (h c) -> p h c", c=64)

            def blocks(out_bk, lhs, rhs, nr):
                for r in range(nr):
                    R_ = slice(64 * r, 64 * r + 64)
                    for h in range(H):
                        C_ = slice(64 * h, 64 * h + 64)
                        kb.op("pe", lambda e, R_=R_, C_=C_: e.matmul(out_bk[R_, C_], lhs[R_, C_], rhs[R_, C_], start=True, stop=True),
                              reads=[lhs, rhs], writes=[out_bk])

            nblk = (Lp + 511) // 512

            def pass_gen(x):
                sfx = f"_{x}"
                F = lambda n: kb.sb(st, n + sfx, [128, 512], F32)
                Bf = lambda n: kb.sb(st, n + sfx, [128, 512], BF16)
                g_t, b_t, cs, gc, ngc, egc, bg, kd = [F(n) for n in ("g_t", "b_t", "cs", "gc", "ngc", "egc", "bg", "kd")]
                tot = kb.sb(st, "tot" + sfx, [128, 8], F32)
                egl = kb.sb(st, "egl" + sfx, [128, 4, 8], F32)
                knT, qnT, kbT, qgT = [kb.sb(st, n + sfx, [128, 4, 512], BF16) for n in ("knT", "qnT", "kbT", "qgT")]
                ktok, vtok, vb, kbg, kdec, At, IY, nwT, vn = [Bf(n) for n in ("ktok", "vtok", "vb", "kbg", "kdec", "At", "IY", "nwT", "vn")]
                Xs = [Bf(f"X{i}") for i in range(2)]
                Ys = [Bf(f"Y{i}") for i in range(2)]
                Rs = [Bf(f"R{i}") for i in range(2)]
                gtk = kb.sb(st, "gtk" + sfx, [128, 32], F32)
                BD = kb.sb(st, "BD" + sfx, [128, 2, 512], F32)
                E, tt, u_sb, o_sb = [F(n) for n in ("E", "tt", "u_sb", "o_sb")]
                S = kb.sb(st, "S" + sfx, [128, 256], F32)
                Sb = kb.sb(st, "Sb" + sfx, [128, 256], BF16)
                for t_ in [ktok, vtok, vb, kbg, kdec, At, IY, vn] + Xs + Ys + Rs:
                    kb.op("pool", lambda e, t_=t_: e.memset(t_[:], 0.0), writes=[t_])
                kb.op("pool", lambda e: e.memset(gtk[:], 0.0), writes=[gtk])
                kb.op("pool", lambda e: e.memset(o_sb[:], 0.0), writes=[o_sb])
                kb.op("dve", lambda e: e.memset(S[:], 0.0), writes=[S])
                kb.op("dve", lambda e: e.memset(Sb[:], 0.0), writes=[Sb])
                odram = self.OF if x == 0 else self.OB
                border = range(nblk) if x == 0 else range(nblk - 1, -1, -1)
                for b in border:
                    p0 = 512 * b
                    ntok = min(512, Lp - p0)
                    nch = ntok // 64
                    N = slice(0, ntok)
                    P = slice(p0, p0 + ntok)
                    G = slice(0, 8)
                    self.dma("sp", g_t[G, N], self.GT[x, :, P], [self.GT], [g_t], g_t)
                    self.dma("sp", b_t[G, N], self.GT[2 + x, :, P], [self.GT], [b_t], b_t)
                    self.dma("sp", knT[:, :, N], self.GKT[:, :, P].rearrange("c p t -> p c t"), [self.GKT], [knT], knT)
                    self.dma("sp", qnT[:, :, N], self.GQT[:, :, P].rearrange("c p t -> p c t"), [self.GQT], [qnT], qnT)
                    kb.op("dve", lambda e: e.tensor_tensor_scan(cs[G, N], cmask[G, N], g_t[G, N], 0.0, ALU.mult, ALU.add),
                          reads=[cmask, g_t], writes=[cs])
                    cs3 = cs[G, N].rearrange("p (n c) -> p n c", c=64)
                    kb.op("dve", lambda e: e.tensor_copy(tot[G, 0:nch].unsqueeze(2), cs3[:, :, 63:64]), reads=[cs], writes=[tot])
                    totb = tot[G, 0:nch].unsqueeze(2).to_broadcast([8, nch, 64])
                    gc3 = gc[G, N].rearrange("p (n c) -> p n c", c=64)
                    if x == 0:
                        kb.op("dve", lambda e: e.tensor_copy(gc[G, N], cs[G, N]), reads=[cs], writes=[gc])
                    else:
                        kb.op("dve", lambda e: e.tensor_tensor(gc3, totb, cs3, ALU.subtract), reads=[tot, cs], writes=[gc])
                        kb.op("dve", lambda e: e.tensor_tensor(gc[G, N], gc[G, N], g_t[G, N], ALU.add), reads=[gc, g_t], writes=[gc])
                    kb.op("dve", lambda e: e.tensor_scalar_mul(ngc[G, N], gc[G, N], -1.0), reads=[gc], writes=[ngc])
                    kb.op("act", lambda e: e.activation(out=egc[G, N], in_=gc[G, N], func=AF.Exp), reads=[gc], writes=[egc])
                    kb.op("dve", lambda e: e.tensor_tensor(bg[G, N], b_t[G, N], egc[G, N], ALU.mult), reads=[b_t, egc], writes=[bg])
                    kd3 = kd[G, N].rearrange("p (n c) -> p n c", c=64)
                    kb.op("dve", lambda e: e.tensor_tensor(kd3, totb, gc3, ALU.subtract), reads=[tot, gc], writes=[kd])
                    kb.op("act", lambda e: e.activation(out=kd[G, N], in_=kd[G, N], func=AF.Exp), reads=[kd], writes=[kd])
                    bk = bank()
                    for j in range(4):
                        kb.op("pe", lambda e, j=j, bk=bk: e.matmul(bk[:, 8 * j:8 * j + nch], dmask[G, 128 * j:128 * j + 128], tot[G, 0:nch],
                                                                   start=True, stop=True), reads=[dmask, tot], writes=[bk])
                    kb.op("act", lambda e, bk=bk: e.activation(out=egl[:, :, 0:nch], in_=bk[:, 0:32].rearrange("p (j n) -> p j n", n=8)[:, :, 0:nch],
                                                               func=AF.Exp), reads=[bk], writes=[egl])
                    yield
                    for j in range(4):
                        bk = bank()
                        kb.op("pe", lambda e, j=j, bk=bk: e.matmul(bk[:, N], dmask[G, 128 * j:128 * j + 128], b_t[G, N], start=True, stop=True),
                              reads=[dmask, b_t], writes=[bk])
                        kb.op("dve", lambda e, j=j, bk=bk: e.tensor_tensor(kbT[:, j, N], knT[:, j, N], bk[:, N], ALU.mult),
                              reads=[knT, bk], writes=[kbT])
                        bk = bank()
                        kb.op("pe", lambda e, j=j, bk=bk: e.matmul(bk[:, N], dmask[G, 128 * j:128 * j + 128], egc[G, N], start=True, stop=True),
                              reads=[dmask, egc], writes=[bk])
                        kb.op("dve", lambda e, j=j, bk=bk: e.tensor_tensor(qgT[:, j, N], qnT[:, j, N], bk[:, N], ALU.mult),
                              reads=[qnT, bk], writes=[qgT])
                        yield
                    ntl = (ntok + 127) // 128
                    torder = range(ntl) if x == 0 else range(ntl - 1, -1, -1)
                    for tl in torder:
                        nt = min(128, ntok - tl * 128)
                        nr = nt // 64
                        TP = slice(0, nt)
                        TC = slice(tl * 128, tl * 128 + nt)
                        rows = slice(p0 + tl * 128, p0 + tl * 128 + nt)
                        self.dma("sp", ktok[TP, :], self.GKTOK[rows, :], [self.GKTOK], [ktok], ktok)
                        self.dma("sp", vtok[TP, :], self.GVTOK[rows, :], [self.GVTOK], [vtok], vtok)
                        bk = bank()
                        for qi, src in enumerate((b_t, bg, kd)):
                            kb.op("pe", lambda e, qi=qi, src=src, bk=bk: e.matmul(bk[TP, 8 * qi:8 * qi + 8], src[G, TC], self.ident_f[G, 0:8],
                                                                                 start=True, stop=True), reads=[src, self.ident_f], writes=[bk])
                        kb.op("act", lambda e, bk=bk: e.copy(gtk[TP, 0:24], bk[TP, 0:24]), reads=[bk], writes=[gtk])
                        for dst, src, c0, en in ((vb, vtok, 0, "pool"), (kbg, ktok, 8, "dve"), (kdec, ktok, 16, "pool")):
                            kb.op(en, lambda e, dst=dst, src=src, c0=c0: e.tensor_tensor(
                                v3(dst[TP, :]), v3(src[TP, :]), gtk[TP, c0:c0 + 8].unsqueeze(2).to_broadcast([nt, 8, 64]), ALU.mult),
                                reads=[src, gtk], writes=[dst])
                        yield
                        PA = bank()
                        PB = bank()
                        for cls in (0, 1):
                            for r in range(nr):
                                R_ = slice(64 * r, 64 * r + 64)
                                cc = slice(tl * 128 + 64 * r, tl * 128 + 64 * r + 64)
                                for h in range(H):
                                    j, m = h // 2, h % 2
                                    if (m == r) != (cls == 0):
                                        continue
                                    M_ = slice(64 * m, 64 * m + 64)
                                    C_ = slice(64 * h, 64 * h + 64)
                                    kb.op("pe", lambda e, R_=R_, C_=C_, M_=M_, j=j, cc=cc: e.matmul(PA[R_, C_], knT[M_, j, cc], kbT[M_, j, cc],
                                                                                                     start=True, stop=True),
                                          reads=[knT, kbT], writes=[PA])
                                    kb.op("pe", lambda e, R_=R_, C_=C_, M_=M_, j=j, cc=cc: e.matmul(PB[R_, C_], knT[M_, j, cc], qnT[M_, j, cc],
                                                                                                     start=True, stop=True),
                                          reads=[knT, qnT], writes=[PB])
                        PD = bank()
                        for r in range(nr):
                            cc = slice(tl * 128 + 64 * r, tl * 128 + 64 * r + 64)
                            kb.op("dve", lambda e, r=r, cc=cc: e.tensor_tensor(v3(BD[G, r, :]), v3(dmask[G, :]),
                                                                               gc[G, cc].unsqueeze(1).to_broadcast([8, 8, 64]), ALU.mult),
                                  reads=[dmask, gc], writes=[BD])
                            kb.op("pe", lambda e, r=r: e.matmul(PD[64 * r:64 * r + 64, :], self.ones_f[G, 0:64], BD[G, r, :], start=True, stop=False),
                                  reads=[self.ones_f, BD], writes=[PD])
                        kb.op("pe", lambda e: e.matmul(PD[TP, :], ngc[G, TC], dmask[G, :], start=False, stop=False),
                              reads=[ngc, dmask], writes=[PD])
                        mk = masks[x]
                        kb.op("pe", lambda e: e.matmul(v3(PD[TP, :]), self.ident_b[TP, TP], mk[TP, :].unsqueeze(1).to_broadcast([nt, 8, 64]),
                                                       start=False, stop=True), reads=[self.ident_b, mk], writes=[PD])
                        kb.op("act", lambda e: e.activation(out=E[TP, :], in_=PD[TP, :], func=AF.Exp), reads=[PD], writes=[E])
                        yield
                        kb.op("dve", lambda e: e.tensor_tensor(At[TP, :], PB[TP, :], E[TP, :], ALU.mult), reads=[PB, E], writes=[At])
                        kb.op("dve", lambda e: e.tensor_tensor(tt[TP, :], PA[TP, :], E[TP, :], ALU.mult), reads=[PA, E], writes=[tt])
                        X, Y, Rr = Xs[0], Ys[0], Rs[0]
                        kb.op("pool", lambda e: e.tensor_tensor(v3(X[TP, :]), v3(tt[TP, :]), negoff[TP, :].unsqueeze(1).to_broadcast([nt, 8, 64]),
                                                                ALU.mult), reads=[tt, negoff], writes=[X])
                        kb.op("pool", lambda e: e.tensor_tensor(v3(Rr[TP, :]), v3(X[TP, :]), identp[TP, :].unsqueeze(1).to_broadcast([nt, 8, 64]),
                                                                ALU.add), reads=[X, identp], writes=[Rr])
                        PY = bank()
                        for r in range(nr):
                            R_ = slice(64 * r, 64 * r + 64)
                            for h in range(H):
                                C_ = slice(64 * h, 64 * h + 64)
                                kb.op("pe", lambda e, R_=R_, C_=C_: e.matmul(PY[R_, C_], X[R_, C_], self.ident_b[R_, R_], start=True, stop=True),
                                      reads=[X, self.ident_b], writes=[PY])
                        kb.op("act", lambda e: e.copy(Y[TP, :], PY[TP, :]), reads=[PY], writes=[Y])
                        yield
                        for jj in range(5):
                            Xn, Yn, Rn = Xs[(jj + 1) % 2], Ys[(jj + 1) % 2], Rs[(jj + 1) % 2]
                            if jj < 4:
                                PX = bank()
                                blocks(PX, Y, X, nr)
                                kb.op("act", lambda e, PX=PX, Xn=Xn: e.copy(Xn[TP, :], PX[TP, :]), reads=[PX], writes=[Xn])
                            PY = bank()
                            blocks(PY, X, Y, nr)
                            kb.op("dve", lambda e, PY=PY: e.tensor_tensor(v3(IY[TP, :]), v3(PY[TP, :]),
                                                                          identpf[TP, :].unsqueeze(1).to_broadcast([nt, 8, 64]), ALU.add),
                                  reads=[PY, identpf], writes=[IY])
                            if jj < 4:
                                kb.op("dve", lambda e, PY=PY, Yn=Yn: e.tensor_copy(Yn[TP, :], PY[TP, :]), reads=[PY], writes=[Yn])
                            yield
                            PR = bank()
                            blocks(PR, IY, Rr, nr)
                            kb.op("act", lambda e, PR=PR, Rn=Rn: e.copy(Rn[TP, :], PR[TP, :]), reads=[PR], writes=[Rn])
                            X, Y, Rr = Xn, Yn, Rn
                            yield
                        Tt = Rr
                        PU = bank()
                        blocks(PU, Tt, vb, nr)
                        kb.op("act", lambda e, PU=PU: e.copy(u_sb[TP, :], PU[TP, :]), reads=[PU], writes=[u_sb])
                        PW = bank()
                        for cls in (0, 1):
                            for r in range(nr):
                                R_ = slice(64 * r, 64 * r + 64)
                                for h in range(H):
                                    j, m = h // 2, h % 2
                                    if (m == r) != (cls == 0):
                                        continue
                                    C_ = slice(64 * h, 64 * h + 64)
                                    kb.op("pe", lambda e, R_=R_, C_=C_, j=j, m=m, r=r: e.matmul(
                                        PW[64 * m:64 * m + 64, 128 * j + 64 * r:128 * j + 64 * r + 64], kbg[R_, C_], Tt[R_, C_], start=True, stop=True),
                                        reads=[kbg, Tt], writes=[PW])
                        if nr == 2:
                            kb.op("act", lambda e, PW=PW: e.mul(nwT[:, :], PW[:, :], -1.0), reads=[PW], writes=[nwT])
                        else:
                            kb.op("act", lambda e, PW=PW: e.mul(nwT[:, :].rearrange("p (j r c) -> p j r c", j=4, r=2)[:, :, 0, :],
                                                               PW[:, :].rearrange("p (j r c) -> p j r c", j=4, r=2)[:, :, 0, :], -1.0),
                                  reads=[PW], writes=[nwT])
                        yield
                        rorder = range(nr) if x == 0 else range(nr - 1, -1, -1)
                        for r in rorder:
                            R_ = slice(64 * r, 64 * r + 64)
                            nb = tl * 2 + r
                            cc = slice(tl * 128 + 64 * r, tl * 128 + 64 * r + 64)
                            PV = bank()
                            for mm_ in (0, 1):
                                for h in range(H):
                                    j, m = h // 2, h % 2
                                    if m != mm_:
                                        continue
                                    M_ = slice(64 * m, 64 * m + 64)
                                    kb.op("pe", lambda e, h=h, j=j, M_=M_, r=r, R_=R_, PV=PV: e.matmul(
                                        PV[R_, 64 * h:64 * h + 64], nwT[M_, 128 * j + 64 * r:128 * j + 64 * r + 64], Sb[M_, 64 * j:64 * j + 64],
                                        start=True, stop=True), reads=[nwT, Sb], writes=[PV])
                            kb.op("dve", lambda e, R_=R_, PV=PV: e.tensor_tensor(vn[R_, :], u_sb[R_, :], PV[R_, :], ALU.add),
                                  reads=[u_sb, PV], writes=[vn])
                            PO = bank()
                            for mm_ in (1 - r, r):
                                for h in range(H):
                                    j, m = h // 2, h % 2
                                    if m != mm_:
                                        continue
                                    M_ = slice(64 * m, 64 * m + 64)
                                    C_ = slice(64 * h, 64 * h + 64)
                                    kb.op("pe", lambda e, M_=M_, C_=C_, j=j, R_=R_, cc=cc, PO=PO: e.matmul(
                                        PO[R_, C_], qgT[M_, j, cc], Sb[M_, 64 * j:64 * j + 64], start=True, stop=True),
                                        reads=[qgT, Sb], writes=[PO])
                            yield
                            PO2 = bank()
                            for h in range(H):
                                C_ = slice(64 * h, 64 * h + 64)
                                kb.op("pe", lambda e, C_=C_, R_=R_, PO2=PO2: e.matmul(PO2[R_, C_], At[R_, C_], vn[R_, C_], start=True, stop=True),
                                      reads=[At, vn], writes=[PO2])
                            PS_ = bank()
                            for h in range(H):
                                j, m = h // 2, h % 2
                                C_ = slice(64 * h, 64 * h + 64)
                                kb.op("pe", lambda e, j=j, m=m, R_=R_, C_=C_, PS_=PS_: e.matmul(
                                    PS_[64 * m:64 * m + 64, 64 * j:64 * j + 64], kdec[R_, C_], vn[R_, C_], start=True, stop=True),
                                    reads=[kdec, vn], writes=[PS_])
                            kb.op("act", lambda e, R_=R_, PO=PO: e.copy(o_sb[R_, :], PO[R_, :]), reads=[PO], writes=[o_sb])
                            kb.op("dve", lambda e, R_=R_, PO2=PO2: e.tensor_tensor(o_sb[R_, :], o_sb[R_, :], PO2[R_, :], ALU.add),
                                  reads=[o_sb, PO2], writes=[o_sb])
                            kb.op("dve", lambda e, nb=nb: e.tensor_tensor(v3(S[:, :]), v3(S[:, :]),
                                                                          egl[:, :, nb:nb + 1].to_broadcast([128, 4, 64]), ALU.mult),
                                  reads=[S, egl], writes=[S])
                            kb.op("dve", lambda e, PS_=PS_: e.tensor_tensor(S[:, :], S[:, :], PS_[:, 0:256], ALU.add), reads=[S, PS_], writes=[S])
                            kb.op("act", lambda e: e.copy(Sb[:, :], S[:, :]), reads=[S], writes=[Sb])
                            yield
                        self.dma("sp", odram[rows, :], o_sb[TP, :], [o_sb], [odram], o_sb)

            gens = [pass_gen(0), pass_gen(1)]
            while gens:
                for g in list(gens):
                    try:
                        next(g)
                    except StopIteration:
                        gens.remove(g)

            ofs = [kb.sb(st, f"of_t{i}", [128, 512], F32) for i in range(2)]
            obs = [kb.sb(st, f"ob_t{i}", [128, 512], F32) for i in range(2)]
            zts = [kb.sb(st, f"z_t{i}", [128, 512], F32) for i in range(2)]
            osum = kb.sb(st, "osum", [128, 512], F32)
            osq = kb.sb(st, "osq", [128, 512], F32)
            ssum = kb.sb(st, "ssum", [128, 16], F32)
            gout = kb.sb(st, "gout", [128, 512], BF16)
            gTs = [kb.sb(st, f"gT{i}", [128, 4, 128], BF16) for i in range(2)]
            for t_ in ofs + obs + zts:
                kb.op("pool", lambda e, t_=t_: e.memset(t_[:], 0.0), writes=[t_])
            kb.op("pool", lambda e: e.memset(gout[:], 0.0), writes=[gout])
            ntp = (Lp + 127) // 128
            for t in range(ntp):
                nt = min(128, Lp - 128 * t)
                TP = slice(0, nt)
                rows = slice(128 * t, 128 * t + nt)
                of_t, ob_t, z_t, gT = ofs[t % 2], obs[t % 2], zts[t % 2], gTs[t % 2]
                self.dma("sp", of_t[TP, :], self.OF[rows, :], [self.OF], [of_t], of_t)
                self.dma("sp", ob_t[TP, :], self.OB[rows, :], [self.OB], [ob_t], ob_t)
                self.dma("sp", z_t[TP, :], self.Z[rows, :], [self.Z], [z_t], z_t)
                kb.op("pool", lambda e: e.tensor_tensor(osum[TP, :], of_t[TP, :], ob_t[TP, :], ALU.add), reads=[of_t, ob_t], writes=[osum])
                kb.op("act", lambda e: e.activation(out=osq[TP, :], in_=osum[TP, :], func=AF.Square), reads=[osum], writes=[osq])
                kb.op("dve", lambda e: e.tensor_reduce(ssum[TP, 0:8], v3(osq[TP, :]), AX.X, ALU.add), reads=[osq], writes=[ssum])
                kb.op("act", lambda e: e.activation(out=ssum[TP, 8:16], in_=ssum[TP, 0:8], func=AF.Ln, bias=self.cst[TP, 1:2],
                                                    scale=1.0 / 64.0), reads=[ssum, self.cst], writes=[ssum])
                kb.op("act", lambda e: e.activation(out=ssum[TP, 8:16], in_=ssum[TP, 8:16], func=AF.Exp, scale=-0.5),
                      reads=[ssum], writes=[ssum])
                kb.op("dve", lambda e: e.tensor_tensor(v3(osum[TP, :]), v3(osum[TP, :]),
                                                       ssum[TP, 8:16].unsqueeze(2).to_broadcast([nt, 8, 64]), ALU.mult),
                      reads=[osum, ssum], writes=[osum])
                kb.op("pool", lambda e: e.tensor_tensor(v3(osum[TP, :]), v3(osum[TP, :]),
                                                        gnorm[TP, :].unsqueeze(1).to_broadcast([nt, 8, 64]), ALU.mult),
                      reads=[osum, gnorm], writes=[osum])
                kb.op("dve", lambda e: e.tensor_tensor(gout[TP, :], osum[TP, :], z_t[TP, :], ALU.mult), reads=[osum, z_t], writes=[gout])
                for j in range(4):
                    kb.op("pe", lambda e, j=j: e.transpose(pT[:, j, 0:nt], gout[TP, 128 * j:128 * j + 128], self.ident_b[TP, TP]),
                          reads=[gout, self.ident_b], writes=[pT])
                kb.op("act", lambda e: e.copy(gT[:, :, 0:nt], pT[:, :, 0:nt]), reads=[pT], writes=[gT])
                c0 = 64 if t == 0 else 0
                x0 = 128 * t + c0 - 64
                if nt > c0:
                    self.dma("sp", self.CATT[4:8, :, x0:x0 + nt - c0].rearrange("c p t -> p c t"), gT[:, :, c0:nt],
                             [gT], [self.CATT], gT)
        kb.barrier()

    def phase3(self, si):
        kb = self.kb
        Lx = self.seq_lens[si]
        Lp = Lx + 64
        nkc = (Lp + 127) // 128
        with ExitStack() as st:
            kt = kb.sb(st, "kt", [128, 8, nkc * 128], BF16)
            va = kb.sb(st, "va", [128, nkc, 8, 128], BF16)
            kb.op("pool", lambda e: e.memset(va[:, :, :, 64:128], 1.0), writes=[va])
            kb.op("pool", lambda e: e.memset(va[:, :, :, 0:64], 0.0), writes=[va])
            self.dma("sp", kt[0:96, :, 0:Lp], self.KT[:, :, 0:Lp].rearrange("h r t -> r h t"), [self.KT], [kt], kt)
            for kc in range(nkc):
                nk = min(128, Lp - kc * 128)
                self.dma("sp", va[0:nk, kc, :, 0:64], self.V[kc * 128:kc * 128 + nk, :].rearrange("t (h e) -> t h e", h=8),
                         [self.V], [va], va)
            kb.op("pool", lambda e: e.memset(va[0:32, 0, :, :], 0.0), writes=[va])
            kb.op("pool", lambda e: e.memset(va[32:48, 0, :, :], 0.0), writes=[va])
            qts = [kb.sb(st, f"qt{i}", [128, 8, 512], BF16) for i in range(2)]
            pts = [kb.sb(st, f"pt{i}", [128, 512], BF16) for i in range(5)]
            rec = kb.sb(st, "rec", [128, 512], F32)
            mo = kb.sb(st, "mo", [128, 4, 512], BF16)
            pss = [kb.ps(st, f"pss{i}", [128, 512], F32) for i in range(5)]
            self._p3cnt = 0
            pos = [kb.ps(st, f"pos{i}", [128, 512], F32) for i in range(2)]
            nqb = Lx // 512 if Lx % 512 == 0 else (Lx + 511) // 512
            cnt = 0
            for qb in range(nqb):
                nq = min(512, Lx - qb * 512)
                Q = slice(0, nq)
                qt = qts[qb % 2]
                self.dma("sp", qt[0:96, :, Q], self.QT[:, :, qb * 512:qb * 512 + nq].rearrange("h r t -> r h t"),
                         [self.QT], [qt], qt)
                for h in range(H):
                    po = pos[h % 2]
                    LOOK = 3
                    slots = {}

                    def emit_s(kc, h=h):
                        nk = min(128, Lp - kc * 128)
                        ps_ = pss[self._p3cnt % 5]
                        pt = pts[self._p3cnt % 5]
                        self._p3cnt += 1
                        slots[kc] = (ps_, pt, nk)
                        kb.op("pe", lambda e: e.matmul(ps_[0:nk, Q], kt[0:96, h, kc * 128:kc * 128 + nk], qt[0:96, h, Q], start=True, stop=True),
                              reads=[kt, qt], writes=[ps_])

                    for kc in range(min(LOOK, nkc)):
                        emit_s(kc)
                    for kc in range(nkc):
                        ps_, pt, nk = slots.pop(kc)
                        kb.op("act", lambda e, ps_=ps_, pt=pt, nk=nk: e.activation(out=pt[0:nk, Q], in_=ps_[0:nk, Q], func=AF.Exp),
                              reads=[ps_], writes=[pt])
                        if kc + LOOK < nkc:
                            emit_s(kc + LOOK)
                        kb.op("pe", lambda e, po=po, pt=pt, kc=kc, nk=nk, h=h: e.matmul(
                            po[:, Q], va[0:nk, kc, h, :], pt[0:nk, Q], start=(kc == 0), stop=(kc == nkc - 1)),
                            reads=[va, pt], writes=[po])
                    kb.op("dve", lambda e, po=po: e.reciprocal(rec[0:64, Q], po[64:128, Q]), reads=[po], writes=[rec])
                    kb.op("dve", lambda e, po=po, h=h: e.tensor_tensor(mo[64 * (h % 2):64 * (h % 2) + 64, h // 2, Q], po[0:64, Q],
                                                                       rec[0:64, Q], ALU.mult), reads=[po, rec], writes=[mo])
                self.dma("sp", self.CATT[0:4, :, qb * 512:qb * 512 + nq].rearrange("c p t -> p c t"), mo[:, :, Q],
                         [mo], [self.CATT], mo)
        kb.barrier()

    def phase4a(self, si):
        kb = self.kb
        Lx = self.seq_lens[si]
        W = self.w
        with ExitStack() as st:
            w_out = kb.sb(st, "w_out", [128, 8, D], BF16)
            for kc in range(8):
                self.dma("sp", w_out[:, kc, :], self.Wb["w_out"][kc * 128:kc * 128 + 128, :], [self.Wb["w_out"]], [w_out], w_out)
            lng = self.load_bcast_vec(st, "lng", W["ln_in_g"], D)
            lnb = self.load_bcast_vec(st, "lnb", W["ln_in_b"], D)
            l1g = self.load_bcast_vec(st, "l1g", W["ln1_g"], D)
            l1b = self.load_bcast_vec(st, "l1b", W["ln1_b"], D)
            xts = [kb.sb(st, f"xt{i}", [128, D], F32) for i in range(3)]
            cats = [kb.sb(st, f"cat{i}", [128, 8, 128], BF16) for i in range(3)]
            hress = [kb.sb(st, f"hres{i}", [128, D], F32) for i in range(3)]
            r1s = [kb.sb(st, f"r1{i}", [128, D], F32) for i in range(2)]
            h1s = [kb.sb(st, f"h1{i}", [128, D], F32) for i in range(2)]
            h1bs = [kb.sb(st, f"h1b{i}", [128, D], BF16) for i in range(2)]
            h1ts = [kb.sb(st, f"h1t{i}", [128, 8, 128], BF16) for i in range(2)]
            statss = [kb.sb(st, f"stats{i}", [128, 32], F32) for i in range(5)]
            pT = kb.ps(st, "pT", [128, 8, 128], BF16)
            pb = [kb.ps(st, f"pb{i}", [128, 512], F32) for i in range(4)]
            ntile = Lx // 128

            def stage_a(k):
                xt, cat = xts[k % 3], cats[k % 3]
                rows = slice(128 * k, 128 * k + 128)
                self.dma("sp", xt[:, :], self.x_in[si][rows, :], [self.xin_reg], [xt], xt)
                self.dma("sp", cat[:, :, :], self.CATT[:, :, rows].rearrange("c p t -> p c t"), [self.CATT], [cat], cat)
                return self.layer_norm_gen(xt, xt[:, :], lng, lnb, hress[k % 3], hress[k % 3][:, :], xt, statss[k % 3])

            def stage_b(k):
                cat, hres, r1 = cats[k % 3], hress[k % 3], r1s[k % 2]
                for nh in range(2):
                    bk = pb[(2 * k + nh) % 4]
                    F = slice(nh * 512, nh * 512 + 512)
                    for kc in range(8):
                        kb.op("pe", lambda e, kc=kc, bk=bk, F=F, cat=cat: e.matmul(bk[:, :], cat[:, kc, :], w_out[:, kc, F],
                                                                                   start=(kc == 0), stop=(kc == 7)),
                              reads=[cat, w_out], writes=[bk])
                    kb.op("dve", lambda e, bk=bk, F=F: e.scalar_tensor_tensor(r1[:, F], hres[:, F], DN_ALPHA, bk[:, :],
                                                                             ALU.mult, ALU.add), reads=[hres, bk], writes=[r1])

            def stage_c_ln(k):
                r1, h1 = r1s[k % 2], h1s[k % 2]
                return self.layer_norm_gen(r1, r1[:, :], l1g, l1b, h1, h1[:, :], r1, statss[3 + k % 2])

            def stage_c_out(k):
                h1, h1b, h1t = h1s[k % 2], h1bs[k % 2], h1ts[k % 2]
                rows = slice(128 * k, 128 * k + 128)
                self.dma("sp", self.H1[rows, :], h1[:, :], [h1], [self.H1], h1)
                kb.op("act", lambda e: e.copy(h1b[:, :], h1[:, :]), reads=[h1], writes=[h1b])
                self.transpose_tile(h1b, lambda kk: h1b[:, kk * 128:(kk + 1) * 128], pT, h1t,
                                    lambda k0, k1: h1t[:, k0:k1, :])
                self.dma("sp", self.H1T[:, :, rows].rearrange("c p t -> p c t"), h1t[:, :, :], [h1t], [self.H1T], h1t)

            self.run_gens(stage_a(0), stage_a(1) if ntile > 1 else None)
            for k in range(ntile):
                stage_b(k)
                ga = stage_a(k + 2) if k + 2 < ntile else None
                self.run_gens(ga, stage_c_ln(k))
                stage_c_out(k)
        kb.barrier()

    def phase4b(self, si):
        kb = self.kb
        Lx = self.seq_lens[si]
        W = self.w
        with ExitStack() as st:
            w1 = kb.sb(st, "w_ff1", [128, 8, DFF], BF16)
            w2 = kb.sb(st, "w_ff2", [128, 32, D], BF16)
            for kc in range(8):
                self.dma("sp", w1[:, kc, :], self.Wb["w_ff1"][kc * 128:kc * 128 + 128, :], [self.Wb["w_ff1"]], [w1], w1)
            for kc in range(32):
                self.dma("sp", w2[:, kc, :], self.Wb["w_ff2"][kc * 128:kc * 128 + 128, :], [self.Wb["w_ff2"]], [w2], w2)
            l2g = self.load_bcast_vec(st, "l2g", W["ln2_g"], D)
            l2b = self.load_bcast_vec(st, "l2b", W["ln2_b"], D)
            aT = kb.sb(st, "aT", [128, 32, 512], BF16)
            h1T = kb.sb(st, "h1T", [128, 8, 512], BF16)
            rls = [kb.sb(st, f"rl{i}", [128, 512], BF16) for i in range(2)]
            h1 = kb.sb(st, "h1", [128, D], F32)
            r2 = kb.sb(st, "r2", [128, D], F32)
            yo = kb.sb(st, "yo", [128, D], F32)
            lntmp = kb.sb(st, "lntmp", [128, D], F32)
            stats = kb.sb(st, "stats", [128, 32], F32)
            pb = [kb.ps(st, f"pb{i}", [128, 512], F32) for i in range(6)]
            bi = 0
            for b in range((Lx + 511) // 512):
                n = min(512, Lx - 512 * b)
                N = slice(0, n)
                cols = slice(512 * b, 512 * b + n)
                self.dma("sp", h1T[:, :, N], self.H1T[:, :, cols].rearrange("c p t -> p c t"), [self.H1T], [h1T], h1T)
                for mc in range(32):
                    bk = pb[bi % 6]
                    bi += 1
                    rl = rls[mc % 2]
                    for kc in range(8):
                        kb.op("pe", lambda e, kc=kc, mc=mc, bk=bk: e.matmul(bk[:, N], w1[:, kc, mc * 128:mc * 128 + 128], h1T[:, kc, N],
                                                                            start=(kc == 0), stop=(kc == 7)),
                              reads=[w1, h1T], writes=[bk])
                    kb.op("act", lambda e, bk=bk, rl=rl: e.activation(out=rl[:, N], in_=bk[:, N], func=AF.Relu), reads=[bk], writes=[rl])
                    kb.op("pool", lambda e, rl=rl, mc=mc: e.tensor_tensor(aT[:, mc, N], rl[:, N], rl[:, N], ALU.mult),
                          reads=[rl], writes=[aT])
                for tl in range(n // 128):
                    rows = slice(512 * b + 128 * tl, 512 * b + 128 * tl + 128)
                    self.dma("sp", h1[:, :], self.H1[rows, :], [self.H1], [h1], h1)
                    for nh in range(2):
                        bk = pb[bi % 6]
                        bi += 1
                        F = slice(nh * 512, nh * 512 + 512)
                        for mc in range(32):
                            kb.op("pe", lambda e, mc=mc, bk=bk, F=F, tl=tl: e.matmul(bk[:, :], aT[:, mc, tl * 128:tl * 128 + 128],
                                                                                     w2[:, mc, F], start=(mc == 0), stop=(mc == 31)),
                                  reads=[aT, w2], writes=[bk])
                        kb.op("dve", lambda e, bk=bk, F=F: e.scalar_tensor_tensor(r2[:, F], h1[:, F], DN_ALPHA, bk[:, :],
                                                                                 ALU.mult, ALU.add), reads=[h1, bk], writes=[r2])
                    self.layer_norm(r2, r2[:, :], l2g, l2b, yo, yo[:, :], r2, stats)
                    self.dma("sp", self.y_out[si][rows, :], yo[:, :], [yo], [self.yout_reg], yo)
        kb.barrier()


WEIGHT_NAMES = ["meta_tokens", "ln_in_g", "ln_in_b", "w_in", "g_cq", "g_ckv", "w_uq", "w_uk", "w_uv", "conv_w",
                "a_log_f", "a_log_b", "dt_bias_f", "dt_bias_b", "gdn_norm_g", "w_out", "ln1_g", "ln1_b",
                "w_ff1", "w_ff2", "ln2_g", "ln2_b"]


def build_prog(seq_lens, debug=False, phases=None):
    p = Prog(seq_lens, debug=debug)
    p.declare()
    p.setup_consts()
    for si in range(len(seq_lens)):
        for name in ["phase1", "phase2a", "phase2b", "phase3", "phase4a", "phase4b"]:
            if phases is not None and name not in phases:
                continue
            if not hasattr(p, name):
                continue
            getattr(p, name)(si)
    p.finish()
    return p


SEQ_LENS = [2048, 2048, 2048, 2048, 4096]
_PROG_CACHE = {}


def kernel(**inputs):
    n = 8
    x_prompt = np.asarray(inputs["x_prompt"], dtype=np.float32)
    x_sample = np.asarray(inputs["x_sample"], dtype=np.float32)
    wmap = {}
    for k in WEIGHT_NAMES:
        a = np.asarray(inputs[k], dtype=np.float32)
        if k not in ("meta_tokens", "ln_in_g", "ln_in_b"):
            a = a[0]
        wmap[k] = np.ascontiguousarray(a)
    prog = build_prog(SEQ_LENS, debug=False)
    in_maps = []
    for c in range(n):
        m = dict(wmap)
        for i in range(4):
            m[f"x{i}"] = np.ascontiguousarray(x_sample[4 * c + i])
        m["x4"] = np.ascontiguousarray(x_prompt[c // 2])
        in_maps.append(m)
    res = run_bass_kernel_spmd(prog.nc, in_maps, core_ids=list(range(n)))
    y_sample = np.empty_like(x_sample)
    y_prompt = np.empty_like(x_prompt)
    for c in range(n):
        r = res.results[c]
        for i in range(4):
            y_sample[4 * c + i] = np.asarray(r[f"y{i}"], dtype=np.float32)
        if c % 2 == 0:
            y_prompt[c // 2] = np.asarray(r["y4"], dtype=np.float32)
    return (y_prompt, y_sample)
```

```python
import numpy as np
from contextlib import ExitStack
import concourse.bass as bass
import concourse.mybir as mybir
from concourse.bass_utils import run_bass_kernel_spmd

F32 = mybir.dt.float32
BF16 = mybir.dt.bfloat16
I32 = mybir.dt.int32
AF = mybir.ActivationFunctionType
ALU = mybir.AluOpType
AX = mybir.AxisListType

D = 1024
NIN = 2752
H = 8
DFF = 4096
C_Q0, C_KV0, C_KR0, C_QKV0, C_Z0, C_G0 = 0, 384, 640, 672, 2208, 2720
DN_ALPHA = 2.0 ** 0.25
LN_EPS = 1e-5
RMS_EPS = 1e-6
QSCALE = 96.0 ** -0.5
NEG = -30000.0


class Reg:
    __slots__ = ("name", "w", "r", "sem", "cnt")

    def __init__(self, name):
        self.name = name
        self.w = {}
        self.r = {}
        self.sem = None
        self.cnt = 0


class Buf:
    def __init__(self, t, reg):
        self.t = t
        self.reg = reg

    def __getitem__(self, k):
        return self.t[k]


class _PEProxy:
    def __init__(self, kb):
        self.kb = kb
        self.e = kb.engs["pe"]

    def matmul(self, out, lhsT, rhs, **kw):
        self.kb._pe_pos(lhsT, out)
        return self.e.matmul(out, lhsT, rhs, **kw)

    def transpose(self, out, in_, ident):
        self.kb._pe_pos(in_, out)
        return self.e.transpose(out, in_, ident)


class KB:
    def __init__(self, nc, es):
        self.nc = nc
        self.es = es
        self.engs = {"pe": nc.tensor, "act": nc.scalar, "dve": nc.vector, "pool": nc.gpsimd, "sp": nc.sync}
        self.sem = {}
        self.cnt = {}
        self.waited = {}
        self.semname = {}
        for n in self.engs:
            self.sem[n] = es.enter_context(nc.semaphore("e_" + n))
            self.cnt[n] = 0
            self.waited[n] = {}
        self.nreg = 0
        self.all_dma_regs = []
        self.dpool = []
        self.dfree = {"sw": [], "hw": []}
        self.dkind = {}
        self.ninst = 0
        self.pe_live = set()
        self.nfence = 0
        self.pend = {n: False for n in self.engs}
        self.last_pe_w = None
        self.last_ins = {n: None for n in self.engs}
        self.pe_proxy = _PEProxy(self)

    def reg(self, name):
        self.nreg += 1
        return Reg(f"{name}_{self.nreg}")

    def sb(self, stack, name, shape, dt):
        self.nreg += 1
        nm = f"{name}_{self.nreg}"
        t = stack.enter_context(self.nc.sbuf_tensor(nm, list(shape), dt))
        return Buf(t, Reg(nm))

    def ps(self, stack, name, shape, dt):
        self.nreg += 1
        nm = f"{name}_{self.nreg}"
        t = stack.enter_context(self.nc.psum_tensor(nm, list(shape), dt))
        return Buf(t, Reg(nm))

    def _regs(self, xs):
        out = []
        for x in xs:
            if x is None:
                continue
            out.append(x.reg if isinstance(x, Buf) else x)
        return out

    def _flush(self, en):
        if self.pend[en]:
            self.last_ins[en].then_inc(self.sem[en], 1)
            self.cnt[en] += 1
            self.pend[en] = False

    def _wait(self, en, key, sem, val):
        w = self.waited[en]
        if w.get(key, 0) >= val:
            return
        if key in self.engs and val > self.cnt[key]:
            assert self.pend[key] and val == self.cnt[key] + 1
            self._flush(key)
        self.engs[en].wait_ge(sem, val)
        w[key] = val

    def op(self, en, fn, reads=(), writes=(), dma=None):
        reads = self._regs(reads)
        writes = self._regs(writes)
        deps = {}
        for R in reads:
            for k, v in R.w.items():
                if deps.get(k, (None, 0))[1] < v[1]:
                    deps[k] = v
        for R in writes:
            for dd in (R.w, R.r):
                for k, v in dd.items():
                    if deps.get(k, (None, 0))[1] < v[1]:
                        deps[k] = v
        for k, (sem, val) in deps.items():
            if k == "pe" and en == "pe" and dma is None:
                continue
            self._wait(en, k, sem, val)
        if dma is None and en == "pe":
            wkey = tuple(id(R) for R in writes)
            if self.pend["pe"] and wkey != self.last_pe_w:
                self._flush("pe")
            self.last_pe_w = wkey
        ins = fn(self.pe_proxy if en == "pe" else self.engs[en])
        self.ninst += 1
        if dma is not None:
            R = dma.reg if isinstance(dma, Buf) else dma
            if R.sem is None:
                kind = "sw" if en == "pool" else "hw"
                if self.dfree[kind]:
                    R.sem = self.dfree[kind].pop()
                else:
                    R.sem = len(self.dpool)
                    self.dpool.append([self.es.enter_context(self.nc.semaphore(f"ds{R.sem}")), 0])
                    self.dkind[R.sem] = kind
                self.all_dma_regs.append(R)
            assert self.dkind[R.sem] == ("sw" if en == "pool" else "hw"), "mixed DMA queues on one region semaphore"
            ent = self.dpool[R.sem]
            ent[1] += 16
            ins.then_inc(ent[0], 16)
            key, tok = f"ds{R.sem}", (ent[0], ent[1])
        else:
            if en == "pe":
                self.last_ins[en] = ins
                self.pend[en] = True
                key, tok = en, (self.sem[en], self.cnt[en] + 1)
            else:
                self.cnt[en] += 1
                ins.then_inc(self.sem[en], 1)
                key, tok = en, (self.sem[en], self.cnt[en])
        for R in reads:
            R.r[key] = tok
        for R in writes:
            R.w[key] = tok
        return tok

    def _pe_pos(self, kap, oap):
        k0, kn = kap.base_partition(), kap.partition_size()
        kq = 32 if kn <= 32 else (64 if kn <= 64 else 128)
        m0, mn = oap.base_partition(), oap.partition_size()
        key = (k0, kq, m0, mn)
        if key in self.pe_live:
            return
        conflict = False
        for (a0, aq, b0, bn) in self.pe_live:
            if (a0, aq) != (k0, kq) and not (m0 + mn <= b0 or b0 + bn <= m0):
                conflict = True
                break
        if conflict:
            self._flush("pe")
            if self.cnt["pe"] > 0:
                self.engs["pe"].wait_ge(self.sem["pe"], self.cnt["pe"])
                self.waited["pe"]["pe"] = self.cnt["pe"]
            self.pe_live = set()
            self.nfence += 1
        self.pe_live.add(key)

    def barrier(self):
        for en in self.engs:
            self._flush(en)
        for en in self.engs:
            for o in self.engs:
                if o != en and self.cnt[o] > 0:
                    self._wait(en, o, self.sem[o], self.cnt[o])
            for i, ent in enumerate(self.dpool):
                if ent[1] > 0:
                    self._wait(en, f"ds{i}", ent[0], ent[1])
        for R in self.all_dma_regs:
            self.dfree[self.dkind[R.sem]].append(R.sem)
            R.sem = None
        self.all_dma_regs = []


def bc(ap, shape):
    return ap.to_broadcast(list(shape))


class _Stop(Exception):
    pass


class Prog:
    stop_at = None

    def _ck(self, n):
        return self.stop_at is not None and n == self.stop_at

    def __init__(self, seq_lens, debug=False):
        self.seq_lens = list(seq_lens)
        self.debug = debug
        self.nc = bass.Bass("TRN2", target_bir_lowering=False)
        self.es = ExitStack()
        self.kb = KB(self.nc, self.es)
        self.maxLx = max(self.seq_lens)
        self.maxLp = self.maxLx + 64

    def dram_in(self, name, shape, dt=F32):
        return self.nc.dram_tensor(name, list(shape), dt, kind="ExternalInput").ap()

    def dram_out(self, name, shape, dt=F32):
        return self.nc.dram_tensor(name, list(shape), dt, kind="ExternalOutput").ap()

    def dram_scr(self, name, shape, dt):
        kind = "ExternalOutput" if self.debug else "Internal"
        ap = self.nc.dram_tensor(name, list(shape), dt, kind=kind).ap()
        return Buf(ap, self.kb.reg(name))

    def declare(self):
        nseq = len(self.seq_lens)
        self.x_in = [self.dram_in(f"x{i}", [L, D]) for i, L in enumerate(self.seq_lens)]
        self.y_out = [self.dram_out(f"y{i}", [L, D]) for i, L in enumerate(self.seq_lens)]
        self.xin_reg = self.kb.reg("xin")
        self.yout_reg = self.kb.reg("yout")
        self.w = {}
        for name, shape in [
            ("meta_tokens", [16, D]), ("ln_in_g", [D]), ("ln_in_b", [D]), ("w_in", [D, NIN]),
            ("g_cq", [384]), ("g_ckv", [256]), ("w_uq", [384, 768]), ("w_uk", [256, 512]), ("w_uv", [256, 512]),
            ("conv_w", [4, 1536]), ("a_log_f", [8]), ("a_log_b", [8]), ("dt_bias_f", [8]), ("dt_bias_b", [8]),
            ("gdn_norm_g", [64]), ("w_out", [D, D]), ("ln1_g", [D]), ("ln1_b", [D]),
            ("w_ff1", [D, DFF]), ("w_ff2", [DFF, D]), ("ln2_g", [D]), ("ln2_b", [D]),
        ]:
            self.w[name] = self.dram_in(name, shape)
        self.w_reg = self.kb.reg("weights")
        self.Wb = {}
        for name, shape in [("w_in", [D, NIN]), ("w_uq", [384, 768]), ("w_uk", [256, 512]), ("w_uv", [256, 512]),
                            ("w_out", [D, D]), ("w_ff1", [D, DFF]), ("w_ff2", [DFF, D])]:
            self.Wb[name] = self.dram_scr("wb_" + name, shape, BF16)
        Lp, Lx = self.maxLp, self.maxLx
        self.QT = self.dram_scr("s_qt", [H, 96, Lx], BF16)
        self.KT = self.dram_scr("s_kt", [H, 96, Lp], BF16)
        self.V = self.dram_scr("s_v", [Lp, 512], BF16)
        self.QKVT = self.dram_scr("s_qkvt", [12, 128, Lp + 4], F32)
        self.Z = self.dram_scr("s_z", [Lp, 512], F32)
        self.GT = self.dram_scr("s_gt", [4, 8, Lp], F32)
        self.GQT = self.dram_scr("s_gqt", [4, 128, Lp], BF16)
        self.GKT = self.dram_scr("s_gkt", [4, 128, Lp], BF16)
        self.GKTOK = self.dram_scr("s_gktok", [Lp, 512], BF16)
        self.GVTOK = self.dram_scr("s_gvtok", [Lp, 512], BF16)
        self.OF = self.dram_scr("s_of", [Lp, 512], F32)
        self.OB = self.dram_scr("s_ob", [Lp, 512], F32)
        self.CATT = self.dram_scr("s_catt", [8, 128, Lx], BF16)
        self.H1 = self.dram_scr("s_h1", [Lx, D], F32)
        self.H1T = self.dram_scr("s_h1t", [8, 128, Lx], BF16)

    def dma(self, en, out_ap, in_ap, reads, writes, sem):
        return self.kb.op(en, lambda e: e.dma_start(out=out_ap, in_=in_ap), reads=reads, writes=writes, dma=sem)

    def rsqrt(self, out_buf, out_ap, in_buf, in_ap, scale, eps_ap, tmp_buf, tmp_ap):
        kb = self.kb
        kb.op("act", lambda e: e.activation(out=tmp_ap, in_=in_ap, func=AF.Ln, bias=eps_ap, scale=scale),
              reads=[in_buf, self.cst], writes=[tmp_buf])
        kb.op("act", lambda e: e.activation(out=out_ap, in_=tmp_ap, func=AF.Exp, scale=-0.5),
              reads=[tmp_buf], writes=[out_buf])

    def setup_consts(self):
        kb, nc = self.kb, self.nc
        es = self.es
        self.cst = kb.sb(es, "cst", [128, 16], F32)
        kb.op("dve", lambda e: e.memset(self.cst[:, 0:1], LN_EPS), writes=[self.cst])
        kb.op("dve", lambda e: e.memset(self.cst[:, 1:2], RMS_EPS), writes=[self.cst])
        kb.op("dve", lambda e: e.memset(self.cst[:, 2:3], 1.0), writes=[self.cst])
        kb.op("dve", lambda e: e.memset(self.cst[:, 3:4], 0.0), writes=[self.cst])
        self.io_i = kb.sb(es, "io_i", [128, 128], I32)
        self.io_f = kb.sb(es, "io_f", [128, 128], F32)
        kb.op("pool", lambda e: e.iota(self.io_i[:], [[1, 128]], base=0, channel_multiplier=-1), writes=[self.io_i])
        kb.op("dve", lambda e: e.tensor_copy(self.io_f[:], self.io_i[:]), reads=[self.io_i], writes=[self.io_f])
        self.ident_f = kb.sb(es, "ident_f", [128, 128], F32)
        self.ident_b = kb.sb(es, "ident_b", [128, 128], BF16)
        kb.op("dve", lambda e: e.tensor_single_scalar(self.ident_f[:], self.io_f[:], 0.0, ALU.is_equal),
              reads=[self.io_f], writes=[self.ident_f])
        kb.op("dve", lambda e: e.tensor_copy(self.ident_b[:], self.ident_f[:]), reads=[self.ident_f], writes=[self.ident_b])
        self.ones_f = kb.sb(es, "ones_f", [128, 128], F32)
        kb.op("dve", lambda e: e.memset(self.ones_f[:], 1.0), writes=[self.ones_f])
        for name, wb in self.Wb.items():
            src = self.w[name]
            nrow = src.shape[0]
            step = 256
            for r0 in range(0, nrow, step):
                r1 = min(nrow, r0 + step)
                self.dma("pool", wb[r0:r1, :], src[r0:r1, :], [self.w_reg], [wb], wb)
        kb.barrier()


    def finish(self):
        self.kb.barrier()
        self.es.close()

    def load_bcast_vec(self, st, name, vec_ap, n):
        b = self.kb.sb(st, name, [128, n], F32)
        self.dma("sp", b[:, :], vec_ap.partition_broadcast(128), reads=[self.w_reg], writes=[b], sem=b)
        return b

    def load_x_tile(self, si, t, xt):
        kb = self.kb
        Lx = self.seq_lens[si]
        x = self.x_in[si]
        if t == 0:
            kb.op("dve", lambda e: e.memset(xt[0:48, :], 0.0), writes=[xt])
            self.dma("sp", xt[48:64, :], self.w["meta_tokens"][:, :], reads=[self.w_reg], writes=[xt], sem=xt)
            self.dma("sp", xt[64:128, :], x[0:64, :], reads=[self.xin_reg], writes=[xt], sem=xt)
            return 128
        r0 = 128 * t - 64
        nt = min(128, Lx - r0)
        self.dma("sp", xt[0:nt, :], x[r0:r0 + nt, :], reads=[self.xin_reg], writes=[xt], sem=xt)
        return nt

    def layer_norm_gen(self, xin, xin_ap, g_bc, b_bc, out_buf, out_ap, tmp, stats):
        kb = self.kb
        for c in range(2):
            kb.op("dve", lambda e, c=c: e.bn_stats(stats[:, 6 * c:6 * c + 6], xin_ap[:, 512 * c:512 * c + 512]),
                  reads=[xin], writes=[stats])
        yield
        kb.op("dve", lambda e: e.bn_aggr(stats[:, 16:18], stats[:, 0:12].rearrange("p (a b) -> p a b", a=2)),
              reads=[stats], writes=[stats])
        yield
        kb.op("act", lambda e: e.activation(out=stats[:, 20:21], in_=stats[:, 17:18], func=AF.Ln,
                                            bias=self.cst[:, 0:1], scale=1.0), reads=[stats, self.cst], writes=[stats])
        yield
        kb.op("act", lambda e: e.activation(out=stats[:, 21:22], in_=stats[:, 20:21], func=AF.Exp, scale=-0.5),
              reads=[stats], writes=[stats])
        yield
        kb.op("dve", lambda e: e.scalar_tensor_tensor(stats[:, 22:23], stats[:, 16:17], -1.0, stats[:, 21:22], ALU.mult, ALU.mult),
              reads=[stats], writes=[stats])
        yield
        kb.op("act", lambda e: e.activation(out=tmp[:, :], in_=xin_ap, func=AF.Identity, bias=stats[:, 22:23], scale=stats[:, 21:22]),
              reads=[xin, stats], writes=[tmp])
        yield
        kb.op("dve", lambda e: e.tensor_tensor(tmp[:, :], tmp[:, :], g_bc[:, :], ALU.mult),
              reads=[tmp, g_bc], writes=[tmp])
        yield
        kb.op("dve", lambda e: e.tensor_tensor(out_ap, tmp[:, :], b_bc[:, :], ALU.add),
              reads=[tmp, b_bc], writes=[out_buf])

    @staticmethod
    def run_gens(*gens):
        gens = [g for g in gens if g is not None]
        while gens:
            for g in list(gens):
                try:
                    next(g)
                except StopIteration:
                    gens.remove(g)

    def layer_norm(self, *a):
        self.run_gens(self.layer_norm_gen(*a))

    def transpose_tile(self, src, src_ap_fn, pT, dst, dst_ap_fn, nk=8):
        kb = self.kb
        for k in range(nk):
            kb.op("pe", lambda e, k=k: e.transpose(pT[:, k, :], src_ap_fn(k), self.ident_b[:, :]),
                  reads=[src, self.ident_b], writes=[pT])
        h = nk // 2
        kb.op("act", lambda e: e.copy(dst_ap_fn(0, h), pT[:, 0:h, :]), reads=[pT], writes=[dst])
        kb.op("dve", lambda e: e.tensor_copy(dst_ap_fn(h, nk), pT[:, h:nk, :]), reads=[pT], writes=[dst])

    def phase1(self, si):
        kb, nc = self.kb, self.nc
        Lx = self.seq_lens[si]
        Lp = Lx + 64
        W = self.w
        with ExitStack() as st:
            w_in = kb.sb(st, "w_in", [128, 8, NIN], BF16)
            wqA = kb.sb(st, "wqA", [128, 3, 8, 96], BF16)
            wqB = kb.sb(st, "wqB", [128, 3, 8, 96], BF16)
            wkrA = kb.sb(st, "wkrA", [128, 8, 96], BF16)
            wkrB = kb.sb(st, "wkrB", [128, 8, 96], BF16)
            w_uk = kb.sb(st, "w_uk", [128, 2, 512], BF16)
            w_uv = kb.sb(st, "w_uv", [128, 2, 512], BF16)
            kb.op("dve", lambda e: e.memset(wqB[:], 0.0), writes=[wqB])
            kb.op("dve", lambda e: e.memset(wkrA[:], 0.0), writes=[wkrA])
            kb.op("dve", lambda e: e.memset(wkrB[:], 0.0), writes=[wkrB])
            for kc in range(8):
                rows = slice(kc * 128, kc * 128 + 128)
                self.dma("sp", w_in[:, kc, :], self.Wb["w_in"][rows, :], [self.Wb["w_in"]], [w_in], w_in)
                self.dma("sp", wkrA[:, kc, 64:96], self.Wb["w_in"][rows, C_KR0:C_KR0 + 32], [self.Wb["w_in"]], [wkrA], wkrA)
                self.dma("sp", wkrB[:, kc, 64:80], self.Wb["w_in"][rows, C_KR0 + 16:C_KR0 + 32], [self.Wb["w_in"]], [wkrB], wkrB)
                self.dma("sp", wkrB[:, kc, 80:96], self.Wb["w_in"][rows, C_KR0:C_KR0 + 16], [self.Wb["w_in"]], [wkrB], wkrB)
            for kc in range(3):
                rows = slice(kc * 128, kc * 128 + 128)
                wq3 = self.Wb["w_uq"][rows, :].rearrange("p (h c) -> p h c", c=96)
                self.dma("sp", wqA[:, kc, :, :], wq3, [self.Wb["w_uq"]], [wqA], wqA)
                self.dma("sp", wqB[:, kc, :, 64:80], wq3[:, :, 80:96], [self.Wb["w_uq"]], [wqB], wqB)
                self.dma("sp", wqB[:, kc, :, 80:96], wq3[:, :, 64:80], [self.Wb["w_uq"]], [wqB], wqB)
            for kc in range(2):
                rows = slice(kc * 128, kc * 128 + 128)
                self.dma("sp", w_uk[:, kc, :], self.Wb["w_uk"][rows, :], [self.Wb["w_uk"]], [w_uk], w_uk)
                self.dma("sp", w_uv[:, kc, :], self.Wb["w_uv"][rows, :], [self.Wb["w_uv"]], [w_uv], w_uv)
            lng = self.load_bcast_vec(st, "lng", W["ln_in_g"], D)
            lnb = self.load_bcast_vec(st, "lnb", W["ln_in_b"], D)
            gcq = kb.sb(st, "gcq", [128, 3], F32)
            gckv = kb.sb(st, "gckv", [128, 2], F32)
            for kc in range(3):
                self.dma("sp", gcq[:, kc:kc + 1], W["g_cq"][kc * 128:kc * 128 + 128].rearrange("(p o) -> p o", o=1),
                         [self.w_reg], [gcq], gcq)
            for kc in range(2):
                self.dma("sp", gckv[:, kc:kc + 1], W["g_ckv"][kc * 128:kc * 128 + 128].rearrange("(p o) -> p o", o=1),
                         [self.w_reg], [gckv], gckv)
            gpar = kb.sb(st, "gpar", [128, 4], F32)
            kb.op("dve", lambda e: e.memset(gpar[:], 0.0), writes=[gpar])
            for off, sfx in ((0, "f"), (32, "b")):
                self.dma("sp", gpar[off:off + 8, 0:1], W["dt_bias_" + sfx].rearrange("(p o) -> p o", o=1),
                         [self.w_reg], [gpar], gpar)
                self.dma("sp", gpar[off:off + 8, 1:2], W["a_log_" + sfx].rearrange("(p o) -> p o", o=1),
                         [self.w_reg], [gpar], gpar)
            kb.op("act", lambda e: e.activation(out=gpar[0:40, 2:3], in_=gpar[0:40, 1:2], func=AF.Exp),
                  reads=[gpar], writes=[gpar])
            kb.op("dve", lambda e: e.tensor_scalar_mul(gpar[0:40, 2:3], gpar[0:40, 2:3], -1.0), reads=[gpar], writes=[gpar])
            ropeC = kb.sb(st, "ropeC", [128, 512], F32)
            ropeS = kb.sb(st, "ropeS", [128, 512], F32)
            rp = kb.sb(st, "rp", [128, 8], F32)
            rpi = kb.sb(st, "rpi", [128, 2], I32)
            R = slice(64, 96)
            kb.op("pool", lambda e: e.iota(rpi[R, 0:1], [[0, 1]], base=0, channel_multiplier=1), writes=[rpi])
            kb.op("dve", lambda e: e.tensor_copy(rp[R, 0:1], rpi[R, 0:1]), reads=[rpi], writes=[rp])
            kb.op("dve", lambda e: e.tensor_single_scalar(rp[R, 1:2], rp[R, 0:1], 16.0, ALU.is_ge), reads=[rp], writes=[rp])
            kb.op("dve", lambda e: e.scalar_tensor_tensor(rp[R, 2:3], rp[R, 1:2], -16.0, rp[R, 0:1], ALU.mult, ALU.add),
                  reads=[rp], writes=[rp])
            kb.op("act", lambda e: e.activation(out=rp[R, 3:4], in_=rp[R, 2:3], func=AF.Exp,
                                                scale=-float(np.log(10000.0)) / 16.0), reads=[rp], writes=[rp])
            kb.op("dve", lambda e: e.tensor_scalar_mul(rp[R, 3:4], rp[R, 3:4], float(1.0 / (2.0 * np.pi))),
                  reads=[rp], writes=[rp])
            kb.op("dve", lambda e: e.tensor_scalar(rp[R, 4:5], rp[R, 1:2], float(4.0 * np.pi), float(-2.0 * np.pi),
                                                   ALU.mult, ALU.add), reads=[rp], writes=[rp])
            kb.op("dve", lambda e: e.memset(rp[R, 5:6], float(2.0 * np.pi)), writes=[rp])
            posi = kb.sb(st, "posi", [128, 512], I32)
            posf = kb.sb(st, "posf", [128, 512], F32)
            ru = kb.sb(st, "ru", [128, 512], F32)
            rui = kb.sb(st, "rui", [128, 512], I32)
            ruf = kb.sb(st, "ruf", [128, 512], F32)

            def rope_tables(c):
                kb.op("pool", lambda e: e.iota(posi[R, :], [[1, 512]], base=512 * c - 48, channel_multiplier=0),
                      writes=[posi])
                kb.op("dve", lambda e: e.tensor_copy(posf[R, :], posi[R, :]), reads=[posi], writes=[posf])
                for tab, off, sc in ((ropeC, 0.25, 5), (ropeS, 0.0, 4)):
                    kb.op("dve", lambda e, off=off: e.tensor_scalar(ru[R, :], posf[R, :], rp[R, 3:4], off, ALU.mult, ALU.add),
                          reads=[posf, rp], writes=[ru])
                    kb.op("dve", lambda e: e.tensor_copy(rui[R, :], ru[R, :]), reads=[ru], writes=[rui])
                    kb.op("dve", lambda e: e.tensor_copy(ruf[R, :], rui[R, :]), reads=[rui], writes=[ruf])
                    kb.op("dve", lambda e: e.tensor_tensor(ru[R, :], ru[R, :], ruf[R, :], ALU.subtract),
                          reads=[ru, ruf], writes=[ru])
                    kb.op("act", lambda e, tab=tab, sc=sc: e.activation(
                        out=tab[R, :], in_=ru[R, :], func=AF.Sin, scale=rp[R, sc:sc + 1]),
                        reads=[ru, rp], writes=[tab])
            xts = [kb.sb(st, f"xt{i}", [128, D], F32) for i in range(4)]
            for xt in xts:
                kb.op("pool", lambda e, xt=xt: e.memset(xt[:], 0.0), writes=[xt])
            statss = [kb.sb(st, f"stats{i}", [128, 32], F32) for i in range(4)]
            hbs = [kb.sb(st, f"hb{i}", [128, D], BF16) for i in range(4)]
            hTs = [kb.sb(st, f"hT{i}", [128, 8, 512], BF16) for i in range(2)]
            stage = kb.sb(st, "stage", [128, 12, 512], F32)
            cq = kb.sb(st, "cq", [128, 3, 512], F32)
            sq = kb.sb(st, "sq", [128, 3, 512], F32)
            rstd = kb.sb(st, "rstd", [128, 512], F32)
            rtmp = kb.sb(st, "rtmp", [128, 512], F32)
            cqn = kb.sb(st, "cqn", [128, 3, 512], BF16)
            ckvn = kb.sb(st, "ckvn", [128, 2, 512], BF16)
            qs = kb.sb(st, "qs", [128, 8, 512], BF16)
            ks = kb.sb(st, "ks", [128, 8, 512], BF16)
            t1 = kb.sb(st, "t1", [128, 512], F32)
            t2 = kb.sb(st, "t2", [128, 512], F32)
            vs = kb.sb(st, "vs", [128, 512], BF16)
            zs = kb.sb(st, "zs", [128, 512], F32)
            ze = kb.sb(st, "ze", [128, 512], F32)
            gs = kb.sb(st, "gs", [128, 512], F32)
            ga = kb.sb(st, "ga", [128, 512], F32)
            gb = kb.sb(st, "gb", [128, 512], F32)
            lnq = kb.sb(st, "lnq", [128, 1], F32)
            kb.op("dve", lambda e: e.memset(lnq[:], float(np.log(QSCALE))), writes=[lnq])
            kb.op("dve", lambda e: e.memset(gs[:], 0.0), writes=[gs])
            gs2 = kb.sb(st, "gs2", [128, 512], F32)
            kb.op("dve", lambda e: e.memset(gs2[:], 0.0), writes=[gs2])
            pT = kb.ps(st, "pT", [128, 8, 128], BF16)
            pb = [kb.ps(st, f"pb{i}", [128, 512], F32) for i in range(7)]
            self._pbi = 0

            def bank():
                self._pbi = (self._pbi + 1) % 7
                return pb[self._pbi]

            nblk = (Lp + 511) // 512

            def ln_part(b):
                p0_ = 512 * b
                ntok_ = min(512, Lp - p0_)
                ntl_ = (ntok_ + 127) // 128
                gens = []
                for tl in range(ntl_):
                    xt = xts[tl]
                    self.load_x_tile(si, 4 * b + tl, xt)
                    gens.append(self.layer_norm_gen(xt, xt[:, :], lng, lnb, hbs[tl], hbs[tl][:, :], xt, statss[tl]))
                self.run_gens(*gens)

            def tr_part(b):
                p0_ = 512 * b
                ntok_ = min(512, Lp - p0_)
                hT_ = hTs[b % 2]
                for tl in range((ntok_ + 127) // 128):
                    hb = hbs[tl]
                    self.transpose_tile(hb, lambda k, hb=hb: hb[:, k * 128:(k + 1) * 128], pT, hT_,
                                        lambda k0, k1, tl=tl: hT_[:, k0:k1, tl * 128:(tl + 1) * 128])

            ln_part(0)
            tr_part(0)
            for b in range(nblk):
                p0 = 512 * b
                ntok = min(512, Lp - p0)
                ntl = (ntok + 127) // 128
                hT = hTs[b % 2]
                if b + 1 < nblk:
                    ln_part(b + 1)
                N = slice(0, ntok)

                def proj(col0, m, out_ap_fn=None):
                    bk = bank()
                    o = bk[0:m, N] if out_ap_fn is None else out_ap_fn(bk)
                    for kc in range(8):
                        kb.op("pe", lambda e, kc=kc: e.matmul(o, w_in[:, kc, col0:col0 + m], hT[:, kc, N],
                                                               start=(kc == 0), stop=(kc == 7)),
                              reads=[w_in, hT], writes=[bk])
                    return bk

                for mc in range(12):
                    bk = proj(C_QKV0 + mc * 128, 128)
                    if mc % 2 == 0:
                        kb.op("act", lambda e, bk=bk, mc=mc: e.copy(stage[:, mc, N], bk[:, N]), reads=[bk], writes=[stage])
                    else:
                        kb.op("dve", lambda e, bk=bk, mc=mc: e.tensor_copy(stage[:, mc, N], bk[:, N]), reads=[bk], writes=[stage])
                self.dma("sp", self.QKVT[:, :, 1 + p0:1 + p0 + ntok].rearrange("c p t -> p c t"), stage[:, :, N],
                         [stage], [self.QKVT], stage)

                def rms_fm(col0, nch, gpp, outn, lnbias):
                    banks = [proj(col0 + mc * 128, 128) for mc in range(nch)]
                    for mc, bk in enumerate(banks):
                        kb.op("act", lambda e, bk=bk, mc=mc: e.copy(cq[:, mc, N], bk[:, N]), reads=[bk], writes=[cq])
                        kb.op("act", lambda e, bk=bk, mc=mc: e.activation(out=sq[:, mc, N], in_=bk[:, N], func=AF.Square),
                              reads=[bk], writes=[sq])
                    bs = bank()
                    for mc in range(nch):
                        kb.op("pe", lambda e, mc=mc: e.matmul(bs[:, N], self.ones_f[:, :], sq[:, mc, N],
                                                               start=(mc == 0), stop=(mc == nch - 1)),
                              reads=[self.ones_f, sq], writes=[bs])
                    kb.op("act", lambda e: e.activation(out=rtmp[:, N], in_=bs[:, N], func=AF.Ln, bias=self.cst[:, 1:2],
                                                        scale=1.0 / (nch * 128)), reads=[bs, self.cst], writes=[rtmp])
                    if lnbias is None:
                        kb.op("act", lambda e: e.activation(out=rstd[:, N], in_=rtmp[:, N], func=AF.Exp, scale=-0.5),
                              reads=[rtmp], writes=[rstd])
                    else:
                        kb.op("act", lambda e: e.activation(out=rstd[:, N], in_=rtmp[:, N], func=AF.Exp, scale=-0.5,
                                                            bias=lnbias[:, 0:1]), reads=[rtmp, lnbias], writes=[rstd])
                    for mc in range(nch):
                        kb.op("dve", lambda e, mc=mc: e.scalar_tensor_tensor(outn[:, mc, N], cq[:, mc, N], gpp[:, mc:mc + 1],
                                                                             rstd[:, N], ALU.mult, ALU.mult),
                              reads=[cq, gpp, rstd], writes=[outn])

                rms_fm(C_KV0, 2, gckv, ckvn, None)
                for h in range(H):
                    bk = bank()
                    for kc in range(2):
                        kb.op("pe", lambda e, kc=kc, h=h, bk=bk: e.matmul(bk[0:64, N], w_uk[:, kc, h * 64:(h + 1) * 64],
                                                                          ckvn[:, kc, N], start=(kc == 0), stop=(kc == 1)),
                              reads=[w_uk, ckvn], writes=[bk])
                    if h % 2 == 0:
                        kb.op("act", lambda e, bk=bk, h=h: e.copy(ks[0:64, h, N], bk[0:64, N]), reads=[bk], writes=[ks])
                    else:
                        kb.op("dve", lambda e, bk=bk, h=h: e.tensor_copy(ks[0:64, h, N], bk[0:64, N]), reads=[bk], writes=[ks])
                RR = slice(64, 96)
                P = slice(p0, p0 + ntok)
                rope_tables(b)

                def rope_pair(bkA, bkB, dst_fn):
                    kb.op("dve", lambda e: e.tensor_tensor(t1[RR, N], bkA[RR, N], ropeC[RR, N], ALU.mult),
                          reads=[bkA, ropeC], writes=[t1])
                    kb.op("dve", lambda e: e.tensor_tensor(t2[RR, N], bkB[RR, N], ropeS[RR, N], ALU.mult),
                          reads=[bkB, ropeS], writes=[t2])
                    dst_fn()

                bkA = bank()
                bkB = bank()
                for kc in range(8):
                    kb.op("pe", lambda e, kc=kc: e.matmul(bkA[0:96, N], wkrA[:, kc, :], hT[:, kc, N], start=(kc == 0), stop=(kc == 7)),
                          reads=[wkrA, hT], writes=[bkA])
                for kc in range(8):
                    kb.op("pe", lambda e, kc=kc: e.matmul(bkB[0:96, N], wkrB[:, kc, :], hT[:, kc, N], start=(kc == 0), stop=(kc == 7)),
                          reads=[wkrB, hT], writes=[bkB])

                def kdst():
                    kb.op("dve", lambda e: e.tensor_tensor(t1[RR, N], t1[RR, N], t2[RR, N], ALU.add), reads=[t1, t2], writes=[t1])
                    kb.op("act", lambda e: e.copy(ks[RR, 0:4, N], t1[RR, N].unsqueeze(1).to_broadcast([32, 4, ntok])),
                          reads=[t1], writes=[ks])
                    kb.op("dve", lambda e: e.tensor_copy(ks[RR, 4:8, N], t1[RR, N].unsqueeze(1).to_broadcast([32, 4, ntok])),
                          reads=[t1], writes=[ks])
                rope_pair(bkA, bkB, kdst)
                self.dma("sp", self.KT[:, :, P].rearrange("h r t -> r h t"), ks[0:96, :, N], [ks], [self.KT], ks)
                for tl in range(ntl):
                    nt = min(128, ntok - tl * 128)
                    bk = bank()
                    for kc in range(2):
                        kb.op("pe", lambda e, kc=kc, tl=tl, nt=nt, bk=bk: e.matmul(
                            bk[0:nt, :], ckvn[:, kc, tl * 128:tl * 128 + nt], w_uv[:, kc, :], start=(kc == 0), stop=(kc == 1)),
                            reads=[ckvn, w_uv], writes=[bk])
                    kb.op("act", lambda e, bk=bk, nt=nt: e.copy(vs[0:nt, :], bk[0:nt, :]), reads=[bk], writes=[vs])
                    self.dma("sp", self.V[p0 + tl * 128:p0 + tl * 128 + nt, :], vs[0:nt, :], [vs], [self.V], vs)

                c0 = 64 if b == 0 else 0
                if ntok > c0:
                    rms_fm(C_Q0, 3, gcq, cqn, lnq)
                    for h in range(H):
                        bkA = bank()
                        bkB = bank()
                        for kc in range(3):
                            kb.op("pe", lambda e, kc=kc, h=h, bkA=bkA: e.matmul(bkA[0:96, N], wqA[:, kc, h, :], cqn[:, kc, N],
                                                                                 start=(kc == 0), stop=(kc == 2)),
                                  reads=[wqA, cqn], writes=[bkA])
                        for kc in range(3):
                            kb.op("pe", lambda e, kc=kc, h=h, bkB=bkB: e.matmul(bkB[0:96, N], wqB[:, kc, h, :], cqn[:, kc, N],
                                                                                 start=(kc == 0), stop=(kc == 2)),
                                  reads=[wqB, cqn], writes=[bkB])
                        kb.op("act", lambda e, bkA=bkA, h=h: e.copy(qs[0:64, h, N], bkA[0:64, N]), reads=[bkA], writes=[qs])

                        def qdst(h=h):
                            kb.op("dve", lambda e: e.tensor_tensor(qs[RR, h, N], t1[RR, N], t2[RR, N], ALU.add),
                                  reads=[t1, t2], writes=[qs])
                        rope_pair(bkA, bkB, qdst)
                    self.dma("sp", self.QT[:, :, p0 + c0 - 64:p0 + ntok - 64].rearrange("h r t -> r h t"),
                             qs[0:96, :, c0:ntok], [qs], [self.QT], qs)

                for tl in range(ntl):
                    nt = min(128, ntok - tl * 128)
                    bk = bank()
                    for kc in range(8):
                        kb.op("pe", lambda e, kc=kc, tl=tl, nt=nt, bk=bk: e.matmul(
                            bk[0:nt, :], hT[:, kc, tl * 128:tl * 128 + nt], w_in[:, kc, C_Z0:C_Z0 + 512],
                            start=(kc == 0), stop=(kc == 7)), reads=[hT, w_in], writes=[bk])
                    kb.op("act", lambda e, bk=bk, nt=nt: e.activation(out=ze[0:nt, :], in_=bk[0:nt, :], func=AF.Exp, scale=-1.0),
                          reads=[bk], writes=[ze])
                    kb.op("act", lambda e, nt=nt: e.activation(out=ze[0:nt, :], in_=ze[0:nt, :], func=AF.Ln, bias=self.cst[0:nt, 2:3], scale=1.0),
                          reads=[ze, self.cst], writes=[ze])
                    kb.op("act", lambda e, nt=nt: e.activation(out=ze[0:nt, :], in_=ze[0:nt, :], func=AF.Exp, scale=-1.0), reads=[ze], writes=[ze])
                    kb.op("dve", lambda e, bk=bk, nt=nt: e.tensor_tensor(zs[0:nt, :], bk[0:nt, :], ze[0:nt, :], ALU.mult),
                          reads=[bk, ze], writes=[zs])
                    self.dma("sp", self.Z[p0 + tl * 128:p0 + tl * 128 + nt, :], zs[0:nt, :], [zs], [self.Z], zs)

                bk = bank()
                bk2 = bank()
                for g in range(4):
                    bb = bk if g < 2 else bk2
                    ro = 32 * (g % 2)
                    for kc in range(8):
                        kb.op("pe", lambda e, kc=kc, g=g, bb=bb, ro=ro: e.matmul(
                            bb[ro:ro + 8, N], w_in[:, kc, C_G0 + 8 * g:C_G0 + 8 * g + 8], hT[:, kc, N],
                            start=(kc == 0), stop=(kc == 7)), reads=[w_in, hT], writes=[bb])
                A = slice(0, 40)
                kb.op("dve", lambda e: e.tensor_scalar(ga[A, N], bk[A, N], gpar[A, 0:1], None, ALU.add), reads=[bk, gpar], writes=[ga])
                kb.op("act", lambda e: e.activation(out=gb[A, N], in_=ga[A, N], func=AF.Abs), reads=[ga], writes=[gb])
                kb.op("act", lambda e: e.activation(out=gb[A, N], in_=gb[A, N], func=AF.Exp, scale=-1.0), reads=[gb], writes=[gb])
                kb.op("act", lambda e: e.activation(out=gb[A, N], in_=gb[A, N], func=AF.Ln, bias=self.cst[A, 2:3], scale=1.0),
                      reads=[gb, self.cst], writes=[gb])
                kb.op("dve", lambda e: e.scalar_tensor_tensor(ga[A, N], ga[A, N], 0.0, gb[A, N], ALU.max, ALU.add),
                      reads=[ga, gb], writes=[ga])
                kb.op("dve", lambda e: e.tensor_scalar(gs[A, N], ga[A, N], gpar[A, 2:3], None, ALU.mult), reads=[ga, gpar], writes=[gs])
                kb.op("act", lambda e: e.activation(out=gb[A, N], in_=bk2[A, N], func=AF.Exp, scale=-1.0), reads=[bk2], writes=[gb])
                kb.op("act", lambda e: e.activation(out=gb[A, N], in_=gb[A, N], func=AF.Ln, bias=self.cst[A, 2:3], scale=1.0),
                      reads=[gb, self.cst], writes=[gb])
                kb.op("act", lambda e: e.activation(out=gs2[A, N], in_=gb[A, N], func=AF.Exp, scale=-1.0), reads=[gb], writes=[gs2])
                if b == 0:
                    kb.op("dve", lambda e: e.memset(gs[A, 0:48], 0.0), writes=[gs])
                    kb.op("dve", lambda e: e.memset(gs2[A, 0:48], 0.0), writes=[gs2])
                for g in range(4):
                    src = gs if g < 2 else gs2
                    ro = 32 * (g % 2)
                    self.dma("sp", self.GT[g, :, P], src[ro:ro + 8, N], [src], [self.GT], src)
                if b + 1 < nblk:
                    tr_part(b + 1)
        kb.barrier()

    def phase2a(self, si):
        kb = self.kb
        Lx = self.seq_lens[si]
        Lp = Lx + 64
        W = self.w
        with ExitStack() as st:
            cw = kb.sb(st, "cw", [128, 12, 4], F32)
            for k in range(4):
                self.kb.op("sp", lambda e, k=k: e.dma_start(out=cw[:, :, k:k + 1],
                                                            in_=W["conv_w"][k, :].rearrange("(c p o) -> p c o", p=128, o=1),
                                                            allow_slow_non_contiguous=True),
                           reads=[self.w_reg], writes=[cw], dma=cw)
            blk = kb.sb(st, "blk", [128, 128], F32)
            kb.op("dve", lambda e: e.memset(blk[:], 0.0), writes=[blk])
            kb.op("dve", lambda e: e.memset(blk[0:64, 0:64], 1.0), writes=[blk])
            kb.op("dve", lambda e: e.memset(blk[64:128, 64:128], 1.0), writes=[blk])
            raw = kb.sb(st, "raw", [128, 12, 516], F32)
            accs = [kb.sb(st, f"acc{i}", [128, 512], F32) for i in range(3)]
            ss = [kb.sb(st, f"s{i}", [128, 512], F32) for i in range(3)]
            sqs = [kb.sb(st, f"sq{i}", [128, 512], F32) for i in range(3)]
            rns = [kb.sb(st, f"rn{i}", [128, 512], F32) for i in range(3)]
            rtmps = [kb.sb(st, f"rtmp{i}", [128, 512], F32) for i in range(3)]
            qn = kb.sb(st, "qn", [128, 4, 512], BF16)
            kn = kb.sb(st, "kn", [128, 4, 512], BF16)
            vn = kb.sb(st, "vn", [128, 4, 512], BF16)
            ktok = kb.sb(st, "ktok", [128, 512], BF16)
            vtok = kb.sb(st, "vtok", [128, 512], BF16)
            pT = kb.ps(st, "pT", [128, 4, 128], BF16)
            pb = [kb.ps(st, f"pb{i}", [128, 512], F32) for i in range(3)]
            nblk = (Lp + 511) // 512
            for b in range(nblk):
                p0 = 512 * b
                ntok = min(512, Lp - p0)
                N = slice(0, ntok)
                P = slice(p0, p0 + ntok)
                j0 = 1 if b == 0 else 0
                j1 = min(ntok + 3, Lp + 1 - p0)
                self.dma("sp", raw[:, :, j0:j1], self.QKVT[:, :, p0 + j0:p0 + j1].rearrange("c p t -> p c t"),
                         [self.QKVT], [raw], raw)
                if b == 0:
                    kb.op("dve", lambda e: e.memset(raw[:, :, 0:49], 0.0), writes=[raw])
                if b == nblk - 1:
                    kb.op("dve", lambda e: e.memset(raw[:, :, ntok + 1:ntok + 3], 0.0), writes=[raw])
                def stage1(ci):
                    acc = accs[ci % 3]
                    s_ = ss[ci % 3]
                    kb.op("dve", lambda e: e.tensor_scalar(acc[:, N], raw[:, ci, 0:ntok], cw[:, ci, 0:1], None, ALU.mult),
                          reads=[raw, cw], writes=[acc])
                    for k in range(1, 4):
                        kb.op("dve", lambda e, k=k: e.scalar_tensor_tensor(
                            acc[:, N], raw[:, ci, k:k + ntok], cw[:, ci, k:k + 1], acc[:, N], ALU.mult, ALU.add),
                            reads=[raw, cw, acc], writes=[acc])
                    kb.op("act", lambda e: e.activation(out=s_[:, N], in_=acc[:, N], func=AF.Silu), reads=[acc], writes=[s_])
                    if b == 0:
                        kb.op("pool", lambda e: e.memset(s_[:, 0:48], 0.0), writes=[s_])
                    if ci < 8:
                        sq_, rt_, rn_ = sqs[ci % 3], rtmps[ci % 3], rns[ci % 3]
                        kb.op("act", lambda e: e.activation(out=sq_[:, N], in_=s_[:, N], func=AF.Square), reads=[s_], writes=[sq_])
                        bk = pb[ci % 3]
                        kb.op("pe", lambda e: e.matmul(bk[:, N], blk[:, :], sq_[:, N], start=True, stop=True),
                              reads=[blk, sq_], writes=[bk])
                        kb.op("act", lambda e: e.activation(out=rt_[:, N], in_=bk[:, N], func=AF.Ln, bias=self.cst[:, 1:2], scale=1.0),
                              reads=[bk, self.cst], writes=[rt_])
                        kb.op("act", lambda e: e.activation(out=rn_[:, N], in_=rt_[:, N], func=AF.Exp, scale=-0.5), reads=[rt_], writes=[rn_])

                def stage2(ci):
                    s_ = ss[ci % 3]
                    if ci < 4:
                        rn_ = rns[ci % 3]
                        kb.op("dve", lambda e: e.scalar_tensor_tensor(qn[:, ci, N], s_[:, N], 0.125, rn_[:, N], ALU.mult, ALU.mult),
                              reads=[s_, rn_], writes=[qn])
                    elif ci < 8:
                        rn_ = rns[ci % 3]
                        kb.op("dve", lambda e: e.tensor_tensor(kn[:, ci - 4, N], s_[:, N], rn_[:, N], ALU.mult),
                              reads=[s_, rn_], writes=[kn])
                    else:
                        kb.op("act", lambda e: e.copy(vn[:, ci - 8, N], s_[:, N]), reads=[s_], writes=[vn])

                stage1(0)
                for ci in range(12):
                    if ci + 1 < 12:
                        stage1(ci + 1)
                    stage2(ci)
                self.dma("sp", self.GQT[:, :, P].rearrange("c p t -> p c t"), qn[:, :, N], [qn], [self.GQT], qn)
                self.dma("sp", self.GKT[:, :, P].rearrange("c p t -> p c t"), kn[:, :, N], [kn], [self.GKT], kn)
                for tl in range((ntok + 127) // 128):
                    nt = min(128, ntok - tl * 128)
                    rows = slice(p0 + tl * 128, p0 + tl * 128 + nt)
                    for src, dstb, dram in ((kn, ktok, self.GKTOK), (vn, vtok, self.GVTOK)):
                        for j in range(4):
                            kb.op("pe", lambda e, j=j, src=src, tl=tl, nt=nt: e.transpose(
                                pT[0:nt, j, :], src[:, j, tl * 128:tl * 128 + nt], self.ident_b[:, :]),
                                reads=[src, self.ident_b], writes=[pT])
                        kb.op("act" if src is kn else "dve",
                              (lambda e, dstb=dstb, nt=nt: e.copy(dstb[0:nt, :], pT[0:nt, :, :])) if src is kn else
                              (lambda e, dstb=dstb, nt=nt: e.tensor_copy(dstb[0:nt, :], pT[0:nt, :, :])),
                              reads=[pT], writes=[dstb])
                        self.dma("sp", dram[rows, :], dstb[0:nt, :], [dstb], [dram], dstb)
        kb.barrier()

    def phase2b(self, si):
        kb = self.kb
        Lx = self.seq_lens[si]
        Lp = Lx + 64
        W = self.w
        with ExitStack() as st:
            dm_i = kb.sb(st, "dm_i", [128, 512], I32)
            dmask = kb.sb(st, "dmask", [128, 512], F32)
            dtmp = kb.sb(st, "dtmp", [128, 512], F32)
            kb.op("pool", lambda e: e.iota(dm_i[0:8, :], [[1, 512]], base=0, channel_multiplier=-64), writes=[dm_i])
            kb.op("dve", lambda e: e.tensor_copy(dtmp[0:8, :], dm_i[0:8, :]), reads=[dm_i], writes=[dtmp])
            kb.op("dve", lambda e: e.tensor_single_scalar(dmask[0:8, :], dtmp[0:8, :], 0.0, ALU.is_ge), reads=[dtmp], writes=[dmask])
            kb.op("dve", lambda e: e.tensor_single_scalar(dtmp[0:8, :], dtmp[0:8, :], 64.0, ALU.is_lt), reads=[dtmp], writes=[dtmp])
            kb.op("dve", lambda e: e.tensor_tensor(dmask[0:8, :], dmask[0:8, :], dtmp[0:8, :], ALU.mult), reads=[dmask, dtmp], writes=[dmask])
            csm = kb.sb(st, "csm", [128, 64], F32)
            kb.op("dve", lambda e: e.tensor_copy(csm[0:64, :], self.io_f[0:64, 0:64]), reads=[self.io_f], writes=[csm])
            kb.op("dve", lambda e: e.tensor_copy(csm[64:128, :], self.io_f[64:128, 64:128]), reads=[self.io_f], writes=[csm])
            identp = kb.sb(st, "identp", [128, 64], BF16)
            identpf = kb.sb(st, "identpf", [128, 64], F32)
            negoff = kb.sb(st, "negoff", [128, 64], F32)
            kb.op("dve", lambda e: e.tensor_single_scalar(identpf[:, :], csm[:, :], 0.0, ALU.is_equal), reads=[csm], writes=[identpf])
            kb.op("dve", lambda e: e.tensor_copy(identp[:, :], identpf[:, :]), reads=[identpf], writes=[identp])
            kb.op("dve", lambda e: e.tensor_scalar_add(negoff[:, :], identpf[:, :], -1.0), reads=[identpf], writes=[negoff])
            masks = []
            for x, cmpop in ((0, ALU.is_ge), (1, ALU.is_le)):
                mk = kb.sb(st, f"mask{x}", [128, 64], BF16)
                kb.op("dve", lambda e, cmpop=cmpop: e.tensor_single_scalar(dtmp[:, 0:64], csm[:, :], 0.0, cmpop), reads=[csm], writes=[dtmp])
                kb.op("dve", lambda e, mk=mk: e.tensor_scalar(mk[:, :], dtmp[:, 0:64], -NEG, NEG, ALU.mult, ALU.add), reads=[dtmp], writes=[mk])
                masks.append(mk)
            cmask = kb.sb(st, "cmask", [128, 512], F32)
            kb.op("dve", lambda e: e.memset(cmask[0:8, :], 1.0), writes=[cmask])
            kb.op("dve", lambda e: e.memset(cmask[0:8, :].rearrange("p (n c) -> p n c", c=64)[:, :, 0:1], 0.0), writes=[cmask])
            gnorm = self.load_bcast_vec(st, "gnorm", W["gdn_norm_g"], 64)
            pT = kb.ps(st, "pT", [128, 4, 128], BF16)
            pb = [kb.ps(st, f"pb{i}", [128, 512], F32) for i in range(7)]
            self._pbi = 0

            def bank():
                self._pbi = (self._pbi + 1) % 7
                return pb[self._pbi]

            def v3(ap):
                return ap.rearrange("p (h c) -> p h c", c=64)

            def blocks(out_bk, lhs, rhs, nr):
                for r in range(nr):
                    R_ = slice(64 * r, 64 * r + 64)
                    for h in range(H):
                        C_ = slice(64 * h, 64 * h + 64)
                        kb.op("pe", lambda e, R_=R_, C_=C_: e.matmul(out_bk[R_, C_], lhs[R_, C_], rhs[R_, C_], start=True, stop=True),
                              reads=[lhs, rhs], writes=[out_bk])

            nblk = (Lp + 511) // 512

            def pass_gen(x):
                sfx = f"_{x}"
                F = lambda n: kb.sb(st, n + sfx, [128, 512], F32)
                Bf = lambda n: kb.sb(st, n + sfx, [128, 512], BF16)
                g_t, b_t, cs, gc, ngc, egc, bg, kd = [F(n) for n in ("g_t", "b_t", "cs", "gc", "ngc", "egc", "bg", "kd")]
                tot = kb.sb(st, "tot" + sfx, [128, 8], F32)
                egl = kb.sb(st, "egl" + sfx, [128, 4, 8], F32)
                knT, qnT, kbT, qgT = [kb.sb(st, n + sfx, [128, 4, 512], BF16) for n in ("knT", "qnT", "kbT", "qgT")]
                ktok, vtok, vb, kbg, kdec, At, IY, nwT, vn = [Bf(n) for n in ("ktok", "vtok", "vb", "kbg", "kdec", "At", "IY", "nwT", "vn")]
                Xs = [Bf(f"X{i}") for i in range(2)]
                Ys = [Bf(f"Y{i}") for i in range(2)]
                Rs = [Bf(f"R{i}") for i in range(2)]
                gtk = kb.sb(st, "gtk" + sfx, [128, 32], F32)
                BD = kb.sb(st, "BD" + sfx, [128, 2, 512], F32)
                E, tt, u_sb, o_sb = [F(n) for n in ("E", "tt", "u_sb", "o_sb")]
                S = kb.sb(st, "S" + sfx, [128, 256], F32)
                Sb = kb.sb(st, "Sb" + sfx, [128, 256], BF16)
                for t_ in [ktok, vtok, vb, kbg, kdec, At, IY, vn] + Xs + Ys + Rs:
                    kb.op("pool", lambda e, t_=t_: e.memset(t_[:], 0.0), writes=[t_])
                kb.op("pool", lambda e: e.memset(gtk[:], 0.0), writes=[gtk])
                kb.op("pool", lambda e: e.memset(o_sb[:], 0.0), writes=[o_sb])
                kb.op("dve", lambda e: e.memset(S[:], 0.0), writes=[S])
                kb.op("dve", lambda e: e.memset(Sb[:], 0.0), writes=[Sb])
                odram = self.OF if x == 0 else self.OB
                border = range(nblk) if x == 0 else range(nblk - 1, -1, -1)
                for b in border:
                    p0 = 512 * b
                    ntok = min(512, Lp - p0)
                    nch = ntok // 64
                    N = slice(0, ntok)
                    P = slice(p0, p0 + ntok)
                    G = slice(0, 8)
                    self.dma("sp", g_t[G, N], self.GT[x, :, P], [self.GT], [g_t], g_t)
                    self.dma("sp", b_t[G, N], self.GT[2 + x, :, P], [self.GT], [b_t], b_t)
                    self.dma("sp", knT[:, :, N], self.GKT[:, :, P].rearrange("c p t -> p c t"), [self.GKT], [knT], knT)
                    self.dma("sp", qnT[:, :, N], self.GQT[:, :, P].rearrange("c p t -> p c t"), [self.GQT], [qnT], qnT)
                    kb.op("dve", lambda e: e.tensor_tensor_scan(cs[G, N], cmask[G, N], g_t[G, N], 0.0, ALU.mult, ALU.add),
                          reads=[cmask, g_t], writes=[cs])
                    cs3 = cs[G, N].rearrange("p (n c) -> p n c", c=64)
                    kb.op("dve", lambda e: e.tensor_copy(tot[G, 0:nch].unsqueeze(2), cs3[:, :, 63:64]), reads=[cs], writes=[tot])
                    totb = tot[G, 0:nch].unsqueeze(2).to_broadcast([8, nch, 64])
                    gc3 = gc[G, N].rearrange("p (n c) -> p n c", c=64)
                    if x == 0:
                        kb.op("dve", lambda e: e.tensor_copy(gc[G, N], cs[G, N]), reads=[cs], writes=[gc])
                    else:
                        kb.op("dve", lambda e: e.tensor_tensor(gc3, totb, cs3, ALU.subtract), reads=[tot, cs], writes=[gc])
                        kb.op("dve", lambda e: e.tensor_tensor(gc[G, N], gc[G, N], g_t[G, N], ALU.add), reads=[gc, g_t], writes=[gc])
                    kb.op("dve", lambda e: e.tensor_scalar_mul(ngc[G, N], gc[G, N], -1.0), reads=[gc], writes=[ngc])
                    kb.op("act", lambda e: e.activation(out=egc[G, N], in_=gc[G, N], func=AF.Exp), reads=[gc], writes=[egc])
                    kb.op("dve", lambda e: e.tensor_tensor(bg[G, N], b_t[G, N], egc[G, N], ALU.mult), reads=[b_t, egc], writes=[bg])
                    kd3 = kd[G, N].rearrange("p (n c) -> p n c", c=64)
                    kb.op("dve", lambda e: e.tensor_tensor(kd3, totb, gc3, ALU.subtract), reads=[tot, gc], writes=[kd])
                    kb.op("act", lambda e: e.activation(out=kd[G, N], in_=kd[G, N], func=AF.Exp), reads=[kd], writes=[kd])
                    bk = bank()
                    for j in range(4):
                        kb.op("pe", lambda e, j=j, bk=bk: e.matmul(bk[:, 8 * j:8 * j + nch], dmask[G, 128 * j:128 * j + 128], tot[G, 0:nch],
                                                                   start=True, stop=True), reads=[dmask, tot], writes=[bk])
                    kb.op("act", lambda e, bk=bk: e.activation(out=egl[:, :, 0:nch], in_=bk[:, 0:32].rearrange("p (j n) -> p j n", n=8)[:, :, 0:nch],
                                                               func=AF.Exp), reads=[bk], writes=[egl])
                    yield
                    for j in range(4):
                        bk = bank()
                        kb.op("pe", lambda e, j=j, bk=bk: e.matmul(bk[:, N], dmask[G, 128 * j:128 * j + 128], b_t[G, N], start=True, stop=True),
                              reads=[dmask, b_t], writes=[bk])
                        kb.op("dve", lambda e, j=j, bk=bk: e.tensor_tensor(kbT[:, j, N], knT[:, j, N], bk[:, N], ALU.mult),
                              reads=[knT, bk], writes=[kbT])
                        bk = bank()
                        kb.op("pe", lambda e, j=j, bk=bk: e.matmul(bk[:, N], dmask[G, 128 * j:128 * j + 128], egc[G, N], start=True, stop=True),
                              reads=[dmask, egc], writes=[bk])
                        kb.op("dve", lambda e, j=j, bk=bk: e.tensor_tensor(qgT[:, j, N], qnT[:, j, N], bk[:, N], ALU.mult),
                              reads=[qnT, bk], writes=[qgT])
                        yield
                    ntl = (ntok + 127) // 128
                    torder = range(ntl) if x == 0 else range(ntl - 1, -1, -1)
                    for tl in torder:
                        nt = min(128, ntok - tl * 128)
                        nr = nt // 64
                        TP = slice(0, nt)
                        TC = slice(tl * 128, tl * 128 + nt)
                        rows = slice(p0 + tl * 128, p0 + tl * 128 + nt)
                        self.dma("sp", ktok[TP, :], self.GKTOK[rows, :], [self.GKTOK], [ktok], ktok)
                        self.dma("sp", vtok[TP, :], self.GVTOK[rows, :], [self.GVTOK], [vtok], vtok)
                        bk = bank()
                        for qi, src in enumerate((b_t, bg, kd)):
                            kb.op("pe", lambda e, qi=qi, src=src, bk=bk: e.matmul(bk[TP, 8 * qi:8 * qi + 8], src[G, TC], self.ident_f[G, 0:8],
                                                                                 start=True, stop=True), reads=[src, self.ident_f], writes=[bk])
                        kb.op("act", lambda e, bk=bk: e.copy(gtk[TP, 0:24], bk[TP, 0:24]), reads=[bk], writes=[gtk])
                        for dst, src, c0, en in ((vb, vtok, 0, "pool"), (kbg, ktok, 8, "dve"), (kdec, ktok, 16, "pool")):
                            kb.op(en, lambda e, dst=dst, src=src, c0=c0: e.tensor_tensor(
                                v3(dst[TP, :]), v3(src[TP, :]), gtk[TP, c0:c0 + 8].unsqueeze(2).to_broadcast([nt, 8, 64]), ALU.mult),
                                reads=[src, gtk], writes=[dst])
                        yield
                        PA = bank()
                        PB = bank()
                        for cls in (0, 1):
                            for r in range(nr):
                                R_ = slice(64 * r, 64 * r + 64)
                                cc = slice(tl * 128 + 64 * r, tl * 128 + 64 * r + 64)
                                for h in range(H):
                                    j, m = h // 2, h % 2
                                    if (m == r) != (cls == 0):
                                        continue
                                    M_ = slice(64 * m, 64 * m + 64)
                                    C_ = slice(64 * h, 64 * h + 64)
                                    kb.op("pe", lambda e, R_=R_, C_=C_, M_=M_, j=j, cc=cc: e.matmul(PA[R_, C_], knT[M_, j, cc], kbT[M_, j, cc],
                                                                                                     start=True, stop=True),
                                          reads=[knT, kbT], writes=[PA])
                                    kb.op("pe", lambda e, R_=R_, C_=C_, M_=M_, j=j, cc=cc: e.matmul(PB[R_, C_], knT[M_, j, cc], qnT[M_, j, cc],
                                                                                                     start=True, stop=True),
                                          reads=[knT, qnT], writes=[PB])
                        PD = bank()
                        for r in range(nr):
                            cc = slice(tl * 128 + 64 * r, tl * 128 + 64 * r + 64)
                            kb.op("dve", lambda e, r=r, cc=cc: e.tensor_tensor(v3(BD[G, r, :]), v3(dmask[G, :]),
                                                                               gc[G, cc].unsqueeze(1).to_broadcast([8, 8, 64]), ALU.mult),
                                  reads=[dmask, gc], writes=[BD])
                            kb.op("pe", lambda e, r=r: e.matmul(PD[64 * r:64 * r + 64, :], self.ones_f[G, 0:64], BD[G, r, :], start=True, stop=False),
                                  reads=[self.ones_f, BD], writes=[PD])
                        kb.op("pe", lambda e: e.matmul(PD[TP, :], ngc[G, TC], dmask[G, :], start=False, stop=False),
                              reads=[ngc, dmask], writes=[PD])
                        mk = masks[x]
                        kb.op("pe", lambda e: e.matmul(v3(PD[TP, :]), self.ident_b[TP, TP], mk[TP, :].unsqueeze(1).to_broadcast([nt, 8, 64]),
                                                       start=False, stop=True), reads=[self.ident_b, mk], writes=[PD])
                        kb.op("act", lambda e: e.activation(out=E[TP, :], in_=PD[TP, :], func=AF.Exp), reads=[PD], writes=[E])
                        yield
                        kb.op("dve", lambda e: e.tensor_tensor(At[TP, :], PB[TP, :], E[TP, :], ALU.mult), reads=[PB, E], writes=[At])
                        kb.op("dve", lambda e: e.tensor_tensor(tt[TP, :], PA[TP, :], E[TP, :], ALU.mult), reads=[PA, E], writes=[tt])
                        X, Y, Rr = Xs[0], Ys[0], Rs[0]
                        kb.op("pool", lambda e: e.tensor_tensor(v3(X[TP, :]), v3(tt[TP, :]), negoff[TP, :].unsqueeze(1).to_broadcast([nt, 8, 64]),
                                                                ALU.mult), reads=[tt, negoff], writes=[X])
                        kb.op("pool", lambda e: e.tensor_tensor(v3(Rr[TP, :]), v3(X[TP, :]), identp[TP, :].unsqueeze(1).to_broadcast([nt, 8, 64]),
                                                                ALU.add), reads=[X, identp], writes=[Rr])
                        PY = bank()
                        for r in range(nr):
                            R_ = slice(64 * r, 64 * r + 64)
                            for h in range(H):
                                C_ = slice(64 * h, 64 * h + 64)
                                kb.op("pe", lambda e, R_=R_, C_=C_: e.matmul(PY[R_, C_], X[R_, C_], self.ident_b[R_, R_], start=True, stop=True),
                                      reads=[X, self.ident_b], writes=[PY])
                        kb.op("act", lambda e: e.copy(Y[TP, :], PY[TP, :]), reads=[PY], writes=[Y])
                        yield
                        for jj in range(5):
                            Xn, Yn, Rn = Xs[(jj + 1) % 2], Ys[(jj + 1) % 2], Rs[(jj + 1) % 2]
                            if jj < 4:
                                PX = bank()
                                blocks(PX, Y, X, nr)
                                kb.op("act", lambda e, PX=PX, Xn=Xn: e.copy(Xn[TP, :], PX[TP, :]), reads=[PX], writes=[Xn])
                            PY = bank()
                            blocks(PY, X, Y, nr)
                            kb.op("dve", lambda e, PY=PY: e.tensor_tensor(v3(IY[TP, :]), v3(PY[TP, :]),
                                                                          identpf[TP, :].unsqueeze(1).to_broadcast([nt, 8, 64]), ALU.add),
                                  reads=[PY, identpf], writes=[IY])
                            if jj < 4:
                                kb.op("dve", lambda e, PY=PY, Yn=Yn: e.tensor_copy(Yn[TP, :], PY[TP, :]), reads=[PY], writes=[Yn])
                            yield
                            PR = bank()
                            blocks(PR, IY, Rr, nr)
                            kb.op("act", lambda e, PR=PR, Rn=Rn: e.copy(Rn[TP, :], PR[TP, :]), reads=[PR], writes=[Rn])
                            X, Y, Rr = Xn, Yn, Rn
                            yield
                        Tt = Rr
                        PU = bank()
                        blocks(PU, Tt, vb, nr)
                        kb.op("act", lambda e, PU=PU: e.copy(u_sb[TP, :], PU[TP, :]), reads=[PU], writes=[u_sb])
                        PW = bank()
                        for cls in (0, 1):
                            for r in range(nr):
                                R_ = slice(64 * r, 64 * r + 64)
                                for h in range(H):
                                    j, m = h // 2, h % 2
                                    if (m == r) != (cls == 0):
                                        continue
                                    C_ = slice(64 * h, 64 * h + 64)
                                    kb.op("pe", lambda e, R_=R_, C_=C_, j=j, m=m, r=r: e.matmul(
                                        PW[64 * m:64 * m + 64, 128 * j + 64 * r:128 * j + 64 * r + 64], kbg[R_, C_], Tt[R_, C_], start=True, stop=True),
                                        reads=[kbg, Tt], writes=[PW])
                        if nr == 2:
                            kb.op("act", lambda e, PW=PW: e.mul(nwT[:, :], PW[:, :], -1.0), reads=[PW], writes=[nwT])
                        else:
                            kb.op("act", lambda e, PW=PW: e.mul(nwT[:, :].rearrange("p (j r c) -> p j r c", j=4, r=2)[:, :, 0, :],
                                                               PW[:, :].rearrange("p (j r c) -> p j r c", j=4, r=2)[:, :, 0, :], -1.0),
                                  reads=[PW], writes=[nwT])
                        yield
                        rorder = range(nr) if x == 0 else range(nr - 1, -1, -1)
                        for r in rorder:
                            R_ = slice(64 * r, 64 * r + 64)
                            nb = tl * 2 + r
                            cc = slice(tl * 128 + 64 * r, tl * 128 + 64 * r + 64)
                            PV = bank()
                            for mm_ in (0, 1):
                                for h in range(H):
                                    j, m = h // 2, h % 2
                                    if m != mm_:
                                        continue
                                    M_ = slice(64 * m, 64 * m + 64)
                                    kb.op("pe", lambda e, h=h, j=j, M_=M_, r=r, R_=R_, PV=PV: e.matmul(
                                        PV[R_, 64 * h:64 * h + 64], nwT[M_, 128 * j + 64 * r:128 * j + 64 * r + 64], Sb[M_, 64 * j:64 * j + 64],
                                        start=True, stop=True), reads=[nwT, Sb], writes=[PV])
                            kb.op("dve", lambda e, R_=R_, PV=PV: e.tensor_tensor(vn[R_, :], u_sb[R_, :], PV[R_, :], ALU.add),
                                  reads=[u_sb, PV], writes=[vn])
                            PO = bank()
                            for mm_ in (1 - r, r):
                                for h in range(H):
                                    j, m = h // 2, h % 2
                                    if m != mm_:
                                        continue
                                    M_ = slice(64 * m, 64 * m + 64)
                                    C_ = slice(64 * h, 64 * h + 64)
                                    kb.op("pe", lambda e, M_=M_, C_=C_, j=j, R_=R_, cc=cc, PO=PO: e.matmul(
                                        PO[R_, C_], qgT[M_, j, cc], Sb[M_, 64 * j:64 * j + 64], start=True, stop=True),
                                        reads=[qgT, Sb], writes=[PO])
                            yield
                            PO2 = bank()
                            for h in range(H):
                                C_ = slice(64 * h, 64 * h + 64)
                                kb.op("pe", lambda e, C_=C_, R_=R_, PO2=PO2: e.matmul(PO2[R_, C_], At[R_, C_], vn[R_, C_], start=True, stop=True),
                                      reads=[At, vn], writes=[PO2])
                            PS_ = bank()
                            for h in range(H):
                                j, m = h // 2, h % 2
                                C_ = slice(64 * h, 64 * h + 64)
                                kb.op("pe", lambda e, j=j, m=m, R_=R_, C_=C_, PS_=PS_: e.matmul(
                                    PS_[64 * m:64 * m + 64, 64 * j:64 * j + 64], kdec[R_, C_], vn[R_, C_], start=True, stop=True),
                                    reads=[kdec, vn], writes=[PS_])
                            kb.op("act", lambda e, R_=R_, PO=PO: e.copy(o_sb[R_, :], PO[R_, :]), reads=[PO], writes=[o_sb])
                            kb.op("dve", lambda e, R_=R_, PO2=PO2: e.tensor_tensor(o_sb[R_, :], o_sb[R_, :], PO2[R_, :], ALU.add),
                                  reads=[o_sb, PO2], writes=[o_sb])
                            kb.op("dve", lambda e, nb=nb: e.tensor_tensor(v3(S[:, :]), v3(S[:, :]),
                                                                          egl[:, :, nb:nb + 1].to_broadcast([128, 4, 64]), ALU.mult),
                                  reads=[S, egl], writes=[S])
                            kb.op("dve", lambda e, PS_=PS_: e.tensor_tensor(S[:, :], S[:, :], PS_[:, 0:256], ALU.add), reads=[S, PS_], writes=[S])
                            kb.op("act", lambda e: e.copy(Sb[:, :], S[:, :]), reads=[S], writes=[Sb])
                            yield
                        self.dma("sp", odram[rows, :], o_sb[TP, :], [o_sb], [odram], o_sb)

            gens = [pass_gen(0), pass_gen(1)]
            while gens:
                for g in list(gens):
                    try:
                        next(g)
                    except StopIteration:
                        gens.remove(g)

            ofs = [kb.sb(st, f"of_t{i}", [128, 512], F32) for i in range(2)]
            obs = [kb.sb(st, f"ob_t{i}", [128, 512], F32) for i in range(2)]
            zts = [kb.sb(st, f"z_t{i}", [128, 512], F32) for i in range(2)]
            osum = kb.sb(st, "osum", [128, 512], F32)
            osq = kb.sb(st, "osq", [128, 512], F32)
            ssum = kb.sb(st, "ssum", [128, 16], F32)
            gout = kb.sb(st, "gout", [128, 512], BF16)
            gTs = [kb.sb(st, f"gT{i}", [128, 4, 128], BF16) for i in range(2)]
            for t_ in ofs + obs + zts:
                kb.op("pool", lambda e, t_=t_: e.memset(t_[:], 0.0), writes=[t_])
            kb.op("pool", lambda e: e.memset(gout[:], 0.0), writes=[gout])
            ntp = (Lp + 127) // 128
            for t in range(ntp):
                nt = min(128, Lp - 128 * t)
                TP = slice(0, nt)
                rows = slice(128 * t, 128 * t + nt)
                of_t, ob_t, z_t, gT = ofs[t % 2], obs[t % 2], zts[t % 2], gTs[t % 2]
                self.dma("sp", of_t[TP, :], self.OF[rows, :], [self.OF], [of_t], of_t)
                self.dma("sp", ob_t[TP, :], self.OB[rows, :], [self.OB], [ob_t], ob_t)
                self.dma("sp", z_t[TP, :], self.Z[rows, :], [self.Z], [z_t], z_t)
                kb.op("pool", lambda e: e.tensor_tensor(osum[TP, :], of_t[TP, :], ob_t[TP, :], ALU.add), reads=[of_t, ob_t], writes=[osum])
                kb.op("act", lambda e: e.activation(out=osq[TP, :], in_=osum[TP, :], func=AF.Square), reads=[osum], writes=[osq])
                kb.op("dve", lambda e: e.tensor_reduce(ssum[TP, 0:8], v3(osq[TP, :]), AX.X, ALU.add), reads=[osq], writes=[ssum])
                kb.op("act", lambda e: e.activation(out=ssum[TP, 8:16], in_=ssum[TP, 0:8], func=AF.Ln, bias=self.cst[TP, 1:2],
                                                    scale=1.0 / 64.0), reads=[ssum, self.cst], writes=[ssum])
                kb.op("act", lambda e: e.activation(out=ssum[TP, 8:16], in_=ssum[TP, 8:16], func=AF.Exp, scale=-0.5),
                      reads=[ssum], writes=[ssum])
                kb.op("dve", lambda e: e.tensor_tensor(v3(osum[TP, :]), v3(osum[TP, :]),
                                                       ssum[TP, 8:16].unsqueeze(2).to_broadcast([nt, 8, 64]), ALU.mult),
                      reads=[osum, ssum], writes=[osum])
                kb.op("pool", lambda e: e.tensor_tensor(v3(osum[TP, :]), v3(osum[TP, :]),
                                                        gnorm[TP, :].unsqueeze(1).to_broadcast([nt, 8, 64]), ALU.mult),
                      reads=[osum, gnorm], writes=[osum])
                kb.op("dve", lambda e: e.tensor_tensor(gout[TP, :], osum[TP, :], z_t[TP, :], ALU.mult), reads=[osum, z_t], writes=[gout])
                for j in range(4):
                    kb.op("pe", lambda e, j=j: e.transpose(pT[:, j, 0:nt], gout[TP, 128 * j:128 * j + 128], self.ident_b[TP, TP]),
                          reads=[gout, self.ident_b], writes=[pT])
                kb.op("act", lambda e: e.copy(gT[:, :, 0:nt], pT[:, :, 0:nt]), reads=[pT], writes=[gT])
                c0 = 64 if t == 0 else 0
                x0 = 128 * t + c0 - 64
                if nt > c0:
                    self.dma("sp", self.CATT[4:8, :, x0:x0 + nt - c0].rearrange("c p t -> p c t"), gT[:, :, c0:nt],
                             [gT], [self.CATT], gT)
        kb.barrier()

    def phase3(self, si):
        kb = self.kb
        Lx = self.seq_lens[si]
        Lp = Lx + 64
        nkc = (Lp + 127) // 128
        with ExitStack() as st:
            kt = kb.sb(st, "kt", [128, 8, nkc * 128], BF16)
            va = kb.sb(st, "va", [128, nkc, 8, 128], BF16)
            kb.op("pool", lambda e: e.memset(va[:, :, :, 64:128], 1.0), writes=[va])
            kb.op("pool", lambda e: e.memset(va[:, :, :, 0:64], 0.0), writes=[va])
            self.dma("sp", kt[0:96, :, 0:Lp], self.KT[:, :, 0:Lp].rearrange("h r t -> r h t"), [self.KT], [kt], kt)
            for kc in range(nkc):
                nk = min(128, Lp - kc * 128)
                self.dma("sp", va[0:nk, kc, :, 0:64], self.V[kc * 128:kc * 128 + nk, :].rearrange("t (h e) -> t h e", h=8),
                         [self.V], [va], va)
            kb.op("pool", lambda e: e.memset(va[0:32, 0, :, :], 0.0), writes=[va])
            kb.op("pool", lambda e: e.memset(va[32:48, 0, :, :], 0.0), writes=[va])
            qts = [kb.sb(st, f"qt{i}", [128, 8, 512], BF16) for i in range(2)]
            pts = [kb.sb(st, f"pt{i}", [128, 512], BF16) for i in range(5)]
            rec = kb.sb(st, "rec", [128, 512], F32)
            mo = kb.sb(st, "mo", [128, 4, 512], BF16)
            pss = [kb.ps(st, f"pss{i}", [128, 512], F32) for i in range(5)]
            self._p3cnt = 0
            pos = [kb.ps(st, f"pos{i}", [128, 512], F32) for i in range(2)]
            nqb = Lx // 512 if Lx % 512 == 0 else (Lx + 511) // 512
            cnt = 0
            for qb in range(nqb):
                nq = min(512, Lx - qb * 512)
                Q = slice(0, nq)
                qt = qts[qb % 2]
                self.dma("sp", qt[0:96, :, Q], self.QT[:, :, qb * 512:qb * 512 + nq].rearrange("h r t -> r h t"),
                         [self.QT], [qt], qt)
                for h in range(H):
                    po = pos[h % 2]
                    LOOK = 3
                    slots = {}

                    def emit_s(kc, h=h):
                        nk = min(128, Lp - kc * 128)
                        ps_ = pss[self._p3cnt % 5]
                        pt = pts[self._p3cnt % 5]
                        self._p3cnt += 1
                        slots[kc] = (ps_, pt, nk)
                        kb.op("pe", lambda e: e.matmul(ps_[0:nk, Q], kt[0:96, h, kc * 128:kc * 128 + nk], qt[0:96, h, Q], start=True, stop=True),
                              reads=[kt, qt], writes=[ps_])

                    for kc in range(min(LOOK, nkc)):
                        emit_s(kc)
                    for kc in range(nkc):
                        ps_, pt, nk = slots.pop(kc)
                        kb.op("act", lambda e, ps_=ps_, pt=pt, nk=nk: e.activation(out=pt[0:nk, Q], in_=ps_[0:nk, Q], func=AF.Exp),
                              reads=[ps_], writes=[pt])
                        if kc + LOOK < nkc:
                            emit_s(kc + LOOK)
                        kb.op("pe", lambda e, po=po, pt=pt, kc=kc, nk=nk, h=h: e.matmul(
                            po[:, Q], va[0:nk, kc, h, :], pt[0:nk, Q], start=(kc == 0), stop=(kc == nkc - 1)),
                            reads=[va, pt], writes=[po])
                    kb.op("dve", lambda e, po=po: e.reciprocal(rec[0:64, Q], po[64:128, Q]), reads=[po], writes=[rec])
                    kb.op("dve", lambda e, po=po, h=h: e.tensor_tensor(mo[64 * (h % 2):64 * (h % 2) + 64, h // 2, Q], po[0:64, Q],
                                                                       rec[0:64, Q], ALU.mult), reads=[po, rec], writes=[mo])
                self.dma("sp", self.CATT[0:4, :, qb * 512:qb * 512 + nq].rearrange("c p t -> p c t"), mo[:, :, Q],
                         [mo], [self.CATT], mo)
        kb.barrier()

    def phase4a(self, si):
        kb = self.kb
        Lx = self.seq_lens[si]
        W = self.w
        with ExitStack() as st:
            w_out = kb.sb(st, "w_out", [128, 8, D], BF16)
            for kc in range(8):
                self.dma("sp", w_out[:, kc, :], self.Wb["w_out"][kc * 128:kc * 128 + 128, :], [self.Wb["w_out"]], [w_out], w_out)
            lng = self.load_bcast_vec(st, "lng", W["ln_in_g"], D)
            lnb = self.load_bcast_vec(st, "lnb", W["ln_in_b"], D)
            l1g = self.load_bcast_vec(st, "l1g", W["ln1_g"], D)
            l1b = self.load_bcast_vec(st, "l1b", W["ln1_b"], D)
            xts = [kb.sb(st, f"xt{i}", [128, D], F32) for i in range(3)]
            cats = [kb.sb(st, f"cat{i}", [128, 8, 128], BF16) for i in range(3)]
            hress = [kb.sb(st, f"hres{i}", [128, D], F32) for i in range(3)]
            r1s = [kb.sb(st, f"r1{i}", [128, D], F32) for i in range(2)]
            h1s = [kb.sb(st, f"h1{i}", [128, D], F32) for i in range(2)]
            h1bs = [kb.sb(st, f"h1b{i}", [128, D], BF16) for i in range(2)]
            h1ts = [kb.sb(st, f"h1t{i}", [128, 8, 128], BF16) for i in range(2)]
            statss = [kb.sb(st, f"stats{i}", [128, 32], F32) for i in range(5)]
            pT = kb.ps(st, "pT", [128, 8, 128], BF16)
            pb = [kb.ps(st, f"pb{i}", [128, 512], F32) for i in range(4)]
            ntile = Lx // 128

            def stage_a(k):
                xt, cat = xts[k % 3], cats[k % 3]
                rows = slice(128 * k, 128 * k + 128)
                self.dma("sp", xt[:, :], self.x_in[si][rows, :], [self.xin_reg], [xt], xt)
                self.dma("sp", cat[:, :, :], self.CATT[:, :, rows].rearrange("c p t -> p c t"), [self.CATT], [cat], cat)
                return self.layer_norm_gen(xt, xt[:, :], lng, lnb, hress[k % 3], hress[k % 3][:, :], xt, statss[k % 3])

            def stage_b(k):
                cat, hres, r1 = cats[k % 3], hress[k % 3], r1s[k % 2]
                for nh in range(2):
                    bk = pb[(2 * k + nh) % 4]
                    F = slice(nh * 512, nh * 512 + 512)
                    for kc in range(8):
                        kb.op("pe", lambda e, kc=kc, bk=bk, F=F, cat=cat: e.matmul(bk[:, :], cat[:, kc, :], w_out[:, kc, F],
                                                                                   start=(kc == 0), stop=(kc == 7)),
                              reads=[cat, w_out], writes=[bk])
                    kb.op("dve", lambda e, bk=bk, F=F: e.scalar_tensor_tensor(r1[:, F], hres[:, F], DN_ALPHA, bk[:, :],
                                                                             ALU.mult, ALU.add), reads=[hres, bk], writes=[r1])

            def stage_c_ln(k):
                r1, h1 = r1s[k % 2], h1s[k % 2]
                return self.layer_norm_gen(r1, r1[:, :], l1g, l1b, h1, h1[:, :], r1, statss[3 + k % 2])

            def stage_c_out(k):
                h1, h1b, h1t = h1s[k % 2], h1bs[k % 2], h1ts[k % 2]
                rows = slice(128 * k, 128 * k + 128)
                self.dma("sp", self.H1[rows, :], h1[:, :], [h1], [self.H1], h1)
                kb.op("act", lambda e: e.copy(h1b[:, :], h1[:, :]), reads=[h1], writes=[h1b])
                self.transpose_tile(h1b, lambda kk: h1b[:, kk * 128:(kk + 1) * 128], pT, h1t,
                                    lambda k0, k1: h1t[:, k0:k1, :])
                self.dma("sp", self.H1T[:, :, rows].rearrange("c p t -> p c t"), h1t[:, :, :], [h1t], [self.H1T], h1t)

            self.run_gens(stage_a(0), stage_a(1) if ntile > 1 else None, stage_a(2) if ntile > 2 else None)
            stage_b(0)
            for k in range(ntile):
                if k + 1 < ntile:
                    stage_b(k + 1)
                ga = stage_a(k + 3) if k + 3 < ntile else None
                self.run_gens(ga, stage_c_ln(k))
                stage_c_out(k)
        kb.barrier()

    def phase4b(self, si):
        kb = self.kb
        Lx = self.seq_lens[si]
        W = self.w
        with ExitStack() as st:
            w1 = kb.sb(st, "w_ff1", [128, 8, DFF], BF16)
            w2 = kb.sb(st, "w_ff2", [128, 32, D], BF16)
            for kc in range(8):
                self.dma("sp", w1[:, kc, :], self.Wb["w_ff1"][kc * 128:kc * 128 + 128, :], [self.Wb["w_ff1"]], [w1], w1)
            for kc in range(32):
                self.dma("sp", w2[:, kc, :], self.Wb["w_ff2"][kc * 128:kc * 128 + 128, :], [self.Wb["w_ff2"]], [w2], w2)
            l2g = self.load_bcast_vec(st, "l2g", W["ln2_g"], D)
            l2b = self.load_bcast_vec(st, "l2b", W["ln2_b"], D)
            aT = kb.sb(st, "aT", [128, 32, 512], BF16)
            h1T = kb.sb(st, "h1T", [128, 8, 512], BF16)
            rls = [kb.sb(st, f"rl{i}", [128, 512], BF16) for i in range(2)]
            h1 = kb.sb(st, "h1", [128, D], F32)
            r2 = kb.sb(st, "r2", [128, D], F32)
            yo = kb.sb(st, "yo", [128, D], F32)
            lntmp = kb.sb(st, "lntmp", [128, D], F32)
            stats = kb.sb(st, "stats", [128, 32], F32)
            pb = [kb.ps(st, f"pb{i}", [128, 512], F32) for i in range(6)]
            bi = 0
            for b in range((Lx + 511) // 512):
                n = min(512, Lx - 512 * b)
                N = slice(0, n)
                cols = slice(512 * b, 512 * b + n)
                self.dma("sp", h1T[:, :, N], self.H1T[:, :, cols].rearrange("c p t -> p c t"), [self.H1T], [h1T], h1T)
                for mc in range(32):
                    bk = pb[bi % 6]
                    bi += 1
                    rl = rls[mc % 2]
                    for kc in range(8):
                        kb.op("pe", lambda e, kc=kc, mc=mc, bk=bk: e.matmul(bk[:, N], w1[:, kc, mc * 128:mc * 128 + 128], h1T[:, kc, N],
                                                                            start=(kc == 0), stop=(kc == 7)),
                              reads=[w1, h1T], writes=[bk])
                    kb.op("act", lambda e, bk=bk, rl=rl: e.activation(out=rl[:, N], in_=bk[:, N], func=AF.Relu), reads=[bk], writes=[rl])
                    kb.op("pool", lambda e, rl=rl, mc=mc: e.tensor_tensor(aT[:, mc, N], rl[:, N], rl[:, N], ALU.mult),
                          reads=[rl], writes=[aT])
                for tl in range(n // 128):
                    rows = slice(512 * b + 128 * tl, 512 * b + 128 * tl + 128)
                    self.dma("sp", h1[:, :], self.H1[rows, :], [self.H1], [h1], h1)
                    for nh in range(2):
                        bk = pb[bi % 6]
                        bi += 1
                        F = slice(nh * 512, nh * 512 + 512)
                        for mc in range(32):
                            kb.op("pe", lambda e, mc=mc, bk=bk, F=F, tl=tl: e.matmul(bk[:, :], aT[:, mc, tl * 128:tl * 128 + 128],
                                                                                     w2[:, mc, F], start=(mc == 0), stop=(mc == 31)),
                                  reads=[aT, w2], writes=[bk])
                        kb.op("dve", lambda e, bk=bk, F=F: e.scalar_tensor_tensor(r2[:, F], h1[:, F], DN_ALPHA, bk[:, :],
                                                                                 ALU.mult, ALU.add), reads=[h1, bk], writes=[r2])
                    self.layer_norm(r2, r2[:, :], l2g, l2b, yo, yo[:, :], r2, stats)
                    self.dma("sp", self.y_out[si][rows, :], yo[:, :], [yo], [self.yout_reg], yo)
        kb.barrier()


WEIGHT_NAMES = ["meta_tokens", "ln_in_g", "ln_in_b", "w_in", "g_cq", "g_ckv", "w_uq", "w_uk", "w_uv", "conv_w",
                "a_log_f", "a_log_b", "dt_bias_f", "dt_bias_b", "gdn_norm_g", "w_out", "ln1_g", "ln1_b",
                "w_ff1", "w_ff2", "ln2_g", "ln2_b"]


def build_prog(seq_lens, debug=False, phases=None):
    p = Prog(seq_lens, debug=debug)
    p.declare()
    p.setup_consts()
    for si in range(len(seq_lens)):
        for name in ["phase1", "phase2a", "phase2b", "phase3", "phase4a", "phase4b"]:
            if phases is not None and name not in phases:
                continue
            if not hasattr(p, name):
                continue
            getattr(p, name)(si)
    p.finish()
    return p


SEQ_LENS = [2048, 2048, 2048, 2048, 4096]
_PROG_CACHE = {}


def kernel(**inputs):
    n = 8
    x_prompt = np.asarray(inputs["x_prompt"], dtype=np.float32)
    x_sample = np.asarray(inputs["x_sample"], dtype=np.float32)
    wmap = {}
    for k in WEIGHT_NAMES:
        a = np.asarray(inputs[k], dtype=np.float32)
        if k not in ("meta_tokens", "ln_in_g", "ln_in_b"):
            a = a[0]
        wmap[k] = np.ascontiguousarray(a)
    prog = build_prog(SEQ_LENS, debug=False)
    in_maps = []
    for c in range(n):
        m = dict(wmap)
        for i in range(4):
            m[f"x{i}"] = np.ascontiguousarray(x_sample[4 * c + i])
        m["x4"] = np.ascontiguousarray(x_prompt[c // 2])
        in_maps.append(m)
    res = run_bass_kernel_spmd(prog.nc, in_maps, core_ids=list(range(n)))
    y_sample = np.empty_like(x_sample)
    y_prompt = np.empty_like(x_prompt)
    for c in range(n):
        r = res.results[c]
        for i in range(4):
            y_sample[4 * c + i] = np.asarray(r[f"y{i}"], dtype=np.float32)
        if c % 2 == 0:
            y_prompt[c // 2] = np.asarray(r["y4"], dtype=np.float32)
    return (y_prompt, y_sample)
```

```python
import numpy as np
from contextlib import ExitStack
import concourse.bass as bass
import concourse.mybir as mybir
from concourse.bass_utils import run_bass_kernel_spmd

F32 = mybir.dt.float32
BF16 = mybir.dt.bfloat16
I32 = mybir.dt.int32
AF = mybir.ActivationFunctionType
ALU = mybir.AluOpType
AX = mybir.AxisListType

D = 1024
NIN = 2752
H = 8
DFF = 4096
C_Q0, C_KV0, C_KR0, C_QKV0, C_Z0, C_G0 = 0, 384, 640, 672, 2208, 2720
DN_ALPHA = 2.0 ** 0.25
LN_EPS = 1e-5
RMS_EPS = 1e-6
QSCALE = 96.0 ** -0.5
NEG = -30000.0


class Reg:
    __slots__ = ("name", "w", "r", "sem", "cnt")

    def __init__(self, name):
        self.name = name
        self.w = {}
        self.r = {}
        self.sem = None
        self.cnt = 0


class Buf:
    def __init__(self, t, reg):
        self.t = t
        self.reg = reg

    def __getitem__(self, k):
        return self.t[k]


class _PEProxy:
    def __init__(self, kb):
        self.kb = kb
        self.e = kb.engs["pe"]

    def matmul(self, out, lhsT, rhs, **kw):
        self.kb._pe_pos(lhsT, out)
        return self.e.matmul(out, lhsT, rhs, **kw)

    def transpose(self, out, in_, ident):
        self.kb._pe_pos(in_, out)
        return self.e.transpose(out, in_, ident)


class KB:
    def __init__(self, nc, es):
        self.nc = nc
        self.es = es
        self.engs = {"pe": nc.tensor, "act": nc.scalar, "dve": nc.vector, "pool": nc.gpsimd, "sp": nc.sync}
        self.sem = {}
        self.cnt = {}
        self.waited = {}
        self.semname = {}
        for n in self.engs:
            self.sem[n] = es.enter_context(nc.semaphore("e_" + n))
            self.cnt[n] = 0
            self.waited[n] = {}
        self.nreg = 0
        self.all_dma_regs = []
        self.dpool = []
        self.dfree = {"sw": [], "hw": []}
        self.dkind = {}
        self.ninst = 0
        self.pe_live = set()
        self.nfence = 0
        self.pend = {n: False for n in self.engs}
        self.last_pe_w = None
        self.last_ins = {n: None for n in self.engs}
        self.pe_proxy = _PEProxy(self)

    def reg(self, name):
        self.nreg += 1
        return Reg(f"{name}_{self.nreg}")

    def sb(self, stack, name, shape, dt):
        self.nreg += 1
        nm = f"{name}_{self.nreg}"
        t = stack.enter_context(self.nc.sbuf_tensor(nm, list(shape), dt))
        return Buf(t, Reg(nm))

    def ps(self, stack, name, shape, dt):
        self.nreg += 1
        nm = f"{name}_{self.nreg}"
        t = stack.enter_context(self.nc.psum_tensor(nm, list(shape), dt))
        return Buf(t, Reg(nm))

    def _regs(self, xs):
        out = []
        for x in xs:
            if x is None:
                continue
            out.append(x.reg if isinstance(x, Buf) else x)
        return out

    def _flush(self, en):
        if self.pend[en]:
            self.last_ins[en].then_inc(self.sem[en], 1)
            self.cnt[en] += 1
            self.pend[en] = False

    def _wait(self, en, key, sem, val):
        w = self.waited[en]
        if w.get(key, 0) >= val:
            return
        if key in self.engs and val > self.cnt[key]:
            assert self.pend[key] and val == self.cnt[key] + 1
            self._flush(key)
        self.engs[en].wait_ge(sem, val)
        w[key] = val

    def op(self, en, fn, reads=(), writes=(), dma=None):
        reads = self._regs(reads)
        writes = self._regs(writes)
        deps = {}
        for R in reads:
            for k, v in R.w.items():
                if deps.get(k, (None, 0))[1] < v[1]:
                    deps[k] = v
        for R in writes:
            for dd in (R.w, R.r):
                for k, v in dd.items():
                    if deps.get(k, (None, 0))[1] < v[1]:
                        deps[k] = v
        for k, (sem, val) in deps.items():
            if k == "pe" and en == "pe" and dma is None:
                continue
            self._wait(en, k, sem, val)
        if dma is None and en == "pe":
            wkey = tuple(id(R) for R in writes)
            if self.pend["pe"] and wkey != self.last_pe_w:
                self._flush("pe")
            self.last_pe_w = wkey
        ins = fn(self.pe_proxy if en == "pe" else self.engs[en])
        self.ninst += 1
        if dma is not None:
            R = dma.reg if isinstance(dma, Buf) else dma
            if R.sem is None:
                kind = "sw" if en == "pool" else "hw"
                if self.dfree[kind]:
                    R.sem = self.dfree[kind].pop()
                else:
                    R.sem = len(self.dpool)
                    self.dpool.append([self.es.enter_context(self.nc.semaphore(f"ds{R.sem}")), 0])
                    self.dkind[R.sem] = kind
                self.all_dma_regs.append(R)
            assert self.dkind[R.sem] == ("sw" if en == "pool" else "hw"), "mixed DMA queues on one region semaphore"
            ent = self.dpool[R.sem]
            ent[1] += 16
            ins.then_inc(ent[0], 16)
            key, tok = f"ds{R.sem}", (ent[0], ent[1])
        else:
            if en == "pe":
                self.last_ins[en] = ins
                self.pend[en] = True
                key, tok = en, (self.sem[en], self.cnt[en] + 1)
            else:
                self.cnt[en] += 1
                ins.then_inc(self.sem[en], 1)
                key, tok = en, (self.sem[en], self.cnt[en])
        for R in reads:
            R.r[key] = tok
        for R in writes:
            R.w[key] = tok
        return tok

    def _pe_pos(self, kap, oap):
        k0, kn = kap.base_partition(), kap.partition_size()
        kq = 32 if kn <= 32 else (64 if kn <= 64 else 128)
        m0, mn = oap.base_partition(), oap.partition_size()
        key = (k0, kq, m0, mn)
        if key in self.pe_live:
            return
        conflict = False
        for (a0, aq, b0, bn) in self.pe_live:
            if (a0, aq) != (k0, kq) and not (m0 + mn <= b0 or b0 + bn <= m0):
                conflict = True
                break
        if conflict:
            self._flush("pe")
            if self.cnt["pe"] > 0:
                self.engs["pe"].wait_ge(self.sem["pe"], self.cnt["pe"])
                self.waited["pe"]["pe"] = self.cnt["pe"]
            self.pe_live = set()
            self.nfence += 1
        self.pe_live.add(key)

    def barrier(self):
        for en in self.engs:
            self._flush(en)
        for en in self.engs:
            for o in self.engs:
                if o != en and self.cnt[o] > 0:
                    self._wait(en, o, self.sem[o], self.cnt[o])
            for i, ent in enumerate(self.dpool):
                if ent[1] > 0:
                    self._wait(en, f"ds{i}", ent[0], ent[1])
        for R in self.all_dma_regs:
            self.dfree[self.dkind[R.sem]].append(R.sem)
            R.sem = None
        self.all_dma_regs = []


def bc(ap, shape):
    return ap.to_broadcast(list(shape))


class _Stop(Exception):
    pass


class Prog:
    stop_at = None

    def _ck(self, n):
        return self.stop_at is not None and n == self.stop_at

    def __init__(self, seq_lens, debug=False):
        self.seq_lens = list(seq_lens)
        self.debug = debug
        self.nc = bass.Bass("TRN2", target_bir_lowering=False)
        self.es = ExitStack()
        self.kb = KB(self.nc, self.es)
        self.maxLx = max(self.seq_lens)
        self.maxLp = self.maxLx + 64

    def dram_in(self, name, shape, dt=F32):
        return self.nc.dram_tensor(name, list(shape), dt, kind="ExternalInput").ap()

    def dram_out(self, name, shape, dt=F32):
        return self.nc.dram_tensor(name, list(shape), dt, kind="ExternalOutput").ap()

    def dram_scr(self, name, shape, dt):
        kind = "ExternalOutput" if self.debug else "Internal"
        ap = self.nc.dram_tensor(name, list(shape), dt, kind=kind).ap()
        return Buf(ap, self.kb.reg(name))

    def declare(self):
        nseq = len(self.seq_lens)
        self.x_in = [self.dram_in(f"x{i}", [L, D]) for i, L in enumerate(self.seq_lens)]
        self.y_out = [self.dram_out(f"y{i}", [L, D]) for i, L in enumerate(self.seq_lens)]
        self.xin_reg = self.kb.reg("xin")
        self.yout_reg = self.kb.reg("yout")
        self.w = {}
        for name, shape in [
            ("meta_tokens", [16, D]), ("ln_in_g", [D]), ("ln_in_b", [D]), ("w_in", [D, NIN]),
            ("g_cq", [384]), ("g_ckv", [256]), ("w_uq", [384, 768]), ("w_uk", [256, 512]), ("w_uv", [256, 512]),
            ("conv_w", [4, 1536]), ("a_log_f", [8]), ("a_log_b", [8]), ("dt_bias_f", [8]), ("dt_bias_b", [8]),
            ("gdn_norm_g", [64]), ("w_out", [D, D]), ("ln1_g", [D]), ("ln1_b", [D]),
            ("w_ff1", [D, DFF]), ("w_ff2", [DFF, D]), ("ln2_g", [D]), ("ln2_b", [D]),
        ]:
            self.w[name] = self.dram_in(name, shape)
        self.w_reg = self.kb.reg("weights")
        self.Wb = {}
        for name, shape in [("w_in", [D, NIN]), ("w_uq", [384, 768]), ("w_uk", [256, 512]), ("w_uv", [256, 512]),
                            ("w_out", [D, D]), ("w_ff1", [D, DFF]), ("w_ff2", [DFF, D])]:
            self.Wb[name] = self.dram_scr("wb_" + name, shape, BF16)
        Lp, Lx = self.maxLp, self.maxLx
        self.QT = self.dram_scr("s_qt", [H, 96, Lx], BF16)
        self.KT = self.dram_scr("s_kt", [H, 96, Lp], BF16)
        self.V = self.dram_scr("s_v", [Lp, 512], BF16)
        self.QKVT = self.dram_scr("s_qkvt", [12, 128, Lp + 4], F32)
        self.Z = self.dram_scr("s_z", [Lp, 512], F32)
        self.GT = self.dram_scr("s_gt", [4, 8, Lp], F32)
        self.GQT = self.dram_scr("s_gqt", [4, 128, Lp], BF16)
        self.GKT = self.dram_scr("s_gkt", [4, 128, Lp], BF16)
        self.GKTOK = self.dram_scr("s_gktok", [Lp, 512], BF16)
        self.GVTOK = self.dram_scr("s_gvtok", [Lp, 512], BF16)
        self.OF = self.dram_scr("s_of", [Lp, 512], F32)
        self.OB = self.dram_scr("s_ob", [Lp, 512], F32)
        self.CATT = self.dram_scr("s_catt", [8, 128, Lx], BF16)
        self.H1 = self.dram_scr("s_h1", [Lx, D], F32)
        self.H1T = self.dram_scr("s_h1t", [8, 128, Lx], BF16)

    def dma(self, en, out_ap, in_ap, reads, writes, sem):
        return self.kb.op(en, lambda e: e.dma_start(out=out_ap, in_=in_ap), reads=reads, writes=writes, dma=sem)

    def rsqrt(self, out_buf, out_ap, in_buf, in_ap, scale, eps_ap, tmp_buf, tmp_ap):
        kb = self.kb
        kb.op("act", lambda e: e.activation(out=tmp_ap, in_=in_ap, func=AF.Ln, bias=eps_ap, scale=scale),
              reads=[in_buf, self.cst], writes=[tmp_buf])
        kb.op("act", lambda e: e.activation(out=out_ap, in_=tmp_ap, func=AF.Exp, scale=-0.5),
              reads=[tmp_buf], writes=[out_buf])

    def setup_consts(self):
        kb, nc = self.kb, self.nc
        es = self.es
        self.cst = kb.sb(es, "cst", [128, 16], F32)
        kb.op("dve", lambda e: e.memset(self.cst[:, 0:1], LN_EPS), writes=[self.cst])
        kb.op("dve", lambda e: e.memset(self.cst[:, 1:2], RMS_EPS), writes=[self.cst])
        kb.op("dve", lambda e: e.memset(self.cst[:, 2:3], 1.0), writes=[self.cst])
        kb.op("dve", lambda e: e.memset(self.cst[:, 3:4], 0.0), writes=[self.cst])
        self.io_i = kb.sb(es, "io_i", [128, 128], I32)
        self.io_f = kb.sb(es, "io_f", [128, 128], F32)
        kb.op("pool", lambda e: e.iota(self.io_i[:], [[1, 128]], base=0, channel_multiplier=-1), writes=[self.io_i])
        kb.op("dve", lambda e: e.tensor_copy(self.io_f[:], self.io_i[:]), reads=[self.io_i], writes=[self.io_f])
        self.ident_f = kb.sb(es, "ident_f", [128, 128], F32)
        self.ident_b = kb.sb(es, "ident_b", [128, 128], BF16)
        kb.op("dve", lambda e: e.tensor_single_scalar(self.ident_f[:], self.io_f[:], 0.0, ALU.is_equal),
              reads=[self.io_f], writes=[self.ident_f])
        kb.op("dve", lambda e: e.tensor_copy(self.ident_b[:], self.ident_f[:]), reads=[self.ident_f], writes=[self.ident_b])
        self.ones_f = kb.sb(es, "ones_f", [128, 128], F32)
        kb.op("dve", lambda e: e.memset(self.ones_f[:], 1.0), writes=[self.ones_f])
        for name, wb in self.Wb.items():
            src = self.w[name]
            nrow = src.shape[0]
            step = 256
            for r0 in range(0, nrow, step):
                r1 = min(nrow, r0 + step)
                self.dma("pool", wb[r0:r1, :], src[r0:r1, :], [self.w_reg], [wb], wb)
        kb.barrier()


    def finish(self):
        self.kb.barrier()
        self.es.close()

    def load_bcast_vec(self, st, name, vec_ap, n):
        b = self.kb.sb(st, name, [128, n], F32)
        self.dma("sp", b[:, :], vec_ap.partition_broadcast(128), reads=[self.w_reg], writes=[b], sem=b)
        return b

    def load_x_tile(self, si, t, xt):
        kb = self.kb
        Lx = self.seq_lens[si]
        x = self.x_in[si]
        if t == 0:
            kb.op("dve", lambda e: e.memset(xt[0:48, :], 0.0), writes=[xt])
            self.dma("sp", xt[48:64, :], self.w["meta_tokens"][:, :], reads=[self.w_reg], writes=[xt], sem=xt)
            self.dma("sp", xt[64:128, :], x[0:64, :], reads=[self.xin_reg], writes=[xt], sem=xt)
            return 128
        r0 = 128 * t - 64
        nt = min(128, Lx - r0)
        self.dma("sp", xt[0:nt, :], x[r0:r0 + nt, :], reads=[self.xin_reg], writes=[xt], sem=xt)
        return nt

    def layer_norm_gen(self, xin, xin_ap, g_bc, b_bc, out_buf, out_ap, tmp, stats):
        kb = self.kb
        for c in range(2):
            kb.op("dve", lambda e, c=c: e.bn_stats(stats[:, 6 * c:6 * c + 6], xin_ap[:, 512 * c:512 * c + 512]),
                  reads=[xin], writes=[stats])
        yield
        kb.op("dve", lambda e: e.bn_aggr(stats[:, 16:18], stats[:, 0:12].rearrange("p (a b) -> p a b", a=2)),
              reads=[stats], writes=[stats])
        yield
        kb.op("act", lambda e: e.activation(out=stats[:, 20:21], in_=stats[:, 17:18], func=AF.Ln,
                                            bias=self.cst[:, 0:1], scale=1.0), reads=[stats, self.cst], writes=[stats])
        yield
        kb.op("act", lambda e: e.activation(out=stats[:, 21:22], in_=stats[:, 20:21], func=AF.Exp, scale=-0.5),
              reads=[stats], writes=[stats])
        yield
        kb.op("dve", lambda e: e.scalar_tensor_tensor(stats[:, 22:23], stats[:, 16:17], -1.0, stats[:, 21:22], ALU.mult, ALU.mult),
              reads=[stats], writes=[stats])
        yield
        kb.op("act", lambda e: e.activation(out=tmp[:, :], in_=xin_ap, func=AF.Identity, bias=stats[:, 22:23], scale=stats[:, 21:22]),
              reads=[xin, stats], writes=[tmp])
        yield
        kb.op("dve", lambda e: e.tensor_tensor(tmp[:, :], tmp[:, :], g_bc[:, :], ALU.mult),
              reads=[tmp, g_bc], writes=[tmp])
        yield
        kb.op("dve", lambda e: e.tensor_tensor(out_ap, tmp[:, :], b_bc[:, :], ALU.add),
              reads=[tmp, b_bc], writes=[out_buf])

    @staticmethod
    def run_gens(*gens):
        gens = [g for g in gens if g is not None]
        while gens:
            for g in list(gens):
                try:
                    next(g)
                except StopIteration:
                    gens.remove(g)

    def layer_norm(self, *a):
        self.run_gens(self.layer_norm_gen(*a))

    def transpose_tile(self, src, src_ap_fn, pT, dst, dst_ap_fn, nk=8):
        kb = self.kb
        for k in range(nk):
            kb.op("pe", lambda e, k=k: e.transpose(pT[:, k, :], src_ap_fn(k), self.ident_b[:, :]),
                  reads=[src, self.ident_b], writes=[pT])
        h = nk // 2
        kb.op("act", lambda e: e.copy(dst_ap_fn(0, h), pT[:, 0:h, :]), reads=[pT], writes=[dst])
        kb.op("dve", lambda e: e.tensor_copy(dst_ap_fn(h, nk), pT[:, h:nk, :]), reads=[pT], writes=[dst])

    def phase1(self, si):
        kb, nc = self.kb, self.nc
        Lx = self.seq_lens[si]
        Lp = Lx + 64
        W = self.w
        with ExitStack() as st:
            w_in = kb.sb(st, "w_in", [128, 8, NIN], BF16)
            wqA = kb.sb(st, "wqA", [128, 3, 8, 96], BF16)
            wqB = kb.sb(st, "wqB", [128, 3, 8, 96], BF16)
            wkrA = kb.sb(st, "wkrA", [128, 8, 96], BF16)
            wkrB = kb.sb(st, "wkrB", [128, 8, 96], BF16)
            w_uk = kb.sb(st, "w_uk", [128, 2, 512], BF16)
            w_uv = kb.sb(st, "w_uv", [128, 2, 512], BF16)
            kb.op("dve", lambda e: e.memset(wqB[:], 0.0), writes=[wqB])
            kb.op("dve", lambda e: e.memset(wkrA[:], 0.0), writes=[wkrA])
            kb.op("dve", lambda e: e.memset(wkrB[:], 0.0), writes=[wkrB])
            for kc in range(8):
                rows = slice(kc * 128, kc * 128 + 128)
                self.dma("sp", w_in[:, kc, :], self.Wb["w_in"][rows, :], [self.Wb["w_in"]], [w_in], w_in)
                self.dma("sp", wkrA[:, kc, 64:96], self.Wb["w_in"][rows, C_KR0:C_KR0 + 32], [self.Wb["w_in"]], [wkrA], wkrA)
                self.dma("sp", wkrB[:, kc, 64:80], self.Wb["w_in"][rows, C_KR0 + 16:C_KR0 + 32], [self.Wb["w_in"]], [wkrB], wkrB)
                self.dma("sp", wkrB[:, kc, 80:96], self.Wb["w_in"][rows, C_KR0:C_KR0 + 16], [self.Wb["w_in"]], [wkrB], wkrB)
            for kc in range(3):
                rows = slice(kc * 128, kc * 128 + 128)
                wq3 = self.Wb["w_uq"][rows, :].rearrange("p (h c) -> p h c", c=96)
                self.dma("sp", wqA[:, kc, :, :], wq3, [self.Wb["w_uq"]], [wqA], wqA)
                self.dma("sp", wqB[:, kc, :, 64:80], wq3[:, :, 80:96], [self.Wb["w_uq"]], [wqB], wqB)
                self.dma("sp", wqB[:, kc, :, 80:96], wq3[:, :, 64:80], [self.Wb["w_uq"]], [wqB], wqB)
            for kc in range(2):
                rows = slice(kc * 128, kc * 128 + 128)
                self.dma("sp", w_uk[:, kc, :], self.Wb["w_uk"][rows, :], [self.Wb["w_uk"]], [w_uk], w_uk)
                self.dma("sp", w_uv[:, kc, :], self.Wb["w_uv"][rows, :], [self.Wb["w_uv"]], [w_uv], w_uv)
            lng = self.load_bcast_vec(st, "lng", W["ln_in_g"], D)
            lnb = self.load_bcast_vec(st, "lnb", W["ln_in_b"], D)
            gcq = kb.sb(st, "gcq", [128, 3], F32)
            gckv = kb.sb(st, "gckv", [128, 2], F32)
            for kc in range(3):
                self.dma("sp", gcq[:, kc:kc + 1], W["g_cq"][kc * 128:kc * 128 + 128].rearrange("(p o) -> p o", o=1),
                         [self.w_reg], [gcq], gcq)
            for kc in range(2):
                self.dma("sp", gckv[:, kc:kc + 1], W["g_ckv"][kc * 128:kc * 128 + 128].rearrange("(p o) -> p o", o=1),
                         [self.w_reg], [gckv], gckv)
            gpar = kb.sb(st, "gpar", [128, 4], F32)
            kb.op("dve", lambda e: e.memset(gpar[:], 0.0), writes=[gpar])
            for off, sfx in ((0, "f"), (32, "b")):
                self.dma("sp", gpar[off:off + 8, 0:1], W["dt_bias_" + sfx].rearrange("(p o) -> p o", o=1),
                         [self.w_reg], [gpar], gpar)
                self.dma("sp", gpar[off:off + 8, 1:2], W["a_log_" + sfx].rearrange("(p o) -> p o", o=1),
                         [self.w_reg], [gpar], gpar)
            kb.op("act", lambda e: e.activation(out=gpar[0:40, 2:3], in_=gpar[0:40, 1:2], func=AF.Exp),
                  reads=[gpar], writes=[gpar])
            kb.op("dve", lambda e: e.tensor_scalar_mul(gpar[0:40, 2:3], gpar[0:40, 2:3], -1.0), reads=[gpar], writes=[gpar])
            ropeC = kb.sb(st, "ropeC", [128, 512], F32)
            ropeS = kb.sb(st, "ropeS", [128, 512], F32)
            rp = kb.sb(st, "rp", [128, 8], F32)
            rpi = kb.sb(st, "rpi", [128, 2], I32)
            R = slice(64, 96)
            kb.op("pool", lambda e: e.iota(rpi[R, 0:1], [[0, 1]], base=0, channel_multiplier=1), writes=[rpi])
            kb.op("dve", lambda e: e.tensor_copy(rp[R, 0:1], rpi[R, 0:1]), reads=[rpi], writes=[rp])
            kb.op("dve", lambda e: e.tensor_single_scalar(rp[R, 1:2], rp[R, 0:1], 16.0, ALU.is_ge), reads=[rp], writes=[rp])
            kb.op("dve", lambda e: e.scalar_tensor_tensor(rp[R, 2:3], rp[R, 1:2], -16.0, rp[R, 0:1], ALU.mult, ALU.add),
                  reads=[rp], writes=[rp])
            kb.op("act", lambda e: e.activation(out=rp[R, 3:4], in_=rp[R, 2:3], func=AF.Exp,
                                                scale=-float(np.log(10000.0)) / 16.0), reads=[rp], writes=[rp])
            kb.op("dve", lambda e: e.tensor_scalar_mul(rp[R, 3:4], rp[R, 3:4], float(1.0 / (2.0 * np.pi))),
                  reads=[rp], writes=[rp])
            kb.op("dve", lambda e: e.tensor_scalar(rp[R, 4:5], rp[R, 1:2], float(4.0 * np.pi), float(-2.0 * np.pi),
                                                   ALU.mult, ALU.add), reads=[rp], writes=[rp])
            kb.op("dve", lambda e: e.memset(rp[R, 5:6], float(2.0 * np.pi)), writes=[rp])
            posi = kb.sb(st, "posi", [128, 512], I32)
            posf = kb.sb(st, "posf", [128, 512], F32)
            ru = kb.sb(st, "ru", [128, 512], F32)
            rui = kb.sb(st, "rui", [128, 512], I32)
            ruf = kb.sb(st, "ruf", [128, 512], F32)

            def rope_tables(c):
                kb.op("pool", lambda e: e.iota(posi[R, :], [[1, 512]], base=512 * c - 48, channel_multiplier=0),
                      writes=[posi])
                kb.op("dve", lambda e: e.tensor_copy(posf[R, :], posi[R, :]), reads=[posi], writes=[posf])
                for tab, off, sc in ((ropeC, 0.25, 5), (ropeS, 0.0, 4)):
                    kb.op("dve", lambda e, off=off: e.tensor_scalar(ru[R, :], posf[R, :], rp[R, 3:4], off, ALU.mult, ALU.add),
                          reads=[posf, rp], writes=[ru])
                    kb.op("dve", lambda e: e.tensor_copy(rui[R, :], ru[R, :]), reads=[ru], writes=[rui])
                    kb.op("dve", lambda e: e.tensor_copy(ruf[R, :], rui[R, :]), reads=[rui], writes=[ruf])
                    kb.op("dve", lambda e: e.tensor_tensor(ru[R, :], ru[R, :], ruf[R, :], ALU.subtract),
                          reads=[ru, ruf], writes=[ru])
                    kb.op("act", lambda e, tab=tab, sc=sc: e.activation(
                        out=tab[R, :], in_=ru[R, :], func=AF.Sin, scale=rp[R, sc:sc + 1]),
                        reads=[ru, rp], writes=[tab])
            xts = [kb.sb(st, f"xt{i}", [128, D], F32) for i in range(4)]
            for xt in xts:
                kb.op("pool", lambda e, xt=xt: e.memset(xt[:], 0.0), writes=[xt])
            statss = [kb.sb(st, f"stats{i}", [128, 32], F32) for i in range(4)]
            hbs = [kb.sb(st, f"hb{i}", [128, D], BF16) for i in range(4)]
            hTs = [kb.sb(st, f"hT{i}", [128, 8, 512], BF16) for i in range(2)]
            stage = kb.sb(st, "stage", [128, 12, 512], F32)
            cq = kb.sb(st, "cq", [128, 3, 512], F32)
            sq = kb.sb(st, "sq", [128, 3, 512], F32)
            rstd = kb.sb(st, "rstd", [128, 512], F32)
            rtmp = kb.sb(st, "rtmp", [128, 512], F32)
            cqn = kb.sb(st, "cqn", [128, 3, 512], BF16)
            ckvn = kb.sb(st, "ckvn", [128, 2, 512], BF16)
            qs = kb.sb(st, "qs", [128, 8, 512], BF16)
            ks = kb.sb(st, "ks", [128, 8, 512], BF16)
            t1 = kb.sb(st, "t1", [128, 512], F32)
            t2 = kb.sb(st, "t2", [128, 512], F32)
            vs = kb.sb(st, "vs", [128, 512], BF16)
            zs = kb.sb(st, "zs", [128, 512], F32)
            ze = kb.sb(st, "ze", [128, 512], F32)
            gs = kb.sb(st, "gs", [128, 512], F32)
            ga = kb.sb(st, "ga", [128, 512], F32)
            gb = kb.sb(st, "gb", [128, 512], F32)
            lnq = kb.sb(st, "lnq", [128, 1], F32)
            kb.op("dve", lambda e: e.memset(lnq[:], float(np.log(QSCALE))), writes=[lnq])
            kb.op("dve", lambda e: e.memset(gs[:], 0.0), writes=[gs])
            gs2 = kb.sb(st, "gs2", [128, 512], F32)
            kb.op("dve", lambda e: e.memset(gs2[:], 0.0), writes=[gs2])
            pT = kb.ps(st, "pT", [128, 8, 128], BF16)
            pb = [kb.ps(st, f"pb{i}", [128, 512], F32) for i in range(7)]
            self._pbi = 0

            def bank():
                self._pbi = (self._pbi + 1) % 7
                return pb[self._pbi]

            nblk = (Lp + 511) // 512

            def ln_part(b):
                p0_ = 512 * b
                ntok_ = min(512, Lp - p0_)
                ntl_ = (ntok_ + 127) // 128
                gens = []
                for tl in range(ntl_):
                    xt = xts[tl]
                    self.load_x_tile(si, 4 * b + tl, xt)
                    gens.append(self.layer_norm_gen(xt, xt[:, :], lng, lnb, hbs[tl], hbs[tl][:, :], xt, statss[tl]))
                self.run_gens(*gens)

            def tr_part(b):
                p0_ = 512 * b
                ntok_ = min(512, Lp - p0_)
                hT_ = hTs[b % 2]
                for tl in range((ntok_ + 127) // 128):
                    hb = hbs[tl]
                    self.transpose_tile(hb, lambda k, hb=hb: hb[:, k * 128:(k + 1) * 128], pT, hT_,
                                        lambda k0, k1, tl=tl: hT_[:, k0:k1, tl * 128:(tl + 1) * 128])

            ln_part(0)
            tr_part(0)
            for b in range(nblk):
                p0 = 512 * b
                ntok = min(512, Lp - p0)
                ntl = (ntok + 127) // 128
                hT = hTs[b % 2]
                if b + 1 < nblk:
                    ln_part(b + 1)
                N = slice(0, ntok)

                def proj(col0, m, out_ap_fn=None):
                    bk = bank()
                    o = bk[0:m, N] if out_ap_fn is None else out_ap_fn(bk)
                    for kc in range(8):
                        kb.op("pe", lambda e, kc=kc: e.matmul(o, w_in[:, kc, col0:col0 + m], hT[:, kc, N],
                                                               start=(kc == 0), stop=(kc == 7)),
                              reads=[w_in, hT], writes=[bk])
                    return bk

                for mc in range(12):
                    bk = proj(C_QKV0 + mc * 128, 128)
                    if mc % 2 == 0:
                        kb.op("act", lambda e, bk=bk, mc=mc: e.copy(stage[:, mc, N], bk[:, N]), reads=[bk], writes=[stage])
                    else:
                        kb.op("dve", lambda e, bk=bk, mc=mc: e.tensor_copy(stage[:, mc, N], bk[:, N]), reads=[bk], writes=[stage])
                self.dma("sp", self.QKVT[:, :, 1 + p0:1 + p0 + ntok].rearrange("c p t -> p c t"), stage[:, :, N],
                         [stage], [self.QKVT], stage)

                def rms_fm(col0, nch, gpp, outn, lnbias):
                    banks = [proj(col0 + mc * 128, 128) for mc in range(nch)]
                    for mc, bk in enumerate(banks):
                        kb.op("act", lambda e, bk=bk, mc=mc: e.copy(cq[:, mc, N], bk[:, N]), reads=[bk], writes=[cq])
                        kb.op("act", lambda e, bk=bk, mc=mc: e.activation(out=sq[:, mc, N], in_=bk[:, N], func=AF.Square),
                              reads=[bk], writes=[sq])
                    bs = bank()
                    for mc in range(nch):
                        kb.op("pe", lambda e, mc=mc: e.matmul(bs[:, N], self.ones_f[:, :], sq[:, mc, N],
                                                               start=(mc == 0), stop=(mc == nch - 1)),
                              reads=[self.ones_f, sq], writes=[bs])
                    kb.op("act", lambda e: e.activation(out=rtmp[:, N], in_=bs[:, N], func=AF.Ln, bias=self.cst[:, 1:2],
                                                        scale=1.0 / (nch * 128)), reads=[bs, self.cst], writes=[rtmp])
                    if lnbias is None:
                        kb.op("act", lambda e: e.activation(out=rstd[:, N], in_=rtmp[:, N], func=AF.Exp, scale=-0.5),
                              reads=[rtmp], writes=[rstd])
                    else:
                        kb.op("act", lambda e: e.activation(out=rstd[:, N], in_=rtmp[:, N], func=AF.Exp, scale=-0.5,
                                                            bias=lnbias[:, 0:1]), reads=[rtmp, lnbias], writes=[rstd])
                    for mc in range(nch):
                        kb.op("dve", lambda e, mc=mc: e.scalar_tensor_tensor(outn[:, mc, N], cq[:, mc, N], gpp[:, mc:mc + 1],
                                                                             rstd[:, N], ALU.mult, ALU.mult),
                              reads=[cq, gpp, rstd], writes=[outn])

                rms_fm(C_KV0, 2, gckv, ckvn, None)
                for h in range(H):
                    bk = bank()
                    for kc in range(2):
                        kb.op("pe", lambda e, kc=kc, h=h, bk=bk: e.matmul(bk[0:64, N], w_uk[:, kc, h * 64:(h + 1) * 64],
                                                                          ckvn[:, kc, N], start=(kc == 0), stop=(kc == 1)),
                              reads=[w_uk, ckvn], writes=[bk])
                    if h % 2 == 0:
                        kb.op("act", lambda e, bk=bk, h=h: e.copy(ks[0:64, h, N], bk[0:64, N]), reads=[bk], writes=[ks])
                    else:
                        kb.op("dve", lambda e, bk=bk, h=h: e.tensor_copy(ks[0:64, h, N], bk[0:64, N]), reads=[bk], writes=[ks])
                RR = slice(64, 96)
                P = slice(p0, p0 + ntok)
                rope_tables(b)

                def rope_pair(bkA, bkB, dst_fn):
                    kb.op("dve", lambda e: e.tensor_tensor(t1[RR, N], bkA[RR, N], ropeC[RR, N], ALU.mult),
                          reads=[bkA, ropeC], writes=[t1])
                    kb.op("dve", lambda e: e.tensor_tensor(t2[RR, N], bkB[RR, N], ropeS[RR, N], ALU.mult),
                          reads=[bkB, ropeS], writes=[t2])
                    dst_fn()

                bkA = bank()
                bkB = bank()
                for kc in range(8):
                    kb.op("pe", lambda e, kc=kc: e.matmul(bkA[0:96, N], wkrA[:, kc, :], hT[:, kc, N], start=(kc == 0), stop=(kc == 7)),
                          reads=[wkrA, hT], writes=[bkA])
                for kc in range(8):
                    kb.op("pe", lambda e, kc=kc: e.matmul(bkB[0:96, N], wkrB[:, kc, :], hT[:, kc, N], start=(kc == 0), stop=(kc == 7)),
                          reads=[wkrB, hT], writes=[bkB])

                def kdst():
                    kb.op("dve", lambda e: e.tensor_tensor(t1[RR, N], t1[RR, N], t2[RR, N], ALU.add), reads=[t1, t2], writes=[t1])
                    kb.op("act", lambda e: e.copy(ks[RR, 0:4, N], t1[RR, N].unsqueeze(1).to_broadcast([32, 4, ntok])),
                          reads=[t1], writes=[ks])
                    kb.op("dve", lambda e: e.tensor_copy(ks[RR, 4:8, N], t1[RR, N].unsqueeze(1).to_broadcast([32, 4, ntok])),
                          reads=[t1], writes=[ks])
                rope_pair(bkA, bkB, kdst)
                self.dma("sp", self.KT[:, :, P].rearrange("h r t -> r h t"), ks[0:96, :, N], [ks], [self.KT], ks)
                for tl in range(ntl):
                    nt = min(128, ntok - tl * 128)
                    bk = bank()
                    for kc in range(2):
                        kb.op("pe", lambda e, kc=kc, tl=tl, nt=nt, bk=bk: e.matmul(
                            bk[0:nt, :], ckvn[:, kc, tl * 128:tl * 128 + nt], w_uv[:, kc, :], start=(kc == 0), stop=(kc == 1)),
                            reads=[ckvn, w_uv], writes=[bk])
                    kb.op("act", lambda e, bk=bk, nt=nt: e.copy(vs[0:nt, :], bk[0:nt, :]), reads=[bk], writes=[vs])
                    self.dma("sp", self.V[p0 + tl * 128:p0 + tl * 128 + nt, :], vs[0:nt, :], [vs], [self.V], vs)

                c0 = 64 if b == 0 else 0
                if ntok > c0:
                    rms_fm(C_Q0, 3, gcq, cqn, lnq)
                    for h in range(H):
                        bkA = bank()
                        bkB = bank()
                        for kc in range(3):
                            kb.op("pe", lambda e, kc=kc, h=h, bkA=bkA: e.matmul(bkA[0:96, N], wqA[:, kc, h, :], cqn[:, kc, N],
                                                                                 start=(kc == 0), stop=(kc == 2)),
                                  reads=[wqA, cqn], writes=[bkA])
                        for kc in range(3):
                            kb.op("pe", lambda e, kc=kc, h=h, bkB=bkB: e.matmul(bkB[0:96, N], wqB[:, kc, h, :], cqn[:, kc, N],
                                                                                 start=(kc == 0), stop=(kc == 2)),
                                  reads=[wqB, cqn], writes=[bkB])
                        kb.op("act", lambda e, bkA=bkA, h=h: e.copy(qs[0:64, h, N], bkA[0:64, N]), reads=[bkA], writes=[qs])

                        def qdst(h=h):
                            kb.op("dve", lambda e: e.tensor_tensor(qs[RR, h, N], t1[RR, N], t2[RR, N], ALU.add),
                                  reads=[t1, t2], writes=[qs])
                        rope_pair(bkA, bkB, qdst)
                    self.dma("sp", self.QT[:, :, p0 + c0 - 64:p0 + ntok - 64].rearrange("h r t -> r h t"),
                             qs[0:96, :, c0:ntok], [qs], [self.QT], qs)

                for tl in range(ntl):
                    nt = min(128, ntok - tl * 128)
                    bk = bank()
                    for kc in range(8):
                        kb.op("pe", lambda e, kc=kc, tl=tl, nt=nt, bk=bk: e.matmul(
                            bk[0:nt, :], hT[:, kc, tl * 128:tl * 128 + nt], w_in[:, kc, C_Z0:C_Z0 + 512],
                            start=(kc == 0), stop=(kc == 7)), reads=[hT, w_in], writes=[bk])
                    kb.op("act", lambda e, bk=bk, nt=nt: e.activation(out=ze[0:nt, :], in_=bk[0:nt, :], func=AF.Exp, scale=-1.0),
                          reads=[bk], writes=[ze])
                    kb.op("act", lambda e, nt=nt: e.activation(out=ze[0:nt, :], in_=ze[0:nt, :], func=AF.Ln, bias=self.cst[0:nt, 2:3], scale=1.0),
                          reads=[ze, self.cst], writes=[ze])
                    kb.op("act", lambda e, nt=nt: e.activation(out=ze[0:nt, :], in_=ze[0:nt, :], func=AF.Exp, scale=-1.0), reads=[ze], writes=[ze])
                    kb.op("dve", lambda e, bk=bk, nt=nt: e.tensor_tensor(zs[0:nt, :], bk[0:nt, :], ze[0:nt, :], ALU.mult),
                          reads=[bk, ze], writes=[zs])
                    self.dma("sp", self.Z[p0 + tl * 128:p0 + tl * 128 + nt, :], zs[0:nt, :], [zs], [self.Z], zs)

                bk = bank()
                bk2 = bank()
                for g in range(4):
                    bb = bk if g < 2 else bk2
                    ro = 32 * (g % 2)
                    for kc in range(8):
                        kb.op("pe", lambda e, kc=kc, g=g, bb=bb, ro=ro: e.matmul(
                            bb[ro:ro + 8, N], w_in[:, kc, C_G0 + 8 * g:C_G0 + 8 * g + 8], hT[:, kc, N],
                            start=(kc == 0), stop=(kc == 7)), reads=[w_in, hT], writes=[bb])
                A = slice(0, 40)
                kb.op("dve", lambda e: e.tensor_scalar(ga[A, N], bk[A, N], gpar[A, 0:1], None, ALU.add), reads=[bk, gpar], writes=[ga])
                kb.op("act", lambda e: e.activation(out=gb[A, N], in_=ga[A, N], func=AF.Abs), reads=[ga], writes=[gb])
                kb.op("act", lambda e: e.activation(out=gb[A, N], in_=gb[A, N], func=AF.Exp, scale=-1.0), reads=[gb], writes=[gb])
                kb.op("act", lambda e: e.activation(out=gb[A, N], in_=gb[A, N], func=AF.Ln, bias=self.cst[A, 2:3], scale=1.0),
                      reads=[gb, self.cst], writes=[gb])
                kb.op("dve", lambda e: e.scalar_tensor_tensor(ga[A, N], ga[A, N], 0.0, gb[A, N], ALU.max, ALU.add),
                      reads=[ga, gb], writes=[ga])
                kb.op("dve", lambda e: e.tensor_scalar(gs[A, N], ga[A, N], gpar[A, 2:3], None, ALU.mult), reads=[ga, gpar], writes=[gs])
                kb.op("act", lambda e: e.activation(out=gb[A, N], in_=bk2[A, N], func=AF.Exp, scale=-1.0), reads=[bk2], writes=[gb])
                kb.op("act", lambda e: e.activation(out=gb[A, N], in_=gb[A, N], func=AF.Ln, bias=self.cst[A, 2:3], scale=1.0),
                      reads=[gb, self.cst], writes=[gb])
                kb.op("act", lambda e: e.activation(out=gs2[A, N], in_=gb[A, N], func=AF.Exp, scale=-1.0), reads=[gb], writes=[gs2])
                if b == 0:
                    kb.op("dve", lambda e: e.memset(gs[A, 0:48], 0.0), writes=[gs])
                    kb.op("dve", lambda e: e.memset(gs2[A, 0:48], 0.0), writes=[gs2])
                for g in range(4):
                    src = gs if g < 2 else gs2
                    ro = 32 * (g % 2)
                    self.dma("sp", self.GT[g, :, P], src[ro:ro + 8, N], [src], [self.GT], src)
                if b + 1 < nblk:
                    tr_part(b + 1)
        kb.barrier()

    def phase2a(self, si):
        kb = self.kb
        Lx = self.seq_lens[si]
        Lp = Lx + 64
        W = self.w
        with ExitStack() as st:
            cw = kb.sb(st, "cw", [128, 12, 4], F32)
            for k in range(4):
                self.kb.op("sp", lambda e, k=k: e.dma_start(out=cw[:, :, k:k + 1],
                                                            in_=W["conv_w"][k, :].rearrange("(c p o) -> p c o", p=128, o=1),
                                                            allow_slow_non_contiguous=True),
                           reads=[self.w_reg], writes=[cw], dma=cw)
            blk = kb.sb(st, "blk", [128, 128], F32)
            kb.op("dve", lambda e: e.memset(blk[:], 0.0), writes=[blk])
            kb.op("dve", lambda e: e.memset(blk[0:64, 0:64], 1.0), writes=[blk])
            kb.op("dve", lambda e: e.memset(blk[64:128, 64:128], 1.0), writes=[blk])
            raw = kb.sb(st, "raw", [128, 12, 516], F32)
            accs = [kb.sb(st, f"acc{i}", [128, 512], F32) for i in range(4)]
            ss = [kb.sb(st, f"s{i}", [128, 512], F32) for i in range(4)]
            sqs = [kb.sb(st, f"sq{i}", [128, 512], F32) for i in range(4)]
            rns = [kb.sb(st, f"rn{i}", [128, 512], F32) for i in range(4)]
            rtmps = [kb.sb(st, f"rtmp{i}", [128, 512], F32) for i in range(4)]
            qn = kb.sb(st, "qn", [128, 4, 512], BF16)
            kn = kb.sb(st, "kn", [128, 4, 512], BF16)
            vn = kb.sb(st, "vn", [128, 4, 512], BF16)
            ktok = kb.sb(st, "ktok", [128, 512], BF16)
            vtok = kb.sb(st, "vtok", [128, 512], BF16)
            pT = kb.ps(st, "pT", [128, 4, 128], BF16)
            pb = [kb.ps(st, f"pb{i}", [128, 512], F32) for i in range(4)]
            nblk = (Lp + 511) // 512
            for b in range(nblk):
                p0 = 512 * b
                ntok = min(512, Lp - p0)
                N = slice(0, ntok)
                P = slice(p0, p0 + ntok)
                j0 = 1 if b == 0 else 0
                j1 = min(ntok + 3, Lp + 1 - p0)
                self.dma("sp", raw[:, :, j0:j1], self.QKVT[:, :, p0 + j0:p0 + j1].rearrange("c p t -> p c t"),
                         [self.QKVT], [raw], raw)
                if b == 0:
                    kb.op("dve", lambda e: e.memset(raw[:, :, 0:49], 0.0), writes=[raw])
                if b == nblk - 1:
                    kb.op("dve", lambda e: e.memset(raw[:, :, ntok + 1:ntok + 3], 0.0), writes=[raw])
                def stage1(ci):
                    acc = accs[ci % 4]
                    s_ = ss[ci % 4]
                    kb.op("dve", lambda e: e.tensor_scalar(acc[:, N], raw[:, ci, 0:ntok], cw[:, ci, 0:1], None, ALU.mult),
                          reads=[raw, cw], writes=[acc])
                    for k in range(1, 4):
                        kb.op("dve", lambda e, k=k: e.scalar_tensor_tensor(
                            acc[:, N], raw[:, ci, k:k + ntok], cw[:, ci, k:k + 1], acc[:, N], ALU.mult, ALU.add),
                            reads=[raw, cw, acc], writes=[acc])
                    kb.op("act", lambda e: e.activation(out=s_[:, N], in_=acc[:, N], func=AF.Silu), reads=[acc], writes=[s_])
                    if b == 0:
                        kb.op("pool", lambda e: e.memset(s_[:, 0:48], 0.0), writes=[s_])
                    if ci < 8:
                        sq_ = sqs[ci % 4]
                        kb.op("act", lambda e: e.activation(out=sq_[:, N], in_=s_[:, N], func=AF.Square), reads=[s_], writes=[sq_])
                        bk = pb[ci % 4]
                        kb.op("pe", lambda e: e.matmul(bk[:, N], blk[:, :], sq_[:, N], start=True, stop=True),
                              reads=[blk, sq_], writes=[bk])

                def stage1b(ci):
                    if ci < 8:
                        rt_, rn_ = rtmps[ci % 4], rns[ci % 4]
                        bk = pb[ci % 4]
                        kb.op("act", lambda e: e.activation(out=rt_[:, N], in_=bk[:, N], func=AF.Ln, bias=self.cst[:, 1:2], scale=1.0),
                              reads=[bk, self.cst], writes=[rt_])
                        kb.op("act", lambda e: e.activation(out=rn_[:, N], in_=rt_[:, N], func=AF.Exp, scale=-0.5), reads=[rt_], writes=[rn_])

                def stage2(ci):
                    s_ = ss[ci % 4]
                    if ci < 4:
                        rn_ = rns[ci % 4]
                        kb.op("dve", lambda e: e.scalar_tensor_tensor(qn[:, ci, N], s_[:, N], 0.125, rn_[:, N], ALU.mult, ALU.mult),
                              reads=[s_, rn_], writes=[qn])
                    elif ci < 8:
                        rn_ = rns[ci % 4]
                        kb.op("dve", lambda e: e.tensor_tensor(kn[:, ci - 4, N], s_[:, N], rn_[:, N], ALU.mult),
                              reads=[s_, rn_], writes=[kn])
                    else:
                        kb.op("act", lambda e: e.copy(vn[:, ci - 8, N], s_[:, N]), reads=[s_], writes=[vn])

                stage1(0)
                stage1(1)
                stage1b(0)
                for ci in range(12):
                    if ci + 2 < 12:
                        stage1(ci + 2)
                    if ci + 1 < 12:
                        stage1b(ci + 1)
                    stage2(ci)
                self.dma("sp", self.GQT[:, :, P].rearrange("c p t -> p c t"), qn[:, :, N], [qn], [self.GQT], qn)
                self.dma("sp", self.GKT[:, :, P].rearrange("c p t -> p c t"), kn[:, :, N], [kn], [self.GKT], kn)
                for tl in range((ntok + 127) // 128):
                    nt = min(128, ntok - tl * 128)
                    rows = slice(p0 + tl * 128, p0 + tl * 128 + nt)
                    for src, dstb, dram in ((kn, ktok, self.GKTOK), (vn, vtok, self.GVTOK)):
                        for j in range(4):
                            kb.op("pe", lambda e, j=j, src=src, tl=tl, nt=nt: e.transpose(
                                pT[0:nt, j, :], src[:, j, tl * 128:tl * 128 + nt], self.ident_b[:, :]),
                                reads=[src, self.ident_b], writes=[pT])
                        kb.op("act" if src is kn else "dve",
                              (lambda e, dstb=dstb, nt=nt: e.copy(dstb[0:nt, :], pT[0:nt, :, :])) if src is kn else
                              (lambda e, dstb=dstb, nt=nt: e.tensor_copy(dstb[0:nt, :], pT[0:nt, :, :])),
                              reads=[pT], writes=[dstb])
                        self.dma("sp", dram[rows, :], dstb[0:nt, :], [dstb], [dram], dstb)
        kb.barrier()

    def phase2b(self, si):
        kb = self.kb
        Lx = self.seq_lens[si]
        Lp = Lx + 64
        W = self.w
        with ExitStack() as st:
            dm_i = kb.sb(st, "dm_i", [128, 512], I32)
            dmask = kb.sb(st, "dmask", [128, 512], F32)
            dtmp = kb.sb(st, "dtmp", [128, 512], F32)
            kb.op("pool", lambda e: e.iota(dm_i[0:8, :], [[1, 512]], base=0, channel_multiplier=-64), writes=[dm_i])
            kb.op("dve", lambda e: e.tensor_copy(dtmp[0:8, :], dm_i[0:8, :]), reads=[dm_i], writes=[dtmp])
            kb.op("dve", lambda e: e.tensor_single_scalar(dmask[0:8, :], dtmp[0:8, :], 0.0, ALU.is_ge), reads=[dtmp], writes=[dmask])
            kb.op("dve", lambda e: e.tensor_single_scalar(dtmp[0:8, :], dtmp[0:8, :], 64.0, ALU.is_lt), reads=[dtmp], writes=[dtmp])
            kb.op("dve", lambda e: e.tensor_tensor(dmask[0:8, :], dmask[0:8, :], dtmp[0:8, :], ALU.mult), reads=[dmask, dtmp], writes=[dmask])
            csm = kb.sb(st, "csm", [128, 64], F32)
            kb.op("dve", lambda e: e.tensor_copy(csm[0:64, :], self.io_f[0:64, 0:64]), reads=[self.io_f], writes=[csm])
            kb.op("dve", lambda e: e.tensor_copy(csm[64:128, :], self.io_f[64:128, 64:128]), reads=[self.io_f], writes=[csm])
            identp = kb.sb(st, "identp", [128, 64], BF16)
            identpf = kb.sb(st, "identpf", [128, 64], F32)
            negoff = kb.sb(st, "negoff", [128, 64], F32)
            kb.op("dve", lambda e: e.tensor_single_scalar(identpf[:, :], csm[:, :], 0.0, ALU.is_equal), reads=[csm], writes=[identpf])
            kb.op("dve", lambda e: e.tensor_copy(identp[:, :], identpf[:, :]), reads=[identpf], writes=[identp])
            kb.op("dve", lambda e: e.tensor_scalar_add(negoff[:, :], identpf[:, :], -1.0), reads=[identpf], writes=[negoff])
            masks = []
            for x, cmpop in ((0, ALU.is_ge), (1, ALU.is_le)):
                mk = kb.sb(st, f"mask{x}", [128, 64], BF16)
                kb.op("dve", lambda e, cmpop=cmpop: e.tensor_single_scalar(dtmp[:, 0:64], csm[:, :], 0.0, cmpop), reads=[csm], writes=[dtmp])
                kb.op("dve", lambda e, mk=mk: e.tensor_scalar(mk[:, :], dtmp[:, 0:64], -NEG, NEG, ALU.mult, ALU.add), reads=[dtmp], writes=[mk])
                masks.append(mk)
            cmask = kb.sb(st, "cmask", [128, 512], F32)
            kb.op("dve", lambda e: e.memset(cmask[0:8, :], 1.0), writes=[cmask])
            kb.op("dve", lambda e: e.memset(cmask[0:8, :].rearrange("p (n c) -> p n c", c=64)[:, :, 0:1], 0.0), writes=[cmask])
            gnorm = self.load_bcast_vec(st, "gnorm", W["gdn_norm_g"], 64)
            pT = kb.ps(st, "pT", [128, 4, 128], BF16)
            pb = [kb.ps(st, f"pb{i}", [128, 512], F32) for i in range(7)]
            self._pbi = 0

            def bank():
                self._pbi = (self._pbi + 1) % 7
                return pb[self._pbi]

            def v3(ap):
                return ap.rearrange("p (h c) -> p h c", c=64)

            def blocks(out_bk, lhs, rhs, nr):
                for r in range(nr):
                    R_ = slice(64 * r, 64 * r + 64)
                    for h in range(H):
                        C_ = slice(64 * h, 64 * h + 64)
                        kb.op("pe", lambda e, R_=R_, C_=C_: e.matmul(out_bk[R_, C_], lhs[R_, C_], rhs[R_, C_], start=True, stop=True),
                              reads=[lhs, rhs], writes=[out_bk])

            nblk = (Lp + 511) // 512

            def pass_gen(x):
                sfx = f"_{x}"
                F = lambda n: kb.sb(st, n + sfx, [128, 512], F32)
                Bf = lambda n: kb.sb(st, n + sfx, [128, 512], BF16)
                g_t, b_t, cs, gc, ngc, egc, bg, kd = [F(n) for n in ("g_t", "b_t", "cs", "gc", "ngc", "egc", "bg", "kd")]
                tot = kb.sb(st, "tot" + sfx, [128, 8], F32)
                egl = kb.sb(st, "egl" + sfx, [128, 4, 8], F32)
                knT, qnT, kbT, qgT = [kb.sb(st, n + sfx, [128, 4, 512], BF16) for n in ("knT", "qnT", "kbT", "qgT")]
                ktok, vtok, vb, kbg, kdec, At, IY, nwT, vn = [Bf(n) for n in ("ktok", "vtok", "vb", "kbg", "kdec", "At", "IY", "nwT", "vn")]
                Xs = [Bf(f"X{i}") for i in range(2)]
                Ys = [Bf(f"Y{i}") for i in range(2)]
                Rs = [Bf(f"R{i}") for i in range(2)]
                gtk = kb.sb(st, "gtk" + sfx, [128, 32], F32)
                BD = kb.sb(st, "BD" + sfx, [128, 2, 512], F32)
                E, tt, u_sb, o_sb = [F(n) for n in ("E", "tt", "u_sb", "o_sb")]
                S = kb.sb(st, "S" + sfx, [128, 256], F32)
                Sb = kb.sb(st, "Sb" + sfx, [128, 256], BF16)
                for t_ in [ktok, vtok, vb, kbg, kdec, At, IY, vn] + Xs + Ys + Rs:
                    kb.op("pool", lambda e, t_=t_: e.memset(t_[:], 0.0), writes=[t_])
                kb.op("pool", lambda e: e.memset(gtk[:], 0.0), writes=[gtk])
                kb.op("pool", lambda e: e.memset(o_sb[:], 0.0), writes=[o_sb])
                kb.op("dve", lambda e: e.memset(S[:], 0.0), writes=[S])
                kb.op("dve", lambda e: e.memset(Sb[:], 0.0), writes=[Sb])
                odram = self.OF if x == 0 else self.OB
                border = range(nblk) if x == 0 else range(nblk - 1, -1, -1)
                for b in border:
                    p0 = 512 * b
                    ntok = min(512, Lp - p0)
                    nch = ntok // 64
                    N = slice(0, ntok)
                    P = slice(p0, p0 + ntok)
                    G = slice(0, 8)
                    self.dma("sp", g_t[G, N], self.GT[x, :, P], [self.GT], [g_t], g_t)
                    self.dma("sp", b_t[G, N], self.GT[2 + x, :, P], [self.GT], [b_t], b_t)
                    self.dma("sp", knT[:, :, N], self.GKT[:, :, P].rearrange("c p t -> p c t"), [self.GKT], [knT], knT)
                    self.dma("sp", qnT[:, :, N], self.GQT[:, :, P].rearrange("c p t -> p c t"), [self.GQT], [qnT], qnT)
                    kb.op("dve", lambda e: e.tensor_tensor_scan(cs[G, N], cmask[G, N], g_t[G, N], 0.0, ALU.mult, ALU.add),
                          reads=[cmask, g_t], writes=[cs])
                    cs3 = cs[G, N].rearrange("p (n c) -> p n c", c=64)
                    kb.op("dve", lambda e: e.tensor_copy(tot[G, 0:nch].unsqueeze(2), cs3[:, :, 63:64]), reads=[cs], writes=[tot])
                    totb = tot[G, 0:nch].unsqueeze(2).to_broadcast([8, nch, 64])
                    gc3 = gc[G, N].rearrange("p (n c) -> p n c", c=64)
                    if x == 0:
                        kb.op("dve", lambda e: e.tensor_copy(gc[G, N], cs[G, N]), reads=[cs], writes=[gc])
                    else:
                        kb.op("dve", lambda e: e.tensor_tensor(gc3, totb, cs3, ALU.subtract), reads=[tot, cs], writes=[gc])
                        kb.op("dve", lambda e: e.tensor_tensor(gc[G, N], gc[G, N], g_t[G, N], ALU.add), reads=[gc, g_t], writes=[gc])
                    kb.op("dve", lambda e: e.tensor_scalar_mul(ngc[G, N], gc[G, N], -1.0), reads=[gc], writes=[ngc])
                    kb.op("act", lambda e: e.activation(out=egc[G, N], in_=gc[G, N], func=AF.Exp), reads=[gc], writes=[egc])
                    kb.op("dve", lambda e: e.tensor_tensor(bg[G, N], b_t[G, N], egc[G, N], ALU.mult), reads=[b_t, egc], writes=[bg])
                    kd3 = kd[G, N].rearrange("p (n c) -> p n c", c=64)
                    kb.op("dve", lambda e: e.tensor_tensor(kd3, totb, gc3, ALU.subtract), reads=[tot, gc], writes=[kd])
                    kb.op("act", lambda e: e.activation(out=kd[G, N], in_=kd[G, N], func=AF.Exp), reads=[kd], writes=[kd])
                    bk = bank()
                    for j in range(4):
                        kb.op("pe", lambda e, j=j, bk=bk: e.matmul(bk[:, 8 * j:8 * j + nch], dmask[G, 128 * j:128 * j + 128], tot[G, 0:nch],
                                                                   start=True, stop=True), reads=[dmask, tot], writes=[bk])
                    kb.op("act", lambda e, bk=bk: e.activation(out=egl[:, :, 0:nch], in_=bk[:, 0:32].rearrange("p (j n) -> p j n", n=8)[:, :, 0:nch],
                                                               func=AF.Exp), reads=[bk], writes=[egl])
                    yield
                    for j in range(4):
                        bk = bank()
                        kb.op("pe", lambda e, j=j, bk=bk: e.matmul(bk[:, N], dmask[G, 128 * j:128 * j + 128], b_t[G, N], start=True, stop=True),
                              reads=[dmask, b_t], writes=[bk])
                        kb.op("dve", lambda e, j=j, bk=bk: e.tensor_tensor(kbT[:, j, N], knT[:, j, N], bk[:, N], ALU.mult),
                              reads=[knT, bk], writes=[kbT])
                        bk = bank()
                        kb.op("pe", lambda e, j=j, bk=bk: e.matmul(bk[:, N], dmask[G, 128 * j:128 * j + 128], egc[G, N], start=True, stop=True),
                              reads=[dmask, egc], writes=[bk])
                        kb.op("dve", lambda e, j=j, bk=bk: e.tensor_tensor(qgT[:, j, N], qnT[:, j, N], bk[:, N], ALU.mult),
                              reads=[qnT, bk], writes=[qgT])
                        yield
                    ntl = (ntok + 127) // 128
                    torder = range(ntl) if x == 0 else range(ntl - 1, -1, -1)
                    for tl in torder:
                        nt = min(128, ntok - tl * 128)
                        nr = nt // 64
                        TP = slice(0, nt)
                        TC = slice(tl * 128, tl * 128 + nt)
                        rows = slice(p0 + tl * 128, p0 + tl * 128 + nt)
                        self.dma("sp", ktok[TP, :], self.GKTOK[rows, :], [self.GKTOK], [ktok], ktok)
                        self.dma("sp", vtok[TP, :], self.GVTOK[rows, :], [self.GVTOK], [vtok], vtok)
                        bk = bank()
                        for qi, src in enumerate((b_t, bg, kd)):
                            kb.op("pe", lambda e, qi=qi, src=src, bk=bk: e.matmul(bk[TP, 8 * qi:8 * qi + 8], src[G, TC], self.ident_f[G, 0:8],
                                                                                 start=True, stop=True), reads=[src, self.ident_f], writes=[bk])
                        kb.op("act", lambda e, bk=bk: e.copy(gtk[TP, 0:24], bk[TP, 0:24]), reads=[bk], writes=[gtk])
                        for dst, src, c0, en in ((vb, vtok, 0, "pool"), (kbg, ktok, 8, "dve"), (kdec, ktok, 16, "pool")):
                            kb.op(en, lambda e, dst=dst, src=src, c0=c0: e.tensor_tensor(
                                v3(dst[TP, :]), v3(src[TP, :]), gtk[TP, c0:c0 + 8].unsqueeze(2).to_broadcast([nt, 8, 64]), ALU.mult),
                                reads=[src, gtk], writes=[dst])
                        yield
                        PA = bank()
                        PB = bank()
                        for cls in (0, 1):
                            for r in range(nr):
                                R_ = slice(64 * r, 64 * r + 64)
                                cc = slice(tl * 128 + 64 * r, tl * 128 + 64 * r + 64)
                                for h in range(H):
                                    j, m = h // 2, h % 2
                                    if (m == r) != (cls == 0):
                                        continue
                                    M_ = slice(64 * m, 64 * m + 64)
                                    C_ = slice(64 * h, 64 * h + 64)
                                    kb.op("pe", lambda e, R_=R_, C_=C_, M_=M_, j=j, cc=cc: e.matmul(PA[R_, C_], knT[M_, j, cc], kbT[M_, j, cc],
                                                                                                     start=True, stop=True),
                                          reads=[knT, kbT], writes=[PA])
                                    kb.op("pe", lambda e, R_=R_, C_=C_, M_=M_, j=j, cc=cc: e.matmul(PB[R_, C_], knT[M_, j, cc], qnT[M_, j, cc],
                                                                                                     start=True, stop=True),
                                          reads=[knT, qnT], writes=[PB])
                        PD = bank()
                        for r in range(nr):
                            cc = slice(tl * 128 + 64 * r, tl * 128 + 64 * r + 64)
                            kb.op("dve", lambda e, r=r, cc=cc: e.tensor_tensor(v3(BD[G, r, :]), v3(dmask[G, :]),
                                                                               gc[G, cc].unsqueeze(1).to_broadcast([8, 8, 64]), ALU.mult),
                                  reads=[dmask, gc], writes=[BD])
                            kb.op("pe", lambda e, r=r: e.matmul(PD[64 * r:64 * r + 64, :], self.ones_f[G, 0:64], BD[G, r, :], start=True, stop=False),
                                  reads=[self.ones_f, BD], writes=[PD])
                        kb.op("pe", lambda e: e.matmul(PD[TP, :], ngc[G, TC], dmask[G, :], start=False, stop=False),
                              reads=[ngc, dmask], writes=[PD])
                        mk = masks[x]
                        kb.op("pe", lambda e: e.matmul(v3(PD[TP, :]), self.ident_b[TP, TP], mk[TP, :].unsqueeze(1).to_broadcast([nt, 8, 64]),
                                                       start=False, stop=True), reads=[self.ident_b, mk], writes=[PD])
                        kb.op("act", lambda e: e.activation(out=E[TP, :], in_=PD[TP, :], func=AF.Exp), reads=[PD], writes=[E])
                        yield
                        kb.op("dve", lambda e: e.tensor_tensor(At[TP, :], PB[TP, :], E[TP, :], ALU.mult), reads=[PB, E], writes=[At])
                        kb.op("dve", lambda e: e.tensor_tensor(tt[TP, :], PA[TP, :], E[TP, :], ALU.mult), reads=[PA, E], writes=[tt])
                        X, Y, Rr = Xs[0], Ys[0], Rs[0]
                        kb.op("pool", lambda e: e.tensor_tensor(v3(X[TP, :]), v3(tt[TP, :]), negoff[TP, :].unsqueeze(1).to_broadcast([nt, 8, 64]),
                                                                ALU.mult), reads=[tt, negoff], writes=[X])
                        kb.op("pool", lambda e: e.tensor_tensor(v3(Rr[TP, :]), v3(X[TP, :]), identp[TP, :].unsqueeze(1).to_broadcast([nt, 8, 64]),
                                                                ALU.add), reads=[X, identp], writes=[Rr])
                        PY = bank()
                        for r in range(nr):
                            R_ = slice(64 * r, 64 * r + 64)
                            for h in range(H):
                                C_ = slice(64 * h, 64 * h + 64)
                                kb.op("pe", lambda e, R_=R_, C_=C_: e.matmul(PY[R_, C_], X[R_, C_], self.ident_b[R_, R_], start=True, stop=True),
                                      reads=[X, self.ident_b], writes=[PY])
                        kb.op("act", lambda e: e.copy(Y[TP, :], PY[TP, :]), reads=[PY], writes=[Y])
                        yield
                        for jj in range(5):
                            Xn, Yn, Rn = Xs[(jj + 1) % 2], Ys[(jj + 1) % 2], Rs[(jj + 1) % 2]
                            if jj < 4:
                                PX = bank()
                                blocks(PX, Y, X, nr)
                                kb.op("act", lambda e, PX=PX, Xn=Xn: e.copy(Xn[TP, :], PX[TP, :]), reads=[PX], writes=[Xn])
                            PY = bank()
                            blocks(PY, X, Y, nr)
                            kb.op("dve", lambda e, PY=PY: e.tensor_tensor(v3(IY[TP, :]), v3(PY[TP, :]),
                                                                          identpf[TP, :].unsqueeze(1).to_broadcast([nt, 8, 64]), ALU.add),
                                  reads=[PY, identpf], writes=[IY])
                            if jj < 4:
                                kb.op("dve", lambda e, PY=PY, Yn=Yn: e.tensor_copy(Yn[TP, :], PY[TP, :]), reads=[PY], writes=[Yn])
                            yield
                            PR = bank()
                            blocks(PR, IY, Rr, nr)
                            kb.op("act", lambda e, PR=PR, Rn=Rn: e.copy(Rn[TP, :], PR[TP, :]), reads=[PR], writes=[Rn])
                            X, Y, Rr = Xn, Yn, Rn
                            yield
                        Tt = Rr
                        PU = bank()
                        blocks(PU, Tt, vb, nr)
                        kb.op("act", lambda e, PU=PU: e.copy(u_sb[TP, :], PU[TP, :]), reads=[PU], writes=[u_sb])
                        PW = bank()
                        for cls in (0, 1):
                            for r in range(nr):
                                R_ = slice(64 * r, 64 * r + 64)
                                for h in range(H):
                                    j, m = h // 2, h % 2
                                    if (m == r) != (cls == 0):
                                        continue
                                    C_ = slice(64 * h, 64 * h + 64)
                                    kb.op("pe", lambda e, R_=R_, C_=C_, j=j, m=m, r=r: e.matmul(
                                        PW[64 * m:64 * m + 64, 128 * j + 64 * r:128 * j + 64 * r + 64], kbg[R_, C_], Tt[R_, C_], start=True, stop=True),
                                        reads=[kbg, Tt], writes=[PW])
                        if nr == 2:
                            kb.op("act", lambda e, PW=PW: e.mul(nwT[:, :], PW[:, :], -1.0), reads=[PW], writes=[nwT])
                        else:
                            kb.op("act", lambda e, PW=PW: e.mul(nwT[:, :].rearrange("p (j r c) -> p j r c", j=4, r=2)[:, :, 0, :],
                                                               PW[:, :].rearrange("p (j r c) -> p j r c", j=4, r=2)[:, :, 0, :], -1.0),
                                  reads=[PW], writes=[nwT])
                        yield
                        rorder = range(nr) if x == 0 else range(nr - 1, -1, -1)
                        for r in rorder:
                            R_ = slice(64 * r, 64 * r + 64)
                            nb = tl * 2 + r
                            cc = slice(tl * 128 + 64 * r, tl * 128 + 64 * r + 64)
                            PV = bank()
                            for mm_ in (0, 1):
                                for h in range(H):
                                    j, m = h // 2, h % 2
                                    if m != mm_:
                                        continue
                                    M_ = slice(64 * m, 64 * m + 64)
                                    kb.op("pe", lambda e, h=h, j=j, M_=M_, r=r, R_=R_, PV=PV: e.matmul(
                                        PV[R_, 64 * h:64 * h + 64], nwT[M_, 128 * j + 64 * r:128 * j + 64 * r + 64], Sb[M_, 64 * j:64 * j + 64],
                                        start=True, stop=True), reads=[nwT, Sb], writes=[PV])
                            kb.op("dve", lambda e, R_=R_, PV=PV: e.tensor_tensor(vn[R_, :], u_sb[R_, :], PV[R_, :], ALU.add),
                                  reads=[u_sb, PV], writes=[vn])
                            PO = bank()
                            for mm_ in (1 - r, r):
                                for h in range(H):
                                    j, m = h // 2, h % 2
                                    if m != mm_:
                                        continue
                                    M_ = slice(64 * m, 64 * m + 64)
                                    C_ = slice(64 * h, 64 * h + 64)
                                    kb.op("pe", lambda e, M_=M_, C_=C_, j=j, R_=R_, cc=cc, PO=PO: e.matmul(
                                        PO[R_, C_], qgT[M_, j, cc], Sb[M_, 64 * j:64 * j + 64], start=True, stop=True),
                                        reads=[qgT, Sb], writes=[PO])
                            yield
                            PO2 = bank()
                            for h in range(H):
                                C_ = slice(64 * h, 64 * h + 64)
                                kb.op("pe", lambda e, C_=C_, R_=R_, PO2=PO2: e.matmul(PO2[R_, C_], At[R_, C_], vn[R_, C_], start=True, stop=True),
                                      reads=[At, vn], writes=[PO2])
                            PS_ = bank()
                            for h in range(H):
                                j, m = h // 2, h % 2
                                C_ = slice(64 * h, 64 * h + 64)
                                kb.op("pe", lambda e, j=j, m=m, R_=R_, C_=C_, PS_=PS_: e.matmul(
                                    PS_[64 * m:64 * m + 64, 64 * j:64 * j + 64], kdec[R_, C_], vn[R_, C_], start=True, stop=True),
                                    reads=[kdec, vn], writes=[PS_])
                            kb.op("act", lambda e, R_=R_, PO=PO: e.copy(o_sb[R_, :], PO[R_, :]), reads=[PO], writes=[o_sb])
                            kb.op("dve", lambda e, R_=R_, PO2=PO2: e.tensor_tensor(o_sb[R_, :], o_sb[R_, :], PO2[R_, :], ALU.add),
                                  reads=[o_sb, PO2], writes=[o_sb])
                            kb.op("dve", lambda e, nb=nb: e.tensor_tensor(v3(S[:, :]), v3(S[:, :]),
                                                                          egl[:, :, nb:nb + 1].to_broadcast([128, 4, 64]), ALU.mult),
                                  reads=[S, egl], writes=[S])
                            kb.op("dve", lambda e, PS_=PS_: e.tensor_tensor(S[:, :], S[:, :], PS_[:, 0:256], ALU.add), reads=[S, PS_], writes=[S])
                            kb.op("act", lambda e: e.copy(Sb[:, :], S[:, :]), reads=[S], writes=[Sb])
                            yield
                        self.dma("sp", odram[rows, :], o_sb[TP, :], [o_sb], [odram], o_sb)

            gens = [pass_gen(0), pass_gen(1)]
            while gens:
                for g in list(gens):
                    try:
                        next(g)
                    except StopIteration:
                        gens.remove(g)

            ofs = [kb.sb(st, f"of_t{i}", [128, 512], F32) for i in range(2)]
            obs = [kb.sb(st, f"ob_t{i}", [128, 512], F32) for i in range(2)]
            zts = [kb.sb(st, f"z_t{i}", [128, 512], F32) for i in range(2)]
            osum = kb.sb(st, "osum", [128, 512], F32)
            osq = kb.sb(st, "osq", [128, 512], F32)
            ssum = kb.sb(st, "ssum", [128, 16], F32)
            gout = kb.sb(st, "gout", [128, 512], BF16)
            gTs = [kb.sb(st, f"gT{i}", [128, 4, 128], BF16) for i in range(2)]
            for t_ in ofs + obs + zts:
                kb.op("pool", lambda e, t_=t_: e.memset(t_[:], 0.0), writes=[t_])
            kb.op("pool", lambda e: e.memset(gout[:], 0.0), writes=[gout])
            ntp = (Lp + 127) // 128
            for t in range(ntp):
                nt = min(128, Lp - 128 * t)
                TP = slice(0, nt)
                rows = slice(128 * t, 128 * t + nt)
                of_t, ob_t, z_t, gT = ofs[t % 2], obs[t % 2], zts[t % 2], gTs[t % 2]
                self.dma("sp", of_t[TP, :], self.OF[rows, :], [self.OF], [of_t], of_t)
                self.dma("sp", ob_t[TP, :], self.OB[rows, :], [self.OB], [ob_t], ob_t)
                self.dma("sp", z_t[TP, :], self.Z[rows, :], [self.Z], [z_t], z_t)
                kb.op("pool", lambda e: e.tensor_tensor(osum[TP, :], of_t[TP, :], ob_t[TP, :], ALU.add), reads=[of_t, ob_t], writes=[osum])
                kb.op("act", lambda e: e.activation(out=osq[TP, :], in_=osum[TP, :], func=AF.Square), reads=[osum], writes=[osq])
                kb.op("dve", lambda e: e.tensor_reduce(ssum[TP, 0:8], v3(osq[TP, :]), AX.X, ALU.add), reads=[osq], writes=[ssum])
                kb.op("act", lambda e: e.activation(out=ssum[TP, 8:16], in_=ssum[TP, 0:8], func=AF.Ln, bias=self.cst[TP, 1:2],
                                                    scale=1.0 / 64.0), reads=[ssum, self.cst], writes=[ssum])
                kb.op("act", lambda e: e.activation(out=ssum[TP, 8:16], in_=ssum[TP, 8:16], func=AF.Exp, scale=-0.5),
                      reads=[ssum], writes=[ssum])
                kb.op("dve", lambda e: e.tensor_tensor(v3(osum[TP, :]), v3(osum[TP, :]),
                                                       ssum[TP, 8:16].unsqueeze(2).to_broadcast([nt, 8, 64]), ALU.mult),
                      reads=[osum, ssum], writes=[osum])
                kb.op("pool", lambda e: e.tensor_tensor(v3(osum[TP, :]), v3(osum[TP, :]),
                                                        gnorm[TP, :].unsqueeze(1).to_broadcast([nt, 8, 64]), ALU.mult),
                      reads=[osum, gnorm], writes=[osum])
                kb.op("dve", lambda e: e.tensor_tensor(gout[TP, :], osum[TP, :], z_t[TP, :], ALU.mult), reads=[osum, z_t], writes=[gout])
                for j in range(4):
                    kb.op("pe", lambda e, j=j: e.transpose(pT[:, j, 0:nt], gout[TP, 128 * j:128 * j + 128], self.ident_b[TP, TP]),
                          reads=[gout, self.ident_b], writes=[pT])
                kb.op("act", lambda e: e.copy(gT[:, :, 0:nt], pT[:, :, 0:nt]), reads=[pT], writes=[gT])
                c0 = 64 if t == 0 else 0
                x0 = 128 * t + c0 - 64
                if nt > c0:
                    self.dma("sp", self.CATT[4:8, :, x0:x0 + nt - c0].rearrange("c p t -> p c t"), gT[:, :, c0:nt],
                             [gT], [self.CATT], gT)
        kb.barrier()

    def phase3(self, si):
        kb = self.kb
        Lx = self.seq_lens[si]
        Lp = Lx + 64
        nkc = (Lp + 127) // 128
        with ExitStack() as st:
            kt = kb.sb(st, "kt", [128, 8, nkc * 128], BF16)
            va = kb.sb(st, "va", [128, nkc, 8, 128], BF16)
            kb.op("pool", lambda e: e.memset(va[:, :, :, 64:128], 1.0), writes=[va])
            kb.op("pool", lambda e: e.memset(va[:, :, :, 0:64], 0.0), writes=[va])
            self.dma("sp", kt[0:96, :, 0:Lp], self.KT[:, :, 0:Lp].rearrange("h r t -> r h t"), [self.KT], [kt], kt)
            for kc in range(nkc):
                nk = min(128, Lp - kc * 128)
                self.dma("sp", va[0:nk, kc, :, 0:64], self.V[kc * 128:kc * 128 + nk, :].rearrange("t (h e) -> t h e", h=8),
                         [self.V], [va], va)
            kb.op("pool", lambda e: e.memset(va[0:32, 0, :, :], 0.0), writes=[va])
            kb.op("pool", lambda e: e.memset(va[32:48, 0, :, :], 0.0), writes=[va])
            qts = [kb.sb(st, f"qt{i}", [128, 8, 512], BF16) for i in range(2)]
            pts = [kb.sb(st, f"pt{i}", [128, 512], BF16) for i in range(5)]
            rec = kb.sb(st, "rec", [128, 512], F32)
            mo = kb.sb(st, "mo", [128, 4, 512], BF16)
            pss = [kb.ps(st, f"pss{i}", [128, 512], F32) for i in range(5)]
            self._p3cnt = 0
            pos = [kb.ps(st, f"pos{i}", [128, 512], F32) for i in range(2)]
            nqb = Lx // 512 if Lx % 512 == 0 else (Lx + 511) // 512
            cnt = 0
            for qb in range(nqb):
                nq = min(512, Lx - qb * 512)
                Q = slice(0, nq)
                qt = qts[qb % 2]
                self.dma("sp", qt[0:96, :, Q], self.QT[:, :, qb * 512:qb * 512 + nq].rearrange("h r t -> r h t"),
                         [self.QT], [qt], qt)
                for h in range(H):
                    po = pos[h % 2]
                    LOOK = 3
                    slots = {}

                    def emit_s(kc, h=h):
                        nk = min(128, Lp - kc * 128)
                        ps_ = pss[self._p3cnt % 5]
                        pt = pts[self._p3cnt % 5]
                        self._p3cnt += 1
                        slots[kc] = (ps_, pt, nk)
                        kb.op("pe", lambda e: e.matmul(ps_[0:nk, Q], kt[0:96, h, kc * 128:kc * 128 + nk], qt[0:96, h, Q], start=True, stop=True),
                              reads=[kt, qt], writes=[ps_])

                    for kc in range(min(LOOK, nkc)):
                        emit_s(kc)
                    for kc in range(nkc):
                        ps_, pt, nk = slots.pop(kc)
                        kb.op("act", lambda e, ps_=ps_, pt=pt, nk=nk: e.activation(out=pt[0:nk, Q], in_=ps_[0:nk, Q], func=AF.Exp),
                              reads=[ps_], writes=[pt])
                        if kc + LOOK < nkc:
                            emit_s(kc + LOOK)
                        kb.op("pe", lambda e, po=po, pt=pt, kc=kc, nk=nk, h=h: e.matmul(
                            po[:, Q], va[0:nk, kc, h, :], pt[0:nk, Q], start=(kc == 0), stop=(kc == nkc - 1)),
                            reads=[va, pt], writes=[po])
                    kb.op("dve", lambda e, po=po: e.reciprocal(rec[0:64, Q], po[64:128, Q]), reads=[po], writes=[rec])
                    kb.op("dve", lambda e, po=po, h=h: e.tensor_tensor(mo[64 * (h % 2):64 * (h % 2) + 64, h // 2, Q], po[0:64, Q],
                                                                       rec[0:64, Q], ALU.mult), reads=[po, rec], writes=[mo])
                self.dma("sp", self.CATT[0:4, :, qb * 512:qb * 512 + nq].rearrange("c p t -> p c t"), mo[:, :, Q],
                         [mo], [self.CATT], mo)
        kb.barrier()

    def phase4a(self, si):
        kb = self.kb
        Lx = self.seq_lens[si]
        W = self.w
        with ExitStack() as st:
            w_out = kb.sb(st, "w_out", [128, 8, D], BF16)
            for kc in range(8):
                self.dma("sp", w_out[:, kc, :], self.Wb["w_out"][kc * 128:kc * 128 + 128, :], [self.Wb["w_out"]], [w_out], w_out)
            lng = self.load_bcast_vec(st, "lng", W["ln_in_g"], D)
            lnb = self.load_bcast_vec(st, "lnb", W["ln_in_b"], D)
            l1g = self.load_bcast_vec(st, "l1g", W["ln1_g"], D)
            l1b = self.load_bcast_vec(st, "l1b", W["ln1_b"], D)
            xts = [kb.sb(st, f"xt{i}", [128, D], F32) for i in range(3)]
            cats = [kb.sb(st, f"cat{i}", [128, 8, 128], BF16) for i in range(3)]
            hress = [kb.sb(st, f"hres{i}", [128, D], F32) for i in range(3)]
            r1s = [kb.sb(st, f"r1{i}", [128, D], F32) for i in range(3)]
            h1s = [kb.sb(st, f"h1{i}", [128, D], F32) for i in range(2)]
            h1bs = [kb.sb(st, f"h1b{i}", [128, D], BF16) for i in range(2)]
            h1ts = [kb.sb(st, f"h1t{i}", [128, 8, 128], BF16) for i in range(2)]
            statss = [kb.sb(st, f"stats{i}", [128, 32], F32) for i in range(5)]
            pT = kb.ps(st, "pT", [128, 8, 128], BF16)
            pb = [kb.ps(st, f"pb{i}", [128, 512], F32) for i in range(6)]
            ntile = Lx // 128

            def stage_a(k):
                xt, cat = xts[k % 3], cats[k % 3]
                rows = slice(128 * k, 128 * k + 128)
                self.dma("sp", xt[:, :], self.x_in[si][rows, :], [self.xin_reg], [xt], xt)
                self.dma("sp", cat[:, :, :], self.CATT[:, :, rows].rearrange("c p t -> p c t"), [self.CATT], [cat], cat)
                return self.layer_norm_gen(xt, xt[:, :], lng, lnb, hress[k % 3], hress[k % 3][:, :], xt, statss[k % 3])

            def stage_b_mm(k):
                cat = cats[k % 3]
                for nh in range(2):
                    bk = pb[(2 * k + nh) % 6]
                    F = slice(nh * 512, nh * 512 + 512)
                    for kc in range(8):
                        kb.op("pe", lambda e, kc=kc, bk=bk, F=F, cat=cat: e.matmul(bk[:, :], cat[:, kc, :], w_out[:, kc, F],
                                                                                   start=(kc == 0), stop=(kc == 7)),
                              reads=[cat, w_out], writes=[bk])

            def stage_b_r1(k):
                hres, r1 = hress[k % 3], r1s[k % 3]
                for nh in range(2):
                    bk = pb[(2 * k + nh) % 6]
                    F = slice(nh * 512, nh * 512 + 512)
                    kb.op("dve", lambda e, bk=bk, F=F: e.scalar_tensor_tensor(r1[:, F], hres[:, F], DN_ALPHA, bk[:, :],
                                                                             ALU.mult, ALU.add), reads=[hres, bk], writes=[r1])

            def stage_c_ln(k):
                r1, h1 = r1s[k % 3], h1s[k % 2]
                return self.layer_norm_gen(r1, r1[:, :], l1g, l1b, h1, h1[:, :], r1, statss[3 + k % 2])

            def stage_c_out(k):
                h1, h1b, h1t = h1s[k % 2], h1bs[k % 2], h1ts[k % 2]
                rows = slice(128 * k, 128 * k + 128)
                self.dma("sp", self.H1[rows, :], h1[:, :], [h1], [self.H1], h1)
                kb.op("act", lambda e: e.copy(h1b[:, :], h1[:, :]), reads=[h1], writes=[h1b])
                self.transpose_tile(h1b, lambda kk: h1b[:, kk * 128:(kk + 1) * 128], pT, h1t,
                                    lambda k0, k1: h1t[:, k0:k1, :])
                self.dma("sp", self.H1T[:, :, rows].rearrange("c p t -> p c t"), h1t[:, :, :], [h1t], [self.H1T], h1t)

            self.run_gens(stage_a(0), stage_a(1) if ntile > 1 else None, stage_a(2) if ntile > 2 else None)
            for k0 in range(min(2, ntile)):
                stage_b_mm(k0)
                stage_b_r1(k0)
            for k in range(ntile):
                if k + 2 < ntile:
                    stage_b_mm(k + 2)
                ga = stage_a(k + 3) if k + 3 < ntile else None
                self.run_gens(ga, stage_c_ln(k))
                if k + 2 < ntile:
                    stage_b_r1(k + 2)
                stage_c_out(k)
        kb.barrier()

    def phase4b(self, si):
        kb = self.kb
        Lx = self.seq_lens[si]
        W = self.w
        with ExitStack() as st:
            w1 = kb.sb(st, "w_ff1", [128, 8, DFF], BF16)
            w2 = kb.sb(st, "w_ff2", [128, 32, D], BF16)
            for kc in range(8):
                self.dma("sp", w1[:, kc, :], self.Wb["w_ff1"][kc * 128:kc * 128 + 128, :], [self.Wb["w_ff1"]], [w1], w1)
            for kc in range(32):
                self.dma("sp", w2[:, kc, :], self.Wb["w_ff2"][kc * 128:kc * 128 + 128, :], [self.Wb["w_ff2"]], [w2], w2)
            l2g = self.load_bcast_vec(st, "l2g", W["ln2_g"], D)
            l2b = self.load_bcast_vec(st, "l2b", W["ln2_b"], D)
            aT = kb.sb(st, "aT", [128, 32, 512], BF16)
            h1T = kb.sb(st, "h1T", [128, 8, 512], BF16)
            rls = [kb.sb(st, f"rl{i}", [128, 512], BF16) for i in range(2)]
            h1 = kb.sb(st, "h1", [128, D], F32)
            r2 = kb.sb(st, "r2", [128, D], F32)
            yo = kb.sb(st, "yo", [128, D], F32)
            lntmp = kb.sb(st, "lntmp", [128, D], F32)
            stats = kb.sb(st, "stats", [128, 32], F32)
            pb = [kb.ps(st, f"pb{i}", [128, 512], F32) for i in range(6)]
            bi = 0
            for b in range((Lx + 511) // 512):
                n = min(512, Lx - 512 * b)
                N = slice(0, n)
                cols = slice(512 * b, 512 * b + n)
                self.dma("sp", h1T[:, :, N], self.H1T[:, :, cols].rearrange("c p t -> p c t"), [self.H1T], [h1T], h1T)
                for mc in range(32):
                    bk = pb[bi % 6]
                    bi += 1
                    rl = rls[mc % 2]
                    for kc in range(8):
                        kb.op("pe", lambda e, kc=kc, mc=mc, bk=bk: e.matmul(bk[:, N], w1[:, kc, mc * 128:mc * 128 + 128], h1T[:, kc, N],
                                                                            start=(kc == 0), stop=(kc == 7)),
                              reads=[w1, h1T], writes=[bk])
                    kb.op("act", lambda e, bk=bk, rl=rl: e.activation(out=rl[:, N], in_=bk[:, N], func=AF.Relu), reads=[bk], writes=[rl])
                    kb.op("pool", lambda e, rl=rl, mc=mc: e.tensor_tensor(aT[:, mc, N], rl[:, N], rl[:, N], ALU.mult),
                          reads=[rl], writes=[aT])
                for tl in range(n // 128):
                    rows = slice(512 * b + 128 * tl, 512 * b + 128 * tl + 128)
                    self.dma("sp", h1[:, :], self.H1[rows, :], [self.H1], [h1], h1)
                    for nh in range(2):
                        bk = pb[bi % 6]
                        bi += 1
                        F = slice(nh * 512, nh * 512 + 512)
                        for mc in range(32):
                            kb.op("pe", lambda e, mc=mc, bk=bk, F=F, tl=tl: e.matmul(bk[:, :], aT[:, mc, tl * 128:tl * 128 + 128],
                                                                                     w2[:, mc, F], start=(mc == 0), stop=(mc == 31)),
                                  reads=[aT, w2], writes=[bk])
                        kb.op("dve", lambda e, bk=bk, F=F: e.scalar_tensor_tensor(r2[:, F], h1[:, F], DN_ALPHA, bk[:, :],
                                                                                 ALU.mult, ALU.add), reads=[h1, bk], writes=[r2])
                    self.layer_norm(r2, r2[:, :], l2g, l2b, yo, yo[:, :], r2, stats)
                    self.dma("sp", self.y_out[si][rows, :], yo[:, :], [yo], [self.yout_reg], yo)
        kb.barrier()


WEIGHT_NAMES = ["meta_tokens", "ln_in_g", "ln_in_b", "w_in", "g_cq", "g_ckv", "w_uq", "w_uk", "w_uv", "conv_w",
                "a_log_f", "a_log_b", "dt_bias_f", "dt_bias_b", "gdn_norm_g", "w_out", "ln1_g", "ln1_b",
                "w_ff1", "w_ff2", "ln2_g", "ln2_b"]


def build_prog(seq_lens, debug=False, phases=None):
    p = Prog(seq_lens, debug=debug)
    p.declare()
    p.setup_consts()
    for si in range(len(seq_lens)):
        for name in ["phase1", "phase2a", "phase2b", "phase3", "phase4a", "phase4b"]:
            if phases is not None and name not in phases:
                continue
            if not hasattr(p, name):
                continue
            getattr(p, name)(si)
    p.finish()
    return p


SEQ_LENS = [2048, 2048, 2048, 2048, 4096]
_PROG_CACHE = {}


def kernel(**inputs):
    n = 8
    x_prompt = np.asarray(inputs["x_prompt"], dtype=np.float32)
    x_sample = np.asarray(inputs["x_sample"], dtype=np.float32)
    wmap = {}
    for k in WEIGHT_NAMES:
        a = np.asarray(inputs[k], dtype=np.float32)
        if k not in ("meta_tokens", "ln_in_g", "ln_in_b"):
            a = a[0]
        wmap[k] = np.ascontiguousarray(a)
    prog = build_prog(SEQ_LENS, debug=False)
    in_maps = []
    for c in range(n):
        m = dict(wmap)
        for i in range(4):
            m[f"x{i}"] = np.ascontiguousarray(x_sample[4 * c + i])
        m["x4"] = np.ascontiguousarray(x_prompt[c // 2])
        in_maps.append(m)
    res = run_bass_kernel_spmd(prog.nc, in_maps, core_ids=list(range(n)))
    y_sample = np.empty_like(x_sample)
    y_prompt = np.empty_like(x_prompt)
    for c in range(n):
        r = res.results[c]
        for i in range(4):
            y_sample[4 * c + i] = np.asarray(r[f"y{i}"], dtype=np.float32)
        if c % 2 == 0:
            y_prompt[c // 2] = np.asarray(r["y4"], dtype=np.float32)
    return (y_prompt, y_sample)
```

```python
import numpy as np
from contextlib import ExitStack
import concourse.bass as bass
import concourse.mybir as mybir
from concourse.bass_utils import run_bass_kernel_spmd

F32 = mybir.dt.float32
BF16 = mybir.dt.bfloat16
I32 = mybir.dt.int32
AF = mybir.ActivationFunctionType
ALU = mybir.AluOpType
AX = mybir.AxisListType

D = 1024
NIN = 2752
H = 8
DFF = 4096
C_Q0, C_KV0, C_KR0, C_QKV0, C_Z0, C_G0 = 0, 384, 640, 672, 2208, 2720
DN_ALPHA = 2.0 ** 0.25
LN_EPS = 1e-5
RMS_EPS = 1e-6
QSCALE = 96.0 ** -0.5
NEG = -30000.0


class Reg:
    __slots__ = ("name", "w", "r", "sem", "cnt")

    def __init__(self, name):
        self.name = name
        self.w = {}
        self.r = {}
        self.sem = None
        self.cnt = 0


class Buf:
    def __init__(self, t, reg):
        self.t = t
        self.reg = reg

    def __getitem__(self, k):
        return self.t[k]


class _PEProxy:
    def __init__(self, kb):
        self.kb = kb
        self.e = kb.engs["pe"]

    def matmul(self, out, lhsT, rhs, **kw):
        self.kb._pe_pos(lhsT, out)
        return self.e.matmul(out, lhsT, rhs, **kw)

    def transpose(self, out, in_, ident):
        self.kb._pe_pos(in_, out)
        return self.e.transpose(out, in_, ident)


class KB:
    def __init__(self, nc, es):
        self.nc = nc
        self.es = es
        self.engs = {"pe": nc.tensor, "act": nc.scalar, "dve": nc.vector, "pool": nc.gpsimd, "sp": nc.sync}
        self.sem = {}
        self.cnt = {}
        self.waited = {}
        self.semname = {}
        for n in self.engs:
            self.sem[n] = es.enter_context(nc.semaphore("e_" + n))
            self.cnt[n] = 0
            self.waited[n] = {}
        self.nreg = 0
        self.all_dma_regs = []
        self.dpool = []
        self.dfree = {"sw": [], "hw": []}
        self.dkind = {}
        self.ninst = 0
        self.pe_live = set()
        self.nfence = 0
        self.pend = {n: False for n in self.engs}
        self.last_pe_w = None
        self.last_ins = {n: None for n in self.engs}
        self.pe_proxy = _PEProxy(self)

    def reg(self, name):
        self.nreg += 1
        return Reg(f"{name}_{self.nreg}")

    def sb(self, stack, name, shape, dt):
        self.nreg += 1
        nm = f"{name}_{self.nreg}"
        t = stack.enter_context(self.nc.sbuf_tensor(nm, list(shape), dt))
        return Buf(t, Reg(nm))

    def ps(self, stack, name, shape, dt):
        self.nreg += 1
        nm = f"{name}_{self.nreg}"
        t = stack.enter_context(self.nc.psum_tensor(nm, list(shape), dt))
        return Buf(t, Reg(nm))

    def _regs(self, xs):
        out = []
        for x in xs:
            if x is None:
                continue
            out.append(x.reg if isinstance(x, Buf) else x)
        return out

    def _flush(self, en):
        if self.pend[en]:
            self.last_ins[en].then_inc(self.sem[en], 1)
            self.cnt[en] += 1
            self.pend[en] = False

    def _wait(self, en, key, sem, val):
        w = self.waited[en]
        if w.get(key, 0) >= val:
            return
        if key in self.engs and val > self.cnt[key]:
            assert self.pend[key] and val == self.cnt[key] + 1
            self._flush(key)
        self.engs[en].wait_ge(sem, val)
        w[key] = val

    def op(self, en, fn, reads=(), writes=(), dma=None):
        reads = self._regs(reads)
        writes = self._regs(writes)
        deps = {}
        for R in reads:
            for k, v in R.w.items():
                if deps.get(k, (None, 0))[1] < v[1]:
                    deps[k] = v
        for R in writes:
            for dd in (R.w, R.r):
                for k, v in dd.items():
                    if deps.get(k, (None, 0))[1] < v[1]:
                        deps[k] = v
        for k, (sem, val) in deps.items():
            if k == "pe" and en == "pe" and dma is None:
                continue
            self._wait(en, k, sem, val)
        if dma is None and en == "pe":
            wkey = tuple(id(R) for R in writes)
            if self.pend["pe"] and wkey != self.last_pe_w:
                self._flush("pe")
            self.last_pe_w = wkey
        ins = fn(self.pe_proxy if en == "pe" else self.engs[en])
        self.ninst += 1
        if dma is not None:
            R = dma.reg if isinstance(dma, Buf) else dma
            if R.sem is None:
                kind = "sw" if en == "pool" else "hw"
                if self.dfree[kind]:
                    R.sem = self.dfree[kind].pop()
                else:
                    R.sem = len(self.dpool)
                    self.dpool.append([self.es.enter_context(self.nc.semaphore(f"ds{R.sem}")), 0])
                    self.dkind[R.sem] = kind
                self.all_dma_regs.append(R)
            assert self.dkind[R.sem] == ("sw" if en == "pool" else "hw"), "mixed DMA queues on one region semaphore"
            ent = self.dpool[R.sem]
            ent[1] += 16
            ins.then_inc(ent[0], 16)
            key, tok = f"ds{R.sem}", (ent[0], ent[1])
        else:
            if en == "pe":
                self.last_ins[en] = ins
                self.pend[en] = True
                key, tok = en, (self.sem[en], self.cnt[en] + 1)
            else:
                self.cnt[en] += 1
                ins.then_inc(self.sem[en], 1)
                key, tok = en, (self.sem[en], self.cnt[en])
        for R in reads:
            R.r[key] = tok
        for R in writes:
            R.w[key] = tok
        return tok

    def _pe_pos(self, kap, oap):
        k0, kn = kap.base_partition(), kap.partition_size()
        kq = 32 if kn <= 32 else (64 if kn <= 64 else 128)
        m0, mn = oap.base_partition(), oap.partition_size()
        bank = oap.name
        key = (k0, kq, m0, mn, bank)
        if key in self.pe_live:
            return
        conflict = False
        for (a0, aq, b0, bn, bk) in self.pe_live:
            if bk == bank and (a0, aq) != (k0, kq) and not (m0 + mn <= b0 or b0 + bn <= m0):
                conflict = True
                break
        if conflict:
            self._flush("pe")
            if self.cnt["pe"] > 0:
                self.engs["pe"].wait_ge(self.sem["pe"], self.cnt["pe"])
                self.waited["pe"]["pe"] = self.cnt["pe"]
            self.pe_live = set()
            self.nfence += 1
        self.pe_live.add(key)

    def barrier(self):
        for en in self.engs:
            self._flush(en)
        for en in self.engs:
            for o in self.engs:
                if o != en and self.cnt[o] > 0:
                    self._wait(en, o, self.sem[o], self.cnt[o])
            for i, ent in enumerate(self.dpool):
                if ent[1] > 0:
                    self._wait(en, f"ds{i}", ent[0], ent[1])
        for R in self.all_dma_regs:
            self.dfree[self.dkind[R.sem]].append(R.sem)
            R.sem = None
        self.all_dma_regs = []


def bc(ap, shape):
    return ap.to_broadcast(list(shape))


class _Stop(Exception):
    pass


class Prog:
    stop_at = None

    def _ck(self, n):
        return self.stop_at is not None and n == self.stop_at

    def __init__(self, seq_lens, debug=False):
        self.seq_lens = list(seq_lens)
        self.debug = debug
        self.nc = bass.Bass("TRN2", target_bir_lowering=False)
        self.es = ExitStack()
        self.kb = KB(self.nc, self.es)
        self.maxLx = max(self.seq_lens)
        self.maxLp = self.maxLx + 64

    def dram_in(self, name, shape, dt=F32):
        return self.nc.dram_tensor(name, list(shape), dt, kind="ExternalInput").ap()

    def dram_out(self, name, shape, dt=F32):
        return self.nc.dram_tensor(name, list(shape), dt, kind="ExternalOutput").ap()

    def dram_scr(self, name, shape, dt):
        kind = "ExternalOutput" if self.debug else "Internal"
        ap = self.nc.dram_tensor(name, list(shape), dt, kind=kind).ap()
        return Buf(ap, self.kb.reg(name))

    def declare(self):
        nseq = len(self.seq_lens)
        self.x_in = [self.dram_in(f"x{i}", [L, D]) for i, L in enumerate(self.seq_lens)]
        self.y_out = [self.dram_out(f"y{i}", [L, D]) for i, L in enumerate(self.seq_lens)]
        self.xin_reg = self.kb.reg("xin")
        self.yout_reg = self.kb.reg("yout")
        self.w = {}
        for name, shape in [
            ("meta_tokens", [16, D]), ("ln_in_g", [D]), ("ln_in_b", [D]), ("w_in", [D, NIN]),
            ("g_cq", [384]), ("g_ckv", [256]), ("w_uq", [384, 768]), ("w_uk", [256, 512]), ("w_uv", [256, 512]),
            ("conv_w", [4, 1536]), ("a_log_f", [8]), ("a_log_b", [8]), ("dt_bias_f", [8]), ("dt_bias_b", [8]),
            ("gdn_norm_g", [64]), ("w_out", [D, D]), ("ln1_g", [D]), ("ln1_b", [D]),
            ("w_ff1", [D, DFF]), ("w_ff2", [DFF, D]), ("ln2_g", [D]), ("ln2_b", [D]),
        ]:
            self.w[name] = self.dram_in(name, shape)
        self.w_reg = self.kb.reg("weights")
        self.Wb = {}
        for name, shape in [("w_in", [D, NIN]), ("w_uq", [384, 768]), ("w_uk", [256, 512]), ("w_uv", [256, 512]),
                            ("w_out", [D, D]), ("w_ff1", [D, DFF]), ("w_ff2", [DFF, D])]:
            self.Wb[name] = self.dram_scr("wb_" + name, shape, BF16)
        Lp, Lx = self.maxLp, self.maxLx
        self.QT = self.dram_scr("s_qt", [H, 96, Lx], BF16)
        self.KT = self.dram_scr("s_kt", [H, 96, Lp], BF16)
        self.V = self.dram_scr("s_v", [Lp, 512], BF16)
        self.QKVT = self.dram_scr("s_qkvt", [12, 128, Lp + 4], F32)
        self.Z = self.dram_scr("s_z", [Lp, 512], F32)
        self.GT = self.dram_scr("s_gt", [4, 8, Lp], F32)
        self.GQT = self.dram_scr("s_gqt", [4, 128, Lp], BF16)
        self.GKT = self.dram_scr("s_gkt", [4, 128, Lp], BF16)
        self.GKTOK = self.dram_scr("s_gktok", [Lp, 512], BF16)
        self.GVTOK = self.dram_scr("s_gvtok", [Lp, 512], BF16)
        self.OF = self.dram_scr("s_of", [Lp, 512], F32)
        self.OB = self.dram_scr("s_ob", [Lp, 512], F32)
        self.CATT = self.dram_scr("s_catt", [8, 128, Lx], BF16)
        self.H1 = self.dram_scr("s_h1", [Lx, D], F32)
        self.H1T = self.dram_scr("s_h1t", [8, 128, Lx], BF16)

    def dma(self, en, out_ap, in_ap, reads, writes, sem):
        return self.kb.op(en, lambda e: e.dma_start(out=out_ap, in_=in_ap), reads=reads, writes=writes, dma=sem)

    def rsqrt(self, out_buf, out_ap, in_buf, in_ap, scale, eps_ap, tmp_buf, tmp_ap):
        kb = self.kb
        kb.op("act", lambda e: e.activation(out=tmp_ap, in_=in_ap, func=AF.Ln, bias=eps_ap, scale=scale),
              reads=[in_buf, self.cst], writes=[tmp_buf])
        kb.op("act", lambda e: e.activation(out=out_ap, in_=tmp_ap, func=AF.Exp, scale=-0.5),
              reads=[tmp_buf], writes=[out_buf])

    def setup_consts(self):
        kb, nc = self.kb, self.nc
        es = self.es
        self.cst = kb.sb(es, "cst", [128, 16], F32)
        kb.op("dve", lambda e: e.memset(self.cst[:, 0:1], LN_EPS), writes=[self.cst])
        kb.op("dve", lambda e: e.memset(self.cst[:, 1:2], RMS_EPS), writes=[self.cst])
        kb.op("dve", lambda e: e.memset(self.cst[:, 2:3], 1.0), writes=[self.cst])
        kb.op("dve", lambda e: e.memset(self.cst[:, 3:4], 0.0), writes=[self.cst])
        self.io_i = kb.sb(es, "io_i", [128, 128], I32)
        self.io_f = kb.sb(es, "io_f", [128, 128], F32)
        kb.op("pool", lambda e: e.iota(self.io_i[:], [[1, 128]], base=0, channel_multiplier=-1), writes=[self.io_i])
        kb.op("dve", lambda e: e.tensor_copy(self.io_f[:], self.io_i[:]), reads=[self.io_i], writes=[self.io_f])
        self.ident_f = kb.sb(es, "ident_f", [128, 128], F32)
        self.ident_b = kb.sb(es, "ident_b", [128, 128], BF16)
        kb.op("dve", lambda e: e.tensor_single_scalar(self.ident_f[:], self.io_f[:], 0.0, ALU.is_equal),
              reads=[self.io_f], writes=[self.ident_f])
        kb.op("dve", lambda e: e.tensor_copy(self.ident_b[:], self.ident_f[:]), reads=[self.ident_f], writes=[self.ident_b])
        self.ones_f = kb.sb(es, "ones_f", [128, 128], F32)
        kb.op("dve", lambda e: e.memset(self.ones_f[:], 1.0), writes=[self.ones_f])
        for name, wb in self.Wb.items():
            src = self.w[name]
            nrow = src.shape[0]
            step = 256
            for r0 in range(0, nrow, step):
                r1 = min(nrow, r0 + step)
                self.dma("pool", wb[r0:r1, :], src[r0:r1, :], [self.w_reg], [wb], wb)
        kb.barrier()


    def finish(self):
        self.kb.barrier()
        self.es.close()

    def load_bcast_vec(self, st, name, vec_ap, n):
        b = self.kb.sb(st, name, [128, n], F32)
        self.dma("sp", b[:, :], vec_ap.partition_broadcast(128), reads=[self.w_reg], writes=[b], sem=b)
        return b

    def load_x_tile(self, si, t, xt):
        kb = self.kb
        Lx = self.seq_lens[si]
        x = self.x_in[si]
        if t == 0:
            kb.op("dve", lambda e: e.memset(xt[0:48, :], 0.0), writes=[xt])
            self.dma("sp", xt[48:64, :], self.w["meta_tokens"][:, :], reads=[self.w_reg], writes=[xt], sem=xt)
            self.dma("sp", xt[64:128, :], x[0:64, :], reads=[self.xin_reg], writes=[xt], sem=xt)
            return 128
        r0 = 128 * t - 64
        nt = min(128, Lx - r0)
        self.dma("sp", xt[0:nt, :], x[r0:r0 + nt, :], reads=[self.xin_reg], writes=[xt], sem=xt)
        return nt

    def layer_norm_gen(self, xin, xin_ap, g_bc, b_bc, out_buf, out_ap, tmp, stats):
        kb = self.kb
        for c in range(2):
            kb.op("dve", lambda e, c=c: e.bn_stats(stats[:, 6 * c:6 * c + 6], xin_ap[:, 512 * c:512 * c + 512]),
                  reads=[xin], writes=[stats])
        yield
        kb.op("dve", lambda e: e.bn_aggr(stats[:, 16:18], stats[:, 0:12].rearrange("p (a b) -> p a b", a=2)),
              reads=[stats], writes=[stats])
        yield
        kb.op("act", lambda e: e.activation(out=stats[:, 20:21], in_=stats[:, 17:18], func=AF.Ln,
                                            bias=self.cst[:, 0:1], scale=1.0), reads=[stats, self.cst], writes=[stats])
        yield
        kb.op("act", lambda e: e.activation(out=stats[:, 21:22], in_=stats[:, 20:21], func=AF.Exp, scale=-0.5),
              reads=[stats], writes=[stats])
        yield
        kb.op("dve", lambda e: e.scalar_tensor_tensor(stats[:, 22:23], stats[:, 16:17], -1.0, stats[:, 21:22], ALU.mult, ALU.mult),
              reads=[stats], writes=[stats])
        yield
        kb.op("act", lambda e: e.activation(out=tmp[:, :], in_=xin_ap, func=AF.Identity, bias=stats[:, 22:23], scale=stats[:, 21:22]),
              reads=[xin, stats], writes=[tmp])
        yield
        kb.op("dve", lambda e: e.tensor_tensor(tmp[:, :], tmp[:, :], g_bc[:, :], ALU.mult),
              reads=[tmp, g_bc], writes=[tmp])
        yield
        kb.op("dve", lambda e: e.tensor_tensor(out_ap, tmp[:, :], b_bc[:, :], ALU.add),
              reads=[tmp, b_bc], writes=[out_buf])

    @staticmethod
    def run_gens(*gens):
        gens = [g for g in gens if g is not None]
        while gens:
            for g in list(gens):
                try:
                    next(g)
                except StopIteration:
                    gens.remove(g)

    def layer_norm(self, *a):
        self.run_gens(self.layer_norm_gen(*a))

    def transpose_tile(self, src, src_ap_fn, pT, dst, dst_ap_fn, nk=8):
        kb = self.kb
        for k in range(nk):
            kb.op("pe", lambda e, k=k: e.transpose(pT[:, k, :], src_ap_fn(k), self.ident_b[:, :]),
                  reads=[src, self.ident_b], writes=[pT])
        h = nk // 2
        kb.op("act", lambda e: e.copy(dst_ap_fn(0, h), pT[:, 0:h, :]), reads=[pT], writes=[dst])
        kb.op("dve", lambda e: e.tensor_copy(dst_ap_fn(h, nk), pT[:, h:nk, :]), reads=[pT], writes=[dst])

    def phase1(self, si):
        kb, nc = self.kb, self.nc
        Lx = self.seq_lens[si]
        Lp = Lx + 64
        W = self.w
        with ExitStack() as st:
            w_in = kb.sb(st, "w_in", [128, 8, NIN], BF16)
            wqA = kb.sb(st, "wqA", [128, 3, 8, 96], BF16)
            wqB = kb.sb(st, "wqB", [128, 3, 8, 96], BF16)
            wkrA = kb.sb(st, "wkrA", [128, 8, 96], BF16)
            wkrB = kb.sb(st, "wkrB", [128, 8, 96], BF16)
            w_uk = kb.sb(st, "w_uk", [128, 2, 512], BF16)
            w_uv = kb.sb(st, "w_uv", [128, 2, 512], BF16)
            kb.op("dve", lambda e: e.memset(wqB[:], 0.0), writes=[wqB])
            kb.op("dve", lambda e: e.memset(wkrA[:], 0.0), writes=[wkrA])
            kb.op("dve", lambda e: e.memset(wkrB[:], 0.0), writes=[wkrB])
            for kc in range(8):
                rows = slice(kc * 128, kc * 128 + 128)
                self.dma("sp", w_in[:, kc, :], self.Wb["w_in"][rows, :], [self.Wb["w_in"]], [w_in], w_in)
                self.dma("sp", wkrA[:, kc, 64:96], self.Wb["w_in"][rows, C_KR0:C_KR0 + 32], [self.Wb["w_in"]], [wkrA], wkrA)
                self.dma("sp", wkrB[:, kc, 64:80], self.Wb["w_in"][rows, C_KR0 + 16:C_KR0 + 32], [self.Wb["w_in"]], [wkrB], wkrB)
                self.dma("sp", wkrB[:, kc, 80:96], self.Wb["w_in"][rows, C_KR0:C_KR0 + 16], [self.Wb["w_in"]], [wkrB], wkrB)
            for kc in range(3):
                rows = slice(kc * 128, kc * 128 + 128)
                wq3 = self.Wb["w_uq"][rows, :].rearrange("p (h c) -> p h c", c=96)
                self.dma("sp", wqA[:, kc, :, :], wq3, [self.Wb["w_uq"]], [wqA], wqA)
                self.dma("sp", wqB[:, kc, :, 64:80], wq3[:, :, 80:96], [self.Wb["w_uq"]], [wqB], wqB)
                self.dma("sp", wqB[:, kc, :, 80:96], wq3[:, :, 64:80], [self.Wb["w_uq"]], [wqB], wqB)
            for kc in range(2):
                rows = slice(kc * 128, kc * 128 + 128)
                self.dma("sp", w_uk[:, kc, :], self.Wb["w_uk"][rows, :], [self.Wb["w_uk"]], [w_uk], w_uk)
                self.dma("sp", w_uv[:, kc, :], self.Wb["w_uv"][rows, :], [self.Wb["w_uv"]], [w_uv], w_uv)
            lng = self.load_bcast_vec(st, "lng", W["ln_in_g"], D)
            lnb = self.load_bcast_vec(st, "lnb", W["ln_in_b"], D)
            gcq = kb.sb(st, "gcq", [128, 3], F32)
            gckv = kb.sb(st, "gckv", [128, 2], F32)
            for kc in range(3):
                self.dma("sp", gcq[:, kc:kc + 1], W["g_cq"][kc * 128:kc * 128 + 128].rearrange("(p o) -> p o", o=1),
                         [self.w_reg], [gcq], gcq)
            for kc in range(2):
                self.dma("sp", gckv[:, kc:kc + 1], W["g_ckv"][kc * 128:kc * 128 + 128].rearrange("(p o) -> p o", o=1),
                         [self.w_reg], [gckv], gckv)
            gpar = kb.sb(st, "gpar", [128, 4], F32)
            kb.op("dve", lambda e: e.memset(gpar[:], 0.0), writes=[gpar])
            for off, sfx in ((0, "f"), (32, "b")):
                self.dma("sp", gpar[off:off + 8, 0:1], W["dt_bias_" + sfx].rearrange("(p o) -> p o", o=1),
                         [self.w_reg], [gpar], gpar)
                self.dma("sp", gpar[off:off + 8, 1:2], W["a_log_" + sfx].rearrange("(p o) -> p o", o=1),
                         [self.w_reg], [gpar], gpar)
            kb.op("act", lambda e: e.activation(out=gpar[0:40, 2:3], in_=gpar[0:40, 1:2], func=AF.Exp),
                  reads=[gpar], writes=[gpar])
            kb.op("dve", lambda e: e.tensor_scalar_mul(gpar[0:40, 2:3], gpar[0:40, 2:3], -1.0), reads=[gpar], writes=[gpar])
            ropeC = kb.sb(st, "ropeC", [128, 512], F32)
            ropeS = kb.sb(st, "ropeS", [128, 512], F32)
            rp = kb.sb(st, "rp", [128, 8], F32)
            rpi = kb.sb(st, "rpi", [128, 2], I32)
            R = slice(64, 96)
            kb.op("pool", lambda e: e.iota(rpi[R, 0:1], [[0, 1]], base=0, channel_multiplier=1), writes=[rpi])
            kb.op("dve", lambda e: e.tensor_copy(rp[R, 0:1], rpi[R, 0:1]), reads=[rpi], writes=[rp])
            kb.op("dve", lambda e: e.tensor_single_scalar(rp[R, 1:2], rp[R, 0:1], 16.0, ALU.is_ge), reads=[rp], writes=[rp])
            kb.op("dve", lambda e: e.scalar_tensor_tensor(rp[R, 2:3], rp[R, 1:2], -16.0, rp[R, 0:1], ALU.mult, ALU.add),
                  reads=[rp], writes=[rp])
            kb.op("act", lambda e: e.activation(out=rp[R, 3:4], in_=rp[R, 2:3], func=AF.Exp,
                                                scale=-float(np.log(10000.0)) / 16.0), reads=[rp], writes=[rp])
            kb.op("dve", lambda e: e.tensor_scalar_mul(rp[R, 3:4], rp[R, 3:4], float(1.0 / (2.0 * np.pi))),
                  reads=[rp], writes=[rp])
            kb.op("dve", lambda e: e.tensor_scalar(rp[R, 4:5], rp[R, 1:2], float(4.0 * np.pi), float(-2.0 * np.pi),
                                                   ALU.mult, ALU.add), reads=[rp], writes=[rp])
            kb.op("dve", lambda e: e.memset(rp[R, 5:6], float(2.0 * np.pi)), writes=[rp])
            posi = kb.sb(st, "posi", [128, 512], I32)
            posf = kb.sb(st, "posf", [128, 512], F32)
            ru = kb.sb(st, "ru", [128, 512], F32)
            rui = kb.sb(st, "rui", [128, 512], I32)
            ruf = kb.sb(st, "ruf", [128, 512], F32)

            def rope_tables(c):
                kb.op("pool", lambda e: e.iota(posi[R, :], [[1, 512]], base=512 * c - 48, channel_multiplier=0),
                      writes=[posi])
                kb.op("dve", lambda e: e.tensor_copy(posf[R, :], posi[R, :]), reads=[posi], writes=[posf])
                for tab, off, sc in ((ropeC, 0.25, 5), (ropeS, 0.0, 4)):
                    kb.op("dve", lambda e, off=off: e.tensor_scalar(ru[R, :], posf[R, :], rp[R, 3:4], off, ALU.mult, ALU.add),
                          reads=[posf, rp], writes=[ru])
                    kb.op("dve", lambda e: e.tensor_copy(rui[R, :], ru[R, :]), reads=[ru], writes=[rui])
                    kb.op("dve", lambda e: e.tensor_copy(ruf[R, :], rui[R, :]), reads=[rui], writes=[ruf])
                    kb.op("dve", lambda e: e.tensor_tensor(ru[R, :], ru[R, :], ruf[R, :], ALU.subtract),
                          reads=[ru, ruf], writes=[ru])
                    kb.op("act", lambda e, tab=tab, sc=sc: e.activation(
                        out=tab[R, :], in_=ru[R, :], func=AF.Sin, scale=rp[R, sc:sc + 1]),
                        reads=[ru, rp], writes=[tab])
            xts = [kb.sb(st, f"xt{i}", [128, D], F32) for i in range(4)]
            for xt in xts:
                kb.op("pool", lambda e, xt=xt: e.memset(xt[:], 0.0), writes=[xt])
            statss = [kb.sb(st, f"stats{i}", [128, 32], F32) for i in range(4)]
            hbs = [kb.sb(st, f"hb{i}", [128, D], BF16) for i in range(4)]
            hTs = [kb.sb(st, f"hT{i}", [128, 8, 512], BF16) for i in range(2)]
            stage = kb.sb(st, "stage", [128, 12, 512], F32)
            cq = kb.sb(st, "cq", [128, 3, 512], F32)
            sq = kb.sb(st, "sq", [128, 3, 512], F32)
            rstd = kb.sb(st, "rstd", [128, 512], F32)
            rtmp = kb.sb(st, "rtmp", [128, 512], F32)
            cqn = kb.sb(st, "cqn", [128, 3, 512], BF16)
            ckvn = kb.sb(st, "ckvn", [128, 2, 512], BF16)
            qs = kb.sb(st, "qs", [128, 8, 512], BF16)
            ks = kb.sb(st, "ks", [128, 8, 512], BF16)
            t1 = kb.sb(st, "t1", [128, 512], F32)
            t2 = kb.sb(st, "t2", [128, 512], F32)
            vs = kb.sb(st, "vs", [128, 512], BF16)
            zs = kb.sb(st, "zs", [128, 512], F32)
            ze = kb.sb(st, "ze", [128, 512], F32)
            gs = kb.sb(st, "gs", [128, 512], F32)
            ga = kb.sb(st, "ga", [128, 512], F32)
            gb = kb.sb(st, "gb", [128, 512], F32)
            lnq = kb.sb(st, "lnq", [128, 1], F32)
            kb.op("dve", lambda e: e.memset(lnq[:], float(np.log(QSCALE))), writes=[lnq])
            kb.op("dve", lambda e: e.memset(gs[:], 0.0), writes=[gs])
            gs2 = kb.sb(st, "gs2", [128, 512], F32)
            kb.op("dve", lambda e: e.memset(gs2[:], 0.0), writes=[gs2])
            pT = kb.ps(st, "pT", [128, 8, 128], BF16)
            pb = [kb.ps(st, f"pb{i}", [128, 512], F32) for i in range(7)]
            self._pbi = 0

            def bank():
                self._pbi = (self._pbi + 1) % 7
                return pb[self._pbi]

            nblk = (Lp + 511) // 512

            def ln_part(b):
                p0_ = 512 * b
                ntok_ = min(512, Lp - p0_)
                ntl_ = (ntok_ + 127) // 128
                gens = []
                for tl in range(ntl_):
                    xt = xts[tl]
                    self.load_x_tile(si, 4 * b + tl, xt)
                    gens.append(self.layer_norm_gen(xt, xt[:, :], lng, lnb, hbs[tl], hbs[tl][:, :], xt, statss[tl]))
                self.run_gens(*gens)

            def tr_part(b):
                p0_ = 512 * b
                ntok_ = min(512, Lp - p0_)
                hT_ = hTs[b % 2]
                for tl in range((ntok_ + 127) // 128):
                    hb = hbs[tl]
                    self.transpose_tile(hb, lambda k, hb=hb: hb[:, k * 128:(k + 1) * 128], pT, hT_,
                                        lambda k0, k1, tl=tl: hT_[:, k0:k1, tl * 128:(tl + 1) * 128])

            ln_part(0)
            tr_part(0)
            for b in range(nblk):
                p0 = 512 * b
                ntok = min(512, Lp - p0)
                ntl = (ntok + 127) // 128
                hT = hTs[b % 2]
                if b + 1 < nblk:
                    ln_part(b + 1)
                N = slice(0, ntok)

                def proj(col0, m, out_ap_fn=None):
                    bk = bank()
                    o = bk[0:m, N] if out_ap_fn is None else out_ap_fn(bk)
                    for kc in range(8):
                        kb.op("pe", lambda e, kc=kc: e.matmul(o, w_in[:, kc, col0:col0 + m], hT[:, kc, N],
                                                               start=(kc == 0), stop=(kc == 7)),
                              reads=[w_in, hT], writes=[bk])
                    return bk

                for mc in range(12):
                    bk = proj(C_QKV0 + mc * 128, 128)
                    if mc % 2 == 0:
                        kb.op("act", lambda e, bk=bk, mc=mc: e.copy(stage[:, mc, N], bk[:, N]), reads=[bk], writes=[stage])
                    else:
                        kb.op("dve", lambda e, bk=bk, mc=mc: e.tensor_copy(stage[:, mc, N], bk[:, N]), reads=[bk], writes=[stage])
                self.dma("sp", self.QKVT[:, :, 1 + p0:1 + p0 + ntok].rearrange("c p t -> p c t"), stage[:, :, N],
                         [stage], [self.QKVT], stage)

                def rms_fm(col0, nch, gpp, outn, lnbias):
                    banks = [proj(col0 + mc * 128, 128) for mc in range(nch)]
                    for mc, bk in enumerate(banks):
                        kb.op("act", lambda e, bk=bk, mc=mc: e.copy(cq[:, mc, N], bk[:, N]), reads=[bk], writes=[cq])
                        kb.op("act", lambda e, bk=bk, mc=mc: e.activation(out=sq[:, mc, N], in_=bk[:, N], func=AF.Square),
                              reads=[bk], writes=[sq])
                    bs = bank()
                    for mc in range(nch):
                        kb.op("pe", lambda e, mc=mc: e.matmul(bs[:, N], self.ones_f[:, :], sq[:, mc, N],
                                                               start=(mc == 0), stop=(mc == nch - 1)),
                              reads=[self.ones_f, sq], writes=[bs])
                    kb.op("act", lambda e: e.activation(out=rtmp[:, N], in_=bs[:, N], func=AF.Ln, bias=self.cst[:, 1:2],
                                                        scale=1.0 / (nch * 128)), reads=[bs, self.cst], writes=[rtmp])
                    if lnbias is None:
                        kb.op("act", lambda e: e.activation(out=rstd[:, N], in_=rtmp[:, N], func=AF.Exp, scale=-0.5),
                              reads=[rtmp], writes=[rstd])
                    else:
                        kb.op("act", lambda e: e.activation(out=rstd[:, N], in_=rtmp[:, N], func=AF.Exp, scale=-0.5,
                                                            bias=lnbias[:, 0:1]), reads=[rtmp, lnbias], writes=[rstd])
                    for mc in range(nch):
                        kb.op("dve", lambda e, mc=mc: e.scalar_tensor_tensor(outn[:, mc, N], cq[:, mc, N], gpp[:, mc:mc + 1],
                                                                             rstd[:, N], ALU.mult, ALU.mult),
                              reads=[cq, gpp, rstd], writes=[outn])

                rms_fm(C_KV0, 2, gckv, ckvn, None)
                for h in range(H):
                    bk = bank()
                    for kc in range(2):
                        kb.op("pe", lambda e, kc=kc, h=h, bk=bk: e.matmul(bk[0:64, N], w_uk[:, kc, h * 64:(h + 1) * 64],
                                                                          ckvn[:, kc, N], start=(kc == 0), stop=(kc == 1)),
                              reads=[w_uk, ckvn], writes=[bk])
                    if h % 2 == 0:
                        kb.op("act", lambda e, bk=bk, h=h: e.copy(ks[0:64, h, N], bk[0:64, N]), reads=[bk], writes=[ks])
                    else:
                        kb.op("dve", lambda e, bk=bk, h=h: e.tensor_copy(ks[0:64, h, N], bk[0:64, N]), reads=[bk], writes=[ks])
                RR = slice(64, 96)
                P = slice(p0, p0 + ntok)
                rope_tables(b)

                def rope_pair(bkA, bkB, dst_fn):
                    kb.op("dve", lambda e: e.tensor_tensor(t1[RR, N], bkA[RR, N], ropeC[RR, N], ALU.mult),
                          reads=[bkA, ropeC], writes=[t1])
                    kb.op("dve", lambda e: e.tensor_tensor(t2[RR, N], bkB[RR, N], ropeS[RR, N], ALU.mult),
                          reads=[bkB, ropeS], writes=[t2])
                    dst_fn()

                bkA = bank()
                bkB = bank()
                for kc in range(8):
                    kb.op("pe", lambda e, kc=kc: e.matmul(bkA[0:96, N], wkrA[:, kc, :], hT[:, kc, N], start=(kc == 0), stop=(kc == 7)),
                          reads=[wkrA, hT], writes=[bkA])
                for kc in range(8):
                    kb.op("pe", lambda e, kc=kc: e.matmul(bkB[0:96, N], wkrB[:, kc, :], hT[:, kc, N], start=(kc == 0), stop=(kc == 7)),
                          reads=[wkrB, hT], writes=[bkB])

                def kdst():
                    kb.op("dve", lambda e: e.tensor_tensor(t1[RR, N], t1[RR, N], t2[RR, N], ALU.add), reads=[t1, t2], writes=[t1])
                    kb.op("act", lambda e: e.copy(ks[RR, 0:4, N], t1[RR, N].unsqueeze(1).to_broadcast([32, 4, ntok])),
                          reads=[t1], writes=[ks])
                    kb.op("dve", lambda e: e.tensor_copy(ks[RR, 4:8, N], t1[RR, N].unsqueeze(1).to_broadcast([32, 4, ntok])),
                          reads=[t1], writes=[ks])
                rope_pair(bkA, bkB, kdst)
                self.dma("sp", self.KT[:, :, P].rearrange("h r t -> r h t"), ks[0:96, :, N], [ks], [self.KT], ks)
                for tl in range(ntl):
                    nt = min(128, ntok - tl * 128)
                    bk = bank()
                    for kc in range(2):
                        kb.op("pe", lambda e, kc=kc, tl=tl, nt=nt, bk=bk: e.matmul(
                            bk[0:nt, :], ckvn[:, kc, tl * 128:tl * 128 + nt], w_uv[:, kc, :], start=(kc == 0), stop=(kc == 1)),
                            reads=[ckvn, w_uv], writes=[bk])
                    kb.op("act", lambda e, bk=bk, nt=nt: e.copy(vs[0:nt, :], bk[0:nt, :]), reads=[bk], writes=[vs])
                    self.dma("sp", self.V[p0 + tl * 128:p0 + tl * 128 + nt, :], vs[0:nt, :], [vs], [self.V], vs)

                c0 = 64 if b == 0 else 0
                if ntok > c0:
                    rms_fm(C_Q0, 3, gcq, cqn, lnq)
                    for h in range(H):
                        bkA = bank()
                        bkB = bank()
                        for kc in range(3):
                            kb.op("pe", lambda e, kc=kc, h=h, bkA=bkA: e.matmul(bkA[0:96, N], wqA[:, kc, h, :], cqn[:, kc, N],
                                                                                 start=(kc == 0), stop=(kc == 2)),
                                  reads=[wqA, cqn], writes=[bkA])
                        for kc in range(3):
                            kb.op("pe", lambda e, kc=kc, h=h, bkB=bkB: e.matmul(bkB[0:96, N], wqB[:, kc, h, :], cqn[:, kc, N],
                                                                                 start=(kc == 0), stop=(kc == 2)),
                                  reads=[wqB, cqn], writes=[bkB])
                        kb.op("act", lambda e, bkA=bkA, h=h: e.copy(qs[0:64, h, N], bkA[0:64, N]), reads=[bkA], writes=[qs])

                        def qdst(h=h):
                            kb.op("dve", lambda e: e.tensor_tensor(qs[RR, h, N], t1[RR, N], t2[RR, N], ALU.add),
                                  reads=[t1, t2], writes=[qs])
                        rope_pair(bkA, bkB, qdst)
                    self.dma("sp", self.QT[:, :, p0 + c0 - 64:p0 + ntok - 64].rearrange("h r t -> r h t"),
                             qs[0:96, :, c0:ntok], [qs], [self.QT], qs)

                for tl in range(ntl):
                    nt = min(128, ntok - tl * 128)
                    bk = bank()
                    for kc in range(8):
                        kb.op("pe", lambda e, kc=kc, tl=tl, nt=nt, bk=bk: e.matmul(
                            bk[0:nt, :], hT[:, kc, tl * 128:tl * 128 + nt], w_in[:, kc, C_Z0:C_Z0 + 512],
                            start=(kc == 0), stop=(kc == 7)), reads=[hT, w_in], writes=[bk])
                    kb.op("act", lambda e, bk=bk, nt=nt: e.activation(out=ze[0:nt, :], in_=bk[0:nt, :], func=AF.Exp, scale=-1.0),
                          reads=[bk], writes=[ze])
                    kb.op("act", lambda e, nt=nt: e.activation(out=ze[0:nt, :], in_=ze[0:nt, :], func=AF.Ln, bias=self.cst[0:nt, 2:3], scale=1.0),
                          reads=[ze, self.cst], writes=[ze])
                    kb.op("act", lambda e, nt=nt: e.activation(out=ze[0:nt, :], in_=ze[0:nt, :], func=AF.Exp, scale=-1.0), reads=[ze], writes=[ze])
                    kb.op("dve", lambda e, bk=bk, nt=nt: e.tensor_tensor(zs[0:nt, :], bk[0:nt, :], ze[0:nt, :], ALU.mult),
                          reads=[bk, ze], writes=[zs])
                    self.dma("sp", self.Z[p0 + tl * 128:p0 + tl * 128 + nt, :], zs[0:nt, :], [zs], [self.Z], zs)

                bk = bank()
                bk2 = bank()
                for g in range(4):
                    bb = bk if g < 2 else bk2
                    ro = 32 * (g % 2)
                    for kc in range(8):
                        kb.op("pe", lambda e, kc=kc, g=g, bb=bb, ro=ro: e.matmul(
                            bb[ro:ro + 8, N], w_in[:, kc, C_G0 + 8 * g:C_G0 + 8 * g + 8], hT[:, kc, N],
                            start=(kc == 0), stop=(kc == 7)), reads=[w_in, hT], writes=[bb])
                A = slice(0, 40)
                kb.op("dve", lambda e: e.tensor_scalar(ga[A, N], bk[A, N], gpar[A, 0:1], None, ALU.add), reads=[bk, gpar], writes=[ga])
                kb.op("act", lambda e: e.activation(out=gb[A, N], in_=ga[A, N], func=AF.Abs), reads=[ga], writes=[gb])
                kb.op("act", lambda e: e.activation(out=gb[A, N], in_=gb[A, N], func=AF.Exp, scale=-1.0), reads=[gb], writes=[gb])
                kb.op("act", lambda e: e.activation(out=gb[A, N], in_=gb[A, N], func=AF.Ln, bias=self.cst[A, 2:3], scale=1.0),
                      reads=[gb, self.cst], writes=[gb])
                kb.op("dve", lambda e: e.scalar_tensor_tensor(ga[A, N], ga[A, N], 0.0, gb[A, N], ALU.max, ALU.add),
                      reads=[ga, gb], writes=[ga])
                kb.op("dve", lambda e: e.tensor_scalar(gs[A, N], ga[A, N], gpar[A, 2:3], None, ALU.mult), reads=[ga, gpar], writes=[gs])
                kb.op("act", lambda e: e.activation(out=gb[A, N], in_=bk2[A, N], func=AF.Exp, scale=-1.0), reads=[bk2], writes=[gb])
                kb.op("act", lambda e: e.activation(out=gb[A, N], in_=gb[A, N], func=AF.Ln, bias=self.cst[A, 2:3], scale=1.0),
                      reads=[gb, self.cst], writes=[gb])
                kb.op("act", lambda e: e.activation(out=gs2[A, N], in_=gb[A, N], func=AF.Exp, scale=-1.0), reads=[gb], writes=[gs2])
                if b == 0:
                    kb.op("dve", lambda e: e.memset(gs[A, 0:48], 0.0), writes=[gs])
                    kb.op("dve", lambda e: e.memset(gs2[A, 0:48], 0.0), writes=[gs2])
                for g in range(4):
                    src = gs if g < 2 else gs2
                    ro = 32 * (g % 2)
                    self.dma("sp", self.GT[g, :, P], src[ro:ro + 8, N], [src], [self.GT], src)
                if b + 1 < nblk:
                    tr_part(b + 1)
        kb.barrier()

    def phase2a(self, si):
        kb = self.kb
        Lx = self.seq_lens[si]
        Lp = Lx + 64
        W = self.w
        with ExitStack() as st:
            cw = kb.sb(st, "cw", [128, 12, 4], F32)
            for k in range(4):
                self.kb.op("sp", lambda e, k=k: e.dma_start(out=cw[:, :, k:k + 1],
                                                            in_=W["conv_w"][k, :].rearrange("(c p o) -> p c o", p=128, o=1),
                                                            allow_slow_non_contiguous=True),
                           reads=[self.w_reg], writes=[cw], dma=cw)
            blk = kb.sb(st, "blk", [128, 128], F32)
            kb.op("dve", lambda e: e.memset(blk[:], 0.0), writes=[blk])
            kb.op("dve", lambda e: e.memset(blk[0:64, 0:64], 1.0), writes=[blk])
            kb.op("dve", lambda e: e.memset(blk[64:128, 64:128], 1.0), writes=[blk])
            raw = kb.sb(st, "raw", [128, 12, 516], F32)
            accs = [kb.sb(st, f"acc{i}", [128, 512], F32) for i in range(4)]
            ss = [kb.sb(st, f"s{i}", [128, 512], F32) for i in range(4)]
            sqs = [kb.sb(st, f"sq{i}", [128, 512], F32) for i in range(4)]
            rns = [kb.sb(st, f"rn{i}", [128, 512], F32) for i in range(4)]
            rtmps = [kb.sb(st, f"rtmp{i}", [128, 512], F32) for i in range(4)]
            qn = kb.sb(st, "qn", [128, 4, 512], BF16)
            kn = kb.sb(st, "kn", [128, 4, 512], BF16)
            vn = kb.sb(st, "vn", [128, 4, 512], BF16)
            ktok = kb.sb(st, "ktok", [128, 512], BF16)
            vtok = kb.sb(st, "vtok", [128, 512], BF16)
            pT = kb.ps(st, "pT", [128, 4, 128], BF16)
            pb = [kb.ps(st, f"pb{i}", [128, 512], F32) for i in range(4)]
            nblk = (Lp + 511) // 512
            for b in range(nblk):
                p0 = 512 * b
                ntok = min(512, Lp - p0)
                N = slice(0, ntok)
                P = slice(p0, p0 + ntok)
                j0 = 1 if b == 0 else 0
                j1 = min(ntok + 3, Lp + 1 - p0)
                self.dma("sp", raw[:, :, j0:j1], self.QKVT[:, :, p0 + j0:p0 + j1].rearrange("c p t -> p c t"),
                         [self.QKVT], [raw], raw)
                if b == 0:
                    kb.op("dve", lambda e: e.memset(raw[:, :, 0:49], 0.0), writes=[raw])
                if b == nblk - 1:
                    kb.op("dve", lambda e: e.memset(raw[:, :, ntok + 1:ntok + 3], 0.0), writes=[raw])
                def stage1(ci):
                    acc = accs[ci % 4]
                    s_ = ss[ci % 4]
                    kb.op("dve", lambda e: e.tensor_scalar(acc[:, N], raw[:, ci, 0:ntok], cw[:, ci, 0:1], None, ALU.mult),
                          reads=[raw, cw], writes=[acc])
                    for k in range(1, 4):
                        kb.op("dve", lambda e, k=k: e.scalar_tensor_tensor(
                            acc[:, N], raw[:, ci, k:k + ntok], cw[:, ci, k:k + 1], acc[:, N], ALU.mult, ALU.add),
                            reads=[raw, cw, acc], writes=[acc])
                    kb.op("act", lambda e: e.activation(out=s_[:, N], in_=acc[:, N], func=AF.Silu), reads=[acc], writes=[s_])
                    if b == 0:
                        kb.op("pool", lambda e: e.memset(s_[:, 0:48], 0.0), writes=[s_])
                    if ci < 8:
                        sq_ = sqs[ci % 4]
                        kb.op("act", lambda e: e.activation(out=sq_[:, N], in_=s_[:, N], func=AF.Square), reads=[s_], writes=[sq_])
                        bk = pb[ci % 4]
                        kb.op("pe", lambda e: e.matmul(bk[:, N], blk[:, :], sq_[:, N], start=True, stop=True),
                              reads=[blk, sq_], writes=[bk])

                def stage1b(ci):
                    if ci < 8:
                        rt_, rn_ = rtmps[ci % 4], rns[ci % 4]
                        bk = pb[ci % 4]
                        kb.op("act", lambda e: e.activation(out=rt_[:, N], in_=bk[:, N], func=AF.Ln, bias=self.cst[:, 1:2], scale=1.0),
                              reads=[bk, self.cst], writes=[rt_])
                        kb.op("act", lambda e: e.activation(out=rn_[:, N], in_=rt_[:, N], func=AF.Exp, scale=-0.5), reads=[rt_], writes=[rn_])

                def stage2(ci):
                    s_ = ss[ci % 4]
                    if ci < 4:
                        rn_ = rns[ci % 4]
                        kb.op("dve", lambda e: e.scalar_tensor_tensor(qn[:, ci, N], s_[:, N], 0.125, rn_[:, N], ALU.mult, ALU.mult),
                              reads=[s_, rn_], writes=[qn])
                    elif ci < 8:
                        rn_ = rns[ci % 4]
                        kb.op("dve", lambda e: e.tensor_tensor(kn[:, ci - 4, N], s_[:, N], rn_[:, N], ALU.mult),
                              reads=[s_, rn_], writes=[kn])
                    else:
                        kb.op("act", lambda e: e.copy(vn[:, ci - 8, N], s_[:, N]), reads=[s_], writes=[vn])

                stage1(0)
                stage1(1)
                stage1b(0)
                for ci in range(12):
                    if ci + 2 < 12:
                        stage1(ci + 2)
                    if ci + 1 < 12:
                        stage1b(ci + 1)
                    stage2(ci)
                self.dma("sp", self.GQT[:, :, P].rearrange("c p t -> p c t"), qn[:, :, N], [qn], [self.GQT], qn)
                self.dma("sp", self.GKT[:, :, P].rearrange("c p t -> p c t"), kn[:, :, N], [kn], [self.GKT], kn)
                for tl in range((ntok + 127) // 128):
                    nt = min(128, ntok - tl * 128)
                    rows = slice(p0 + tl * 128, p0 + tl * 128 + nt)
                    for src, dstb, dram in ((kn, ktok, self.GKTOK), (vn, vtok, self.GVTOK)):
                        for j in range(4):
                            kb.op("pe", lambda e, j=j, src=src, tl=tl, nt=nt: e.transpose(
                                pT[0:nt, j, :], src[:, j, tl * 128:tl * 128 + nt], self.ident_b[:, :]),
                                reads=[src, self.ident_b], writes=[pT])
                        kb.op("act" if src is kn else "dve",
                              (lambda e, dstb=dstb, nt=nt: e.copy(dstb[0:nt, :], pT[0:nt, :, :])) if src is kn else
                              (lambda e, dstb=dstb, nt=nt: e.tensor_copy(dstb[0:nt, :], pT[0:nt, :, :])),
                              reads=[pT], writes=[dstb])
                        self.dma("sp", dram[rows, :], dstb[0:nt, :], [dstb], [dram], dstb)
        kb.barrier()

    def phase2b(self, si):
        kb = self.kb
        Lx = self.seq_lens[si]
        Lp = Lx + 64
        W = self.w
        with ExitStack() as st:
            dm_i = kb.sb(st, "dm_i", [128, 512], I32)
            dmask = kb.sb(st, "dmask", [128, 512], F32)
            dtmp = kb.sb(st, "dtmp", [128, 512], F32)
            kb.op("pool", lambda e: e.iota(dm_i[0:8, :], [[1, 512]], base=0, channel_multiplier=-64), writes=[dm_i])
            kb.op("dve", lambda e: e.tensor_copy(dtmp[0:8, :], dm_i[0:8, :]), reads=[dm_i], writes=[dtmp])
            kb.op("dve", lambda e: e.tensor_single_scalar(dmask[0:8, :], dtmp[0:8, :], 0.0, ALU.is_ge), reads=[dtmp], writes=[dmask])
            kb.op("dve", lambda e: e.tensor_single_scalar(dtmp[0:8, :], dtmp[0:8, :], 64.0, ALU.is_lt), reads=[dtmp], writes=[dtmp])
            kb.op("dve", lambda e: e.tensor_tensor(dmask[0:8, :], dmask[0:8, :], dtmp[0:8, :], ALU.mult), reads=[dmask, dtmp], writes=[dmask])
            csm = kb.sb(st, "csm", [128, 64], F32)
            kb.op("dve", lambda e: e.tensor_copy(csm[0:64, :], self.io_f[0:64, 0:64]), reads=[self.io_f], writes=[csm])
            kb.op("dve", lambda e: e.tensor_copy(csm[64:128, :], self.io_f[64:128, 64:128]), reads=[self.io_f], writes=[csm])
            identp = kb.sb(st, "identp", [128, 64], BF16)
            identpf = kb.sb(st, "identpf", [128, 64], F32)
            negoff = kb.sb(st, "negoff", [128, 64], F32)
            kb.op("dve", lambda e: e.tensor_single_scalar(identpf[:, :], csm[:, :], 0.0, ALU.is_equal), reads=[csm], writes=[identpf])
            kb.op("dve", lambda e: e.tensor_copy(identp[:, :], identpf[:, :]), reads=[identpf], writes=[identp])
            kb.op("dve", lambda e: e.tensor_scalar_add(negoff[:, :], identpf[:, :], -1.0), reads=[identpf], writes=[negoff])
            masks = []
            for x, cmpop in ((0, ALU.is_ge), (1, ALU.is_le)):
                mk = kb.sb(st, f"mask{x}", [128, 64], BF16)
                kb.op("dve", lambda e, cmpop=cmpop: e.tensor_single_scalar(dtmp[:, 0:64], csm[:, :], 0.0, cmpop), reads=[csm], writes=[dtmp])
                kb.op("dve", lambda e, mk=mk: e.tensor_scalar(mk[:, :], dtmp[:, 0:64], -NEG, NEG, ALU.mult, ALU.add), reads=[dtmp], writes=[mk])
                masks.append(mk)
            cmask = kb.sb(st, "cmask", [128, 512], F32)
            kb.op("dve", lambda e: e.memset(cmask[0:8, :], 1.0), writes=[cmask])
            kb.op("dve", lambda e: e.memset(cmask[0:8, :].rearrange("p (n c) -> p n c", c=64)[:, :, 0:1], 0.0), writes=[cmask])
            gnorm = self.load_bcast_vec(st, "gnorm", W["gdn_norm_g"], 64)
            pT = kb.ps(st, "pT", [128, 4, 128], BF16)
            pb = [kb.ps(st, f"pb{i}", [128, 512], F32) for i in range(7)]
            self._pbi = 0

            def bank():
                self._pbi = (self._pbi + 1) % 7
                return pb[self._pbi]

            def v3(ap):
                return ap.rearrange("p (h c) -> p h c", c=64)

            def blocks(out_bk, lhs, rhs, nr):
                for r in range(nr):
                    R_ = slice(64 * r, 64 * r + 64)
                    for h in range(H):
                        C_ = slice(64 * h, 64 * h + 64)
                        kb.op("pe", lambda e, R_=R_, C_=C_: e.matmul(out_bk[R_, C_], lhs[R_, C_], rhs[R_, C_], start=True, stop=True),
                              reads=[lhs, rhs], writes=[out_bk])

            nblk = (Lp + 511) // 512

            def pass_gen(x):
                sfx = f"_{x}"
                F = lambda n: kb.sb(st, n + sfx, [128, 512], F32)
                Bf = lambda n: kb.sb(st, n + sfx, [128, 512], BF16)
                g_t, b_t, cs, gc, ngc, egc, bg, kd = [F(n) for n in ("g_t", "b_t", "cs", "gc", "ngc", "egc", "bg", "kd")]
                tot = kb.sb(st, "tot" + sfx, [128, 8], F32)
                egl = kb.sb(st, "egl" + sfx, [128, 4, 8], F32)
                knT, qnT, kbT, qgT = [kb.sb(st, n + sfx, [128, 4, 512], BF16) for n in ("knT", "qnT", "kbT", "qgT")]
                ktok, vtok, vb, kbg, kdec, At, IY, nwT, vn = [Bf(n) for n in ("ktok", "vtok", "vb", "kbg", "kdec", "At", "IY", "nwT", "vn")]
                Xs = [Bf(f"X{i}") for i in range(2)]
                Ys = [Bf(f"Y{i}") for i in range(2)]
                Rs = [Bf(f"R{i}") for i in range(2)]
                gtk = kb.sb(st, "gtk" + sfx, [128, 32], F32)
                BD = kb.sb(st, "BD" + sfx, [128, 2, 512], F32)
                E, tt, u_sb, o_sb = [F(n) for n in ("E", "tt", "u_sb", "o_sb")]
                S = kb.sb(st, "S" + sfx, [128, 256], F32)
                Sb = kb.sb(st, "Sb" + sfx, [128, 256], BF16)
                for t_ in [ktok, vtok, vb, kbg, kdec, At, IY, vn] + Xs + Ys + Rs:
                    kb.op("pool", lambda e, t_=t_: e.memset(t_[:], 0.0), writes=[t_])
                kb.op("pool", lambda e: e.memset(gtk[:], 0.0), writes=[gtk])
                kb.op("pool", lambda e: e.memset(o_sb[:], 0.0), writes=[o_sb])
                kb.op("dve", lambda e: e.memset(S[:], 0.0), writes=[S])
                kb.op("dve", lambda e: e.memset(Sb[:], 0.0), writes=[Sb])
                odram = self.OF if x == 0 else self.OB
                border = range(nblk) if x == 0 else range(nblk - 1, -1, -1)
                for b in border:
                    p0 = 512 * b
                    ntok = min(512, Lp - p0)
                    nch = ntok // 64
                    N = slice(0, ntok)
                    P = slice(p0, p0 + ntok)
                    G = slice(0, 8)
                    self.dma("sp", g_t[G, N], self.GT[x, :, P], [self.GT], [g_t], g_t)
                    self.dma("sp", b_t[G, N], self.GT[2 + x, :, P], [self.GT], [b_t], b_t)
                    self.dma("sp", knT[:, :, N], self.GKT[:, :, P].rearrange("c p t -> p c t"), [self.GKT], [knT], knT)
                    self.dma("sp", qnT[:, :, N], self.GQT[:, :, P].rearrange("c p t -> p c t"), [self.GQT], [qnT], qnT)
                    kb.op("dve", lambda e: e.tensor_tensor_scan(cs[G, N], cmask[G, N], g_t[G, N], 0.0, ALU.mult, ALU.add),
                          reads=[cmask, g_t], writes=[cs])
                    cs3 = cs[G, N].rearrange("p (n c) -> p n c", c=64)
                    kb.op("dve", lambda e: e.tensor_copy(tot[G, 0:nch].unsqueeze(2), cs3[:, :, 63:64]), reads=[cs], writes=[tot])
                    totb = tot[G, 0:nch].unsqueeze(2).to_broadcast([8, nch, 64])
                    gc3 = gc[G, N].rearrange("p (n c) -> p n c", c=64)
                    if x == 0:
                        kb.op("dve", lambda e: e.tensor_copy(gc[G, N], cs[G, N]), reads=[cs], writes=[gc])
                    else:
                        kb.op("dve", lambda e: e.tensor_tensor(gc3, totb, cs3, ALU.subtract), reads=[tot, cs], writes=[gc])
                        kb.op("dve", lambda e: e.tensor_tensor(gc[G, N], gc[G, N], g_t[G, N], ALU.add), reads=[gc, g_t], writes=[gc])
                    kb.op("dve", lambda e: e.tensor_scalar_mul(ngc[G, N], gc[G, N], -1.0), reads=[gc], writes=[ngc])
                    kb.op("act", lambda e: e.activation(out=egc[G, N], in_=gc[G, N], func=AF.Exp), reads=[gc], writes=[egc])
                    kb.op("dve", lambda e: e.tensor_tensor(bg[G, N], b_t[G, N], egc[G, N], ALU.mult), reads=[b_t, egc], writes=[bg])
                    kd3 = kd[G, N].rearrange("p (n c) -> p n c", c=64)
                    kb.op("dve", lambda e: e.tensor_tensor(kd3, totb, gc3, ALU.subtract), reads=[tot, gc], writes=[kd])
                    kb.op("act", lambda e: e.activation(out=kd[G, N], in_=kd[G, N], func=AF.Exp), reads=[kd], writes=[kd])
                    bk = bank()
                    for j in range(4):
                        kb.op("pe", lambda e, j=j, bk=bk: e.matmul(bk[:, 8 * j:8 * j + nch], dmask[G, 128 * j:128 * j + 128], tot[G, 0:nch],
                                                                   start=True, stop=True), reads=[dmask, tot], writes=[bk])
                    kb.op("act", lambda e, bk=bk: e.activation(out=egl[:, :, 0:nch], in_=bk[:, 0:32].rearrange("p (j n) -> p j n", n=8)[:, :, 0:nch],
                                                               func=AF.Exp), reads=[bk], writes=[egl])
                    yield
                    for j in range(4):
                        bk = bank()
                        kb.op("pe", lambda e, j=j, bk=bk: e.matmul(bk[:, N], dmask[G, 128 * j:128 * j + 128], b_t[G, N], start=True, stop=True),
                              reads=[dmask, b_t], writes=[bk])
                        kb.op("dve", lambda e, j=j, bk=bk: e.tensor_tensor(kbT[:, j, N], knT[:, j, N], bk[:, N], ALU.mult),
                              reads=[knT, bk], writes=[kbT])
                        bk = bank()
                        kb.op("pe", lambda e, j=j, bk=bk: e.matmul(bk[:, N], dmask[G, 128 * j:128 * j + 128], egc[G, N], start=True, stop=True),
                              reads=[dmask, egc], writes=[bk])
                        kb.op("dve", lambda e, j=j, bk=bk: e.tensor_tensor(qgT[:, j, N], qnT[:, j, N], bk[:, N], ALU.mult),
                              reads=[qnT, bk], writes=[qgT])
                        yield
                    ntl = (ntok + 127) // 128
                    torder = range(ntl) if x == 0 else range(ntl - 1, -1, -1)
                    for tl in torder:
                        nt = min(128, ntok - tl * 128)
                        nr = nt // 64
                        TP = slice(0, nt)
                        TC = slice(tl * 128, tl * 128 + nt)
                        rows = slice(p0 + tl * 128, p0 + tl * 128 + nt)
                        self.dma("sp", ktok[TP, :], self.GKTOK[rows, :], [self.GKTOK], [ktok], ktok)
                        self.dma("sp", vtok[TP, :], self.GVTOK[rows, :], [self.GVTOK], [vtok], vtok)
                        bk = bank()
                        for qi, src in enumerate((b_t, bg, kd)):
                            kb.op("pe", lambda e, qi=qi, src=src, bk=bk: e.matmul(bk[TP, 8 * qi:8 * qi + 8], src[G, TC], self.ident_f[G, 0:8],
                                                                                 start=True, stop=True), reads=[src, self.ident_f], writes=[bk])
                        kb.op("act", lambda e, bk=bk: e.copy(gtk[TP, 0:24], bk[TP, 0:24]), reads=[bk], writes=[gtk])
                        for dst, src, c0, en in ((vb, vtok, 0, "pool"), (kbg, ktok, 8, "dve"), (kdec, ktok, 16, "pool")):
                            kb.op(en, lambda e, dst=dst, src=src, c0=c0: e.tensor_tensor(
                                v3(dst[TP, :]), v3(src[TP, :]), gtk[TP, c0:c0 + 8].unsqueeze(2).to_broadcast([nt, 8, 64]), ALU.mult),
                                reads=[src, gtk], writes=[dst])
                        yield
                        PA = bank()
                        PB = bank()
                        for cls in (0, 1):
                            for r in range(nr):
                                R_ = slice(64 * r, 64 * r + 64)
                                cc = slice(tl * 128 + 64 * r, tl * 128 + 64 * r + 64)
                                for h in range(H):
                                    j, m = h // 2, h % 2
                                    if (m == r) != (cls == 0):
                                        continue
                                    M_ = slice(64 * m, 64 * m + 64)
                                    C_ = slice(64 * h, 64 * h + 64)
                                    kb.op("pe", lambda e, R_=R_, C_=C_, M_=M_, j=j, cc=cc: e.matmul(PA[R_, C_], knT[M_, j, cc], kbT[M_, j, cc],
                                                                                                     start=True, stop=True),
                                          reads=[knT, kbT], writes=[PA])
                                    kb.op("pe", lambda e, R_=R_, C_=C_, M_=M_, j=j, cc=cc: e.matmul(PB[R_, C_], knT[M_, j, cc], qnT[M_, j, cc],
                                                                                                     start=True, stop=True),
                                          reads=[knT, qnT], writes=[PB])
                        PD = bank()
                        for r in range(nr):
                            cc = slice(tl * 128 + 64 * r, tl * 128 + 64 * r + 64)
                            kb.op("dve", lambda e, r=r, cc=cc: e.tensor_tensor(v3(BD[G, r, :]), v3(dmask[G, :]),
                                                                               gc[G, cc].unsqueeze(1).to_broadcast([8, 8, 64]), ALU.mult),
                                  reads=[dmask, gc], writes=[BD])
                            kb.op("pe", lambda e, r=r: e.matmul(PD[64 * r:64 * r + 64, :], self.ones_f[G, 0:64], BD[G, r, :], start=True, stop=False),
                                  reads=[self.ones_f, BD], writes=[PD])
                        kb.op("pe", lambda e: e.matmul(PD[TP, :], ngc[G, TC], dmask[G, :], start=False, stop=False),
                              reads=[ngc, dmask], writes=[PD])
                        mk = masks[x]
                        kb.op("pe", lambda e: e.matmul(v3(PD[TP, :]), self.ident_b[TP, TP], mk[TP, :].unsqueeze(1).to_broadcast([nt, 8, 64]),
                                                       start=False, stop=True), reads=[self.ident_b, mk], writes=[PD])
                        kb.op("act", lambda e: e.activation(out=E[TP, :], in_=PD[TP, :], func=AF.Exp), reads=[PD], writes=[E])
                        yield
                        kb.op("dve", lambda e: e.tensor_tensor(At[TP, :], PB[TP, :], E[TP, :], ALU.mult), reads=[PB, E], writes=[At])
                        kb.op("dve", lambda e: e.tensor_tensor(tt[TP, :], PA[TP, :], E[TP, :], ALU.mult), reads=[PA, E], writes=[tt])
                        X, Y, Rr = Xs[0], Ys[0], Rs[0]
                        kb.op("pool", lambda e: e.tensor_tensor(v3(X[TP, :]), v3(tt[TP, :]), negoff[TP, :].unsqueeze(1).to_broadcast([nt, 8, 64]),
                                                                ALU.mult), reads=[tt, negoff], writes=[X])
                        kb.op("pool", lambda e: e.tensor_tensor(v3(Rr[TP, :]), v3(X[TP, :]), identp[TP, :].unsqueeze(1).to_broadcast([nt, 8, 64]),
                                                                ALU.add), reads=[X, identp], writes=[Rr])
                        PY = bank()
                        for r in range(nr):
                            R_ = slice(64 * r, 64 * r + 64)
                            for h in range(H):
                                C_ = slice(64 * h, 64 * h + 64)
                                kb.op("pe", lambda e, R_=R_, C_=C_: e.matmul(PY[R_, C_], X[R_, C_], self.ident_b[R_, R_], start=True, stop=True),
                                      reads=[X, self.ident_b], writes=[PY])
                        kb.op("act", lambda e: e.copy(Y[TP, :], PY[TP, :]), reads=[PY], writes=[Y])
                        yield
                        for jj in range(5):
                            Xn, Yn, Rn = Xs[(jj + 1) % 2], Ys[(jj + 1) % 2], Rs[(jj + 1) % 2]
                            if jj < 4:
                                PX = bank()
                                blocks(PX, Y, X, nr)
                                kb.op("act", lambda e, PX=PX, Xn=Xn: e.copy(Xn[TP, :], PX[TP, :]), reads=[PX], writes=[Xn])
                            PY = bank()
                            blocks(PY, X, Y, nr)
                            kb.op("dve", lambda e, PY=PY: e.tensor_tensor(v3(IY[TP, :]), v3(PY[TP, :]),
                                                                          identpf[TP, :].unsqueeze(1).to_broadcast([nt, 8, 64]), ALU.add),
                                  reads=[PY, identpf], writes=[IY])
                            if jj < 4:
                                kb.op("dve", lambda e, PY=PY, Yn=Yn: e.tensor_copy(Yn[TP, :], PY[TP, :]), reads=[PY], writes=[Yn])
                            yield
                            PR = bank()
                            blocks(PR, IY, Rr, nr)
                            kb.op("act", lambda e, PR=PR, Rn=Rn: e.copy(Rn[TP, :], PR[TP, :]), reads=[PR], writes=[Rn])
                            X, Y, Rr = Xn, Yn, Rn
                            yield
                        Tt = Rr
                        PU = bank()
                        blocks(PU, Tt, vb, nr)
                        kb.op("act", lambda e, PU=PU: e.copy(u_sb[TP, :], PU[TP, :]), reads=[PU], writes=[u_sb])
                        PW = bank()
                        for cls in (0, 1):
                            for r in range(nr):
                                R_ = slice(64 * r, 64 * r + 64)
                                for h in range(H):
                                    j, m = h // 2, h % 2
                                    if (m == r) != (cls == 0):
                                        continue
                                    C_ = slice(64 * h, 64 * h + 64)
                                    kb.op("pe", lambda e, R_=R_, C_=C_, j=j, m=m, r=r: e.matmul(
                                        PW[64 * m:64 * m + 64, 128 * j + 64 * r:128 * j + 64 * r + 64], kbg[R_, C_], Tt[R_, C_], start=True, stop=True),
                                        reads=[kbg, Tt], writes=[PW])
                        if nr == 2:
                            kb.op("act", lambda e, PW=PW: e.mul(nwT[:, :], PW[:, :], -1.0), reads=[PW], writes=[nwT])
                        else:
                            kb.op("act", lambda e, PW=PW: e.mul(nwT[:, :].rearrange("p (j r c) -> p j r c", j=4, r=2)[:, :, 0, :],
                                                               PW[:, :].rearrange("p (j r c) -> p j r c", j=4, r=2)[:, :, 0, :], -1.0),
                                  reads=[PW], writes=[nwT])
                        yield
                        rorder = range(nr) if x == 0 else range(nr - 1, -1, -1)
                        for r in rorder:
                            R_ = slice(64 * r, 64 * r + 64)
                            nb = tl * 2 + r
                            cc = slice(tl * 128 + 64 * r, tl * 128 + 64 * r + 64)
                            PV = bank()
                            for mm_ in (0, 1):
                                for h in range(H):
                                    j, m = h // 2, h % 2
                                    if m != mm_:
                                        continue
                                    M_ = slice(64 * m, 64 * m + 64)
                                    kb.op("pe", lambda e, h=h, j=j, M_=M_, r=r, R_=R_, PV=PV: e.matmul(
                                        PV[R_, 64 * h:64 * h + 64], nwT[M_, 128 * j + 64 * r:128 * j + 64 * r + 64], Sb[M_, 64 * j:64 * j + 64],
                                        start=True, stop=True), reads=[nwT, Sb], writes=[PV])
                            kb.op("dve", lambda e, R_=R_, PV=PV: e.tensor_tensor(vn[R_, :], u_sb[R_, :], PV[R_, :], ALU.add),
                                  reads=[u_sb, PV], writes=[vn])
                            PO = bank()
                            for mm_ in (1 - r, r):
                                for h in range(H):
                                    j, m = h // 2, h % 2
                                    if m != mm_:
                                        continue
                                    M_ = slice(64 * m, 64 * m + 64)
                                    C_ = slice(64 * h, 64 * h + 64)
                                    kb.op("pe", lambda e, M_=M_, C_=C_, j=j, R_=R_, cc=cc, PO=PO: e.matmul(
                                        PO[R_, C_], qgT[M_, j, cc], Sb[M_, 64 * j:64 * j + 64], start=True, stop=True),
                                        reads=[qgT, Sb], writes=[PO])
                            yield
                            PO2 = bank()
                            for h in range(H):
                                C_ = slice(64 * h, 64 * h + 64)
                                kb.op("pe", lambda e, C_=C_, R_=R_, PO2=PO2: e.matmul(PO2[R_, C_], At[R_, C_], vn[R_, C_], start=True, stop=True),
                                      reads=[At, vn], writes=[PO2])
                            PS_ = bank()
                            for h in range(H):
                                j, m = h // 2, h % 2
                                C_ = slice(64 * h, 64 * h + 64)
                                kb.op("pe", lambda e, j=j, m=m, R_=R_, C_=C_, PS_=PS_: e.matmul(
                                    PS_[64 * m:64 * m + 64, 64 * j:64 * j + 64], kdec[R_, C_], vn[R_, C_], start=True, stop=True),
                                    reads=[kdec, vn], writes=[PS_])
                            kb.op("act", lambda e, R_=R_, PO=PO: e.copy(o_sb[R_, :], PO[R_, :]), reads=[PO], writes=[o_sb])
                            kb.op("dve", lambda e, R_=R_, PO2=PO2: e.tensor_tensor(o_sb[R_, :], o_sb[R_, :], PO2[R_, :], ALU.add),
                                  reads=[o_sb, PO2], writes=[o_sb])
                            kb.op("dve", lambda e, nb=nb: e.tensor_tensor(v3(S[:, :]), v3(S[:, :]),
                                                                          egl[:, :, nb:nb + 1].to_broadcast([128, 4, 64]), ALU.mult),
                                  reads=[S, egl], writes=[S])
                            kb.op("dve", lambda e, PS_=PS_: e.tensor_tensor(S[:, :], S[:, :], PS_[:, 0:256], ALU.add), reads=[S, PS_], writes=[S])
                            kb.op("act", lambda e: e.copy(Sb[:, :], S[:, :]), reads=[S], writes=[Sb])
                            yield
                        self.dma("sp", odram[rows, :], o_sb[TP, :], [o_sb], [odram], o_sb)

            gens = [pass_gen(0), pass_gen(1)]
            while gens:
                for g in list(gens):
                    try:
                        next(g)
                    except StopIteration:
                        gens.remove(g)

            ofs = [kb.sb(st, f"of_t{i}", [128, 512], F32) for i in range(2)]
            obs = [kb.sb(st, f"ob_t{i}", [128, 512], F32) for i in range(2)]
            zts = [kb.sb(st, f"z_t{i}", [128, 512], F32) for i in range(2)]
            osum = kb.sb(st, "osum", [128, 512], F32)
            osq = kb.sb(st, "osq", [128, 512], F32)
            ssum = kb.sb(st, "ssum", [128, 16], F32)
            gout = kb.sb(st, "gout", [128, 512], BF16)
            gTs = [kb.sb(st, f"gT{i}", [128, 4, 128], BF16) for i in range(2)]
            for t_ in ofs + obs + zts:
                kb.op("pool", lambda e, t_=t_: e.memset(t_[:], 0.0), writes=[t_])
            kb.op("pool", lambda e: e.memset(gout[:], 0.0), writes=[gout])
            ntp = (Lp + 127) // 128
            for t in range(ntp):
                nt = min(128, Lp - 128 * t)
                TP = slice(0, nt)
                rows = slice(128 * t, 128 * t + nt)
                of_t, ob_t, z_t, gT = ofs[t % 2], obs[t % 2], zts[t % 2], gTs[t % 2]
                self.dma("sp", of_t[TP, :], self.OF[rows, :], [self.OF], [of_t], of_t)
                self.dma("sp", ob_t[TP, :], self.OB[rows, :], [self.OB], [ob_t], ob_t)
                self.dma("sp", z_t[TP, :], self.Z[rows, :], [self.Z], [z_t], z_t)
                kb.op("pool", lambda e: e.tensor_tensor(osum[TP, :], of_t[TP, :], ob_t[TP, :], ALU.add), reads=[of_t, ob_t], writes=[osum])
                kb.op("act", lambda e: e.activation(out=osq[TP, :], in_=osum[TP, :], func=AF.Square), reads=[osum], writes=[osq])
                kb.op("dve", lambda e: e.tensor_reduce(ssum[TP, 0:8], v3(osq[TP, :]), AX.X, ALU.add), reads=[osq], writes=[ssum])
                kb.op("act", lambda e: e.activation(out=ssum[TP, 8:16], in_=ssum[TP, 0:8], func=AF.Ln, bias=self.cst[TP, 1:2],
                                                    scale=1.0 / 64.0), reads=[ssum, self.cst], writes=[ssum])
                kb.op("act", lambda e: e.activation(out=ssum[TP, 8:16], in_=ssum[TP, 8:16], func=AF.Exp, scale=-0.5),
                      reads=[ssum], writes=[ssum])
                kb.op("dve", lambda e: e.tensor_tensor(v3(osum[TP, :]), v3(osum[TP, :]),
                                                       ssum[TP, 8:16].unsqueeze(2).to_broadcast([nt, 8, 64]), ALU.mult),
                      reads=[osum, ssum], writes=[osum])
                kb.op("pool", lambda e: e.tensor_tensor(v3(osum[TP, :]), v3(osum[TP, :]),
                                                        gnorm[TP, :].unsqueeze(1).to_broadcast([nt, 8, 64]), ALU.mult),
                      reads=[osum, gnorm], writes=[osum])
                kb.op("dve", lambda e: e.tensor_tensor(gout[TP, :], osum[TP, :], z_t[TP, :], ALU.mult), reads=[osum, z_t], writes=[gout])
                for j in range(4):
                    kb.op("pe", lambda e, j=j: e.transpose(pT[:, j, 0:nt], gout[TP, 128 * j:128 * j + 128], self.ident_b[TP, TP]),
                          reads=[gout, self.ident_b], writes=[pT])
                kb.op("act", lambda e: e.copy(gT[:, :, 0:nt], pT[:, :, 0:nt]), reads=[pT], writes=[gT])
                c0 = 64 if t == 0 else 0
                x0 = 128 * t + c0 - 64
                if nt > c0:
                    self.dma("sp", self.CATT[4:8, :, x0:x0 + nt - c0].rearrange("c p t -> p c t"), gT[:, :, c0:nt],
                             [gT], [self.CATT], gT)
        kb.barrier()

    def phase3(self, si):
        kb = self.kb
        Lx = self.seq_lens[si]
        Lp = Lx + 64
        nkc = (Lp + 127) // 128
        with ExitStack() as st:
            kt = kb.sb(st, "kt", [128, 8, nkc * 128], BF16)
            va = kb.sb(st, "va", [128, nkc, 8, 128], BF16)
            kb.op("pool", lambda e: e.memset(va[:, :, :, 64:128], 1.0), writes=[va])
            kb.op("pool", lambda e: e.memset(va[:, :, :, 0:64], 0.0), writes=[va])
            self.dma("sp", kt[0:96, :, 0:Lp], self.KT[:, :, 0:Lp].rearrange("h r t -> r h t"), [self.KT], [kt], kt)
            for kc in range(nkc):
                nk = min(128, Lp - kc * 128)
                self.dma("sp", va[0:nk, kc, :, 0:64], self.V[kc * 128:kc * 128 + nk, :].rearrange("t (h e) -> t h e", h=8),
                         [self.V], [va], va)
            kb.op("pool", lambda e: e.memset(va[0:32, 0, :, :], 0.0), writes=[va])
            kb.op("pool", lambda e: e.memset(va[32:48, 0, :, :], 0.0), writes=[va])
            qts = [kb.sb(st, f"qt{i}", [128, 8, 512], BF16) for i in range(2)]
            pts = [kb.sb(st, f"pt{i}", [128, 512], BF16) for i in range(5)]
            rec = kb.sb(st, "rec", [128, 512], F32)
            mo = kb.sb(st, "mo", [128, 4, 512], BF16)
            pss = [kb.ps(st, f"pss{i}", [128, 512], F32) for i in range(5)]
            self._p3cnt = 0
            pos = [kb.ps(st, f"pos{i}", [128, 512], F32) for i in range(2)]
            nqb = Lx // 512 if Lx % 512 == 0 else (Lx + 511) // 512
            cnt = 0
            for qb in range(nqb):
                nq = min(512, Lx - qb * 512)
                Q = slice(0, nq)
                qt = qts[qb % 2]
                self.dma("sp", qt[0:96, :, Q], self.QT[:, :, qb * 512:qb * 512 + nq].rearrange("h r t -> r h t"),
                         [self.QT], [qt], qt)
                for h in range(H):
                    po = pos[h % 2]
                    LOOK = 3
                    slots = {}

                    def emit_s(kc, h=h):
                        nk = min(128, Lp - kc * 128)
                        ps_ = pss[self._p3cnt % 5]
                        pt = pts[self._p3cnt % 5]
                        self._p3cnt += 1
                        slots[kc] = (ps_, pt, nk)
                        kb.op("pe", lambda e: e.matmul(ps_[0:nk, Q], kt[0:96, h, kc * 128:kc * 128 + nk], qt[0:96, h, Q], start=True, stop=True),
                              reads=[kt, qt], writes=[ps_])

                    for kc in range(min(LOOK, nkc)):
                        emit_s(kc)
                    for kc in range(nkc):
                        ps_, pt, nk = slots.pop(kc)
                        kb.op("act", lambda e, ps_=ps_, pt=pt, nk=nk: e.activation(out=pt[0:nk, Q], in_=ps_[0:nk, Q], func=AF.Exp),
                              reads=[ps_], writes=[pt])
                        if kc + LOOK < nkc:
                            emit_s(kc + LOOK)
                        kb.op("pe", lambda e, po=po, pt=pt, kc=kc, nk=nk, h=h: e.matmul(
                            po[:, Q], va[0:nk, kc, h, :], pt[0:nk, Q], start=(kc == 0), stop=(kc == nkc - 1)),
                            reads=[va, pt], writes=[po])
                    kb.op("dve", lambda e, po=po: e.reciprocal(rec[0:64, Q], po[64:128, Q]), reads=[po], writes=[rec])
                    kb.op("dve", lambda e, po=po, h=h: e.tensor_tensor(mo[64 * (h % 2):64 * (h % 2) + 64, h // 2, Q], po[0:64, Q],
                                                                       rec[0:64, Q], ALU.mult), reads=[po, rec], writes=[mo])
                self.dma("sp", self.CATT[0:4, :, qb * 512:qb * 512 + nq].rearrange("c p t -> p c t"), mo[:, :, Q],
                         [mo], [self.CATT], mo)
        kb.barrier()

    def phase4a(self, si):
        kb = self.kb
        Lx = self.seq_lens[si]
        W = self.w
        with ExitStack() as st:
            w_out = kb.sb(st, "w_out", [128, 8, D], BF16)
            for kc in range(8):
                self.dma("sp", w_out[:, kc, :], self.Wb["w_out"][kc * 128:kc * 128 + 128, :], [self.Wb["w_out"]], [w_out], w_out)
            lng = self.load_bcast_vec(st, "lng", W["ln_in_g"], D)
            lnb = self.load_bcast_vec(st, "lnb", W["ln_in_b"], D)
            l1g = self.load_bcast_vec(st, "l1g", W["ln1_g"], D)
            l1b = self.load_bcast_vec(st, "l1b", W["ln1_b"], D)
            xts = [kb.sb(st, f"xt{i}", [128, D], F32) for i in range(3)]
            cats = [kb.sb(st, f"cat{i}", [128, 8, 128], BF16) for i in range(3)]
            hress = [kb.sb(st, f"hres{i}", [128, D], F32) for i in range(3)]
            r1s = [kb.sb(st, f"r1{i}", [128, D], F32) for i in range(3)]
            h1s = [kb.sb(st, f"h1{i}", [128, D], F32) for i in range(2)]
            h1bs = [kb.sb(st, f"h1b{i}", [128, D], BF16) for i in range(2)]
            h1ts = [kb.sb(st, f"h1t{i}", [128, 8, 128], BF16) for i in range(2)]
            statss = [kb.sb(st, f"stats{i}", [128, 32], F32) for i in range(5)]
            pT = kb.ps(st, "pT", [128, 8, 128], BF16)
            pb = [kb.ps(st, f"pb{i}", [128, 512], F32) for i in range(6)]
            ntile = Lx // 128

            def stage_a(k):
                xt, cat = xts[k % 3], cats[k % 3]
                rows = slice(128 * k, 128 * k + 128)
                self.dma("sp", xt[:, :], self.x_in[si][rows, :], [self.xin_reg], [xt], xt)
                self.dma("sp", cat[:, :, :], self.CATT[:, :, rows].rearrange("c p t -> p c t"), [self.CATT], [cat], cat)
                return self.layer_norm_gen(xt, xt[:, :], lng, lnb, hress[k % 3], hress[k % 3][:, :], xt, statss[k % 3])

            def stage_b_mm(k):
                cat = cats[k % 3]
                for nh in range(2):
                    bk = pb[(2 * k + nh) % 6]
                    F = slice(nh * 512, nh * 512 + 512)
                    for kc in range(8):
                        kb.op("pe", lambda e, kc=kc, bk=bk, F=F, cat=cat: e.matmul(bk[:, :], cat[:, kc, :], w_out[:, kc, F],
                                                                                   start=(kc == 0), stop=(kc == 7)),
                              reads=[cat, w_out], writes=[bk])

            def stage_b_r1(k):
                hres, r1 = hress[k % 3], r1s[k % 3]
                for nh in range(2):
                    bk = pb[(2 * k + nh) % 6]
                    F = slice(nh * 512, nh * 512 + 512)
                    kb.op("dve", lambda e, bk=bk, F=F: e.scalar_tensor_tensor(r1[:, F], hres[:, F], DN_ALPHA, bk[:, :],
                                                                             ALU.mult, ALU.add), reads=[hres, bk], writes=[r1])

            def stage_c_ln(k):
                r1, h1 = r1s[k % 3], h1s[k % 2]
                return self.layer_norm_gen(r1, r1[:, :], l1g, l1b, h1, h1[:, :], r1, statss[3 + k % 2])

            def stage_c_out(k):
                h1, h1b, h1t = h1s[k % 2], h1bs[k % 2], h1ts[k % 2]
                rows = slice(128 * k, 128 * k + 128)
                self.dma("sp", self.H1[rows, :], h1[:, :], [h1], [self.H1], h1)
                kb.op("act", lambda e: e.copy(h1b[:, :], h1[:, :]), reads=[h1], writes=[h1b])
                self.transpose_tile(h1b, lambda kk: h1b[:, kk * 128:(kk + 1) * 128], pT, h1t,
                                    lambda k0, k1: h1t[:, k0:k1, :])
                self.dma("sp", self.H1T[:, :, rows].rearrange("c p t -> p c t"), h1t[:, :, :], [h1t], [self.H1T], h1t)

            self.run_gens(stage_a(0), stage_a(1) if ntile > 1 else None, stage_a(2) if ntile > 2 else None)
            for k0 in range(min(2, ntile)):
                stage_b_mm(k0)
                stage_b_r1(k0)
            for k in range(ntile):
                if k + 2 < ntile:
                    stage_b_mm(k + 2)
                ga = stage_a(k + 3) if k + 3 < ntile else None
                self.run_gens(ga, stage_c_ln(k))
                if k + 2 < ntile:
                    stage_b_r1(k + 2)
                stage_c_out(k)
        kb.barrier()

    def phase4b(self, si):
        kb = self.kb
        Lx = self.seq_lens[si]
        W = self.w
        with ExitStack() as st:
            w1 = kb.sb(st, "w_ff1", [128, 8, DFF], BF16)
            w2 = kb.sb(st, "w_ff2", [128, 32, D], BF16)
            for kc in range(8):
                self.dma("sp", w1[:, kc, :], self.Wb["w_ff1"][kc * 128:kc * 128 + 128, :], [self.Wb["w_ff1"]], [w1], w1)
            for kc in range(32):
                self.dma("sp", w2[:, kc, :], self.Wb["w_ff2"][kc * 128:kc * 128 + 128, :], [self.Wb["w_ff2"]], [w2], w2)
            l2g = self.load_bcast_vec(st, "l2g", W["ln2_g"], D)
            l2b = self.load_bcast_vec(st, "l2b", W["ln2_b"], D)
            aT = kb.sb(st, "aT", [128, 32, 512], BF16)
            h1T = kb.sb(st, "h1T", [128, 8, 512], BF16)
            rls = [kb.sb(st, f"rl{i}", [128, 512], BF16) for i in range(2)]
            h1 = kb.sb(st, "h1", [128, D], F32)
            r2 = kb.sb(st, "r2", [128, D], F32)
            yo = kb.sb(st, "yo", [128, D], F32)
            lntmp = kb.sb(st, "lntmp", [128, D], F32)
            stats = kb.sb(st, "stats", [128, 32], F32)
            pb = [kb.ps(st, f"pb{i}", [128, 512], F32) for i in range(6)]
            bi = 0
            for b in range((Lx + 511) // 512):
                n = min(512, Lx - 512 * b)
                N = slice(0, n)
                cols = slice(512 * b, 512 * b + n)
                self.dma("sp", h1T[:, :, N], self.H1T[:, :, cols].rearrange("c p t -> p c t"), [self.H1T], [h1T], h1T)
                for mc in range(32):
                    bk = pb[bi % 6]
                    bi += 1
                    rl = rls[mc % 2]
                    for kc in range(8):
                        kb.op("pe", lambda e, kc=kc, mc=mc, bk=bk: e.matmul(bk[:, N], w1[:, kc, mc * 128:mc * 128 + 128], h1T[:, kc, N],
                                                                            start=(kc == 0), stop=(kc == 7)),
                              reads=[w1, h1T], writes=[bk])
                    kb.op("act", lambda e, bk=bk, rl=rl: e.activation(out=rl[:, N], in_=bk[:, N], func=AF.Relu), reads=[bk], writes=[rl])
                    kb.op("pool", lambda e, rl=rl, mc=mc: e.tensor_tensor(aT[:, mc, N], rl[:, N], rl[:, N], ALU.mult),
                          reads=[rl], writes=[aT])
                for tl in range(n // 128):
                    rows = slice(512 * b + 128 * tl, 512 * b + 128 * tl + 128)
                    self.dma("sp", h1[:, :], self.H1[rows, :], [self.H1], [h1], h1)
                    for nh in range(2):
                        bk = pb[bi % 6]
                        bi += 1
                        F = slice(nh * 512, nh * 512 + 512)
                        for mc in range(32):
                            kb.op("pe", lambda e, mc=mc, bk=bk, F=F, tl=tl: e.matmul(bk[:, :], aT[:, mc, tl * 128:tl * 128 + 128],
                                                                                     w2[:, mc, F], start=(mc == 0), stop=(mc == 31)),
                                  reads=[aT, w2], writes=[bk])
                        kb.op("dve", lambda e, bk=bk, F=F: e.scalar_tensor_tensor(r2[:, F], h1[:, F], DN_ALPHA, bk[:, :],
                                                                                 ALU.mult, ALU.add), reads=[h1, bk], writes=[r2])
                    self.layer_norm(r2, r2[:, :], l2g, l2b, yo, yo[:, :], r2, stats)
                    self.dma("sp", self.y_out[si][rows, :], yo[:, :], [yo], [self.yout_reg], yo)
        kb.barrier()


WEIGHT_NAMES = ["meta_tokens", "ln_in_g", "ln_in_b", "w_in", "g_cq", "g_ckv", "w_uq", "w_uk", "w_uv", "conv_w",
                "a_log_f", "a_log_b", "dt_bias_f", "dt_bias_b", "gdn_norm_g", "w_out", "ln1_g", "ln1_b",
                "w_ff1", "w_ff2", "ln2_g", "ln2_b"]


def build_prog(seq_lens, debug=False, phases=None):
    p = Prog(seq_lens, debug=debug)
    p.declare()
    p.setup_consts()
    for si in range(len(seq_lens)):
        for name in ["phase1", "phase2a", "phase2b", "phase3", "phase4a", "phase4b"]:
            if phases is not None and name not in phases:
                continue
            if not hasattr(p, name):
                continue
            getattr(p, name)(si)
    p.finish()
    return p


SEQ_LENS = [2048, 2048, 2048, 2048, 4096]
_PROG_CACHE = {}


def kernel(**inputs):
    n = 8
    x_prompt = np.asarray(inputs["x_prompt"], dtype=np.float32)
    x_sample = np.asarray(inputs["x_sample"], dtype=np.float32)
    wmap = {}
    for k in WEIGHT_NAMES:
        a = np.asarray(inputs[k], dtype=np.float32)
        if k not in ("meta_tokens", "ln_in_g", "ln_in_b"):
            a = a[0]
        wmap[k] = np.ascontiguousarray(a)
    prog = build_prog(SEQ_LENS, debug=False)
    in_maps = []
    for c in range(n):
        m = dict(wmap)
        for i in range(4):
            m[f"x{i}"] = np.ascontiguousarray(x_sample[4 * c + i])
        m["x4"] = np.ascontiguousarray(x_prompt[c // 2])
        in_maps.append(m)
    res = run_bass_kernel_spmd(prog.nc, in_maps, core_ids=list(range(n)))
    y_sample = np.empty_like(x_sample)
    y_prompt = np.empty_like(x_prompt)
    for c in range(n):
        r = res.results[c]
        for i in range(4):
            y_sample[4 * c + i] = np.asarray(r[f"y{i}"], dtype=np.float32)
        if c % 2 == 0:
            y_prompt[c // 2] = np.asarray(r["y4"], dtype=np.float32)
    return (y_prompt, y_sample)
```

```python
import numpy as np
from contextlib import ExitStack
import concourse.bass as bass
import concourse.mybir as mybir
from concourse.bass_utils import run_bass_kernel_spmd

F32 = mybir.dt.float32
BF16 = mybir.dt.bfloat16
I32 = mybir.dt.int32
AF = mybir.ActivationFunctionType
ALU = mybir.AluOpType
AX = mybir.AxisListType

D = 1024
NIN = 2752
H = 8
DFF = 4096
C_Q0, C_KV0, C_KR0, C_QKV0, C_Z0, C_G0 = 0, 384, 640, 672, 2208, 2720
DN_ALPHA = 2.0 ** 0.25
LN_EPS = 1e-5
RMS_EPS = 1e-6
QSCALE = 96.0 ** -0.5
NEG = -30000.0


class Reg:
    __slots__ = ("name", "w", "r", "sem", "cnt")

    def __init__(self, name):
        self.name = name
        self.w = {}
        self.r = {}
        self.sem = None
        self.cnt = 0


class Buf:
    def __init__(self, t, reg):
        self.t = t
        self.reg = reg

    def __getitem__(self, k):
        return self.t[k]


class _PEProxy:
    def __init__(self, kb):
        self.kb = kb
        self.e = kb.engs["pe"]

    def matmul(self, out, lhsT, rhs, **kw):
        self.kb._pe_pos(lhsT, out)
        return self.e.matmul(out, lhsT, rhs, **kw)

    def transpose(self, out, in_, ident):
        self.kb._pe_pos(in_, out)
        return self.e.transpose(out, in_, ident)


class KB:
    def __init__(self, nc, es):
        self.nc = nc
        self.es = es
        self.engs = {"pe": nc.tensor, "act": nc.scalar, "dve": nc.vector, "pool": nc.gpsimd, "sp": nc.sync}
        self.sem = {}
        self.cnt = {}
        self.waited = {}
        self.semname = {}
        for n in self.engs:
            self.sem[n] = es.enter_context(nc.semaphore("e_" + n))
            self.cnt[n] = 0
            self.waited[n] = {}
        self.nreg = 0
        self.all_dma_regs = []
        self.dpool = []
        self.dfree = {"sw": [], "hw": []}
        self.dkind = {}
        self.ninst = 0
        self.pe_live = set()
        self.nfence = 0
        self.pend = {n: False for n in self.engs}
        self.last_pe_w = None
        self.last_ins = {n: None for n in self.engs}
        self.pe_proxy = _PEProxy(self)

    def reg(self, name):
        self.nreg += 1
        return Reg(f"{name}_{self.nreg}")

    def sb(self, stack, name, shape, dt):
        self.nreg += 1
        nm = f"{name}_{self.nreg}"
        t = stack.enter_context(self.nc.sbuf_tensor(nm, list(shape), dt))
        return Buf(t, Reg(nm))

    def ps(self, stack, name, shape, dt):
        self.nreg += 1
        nm = f"{name}_{self.nreg}"
        t = stack.enter_context(self.nc.psum_tensor(nm, list(shape), dt))
        return Buf(t, Reg(nm))

    def _regs(self, xs):
        out = []
        for x in xs:
            if x is None:
                continue
            out.append(x.reg if isinstance(x, Buf) else x)
        return out

    def _flush(self, en):
        if self.pend[en]:
            self.last_ins[en].then_inc(self.sem[en], 1)
            self.cnt[en] += 1
            self.pend[en] = False

    def _wait(self, en, key, sem, val):
        w = self.waited[en]
        if w.get(key, 0) >= val:
            return
        if key in self.engs and val > self.cnt[key]:
            assert self.pend[key] and val == self.cnt[key] + 1
            self._flush(key)
        self.engs[en].wait_ge(sem, val)
        w[key] = val

    def op(self, en, fn, reads=(), writes=(), dma=None):
        reads = self._regs(reads)
        writes = self._regs(writes)
        deps = {}
        for R in reads:
            for k, v in R.w.items():
                if deps.get(k, (None, 0))[1] < v[1]:
                    deps[k] = v
        for R in writes:
            for dd in (R.w, R.r):
                for k, v in dd.items():
                    if deps.get(k, (None, 0))[1] < v[1]:
                        deps[k] = v
        for k, (sem, val) in deps.items():
            if k == "pe" and en == "pe" and dma is None:
                continue
            self._wait(en, k, sem, val)
        if dma is None and en == "pe":
            wkey = tuple(id(R) for R in writes)
            if self.pend["pe"] and wkey != self.last_pe_w:
                self._flush("pe")
            self.last_pe_w = wkey
        ins = fn(self.pe_proxy if en == "pe" else self.engs[en])
        self.ninst += 1
        if dma is not None:
            R = dma.reg if isinstance(dma, Buf) else dma
            if R.sem is None:
                kind = "sw" if en == "pool" else "hw"
                if self.dfree[kind]:
                    R.sem = self.dfree[kind].pop()
                else:
                    R.sem = len(self.dpool)
                    self.dpool.append([self.es.enter_context(self.nc.semaphore(f"ds{R.sem}")), 0])
                    self.dkind[R.sem] = kind
                self.all_dma_regs.append(R)
            assert self.dkind[R.sem] == ("sw" if en == "pool" else "hw"), "mixed DMA queues on one region semaphore"
            ent = self.dpool[R.sem]
            ent[1] += 16
            ins.then_inc(ent[0], 16)
            key, tok = f"ds{R.sem}", (ent[0], ent[1])
        else:
            if en == "pe":
                self.last_ins[en] = ins
                self.pend[en] = True
                key, tok = en, (self.sem[en], self.cnt[en] + 1)
            else:
                self.cnt[en] += 1
                ins.then_inc(self.sem[en], 1)
                key, tok = en, (self.sem[en], self.cnt[en])
        for R in reads:
            R.r[key] = tok
        for R in writes:
            R.w[key] = tok
        return tok

    def _pe_pos(self, kap, oap):
        k0, kn = kap.base_partition(), kap.partition_size()
        kq = 32 if kn <= 32 else (64 if kn <= 64 else 128)
        m0, mn = oap.base_partition(), oap.partition_size()
        bank = oap.name
        key = (k0, kq, m0, mn, bank)
        if key in self.pe_live:
            return
        conflict = False
        for (a0, aq, b0, bn, bk) in self.pe_live:
            if bk == bank and (a0, aq) != (k0, kq) and not (m0 + mn <= b0 or b0 + bn <= m0):
                conflict = True
                break
        if conflict:
            self._flush("pe")
            if self.cnt["pe"] > 0:
                self.engs["pe"].wait_ge(self.sem["pe"], self.cnt["pe"])
                self.waited["pe"]["pe"] = self.cnt["pe"]
            self.pe_live = set()
            self.nfence += 1
        self.pe_live.add(key)

    def barrier(self):
        for en in self.engs:
            self._flush(en)
        for en in self.engs:
            for o in self.engs:
                if o != en and self.cnt[o] > 0:
                    self._wait(en, o, self.sem[o], self.cnt[o])
            for i, ent in enumerate(self.dpool):
                if ent[1] > 0:
                    self._wait(en, f"ds{i}", ent[0], ent[1])
        for R in self.all_dma_regs:
            self.dfree[self.dkind[R.sem]].append(R.sem)
            R.sem = None
        self.all_dma_regs = []


def bc(ap, shape):
    return ap.to_broadcast(list(shape))


class _Stop(Exception):
    pass


class Prog:
    stop_at = None

    def _ck(self, n):
        return self.stop_at is not None and n == self.stop_at

    def __init__(self, seq_lens, debug=False):
        self.seq_lens = list(seq_lens)
        self.debug = debug
        self.nc = bass.Bass("TRN2", target_bir_lowering=False)
        self.es = ExitStack()
        self.kb = KB(self.nc, self.es)
        self.maxLx = max(self.seq_lens)
        self.maxLp = self.maxLx + 64

    def dram_in(self, name, shape, dt=F32):
        return self.nc.dram_tensor(name, list(shape), dt, kind="ExternalInput").ap()

    def dram_out(self, name, shape, dt=F32):
        return self.nc.dram_tensor(name, list(shape), dt, kind="ExternalOutput").ap()

    def dram_scr(self, name, shape, dt):
        kind = "ExternalOutput" if self.debug else "Internal"
        ap = self.nc.dram_tensor(name, list(shape), dt, kind=kind).ap()
        return Buf(ap, self.kb.reg(name))

    def declare(self):
        nseq = len(self.seq_lens)
        self.x_in = [self.dram_in(f"x{i}", [L, D]) for i, L in enumerate(self.seq_lens)]
        self.y_out = [self.dram_out(f"y{i}", [L, D]) for i, L in enumerate(self.seq_lens)]
        self.xin_reg = self.kb.reg("xin")
        self.yout_reg = self.kb.reg("yout")
        self.w = {}
        for name, shape in [
            ("meta_tokens", [16, D]), ("ln_in_g", [D]), ("ln_in_b", [D]), ("w_in", [D, NIN]),
            ("g_cq", [384]), ("g_ckv", [256]), ("w_uq", [384, 768]), ("w_uk", [256, 512]), ("w_uv", [256, 512]),
            ("conv_w", [4, 1536]), ("a_log_f", [8]), ("a_log_b", [8]), ("dt_bias_f", [8]), ("dt_bias_b", [8]),
            ("gdn_norm_g", [64]), ("w_out", [D, D]), ("ln1_g", [D]), ("ln1_b", [D]),
            ("w_ff1", [D, DFF]), ("w_ff2", [DFF, D]), ("ln2_g", [D]), ("ln2_b", [D]),
        ]:
            self.w[name] = self.dram_in(name, shape)
        self.w_reg = self.kb.reg("weights")
        self.Wb = {}
        for name, shape in [("w_in", [D, NIN]), ("w_uq", [384, 768]), ("w_uk", [256, 512]), ("w_uv", [256, 512]),
                            ("w_out", [D, D]), ("w_ff1", [D, DFF]), ("w_ff2", [DFF, D])]:
            self.Wb[name] = self.dram_scr("wb_" + name, shape, BF16)
        Lp, Lx = self.maxLp, self.maxLx
        self.QT = self.dram_scr("s_qt", [H, 96, Lx], BF16)
        self.KT = self.dram_scr("s_kt", [H, 96, Lp], BF16)
        self.V = self.dram_scr("s_v", [Lp, 512], BF16)
        self.QKVT = self.dram_scr("s_qkvt", [12, 128, Lp + 4], F32)
        self.Z = self.dram_scr("s_z", [Lp, 512], F32)
        self.GT = self.dram_scr("s_gt", [4, 8, Lp], F32)
        self.GQT = self.dram_scr("s_gqt", [4, 128, Lp], BF16)
        self.GKT = self.dram_scr("s_gkt", [4, 128, Lp], BF16)
        self.GKTOK = self.dram_scr("s_gktok", [Lp, 512], BF16)
        self.GVTOK = self.dram_scr("s_gvtok", [Lp, 512], BF16)
        self.OF = self.dram_scr("s_of", [Lp, 512], F32)
        self.OB = self.dram_scr("s_ob", [Lp, 512], F32)
        self.CATT = self.dram_scr("s_catt", [8, 128, Lx], BF16)
        self.H1 = self.dram_scr("s_h1", [Lx, D], F32)
        self.H1T = self.dram_scr("s_h1t", [8, 128, Lx], BF16)

    STORES_ON_POOL = True

    def dma(self, en, out_ap, in_ap, reads, writes, sem):
        if self.STORES_ON_POOL and en == "sp" and isinstance(sem, Buf) and any((w is not sem) for w in writes) \
                and all(isinstance(r, Buf) and r is sem for r in reads):
            en = "pool"
        return self.kb.op(en, lambda e: e.dma_start(out=out_ap, in_=in_ap), reads=reads, writes=writes, dma=sem)

    def rsqrt(self, out_buf, out_ap, in_buf, in_ap, scale, eps_ap, tmp_buf, tmp_ap):
        kb = self.kb
        kb.op("act", lambda e: e.activation(out=tmp_ap, in_=in_ap, func=AF.Ln, bias=eps_ap, scale=scale),
              reads=[in_buf, self.cst], writes=[tmp_buf])
        kb.op("act", lambda e: e.activation(out=out_ap, in_=tmp_ap, func=AF.Exp, scale=-0.5),
              reads=[tmp_buf], writes=[out_buf])

    def setup_consts(self):
        kb, nc = self.kb, self.nc
        es = self.es
        self.cst = kb.sb(es, "cst", [128, 16], F32)
        kb.op("dve", lambda e: e.memset(self.cst[:, 0:1], LN_EPS), writes=[self.cst])
        kb.op("dve", lambda e: e.memset(self.cst[:, 1:2], RMS_EPS), writes=[self.cst])
        kb.op("dve", lambda e: e.memset(self.cst[:, 2:3], 1.0), writes=[self.cst])
        kb.op("dve", lambda e: e.memset(self.cst[:, 3:4], 0.0), writes=[self.cst])
        self.io_i = kb.sb(es, "io_i", [128, 128], I32)
        self.io_f = kb.sb(es, "io_f", [128, 128], F32)
        kb.op("pool", lambda e: e.iota(self.io_i[:], [[1, 128]], base=0, channel_multiplier=-1), writes=[self.io_i])
        kb.op("dve", lambda e: e.tensor_copy(self.io_f[:], self.io_i[:]), reads=[self.io_i], writes=[self.io_f])
        self.ident_f = kb.sb(es, "ident_f", [128, 128], F32)
        self.ident_b = kb.sb(es, "ident_b", [128, 128], BF16)
        kb.op("dve", lambda e: e.tensor_single_scalar(self.ident_f[:], self.io_f[:], 0.0, ALU.is_equal),
              reads=[self.io_f], writes=[self.ident_f])
        kb.op("dve", lambda e: e.tensor_copy(self.ident_b[:], self.ident_f[:]), reads=[self.ident_f], writes=[self.ident_b])
        self.ones_f = kb.sb(es, "ones_f", [128, 128], F32)
        kb.op("dve", lambda e: e.memset(self.ones_f[:], 1.0), writes=[self.ones_f])
        for name, wb in self.Wb.items():
            src = self.w[name]
            nrow = src.shape[0]
            step = 256
            for r0 in range(0, nrow, step):
                r1 = min(nrow, r0 + step)
                self.dma("pool", wb[r0:r1, :], src[r0:r1, :], [self.w_reg], [wb], wb)
        kb.barrier()


    def finish(self):
        self.kb.barrier()
        self.es.close()

    def load_bcast_vec(self, st, name, vec_ap, n):
        b = self.kb.sb(st, name, [128, n], F32)
        self.dma("sp", b[:, :], vec_ap.partition_broadcast(128), reads=[self.w_reg], writes=[b], sem=b)
        return b

    def load_x_tile(self, si, t, xt):
        kb = self.kb
        Lx = self.seq_lens[si]
        x = self.x_in[si]
        if t == 0:
            kb.op("dve", lambda e: e.memset(xt[0:48, :], 0.0), writes=[xt])
            self.dma("sp", xt[48:64, :], self.w["meta_tokens"][:, :], reads=[self.w_reg], writes=[xt], sem=xt)
            self.dma("sp", xt[64:128, :], x[0:64, :], reads=[self.xin_reg], writes=[xt], sem=xt)
            return 128
        r0 = 128 * t - 64
        nt = min(128, Lx - r0)
        self.dma("sp", xt[0:nt, :], x[r0:r0 + nt, :], reads=[self.xin_reg], writes=[xt], sem=xt)
        return nt

    def layer_norm_gen(self, xin, xin_ap, g_bc, b_bc, out_buf, out_ap, tmp, stats):
        kb = self.kb
        for c in range(2):
            kb.op("dve", lambda e, c=c: e.bn_stats(stats[:, 6 * c:6 * c + 6], xin_ap[:, 512 * c:512 * c + 512]),
                  reads=[xin], writes=[stats])
        yield
        kb.op("dve", lambda e: e.bn_aggr(stats[:, 16:18], stats[:, 0:12].rearrange("p (a b) -> p a b", a=2)),
              reads=[stats], writes=[stats])
        yield
        kb.op("act", lambda e: e.activation(out=stats[:, 20:21], in_=stats[:, 17:18], func=AF.Ln,
                                            bias=self.cst[:, 0:1], scale=1.0), reads=[stats, self.cst], writes=[stats])
        yield
        kb.op("act", lambda e: e.activation(out=stats[:, 21:22], in_=stats[:, 20:21], func=AF.Exp, scale=-0.5),
              reads=[stats], writes=[stats])
        yield
        kb.op("dve", lambda e: e.scalar_tensor_tensor(stats[:, 22:23], stats[:, 16:17], -1.0, stats[:, 21:22], ALU.mult, ALU.mult),
              reads=[stats], writes=[stats])
        yield
        kb.op("act", lambda e: e.activation(out=tmp[:, :], in_=xin_ap, func=AF.Identity, bias=stats[:, 22:23], scale=stats[:, 21:22]),
              reads=[xin, stats], writes=[tmp])
        yield
        kb.op("dve", lambda e: e.tensor_tensor(tmp[:, :], tmp[:, :], g_bc[:, :], ALU.mult),
              reads=[tmp, g_bc], writes=[tmp])
        yield
        kb.op("dve", lambda e: e.tensor_tensor(out_ap, tmp[:, :], b_bc[:, :], ALU.add),
              reads=[tmp, b_bc], writes=[out_buf])

    @staticmethod
    def run_gens(*gens):
        gens = [g for g in gens if g is not None]
        while gens:
            for g in list(gens):
                try:
                    next(g)
                except StopIteration:
                    gens.remove(g)

    def layer_norm(self, *a):
        self.run_gens(self.layer_norm_gen(*a))

    def transpose_tile(self, src, src_ap_fn, pT, dst, dst_ap_fn, nk=8):
        kb = self.kb
        for k in range(nk):
            kb.op("pe", lambda e, k=k: e.transpose(pT[:, k, :], src_ap_fn(k), self.ident_b[:, :]),
                  reads=[src, self.ident_b], writes=[pT])
        h = nk // 2
        kb.op("act", lambda e: e.copy(dst_ap_fn(0, h), pT[:, 0:h, :]), reads=[pT], writes=[dst])
        kb.op("dve", lambda e: e.tensor_copy(dst_ap_fn(h, nk), pT[:, h:nk, :]), reads=[pT], writes=[dst])

    def phase1(self, si):
        kb, nc = self.kb, self.nc
        Lx = self.seq_lens[si]
        Lp = Lx + 64
        W = self.w
        with ExitStack() as st:
            w_in = kb.sb(st, "w_in", [128, 8, NIN], BF16)
            wqA = kb.sb(st, "wqA", [128, 3, 8, 96], BF16)
            wqB = kb.sb(st, "wqB", [128, 3, 8, 96], BF16)
            wkrA = kb.sb(st, "wkrA", [128, 8, 96], BF16)
            wkrB = kb.sb(st, "wkrB", [128, 8, 96], BF16)
            w_uk = kb.sb(st, "w_uk", [128, 2, 512], BF16)
            w_uv = kb.sb(st, "w_uv", [128, 2, 512], BF16)
            kb.op("dve", lambda e: e.memset(wqB[:], 0.0), writes=[wqB])
            kb.op("dve", lambda e: e.memset(wkrA[:], 0.0), writes=[wkrA])
            kb.op("dve", lambda e: e.memset(wkrB[:], 0.0), writes=[wkrB])
            for kc in range(8):
                rows = slice(kc * 128, kc * 128 + 128)
                self.dma("sp", w_in[:, kc, :], self.Wb["w_in"][rows, :], [self.Wb["w_in"]], [w_in], w_in)
                self.dma("sp", wkrA[:, kc, 64:96], self.Wb["w_in"][rows, C_KR0:C_KR0 + 32], [self.Wb["w_in"]], [wkrA], wkrA)
                self.dma("sp", wkrB[:, kc, 64:80], self.Wb["w_in"][rows, C_KR0 + 16:C_KR0 + 32], [self.Wb["w_in"]], [wkrB], wkrB)
                self.dma("sp", wkrB[:, kc, 80:96], self.Wb["w_in"][rows, C_KR0:C_KR0 + 16], [self.Wb["w_in"]], [wkrB], wkrB)
            for kc in range(3):
                rows = slice(kc * 128, kc * 128 + 128)
                wq3 = self.Wb["w_uq"][rows, :].rearrange("p (h c) -> p h c", c=96)
                self.dma("sp", wqA[:, kc, :, :], wq3, [self.Wb["w_uq"]], [wqA], wqA)
                self.dma("sp", wqB[:, kc, :, 64:80], wq3[:, :, 80:96], [self.Wb["w_uq"]], [wqB], wqB)
                self.dma("sp", wqB[:, kc, :, 80:96], wq3[:, :, 64:80], [self.Wb["w_uq"]], [wqB], wqB)
            for kc in range(2):
                rows = slice(kc * 128, kc * 128 + 128)
                self.dma("sp", w_uk[:, kc, :], self.Wb["w_uk"][rows, :], [self.Wb["w_uk"]], [w_uk], w_uk)
                self.dma("sp", w_uv[:, kc, :], self.Wb["w_uv"][rows, :], [self.Wb["w_uv"]], [w_uv], w_uv)
            lng = self.load_bcast_vec(st, "lng", W["ln_in_g"], D)
            lnb = self.load_bcast_vec(st, "lnb", W["ln_in_b"], D)
            gcq = kb.sb(st, "gcq", [128, 3], F32)
            gckv = kb.sb(st, "gckv", [128, 2], F32)
            for kc in range(3):
                self.dma("sp", gcq[:, kc:kc + 1], W["g_cq"][kc * 128:kc * 128 + 128].rearrange("(p o) -> p o", o=1),
                         [self.w_reg], [gcq], gcq)
            for kc in range(2):
                self.dma("sp", gckv[:, kc:kc + 1], W["g_ckv"][kc * 128:kc * 128 + 128].rearrange("(p o) -> p o", o=1),
                         [self.w_reg], [gckv], gckv)
            gpar = kb.sb(st, "gpar", [128, 4], F32)
            kb.op("dve", lambda e: e.memset(gpar[:], 0.0), writes=[gpar])
            for off, sfx in ((0, "f"), (32, "b")):
                self.dma("sp", gpar[off:off + 8, 0:1], W["dt_bias_" + sfx].rearrange("(p o) -> p o", o=1),
                         [self.w_reg], [gpar], gpar)
                self.dma("sp", gpar[off:off + 8, 1:2], W["a_log_" + sfx].rearrange("(p o) -> p o", o=1),
                         [self.w_reg], [gpar], gpar)
            kb.op("act", lambda e: e.activation(out=gpar[0:40, 2:3], in_=gpar[0:40, 1:2], func=AF.Exp),
                  reads=[gpar], writes=[gpar])
            kb.op("dve", lambda e: e.tensor_scalar_mul(gpar[0:40, 2:3], gpar[0:40, 2:3], -1.0), reads=[gpar], writes=[gpar])
            ropeC = kb.sb(st, "ropeC", [128, 512], F32)
            ropeS = kb.sb(st, "ropeS", [128, 512], F32)
            rp = kb.sb(st, "rp", [128, 8], F32)
            rpi = kb.sb(st, "rpi", [128, 2], I32)
            R = slice(64, 96)
            kb.op("pool", lambda e: e.iota(rpi[R, 0:1], [[0, 1]], base=0, channel_multiplier=1), writes=[rpi])
            kb.op("dve", lambda e: e.tensor_copy(rp[R, 0:1], rpi[R, 0:1]), reads=[rpi], writes=[rp])
            kb.op("dve", lambda e: e.tensor_single_scalar(rp[R, 1:2], rp[R, 0:1], 16.0, ALU.is_ge), reads=[rp], writes=[rp])
            kb.op("dve", lambda e: e.scalar_tensor_tensor(rp[R, 2:3], rp[R, 1:2], -16.0, rp[R, 0:1], ALU.mult, ALU.add),
                  reads=[rp], writes=[rp])
            kb.op("act", lambda e: e.activation(out=rp[R, 3:4], in_=rp[R, 2:3], func=AF.Exp,
                                                scale=-float(np.log(10000.0)) / 16.0), reads=[rp], writes=[rp])
            kb.op("dve", lambda e: e.tensor_scalar_mul(rp[R, 3:4], rp[R, 3:4], float(1.0 / (2.0 * np.pi))),
                  reads=[rp], writes=[rp])
            kb.op("dve", lambda e: e.tensor_scalar(rp[R, 4:5], rp[R, 1:2], float(4.0 * np.pi), float(-2.0 * np.pi),
                                                   ALU.mult, ALU.add), reads=[rp], writes=[rp])
            kb.op("dve", lambda e: e.memset(rp[R, 5:6], float(2.0 * np.pi)), writes=[rp])
            posi = kb.sb(st, "posi", [128, 512], I32)
            posf = kb.sb(st, "posf", [128, 512], F32)
            ru = kb.sb(st, "ru", [128, 512], F32)
            rui = kb.sb(st, "rui", [128, 512], I32)
            ruf = kb.sb(st, "ruf", [128, 512], F32)

            def rope_tables(c):
                kb.op("pool", lambda e: e.iota(posi[R, :], [[1, 512]], base=512 * c - 48, channel_multiplier=0),
                      writes=[posi])
                kb.op("dve", lambda e: e.tensor_copy(posf[R, :], posi[R, :]), reads=[posi], writes=[posf])
                for tab, off, sc in ((ropeC, 0.25, 5), (ropeS, 0.0, 4)):
                    kb.op("dve", lambda e, off=off: e.tensor_scalar(ru[R, :], posf[R, :], rp[R, 3:4], off, ALU.mult, ALU.add),
                          reads=[posf, rp], writes=[ru])
                    kb.op("dve", lambda e: e.tensor_copy(rui[R, :], ru[R, :]), reads=[ru], writes=[rui])
                    kb.op("dve", lambda e: e.tensor_copy(ruf[R, :], rui[R, :]), reads=[rui], writes=[ruf])
                    kb.op("dve", lambda e: e.tensor_tensor(ru[R, :], ru[R, :], ruf[R, :], ALU.subtract),
                          reads=[ru, ruf], writes=[ru])
                    kb.op("act", lambda e, tab=tab, sc=sc: e.activation(
                        out=tab[R, :], in_=ru[R, :], func=AF.Sin, scale=rp[R, sc:sc + 1]),
                        reads=[ru, rp], writes=[tab])
            xts = [kb.sb(st, f"xt{i}", [128, D], F32) for i in range(4)]
            for xt in xts:
                kb.op("pool", lambda e, xt=xt: e.memset(xt[:], 0.0), writes=[xt])
            statss = [kb.sb(st, f"stats{i}", [128, 32], F32) for i in range(4)]
            hbs = [kb.sb(st, f"hb{i}", [128, D], BF16) for i in range(4)]
            hTs = [kb.sb(st, f"hT{i}", [128, 8, 512], BF16) for i in range(2)]
            stage = kb.sb(st, "stage", [128, 12, 512], F32)
            cq = kb.sb(st, "cq", [128, 3, 512], F32)
            sq = kb.sb(st, "sq", [128, 3, 512], F32)
            rstd = kb.sb(st, "rstd", [128, 512], F32)
            rtmp = kb.sb(st, "rtmp", [128, 512], F32)
            cqn = kb.sb(st, "cqn", [128, 3, 512], BF16)
            ckvn = kb.sb(st, "ckvn", [128, 2, 512], BF16)
            qs = kb.sb(st, "qs", [128, 8, 512], BF16)
            ks = kb.sb(st, "ks", [128, 8, 512], BF16)
            t1 = kb.sb(st, "t1", [128, 512], F32)
            t2 = kb.sb(st, "t2", [128, 512], F32)
            vss = [kb.sb(st, f"vs{i}", [128, 512], BF16) for i in range(2)]
            zss = [kb.sb(st, f"zs{i}", [128, 512], F32) for i in range(2)]
            zes = [kb.sb(st, f"ze{i}", [128, 512], F32) for i in range(2)]
            gs = kb.sb(st, "gs", [128, 512], F32)
            ga = kb.sb(st, "ga", [128, 512], F32)
            gb = kb.sb(st, "gb", [128, 512], F32)
            lnq = kb.sb(st, "lnq", [128, 1], F32)
            kb.op("dve", lambda e: e.memset(lnq[:], float(np.log(QSCALE))), writes=[lnq])
            kb.op("dve", lambda e: e.memset(gs[:], 0.0), writes=[gs])
            gs2 = kb.sb(st, "gs2", [128, 512], F32)
            kb.op("dve", lambda e: e.memset(gs2[:], 0.0), writes=[gs2])
            pT = kb.ps(st, "pT", [128, 8, 128], BF16)
            pb = [kb.ps(st, f"pb{i}", [128, 512], F32) for i in range(7)]
            self._pbi = 0

            def bank():
                self._pbi = (self._pbi + 1) % 7
                return pb[self._pbi]

            nblk = (Lp + 511) // 512

            def ln_part(b):
                p0_ = 512 * b
                ntok_ = min(512, Lp - p0_)
                ntl_ = (ntok_ + 127) // 128
                gens = []
                for tl in range(ntl_):
                    xt = xts[tl]
                    self.load_x_tile(si, 4 * b + tl, xt)
                    gens.append(self.layer_norm_gen(xt, xt[:, :], lng, lnb, hbs[tl], hbs[tl][:, :], xt, statss[tl]))
                self.run_gens(*gens)

            def tr_part(b):
                p0_ = 512 * b
                ntok_ = min(512, Lp - p0_)
                hT_ = hTs[b % 2]
                for tl in range((ntok_ + 127) // 128):
                    hb = hbs[tl]
                    self.transpose_tile(hb, lambda k, hb=hb: hb[:, k * 128:(k + 1) * 128], pT, hT_,
                                        lambda k0, k1, tl=tl: hT_[:, k0:k1, tl * 128:(tl + 1) * 128])

            ln_part(0)
            tr_part(0)
            for b in range(nblk):
                p0 = 512 * b
                ntok = min(512, Lp - p0)
                ntl = (ntok + 127) // 128
                hT = hTs[b % 2]
                if b + 1 < nblk:
                    ln_part(b + 1)
                N = slice(0, ntok)

                def proj(col0, m, out_ap_fn=None):
                    bk = bank()
                    o = bk[0:m, N] if out_ap_fn is None else out_ap_fn(bk)
                    for kc in range(8):
                        kb.op("pe", lambda e, kc=kc: e.matmul(o, w_in[:, kc, col0:col0 + m], hT[:, kc, N],
                                                               start=(kc == 0), stop=(kc == 7)),
                              reads=[w_in, hT], writes=[bk])
                    return bk

                for mc in range(12):
                    bk = proj(C_QKV0 + mc * 128, 128)
                    if mc % 2 == 0:
                        kb.op("act", lambda e, bk=bk, mc=mc: e.copy(stage[:, mc, N], bk[:, N]), reads=[bk], writes=[stage])
                    else:
                        kb.op("dve", lambda e, bk=bk, mc=mc: e.tensor_copy(stage[:, mc, N], bk[:, N]), reads=[bk], writes=[stage])
                self.dma("sp", self.QKVT[:, :, 1 + p0:1 + p0 + ntok].rearrange("c p t -> p c t"), stage[:, :, N],
                         [stage], [self.QKVT], stage)

                def rms_fm(col0, nch, gpp, outn, lnbias):
                    banks = [proj(col0 + mc * 128, 128) for mc in range(nch)]
                    for mc, bk in enumerate(banks):
                        kb.op("act", lambda e, bk=bk, mc=mc: e.copy(cq[:, mc, N], bk[:, N]), reads=[bk], writes=[cq])
                        kb.op("act", lambda e, bk=bk, mc=mc: e.activation(out=sq[:, mc, N], in_=bk[:, N], func=AF.Square),
                              reads=[bk], writes=[sq])
                    bs = bank()
                    for mc in range(nch):
                        kb.op("pe", lambda e, mc=mc: e.matmul(bs[:, N], self.ones_f[:, :], sq[:, mc, N],
                                                               start=(mc == 0), stop=(mc == nch - 1)),
                              reads=[self.ones_f, sq], writes=[bs])
                    kb.op("act", lambda e: e.activation(out=rtmp[:, N], in_=bs[:, N], func=AF.Ln, bias=self.cst[:, 1:2],
                                                        scale=1.0 / (nch * 128)), reads=[bs, self.cst], writes=[rtmp])
                    if lnbias is None:
                        kb.op("act", lambda e: e.activation(out=rstd[:, N], in_=rtmp[:, N], func=AF.Exp, scale=-0.5),
                              reads=[rtmp], writes=[rstd])
                    else:
                        kb.op("act", lambda e: e.activation(out=rstd[:, N], in_=rtmp[:, N], func=AF.Exp, scale=-0.5,
                                                            bias=lnbias[:, 0:1]), reads=[rtmp, lnbias], writes=[rstd])
                    for mc in range(nch):
                        kb.op("dve", lambda e, mc=mc: e.scalar_tensor_tensor(outn[:, mc, N], cq[:, mc, N], gpp[:, mc:mc + 1],
                                                                             rstd[:, N], ALU.mult, ALU.mult),
                              reads=[cq, gpp, rstd], writes=[outn])

                rms_fm(C_KV0, 2, gckv, ckvn, None)
                for h in range(H):
                    bk = bank()
                    for kc in range(2):
                        kb.op("pe", lambda e, kc=kc, h=h, bk=bk: e.matmul(bk[0:64, N], w_uk[:, kc, h * 64:(h + 1) * 64],
                                                                          ckvn[:, kc, N], start=(kc == 0), stop=(kc == 1)),
                              reads=[w_uk, ckvn], writes=[bk])
                    if h % 2 == 0:
                        kb.op("act", lambda e, bk=bk, h=h: e.copy(ks[0:64, h, N], bk[0:64, N]), reads=[bk], writes=[ks])
                    else:
                        kb.op("dve", lambda e, bk=bk, h=h: e.tensor_copy(ks[0:64, h, N], bk[0:64, N]), reads=[bk], writes=[ks])
                RR = slice(64, 96)
                P = slice(p0, p0 + ntok)
                rope_tables(b)

                def rope_pair(bkA, bkB, dst_fn):
                    kb.op("dve", lambda e: e.tensor_tensor(t1[RR, N], bkA[RR, N], ropeC[RR, N], ALU.mult),
                          reads=[bkA, ropeC], writes=[t1])
                    kb.op("dve", lambda e: e.tensor_tensor(t2[RR, N], bkB[RR, N], ropeS[RR, N], ALU.mult),
                          reads=[bkB, ropeS], writes=[t2])
                    dst_fn()

                bkA = bank()
                bkB = bank()
                for kc in range(8):
                    kb.op("pe", lambda e, kc=kc: e.matmul(bkA[0:96, N], wkrA[:, kc, :], hT[:, kc, N], start=(kc == 0), stop=(kc == 7)),
                          reads=[wkrA, hT], writes=[bkA])
                for kc in range(8):
                    kb.op("pe", lambda e, kc=kc: e.matmul(bkB[0:96, N], wkrB[:, kc, :], hT[:, kc, N], start=(kc == 0), stop=(kc == 7)),
                          reads=[wkrB, hT], writes=[bkB])

                def kdst():
                    kb.op("dve", lambda e: e.tensor_tensor(t1[RR, N], t1[RR, N], t2[RR, N], ALU.add), reads=[t1, t2], writes=[t1])
                    kb.op("act", lambda e: e.copy(ks[RR, 0:4, N], t1[RR, N].unsqueeze(1).to_broadcast([32, 4, ntok])),
                          reads=[t1], writes=[ks])
                    kb.op("dve", lambda e: e.tensor_copy(ks[RR, 4:8, N], t1[RR, N].unsqueeze(1).to_broadcast([32, 4, ntok])),
                          reads=[t1], writes=[ks])
                rope_pair(bkA, bkB, kdst)
                self.dma("sp", self.KT[:, :, P].rearrange("h r t -> r h t"), ks[0:96, :, N], [ks], [self.KT], ks)
                for tl in range(ntl):
                    nt = min(128, ntok - tl * 128)
                    bk = bank()
                    vs = vss[tl % 2]
                    for kc in range(2):
                        kb.op("pe", lambda e, kc=kc, tl=tl, nt=nt, bk=bk: e.matmul(
                            bk[0:nt, :], ckvn[:, kc, tl * 128:tl * 128 + nt], w_uv[:, kc, :], start=(kc == 0), stop=(kc == 1)),
                            reads=[ckvn, w_uv], writes=[bk])
                    kb.op("act", lambda e, bk=bk, nt=nt: e.copy(vs[0:nt, :], bk[0:nt, :]), reads=[bk], writes=[vs])
                    self.dma("sp", self.V[p0 + tl * 128:p0 + tl * 128 + nt, :], vs[0:nt, :], [vs], [self.V], vs)

                c0 = 64 if b == 0 else 0
                if ntok > c0:
                    rms_fm(C_Q0, 3, gcq, cqn, lnq)
                    for h in range(H):
                        bkA = bank()
                        bkB = bank()
                        for kc in range(3):
                            kb.op("pe", lambda e, kc=kc, h=h, bkA=bkA: e.matmul(bkA[0:96, N], wqA[:, kc, h, :], cqn[:, kc, N],
                                                                                 start=(kc == 0), stop=(kc == 2)),
                                  reads=[wqA, cqn], writes=[bkA])
                        for kc in range(3):
                            kb.op("pe", lambda e, kc=kc, h=h, bkB=bkB: e.matmul(bkB[0:96, N], wqB[:, kc, h, :], cqn[:, kc, N],
                                                                                 start=(kc == 0), stop=(kc == 2)),
                                  reads=[wqB, cqn], writes=[bkB])
                        kb.op("act", lambda e, bkA=bkA, h=h: e.copy(qs[0:64, h, N], bkA[0:64, N]), reads=[bkA], writes=[qs])

                        def qdst(h=h):
                            kb.op("dve", lambda e: e.tensor_tensor(qs[RR, h, N], t1[RR, N], t2[RR, N], ALU.add),
                                  reads=[t1, t2], writes=[qs])
                        rope_pair(bkA, bkB, qdst)
                    self.dma("sp", self.QT[:, :, p0 + c0 - 64:p0 + ntok - 64].rearrange("h r t -> r h t"),
                             qs[0:96, :, c0:ntok], [qs], [self.QT], qs)

                for tl in range(ntl):
                    nt = min(128, ntok - tl * 128)
                    bk = bank()
                    zs, ze = zss[tl % 2], zes[tl % 2]
                    for kc in range(8):
                        kb.op("pe", lambda e, kc=kc, tl=tl, nt=nt, bk=bk: e.matmul(
                            bk[0:nt, :], hT[:, kc, tl * 128:tl * 128 + nt], w_in[:, kc, C_Z0:C_Z0 + 512],
                            start=(kc == 0), stop=(kc == 7)), reads=[hT, w_in], writes=[bk])
                    kb.op("act", lambda e, bk=bk, nt=nt: e.activation(out=ze[0:nt, :], in_=bk[0:nt, :], func=AF.Exp, scale=-1.0),
                          reads=[bk], writes=[ze])
                    kb.op("act", lambda e, nt=nt: e.activation(out=ze[0:nt, :], in_=ze[0:nt, :], func=AF.Ln, bias=self.cst[0:nt, 2:3], scale=1.0),
                          reads=[ze, self.cst], writes=[ze])
                    kb.op("act", lambda e, nt=nt: e.activation(out=ze[0:nt, :], in_=ze[0:nt, :], func=AF.Exp, scale=-1.0), reads=[ze], writes=[ze])
                    kb.op("dve", lambda e, bk=bk, nt=nt: e.tensor_tensor(zs[0:nt, :], bk[0:nt, :], ze[0:nt, :], ALU.mult),
                          reads=[bk, ze], writes=[zs])
                    self.dma("sp", self.Z[p0 + tl * 128:p0 + tl * 128 + nt, :], zs[0:nt, :], [zs], [self.Z], zs)

                bk = bank()
                bk2 = bank()
                for g in range(4):
                    bb = bk if g < 2 else bk2
                    ro = 32 * (g % 2)
                    for kc in range(8):
                        kb.op("pe", lambda e, kc=kc, g=g, bb=bb, ro=ro: e.matmul(
                            bb[ro:ro + 8, N], w_in[:, kc, C_G0 + 8 * g:C_G0 + 8 * g + 8], hT[:, kc, N],
                            start=(kc == 0), stop=(kc == 7)), reads=[w_in, hT], writes=[bb])
                A = slice(0, 40)
                kb.op("dve", lambda e: e.tensor_scalar(ga[A, N], bk[A, N], gpar[A, 0:1], None, ALU.add), reads=[bk, gpar], writes=[ga])
                kb.op("act", lambda e: e.activation(out=gb[A, N], in_=ga[A, N], func=AF.Abs), reads=[ga], writes=[gb])
                kb.op("act", lambda e: e.activation(out=gb[A, N], in_=gb[A, N], func=AF.Exp, scale=-1.0), reads=[gb], writes=[gb])
                kb.op("act", lambda e: e.activation(out=gb[A, N], in_=gb[A, N], func=AF.Ln, bias=self.cst[A, 2:3], scale=1.0),
                      reads=[gb, self.cst], writes=[gb])
                kb.op("dve", lambda e: e.scalar_tensor_tensor(ga[A, N], ga[A, N], 0.0, gb[A, N], ALU.max, ALU.add),
                      reads=[ga, gb], writes=[ga])
                kb.op("dve", lambda e: e.tensor_scalar(gs[A, N], ga[A, N], gpar[A, 2:3], None, ALU.mult), reads=[ga, gpar], writes=[gs])
                kb.op("act", lambda e: e.activation(out=gb[A, N], in_=bk2[A, N], func=AF.Exp, scale=-1.0), reads=[bk2], writes=[gb])
                kb.op("act", lambda e: e.activation(out=gb[A, N], in_=gb[A, N], func=AF.Ln, bias=self.cst[A, 2:3], scale=1.0),
                      reads=[gb, self.cst], writes=[gb])
                kb.op("act", lambda e: e.activation(out=gs2[A, N], in_=gb[A, N], func=AF.Exp, scale=-1.0), reads=[gb], writes=[gs2])
                if b == 0:
                    kb.op("dve", lambda e: e.memset(gs[A, 0:48], 0.0), writes=[gs])
                    kb.op("dve", lambda e: e.memset(gs2[A, 0:48], 0.0), writes=[gs2])
                for g in range(4):
                    src = gs if g < 2 else gs2
                    ro = 32 * (g % 2)
                    self.dma("sp", self.GT[g, :, P], src[ro:ro + 8, N], [src], [self.GT], src)
                if b + 1 < nblk:
                    tr_part(b + 1)
        kb.barrier()

    def phase2a(self, si):
        kb = self.kb
        Lx = self.seq_lens[si]
        Lp = Lx + 64
        W = self.w
        with ExitStack() as st:
            cw = kb.sb(st, "cw", [128, 12, 4], F32)
            for k in range(4):
                self.kb.op("sp", lambda e, k=k: e.dma_start(out=cw[:, :, k:k + 1],
                                                            in_=W["conv_w"][k, :].rearrange("(c p o) -> p c o", p=128, o=1),
                                                            allow_slow_non_contiguous=True),
                           reads=[self.w_reg], writes=[cw], dma=cw)
            blk = kb.sb(st, "blk", [128, 128], F32)
            kb.op("dve", lambda e: e.memset(blk[:], 0.0), writes=[blk])
            kb.op("dve", lambda e: e.memset(blk[0:64, 0:64], 1.0), writes=[blk])
            kb.op("dve", lambda e: e.memset(blk[64:128, 64:128], 1.0), writes=[blk])
            raw = kb.sb(st, "raw", [128, 12, 516], F32)
            accs = [kb.sb(st, f"acc{i}", [128, 512], F32) for i in range(4)]
            ss = [kb.sb(st, f"s{i}", [128, 512], F32) for i in range(4)]
            sqs = [kb.sb(st, f"sq{i}", [128, 512], F32) for i in range(4)]
            rns = [kb.sb(st, f"rn{i}", [128, 512], F32) for i in range(4)]
            rtmps = [kb.sb(st, f"rtmp{i}", [128, 512], F32) for i in range(4)]
            qn = kb.sb(st, "qn", [128, 4, 512], BF16)
            kn = kb.sb(st, "kn", [128, 4, 512], BF16)
            vn = kb.sb(st, "vn", [128, 4, 512], BF16)
            ktok = kb.sb(st, "ktok", [128, 512], BF16)
            vtok = kb.sb(st, "vtok", [128, 512], BF16)
            pT = kb.ps(st, "pT", [128, 4, 128], BF16)
            pb = [kb.ps(st, f"pb{i}", [128, 512], F32) for i in range(4)]
            nblk = (Lp + 511) // 512
            for b in range(nblk):
                p0 = 512 * b
                ntok = min(512, Lp - p0)
                N = slice(0, ntok)
                P = slice(p0, p0 + ntok)
                j0 = 1 if b == 0 else 0
                j1 = min(ntok + 3, Lp + 1 - p0)
                self.dma("sp", raw[:, :, j0:j1], self.QKVT[:, :, p0 + j0:p0 + j1].rearrange("c p t -> p c t"),
                         [self.QKVT], [raw], raw)
                if b == 0:
                    kb.op("dve", lambda e: e.memset(raw[:, :, 0:49], 0.0), writes=[raw])
                if b == nblk - 1:
                    kb.op("dve", lambda e: e.memset(raw[:, :, ntok + 1:ntok + 3], 0.0), writes=[raw])
                def stage1(ci):
                    acc = accs[ci % 4]
                    s_ = ss[ci % 4]
                    kb.op("dve", lambda e: e.tensor_scalar(acc[:, N], raw[:, ci, 0:ntok], cw[:, ci, 0:1], None, ALU.mult),
                          reads=[raw, cw], writes=[acc])
                    for k in range(1, 4):
                        kb.op("dve", lambda e, k=k: e.scalar_tensor_tensor(
                            acc[:, N], raw[:, ci, k:k + ntok], cw[:, ci, k:k + 1], acc[:, N], ALU.mult, ALU.add),
                            reads=[raw, cw, acc], writes=[acc])
                    kb.op("act", lambda e: e.activation(out=s_[:, N], in_=acc[:, N], func=AF.Silu), reads=[acc], writes=[s_])
                    if b == 0:
                        kb.op("pool", lambda e: e.memset(s_[:, 0:48], 0.0), writes=[s_])
                    if ci < 8:
                        sq_ = sqs[ci % 4]
                        kb.op("act", lambda e: e.activation(out=sq_[:, N], in_=s_[:, N], func=AF.Square), reads=[s_], writes=[sq_])
                        bk = pb[ci % 4]
                        kb.op("pe", lambda e: e.matmul(bk[:, N], blk[:, :], sq_[:, N], start=True, stop=True),
                              reads=[blk, sq_], writes=[bk])

                def stage1b(ci):
                    if ci < 8:
                        rt_, rn_ = rtmps[ci % 4], rns[ci % 4]
                        bk = pb[ci % 4]
                        kb.op("act", lambda e: e.activation(out=rt_[:, N], in_=bk[:, N], func=AF.Ln, bias=self.cst[:, 1:2], scale=1.0),
                              reads=[bk, self.cst], writes=[rt_])
                        kb.op("act", lambda e: e.activation(out=rn_[:, N], in_=rt_[:, N], func=AF.Exp, scale=-0.5), reads=[rt_], writes=[rn_])

                def stage2(ci):
                    s_ = ss[ci % 4]
                    if ci < 4:
                        rn_ = rns[ci % 4]
                        kb.op("dve", lambda e: e.scalar_tensor_tensor(qn[:, ci, N], s_[:, N], 0.125, rn_[:, N], ALU.mult, ALU.mult),
                              reads=[s_, rn_], writes=[qn])
                    elif ci < 8:
                        rn_ = rns[ci % 4]
                        kb.op("dve", lambda e: e.tensor_tensor(kn[:, ci - 4, N], s_[:, N], rn_[:, N], ALU.mult),
                              reads=[s_, rn_], writes=[kn])
                    else:
                        kb.op("act", lambda e: e.copy(vn[:, ci - 8, N], s_[:, N]), reads=[s_], writes=[vn])

                stage1(0)
                stage1(1)
                stage1b(0)
                for ci in range(12):
                    if ci + 2 < 12:
                        stage1(ci + 2)
                    if ci + 1 < 12:
                        stage1b(ci + 1)
                    stage2(ci)
                self.dma("sp", self.GQT[:, :, P].rearrange("c p t -> p c t"), qn[:, :, N], [qn], [self.GQT], qn)
                self.dma("sp", self.GKT[:, :, P].rearrange("c p t -> p c t"), kn[:, :, N], [kn], [self.GKT], kn)
                for tl in range((ntok + 127) // 128):
                    nt = min(128, ntok - tl * 128)
                    rows = slice(p0 + tl * 128, p0 + tl * 128 + nt)
                    for src, dstb, dram in ((kn, ktok, self.GKTOK), (vn, vtok, self.GVTOK)):
                        for j in range(4):
                            kb.op("pe", lambda e, j=j, src=src, tl=tl, nt=nt: e.transpose(
                                pT[0:nt, j, :], src[:, j, tl * 128:tl * 128 + nt], self.ident_b[:, :]),
                                reads=[src, self.ident_b], writes=[pT])
                        kb.op("act" if src is kn else "dve",
                              (lambda e, dstb=dstb, nt=nt: e.copy(dstb[0:nt, :], pT[0:nt, :, :])) if src is kn else
                              (lambda e, dstb=dstb, nt=nt: e.tensor_copy(dstb[0:nt, :], pT[0:nt, :, :])),
                              reads=[pT], writes=[dstb])
                        self.dma("sp", dram[rows, :], dstb[0:nt, :], [dstb], [dram], dstb)
        kb.barrier()

    def phase2b(self, si):
        kb = self.kb
        Lx = self.seq_lens[si]
        Lp = Lx + 64
        W = self.w
        with ExitStack() as st:
            dm_i = kb.sb(st, "dm_i", [128, 512], I32)
            dmask = kb.sb(st, "dmask", [128, 512], F32)
            dtmp = kb.sb(st, "dtmp", [128, 512], F32)
            kb.op("pool", lambda e: e.iota(dm_i[0:8, :], [[1, 512]], base=0, channel_multiplier=-64), writes=[dm_i])
            kb.op("dve", lambda e: e.tensor_copy(dtmp[0:8, :], dm_i[0:8, :]), reads=[dm_i], writes=[dtmp])
            kb.op("dve", lambda e: e.tensor_single_scalar(dmask[0:8, :], dtmp[0:8, :], 0.0, ALU.is_ge), reads=[dtmp], writes=[dmask])
            kb.op("dve", lambda e: e.tensor_single_scalar(dtmp[0:8, :], dtmp[0:8, :], 64.0, ALU.is_lt), reads=[dtmp], writes=[dtmp])
            kb.op("dve", lambda e: e.tensor_tensor(dmask[0:8, :], dmask[0:8, :], dtmp[0:8, :], ALU.mult), reads=[dmask, dtmp], writes=[dmask])
            csm = kb.sb(st, "csm", [128, 64], F32)
            kb.op("dve", lambda e: e.tensor_copy(csm[0:64, :], self.io_f[0:64, 0:64]), reads=[self.io_f], writes=[csm])
            kb.op("dve", lambda e: e.tensor_copy(csm[64:128, :], self.io_f[64:128, 64:128]), reads=[self.io_f], writes=[csm])
            identp = kb.sb(st, "identp", [128, 64], BF16)
            identpf = kb.sb(st, "identpf", [128, 64], F32)
            negoff = kb.sb(st, "negoff", [128, 64], F32)
            kb.op("dve", lambda e: e.tensor_single_scalar(identpf[:, :], csm[:, :], 0.0, ALU.is_equal), reads=[csm], writes=[identpf])
            kb.op("dve", lambda e: e.tensor_copy(identp[:, :], identpf[:, :]), reads=[identpf], writes=[identp])
            kb.op("dve", lambda e: e.tensor_scalar_add(negoff[:, :], identpf[:, :], -1.0), reads=[identpf], writes=[negoff])
            masks = []
            for x, cmpop in ((0, ALU.is_ge), (1, ALU.is_le)):
                mk = kb.sb(st, f"mask{x}", [128, 64], BF16)
                kb.op("dve", lambda e, cmpop=cmpop: e.tensor_single_scalar(dtmp[:, 0:64], csm[:, :], 0.0, cmpop), reads=[csm], writes=[dtmp])
                kb.op("dve", lambda e, mk=mk: e.tensor_scalar(mk[:, :], dtmp[:, 0:64], -NEG, NEG, ALU.mult, ALU.add), reads=[dtmp], writes=[mk])
                masks.append(mk)
            cmask = kb.sb(st, "cmask", [128, 512], F32)
            kb.op("dve", lambda e: e.memset(cmask[0:8, :], 1.0), writes=[cmask])
            kb.op("dve", lambda e: e.memset(cmask[0:8, :].rearrange("p (n c) -> p n c", c=64)[:, :, 0:1], 0.0), writes=[cmask])
            gnorm = self.load_bcast_vec(st, "gnorm", W["gdn_norm_g"], 64)
            pT = kb.ps(st, "pT", [128, 4, 128], BF16)
            pb = [kb.ps(st, f"pb{i}", [128, 512], F32) for i in range(7)]
            self._pbi = 0

            def bank():
                self._pbi = (self._pbi + 1) % 7
                return pb[self._pbi]

            def v3(ap):
                return ap.rearrange("p (h c) -> p h c", c=64)

            def blocks(out_bk, lhs, rhs, nr):
                for r in range(nr):
                    R_ = slice(64 * r, 64 * r + 64)
                    for h in range(H):
                        C_ = slice(64 * h, 64 * h + 64)
                        kb.op("pe", lambda e, R_=R_, C_=C_: e.matmul(out_bk[R_, C_], lhs[R_, C_], rhs[R_, C_], start=True, stop=True),
                              reads=[lhs, rhs], writes=[out_bk])

            nblk = (Lp + 511) // 512

            def pass_gen(x):
                sfx = f"_{x}"
                F = lambda n: kb.sb(st, n + sfx, [128, 512], F32)
                Bf = lambda n: kb.sb(st, n + sfx, [128, 512], BF16)
                g_t, b_t, cs, gc, ngc, egc, bg, kd = [F(n) for n in ("g_t", "b_t", "cs", "gc", "ngc", "egc", "bg", "kd")]
                tot = kb.sb(st, "tot" + sfx, [128, 8], F32)
                egl = kb.sb(st, "egl" + sfx, [128, 4, 8], F32)
                knT, qnT, kbT, qgT = [kb.sb(st, n + sfx, [128, 4, 512], BF16) for n in ("knT", "qnT", "kbT", "qgT")]
                ktok, vtok, vb, kbg, kdec, At, IY, nwT, vn = [Bf(n) for n in ("ktok", "vtok", "vb", "kbg", "kdec", "At", "IY", "nwT", "vn")]
                Xs = [Bf(f"X{i}") for i in range(2)]
                Ys = [Bf(f"Y{i}") for i in range(2)]
                Rs = [Bf(f"R{i}") for i in range(2)]
                gtk = kb.sb(st, "gtk" + sfx, [128, 32], F32)
                BD = kb.sb(st, "BD" + sfx, [128, 2, 512], F32)
                E, tt, u_sb, o_sb = [F(n) for n in ("E", "tt", "u_sb", "o_sb")]
                S = kb.sb(st, "S" + sfx, [128, 256], F32)
                Sb = kb.sb(st, "Sb" + sfx, [128, 256], BF16)
                for t_ in [ktok, vtok, vb, kbg, kdec, At, IY, vn] + Xs + Ys + Rs:
                    kb.op("pool", lambda e, t_=t_: e.memset(t_[:], 0.0), writes=[t_])
                kb.op("pool", lambda e: e.memset(gtk[:], 0.0), writes=[gtk])
                kb.op("pool", lambda e: e.memset(o_sb[:], 0.0), writes=[o_sb])
                kb.op("dve", lambda e: e.memset(S[:], 0.0), writes=[S])
                kb.op("dve", lambda e: e.memset(Sb[:], 0.0), writes=[Sb])
                odram = self.OF if x == 0 else self.OB
                border = range(nblk) if x == 0 else range(nblk - 1, -1, -1)
                for b in border:
                    p0 = 512 * b
                    ntok = min(512, Lp - p0)
                    nch = ntok // 64
                    N = slice(0, ntok)
                    P = slice(p0, p0 + ntok)
                    G = slice(0, 8)
                    self.dma("sp", g_t[G, N], self.GT[x, :, P], [self.GT], [g_t], g_t)
                    self.dma("sp", b_t[G, N], self.GT[2 + x, :, P], [self.GT], [b_t], b_t)
                    self.dma("sp", knT[:, :, N], self.GKT[:, :, P].rearrange("c p t -> p c t"), [self.GKT], [knT], knT)
                    self.dma("sp", qnT[:, :, N], self.GQT[:, :, P].rearrange("c p t -> p c t"), [self.GQT], [qnT], qnT)
                    kb.op("dve", lambda e: e.tensor_tensor_scan(cs[G, N], cmask[G, N], g_t[G, N], 0.0, ALU.mult, ALU.add),
                          reads=[cmask, g_t], writes=[cs])
                    cs3 = cs[G, N].rearrange("p (n c) -> p n c", c=64)
                    kb.op("dve", lambda e: e.tensor_copy(tot[G, 0:nch].unsqueeze(2), cs3[:, :, 63:64]), reads=[cs], writes=[tot])
                    totb = tot[G, 0:nch].unsqueeze(2).to_broadcast([8, nch, 64])
                    gc3 = gc[G, N].rearrange("p (n c) -> p n c", c=64)
                    if x == 0:
                        kb.op("dve", lambda e: e.tensor_copy(gc[G, N], cs[G, N]), reads=[cs], writes=[gc])
                    else:
                        kb.op("dve", lambda e: e.tensor_tensor(gc3, totb, cs3, ALU.subtract), reads=[tot, cs], writes=[gc])
                        kb.op("dve", lambda e: e.tensor_tensor(gc[G, N], gc[G, N], g_t[G, N], ALU.add), reads=[gc, g_t], writes=[gc])
                    kb.op("dve", lambda e: e.tensor_scalar_mul(ngc[G, N], gc[G, N], -1.0), reads=[gc], writes=[ngc])
                    kb.op("act", lambda e: e.activation(out=egc[G, N], in_=gc[G, N], func=AF.Exp), reads=[gc], writes=[egc])
                    kb.op("dve", lambda e: e.tensor_tensor(bg[G, N], b_t[G, N], egc[G, N], ALU.mult), reads=[b_t, egc], writes=[bg])
                    kd3 = kd[G, N].rearrange("p (n c) -> p n c", c=64)
                    kb.op("dve", lambda e: e.tensor_tensor(kd3, totb, gc3, ALU.subtract), reads=[tot, gc], writes=[kd])
                    kb.op("act", lambda e: e.activation(out=kd[G, N], in_=kd[G, N], func=AF.Exp), reads=[kd], writes=[kd])
                    bk = bank()
                    for j in range(4):
                        kb.op("pe", lambda e, j=j, bk=bk: e.matmul(bk[:, 8 * j:8 * j + nch], dmask[G, 128 * j:128 * j + 128], tot[G, 0:nch],
                                                                   start=True, stop=True), reads=[dmask, tot], writes=[bk])
                    kb.op("act", lambda e, bk=bk: e.activation(out=egl[:, :, 0:nch], in_=bk[:, 0:32].rearrange("p (j n) -> p j n", n=8)[:, :, 0:nch],
                                                               func=AF.Exp), reads=[bk], writes=[egl])
                    yield
                    for j in range(4):
                        bk = bank()
                        kb.op("pe", lambda e, j=j, bk=bk: e.matmul(bk[:, N], dmask[G, 128 * j:128 * j + 128], b_t[G, N], start=True, stop=True),
                              reads=[dmask, b_t], writes=[bk])
                        kb.op("dve", lambda e, j=j, bk=bk: e.tensor_tensor(kbT[:, j, N], knT[:, j, N], bk[:, N], ALU.mult),
                              reads=[knT, bk], writes=[kbT])
                        bk = bank()
                        kb.op("pe", lambda e, j=j, bk=bk: e.matmul(bk[:, N], dmask[G, 128 * j:128 * j + 128], egc[G, N], start=True, stop=True),
                              reads=[dmask, egc], writes=[bk])
                        kb.op("dve", lambda e, j=j, bk=bk: e.tensor_tensor(qgT[:, j, N], qnT[:, j, N], bk[:, N], ALU.mult),
                              reads=[qnT, bk], writes=[qgT])
                        yield
                    ntl = (ntok + 127) // 128
                    torder = range(ntl) if x == 0 else range(ntl - 1, -1, -1)
                    for tl in torder:
                        nt = min(128, ntok - tl * 128)
                        nr = nt // 64
                        TP = slice(0, nt)
                        TC = slice(tl * 128, tl * 128 + nt)
                        rows = slice(p0 + tl * 128, p0 + tl * 128 + nt)
                        self.dma("sp", ktok[TP, :], self.GKTOK[rows, :], [self.GKTOK], [ktok], ktok)
                        self.dma("sp", vtok[TP, :], self.GVTOK[rows, :], [self.GVTOK], [vtok], vtok)
                        bk = bank()
                        for qi, src in enumerate((b_t, bg, kd)):
                            kb.op("pe", lambda e, qi=qi, src=src, bk=bk: e.matmul(bk[TP, 8 * qi:8 * qi + 8], src[G, TC], self.ident_f[G, 0:8],
                                                                                 start=True, stop=True), reads=[src, self.ident_f], writes=[bk])
                        kb.op("act", lambda e, bk=bk: e.copy(gtk[TP, 0:24], bk[TP, 0:24]), reads=[bk], writes=[gtk])
                        for dst, src, c0, en in ((vb, vtok, 0, "pool"), (kbg, ktok, 8, "dve"), (kdec, ktok, 16, "pool")):
                            kb.op(en, lambda e, dst=dst, src=src, c0=c0: e.tensor_tensor(
                                v3(dst[TP, :]), v3(src[TP, :]), gtk[TP, c0:c0 + 8].unsqueeze(2).to_broadcast([nt, 8, 64]), ALU.mult),
                                reads=[src, gtk], writes=[dst])
                        yield
                        PA = bank()
                        PB = bank()
                        for cls in (0, 1):
                            for r in range(nr):
                                R_ = slice(64 * r, 64 * r + 64)
                                cc = slice(tl * 128 + 64 * r, tl * 128 + 64 * r + 64)
                                for h in range(H):
                                    j, m = h // 2, h % 2
                                    if (m == r) != (cls == 0):
                                        continue
                                    M_ = slice(64 * m, 64 * m + 64)
                                    C_ = slice(64 * h, 64 * h + 64)
                                    kb.op("pe", lambda e, R_=R_, C_=C_, M_=M_, j=j, cc=cc: e.matmul(PA[R_, C_], knT[M_, j, cc], kbT[M_, j, cc],
                                                                                                     start=True, stop=True),
                                          reads=[knT, kbT], writes=[PA])
                                    kb.op("pe", lambda e, R_=R_, C_=C_, M_=M_, j=j, cc=cc: e.matmul(PB[R_, C_], knT[M_, j, cc], qnT[M_, j, cc],
                                                                                                     start=True, stop=True),
                                          reads=[knT, qnT], writes=[PB])
                        PD = bank()
                        for r in range(nr):
                            cc = slice(tl * 128 + 64 * r, tl * 128 + 64 * r + 64)
                            kb.op("dve", lambda e, r=r, cc=cc: e.tensor_tensor(v3(BD[G, r, :]), v3(dmask[G, :]),
                                                                               gc[G, cc].unsqueeze(1).to_broadcast([8, 8, 64]), ALU.mult),
                                  reads=[dmask, gc], writes=[BD])
                            kb.op("pe", lambda e, r=r: e.matmul(PD[64 * r:64 * r + 64, :], self.ones_f[G, 0:64], BD[G, r, :], start=True, stop=False),
                                  reads=[self.ones_f, BD], writes=[PD])
                        kb.op("pe", lambda e: e.matmul(PD[TP, :], ngc[G, TC], dmask[G, :], start=False, stop=False),
                              reads=[ngc, dmask], writes=[PD])
                        mk = masks[x]
                        kb.op("pe", lambda e: e.matmul(v3(PD[TP, :]), self.ident_b[TP, TP], mk[TP, :].unsqueeze(1).to_broadcast([nt, 8, 64]),
                                                       start=False, stop=True), reads=[self.ident_b, mk], writes=[PD])
                        kb.op("act", lambda e: e.activation(out=E[TP, :], in_=PD[TP, :], func=AF.Exp), reads=[PD], writes=[E])
                        yield
                        kb.op("dve", lambda e: e.tensor_tensor(At[TP, :], PB[TP, :], E[TP, :], ALU.mult), reads=[PB, E], writes=[At])
                        kb.op("dve", lambda e: e.tensor_tensor(tt[TP, :], PA[TP, :], E[TP, :], ALU.mult), reads=[PA, E], writes=[tt])
                        X, Y, Rr = Xs[0], Ys[0], Rs[0]
                        kb.op("dve", lambda e: e.tensor_tensor(v3(X[TP, :]), v3(tt[TP, :]), negoff[TP, :].unsqueeze(1).to_broadcast([nt, 8, 64]),
                                                               ALU.mult), reads=[tt, negoff], writes=[X])
                        kb.op("dve", lambda e: e.tensor_tensor(v3(Rr[TP, :]), v3(X[TP, :]), identp[TP, :].unsqueeze(1).to_broadcast([nt, 8, 64]),
                                                               ALU.add), reads=[X, identp], writes=[Rr])
                        PY = bank()
                        for r in range(nr):
                            R_ = slice(64 * r, 64 * r + 64)
                            for h in range(H):
                                C_ = slice(64 * h, 64 * h + 64)
                                kb.op("pe", lambda e, R_=R_, C_=C_: e.matmul(PY[R_, C_], X[R_, C_], self.ident_b[R_, R_], start=True, stop=True),
                                      reads=[X, self.ident_b], writes=[PY])
                        kb.op("act", lambda e: e.copy(Y[TP, :], PY[TP, :]), reads=[PY], writes=[Y])
                        yield
                        for jj in range(5):
                            Xn, Yn, Rn = Xs[(jj + 1) % 2], Ys[(jj + 1) % 2], Rs[(jj + 1) % 2]
                            if jj < 4:
                                PX = bank()
                                blocks(PX, Y, X, nr)
                                kb.op("act", lambda e, PX=PX, Xn=Xn: e.copy(Xn[TP, :], PX[TP, :]), reads=[PX], writes=[Xn])
                            PY = bank()
                            blocks(PY, X, Y, nr)
                            kb.op("dve", lambda e, PY=PY: e.tensor_tensor(v3(IY[TP, :]), v3(PY[TP, :]),
                                                                          identpf[TP, :].unsqueeze(1).to_broadcast([nt, 8, 64]), ALU.add),
                                  reads=[PY, identpf], writes=[IY])
                            if jj < 4:
                                kb.op("dve", lambda e, PY=PY, Yn=Yn: e.tensor_copy(Yn[TP, :], PY[TP, :]), reads=[PY], writes=[Yn])
                            yield
                            PR = bank()
                            blocks(PR, IY, Rr, nr)
                            kb.op("act", lambda e, PR=PR, Rn=Rn: e.copy(Rn[TP, :], PR[TP, :]), reads=[PR], writes=[Rn])
                            X, Y, Rr = Xn, Yn, Rn
                            yield
                        Tt = Rr
                        PU = bank()
                        blocks(PU, Tt, vb, nr)
                        kb.op("act", lambda e, PU=PU: e.copy(u_sb[TP, :], PU[TP, :]), reads=[PU], writes=[u_sb])
                        PW = bank()
                        for cls in (0, 1):
                            for r in range(nr):
                                R_ = slice(64 * r, 64 * r + 64)
                                for h in range(H):
                                    j, m = h // 2, h % 2
                                    if (m == r) != (cls == 0):
                                        continue
                                    C_ = slice(64 * h, 64 * h + 64)
                                    kb.op("pe", lambda e, R_=R_, C_=C_, j=j, m=m, r=r: e.matmul(
                                        PW[64 * m:64 * m + 64, 128 * j + 64 * r:128 * j + 64 * r + 64], kbg[R_, C_], Tt[R_, C_], start=True, stop=True),
                                        reads=[kbg, Tt], writes=[PW])
                        if nr == 2:
                            kb.op("act", lambda e, PW=PW: e.mul(nwT[:, :], PW[:, :], -1.0), reads=[PW], writes=[nwT])
                        else:
                            kb.op("act", lambda e, PW=PW: e.mul(nwT[:, :].rearrange("p (j r c) -> p j r c", j=4, r=2)[:, :, 0, :],
                                                               PW[:, :].rearrange("p (j r c) -> p j r c", j=4, r=2)[:, :, 0, :], -1.0),
                                  reads=[PW], writes=[nwT])
                        yield
                        rorder = range(nr) if x == 0 else range(nr - 1, -1, -1)
                        for r in rorder:
                            R_ = slice(64 * r, 64 * r + 64)
                            nb = tl * 2 + r
                            cc = slice(tl * 128 + 64 * r, tl * 128 + 64 * r + 64)
                            PV = bank()
                            for mm_ in (0, 1):
                                for h in range(H):
                                    j, m = h // 2, h % 2
                                    if m != mm_:
                                        continue
                                    M_ = slice(64 * m, 64 * m + 64)
                                    kb.op("pe", lambda e, h=h, j=j, M_=M_, r=r, R_=R_, PV=PV: e.matmul(
                                        PV[R_, 64 * h:64 * h + 64], nwT[M_, 128 * j + 64 * r:128 * j + 64 * r + 64], Sb[M_, 64 * j:64 * j + 64],
                                        start=True, stop=True), reads=[nwT, Sb], writes=[PV])
                            kb.op("dve", lambda e, R_=R_, PV=PV: e.tensor_tensor(vn[R_, :], u_sb[R_, :], PV[R_, :], ALU.add),
                                  reads=[u_sb, PV], writes=[vn])
                            PO = bank()
                            for mm_ in (1 - r, r):
                                for h in range(H):
                                    j, m = h // 2, h % 2
                                    if m != mm_:
                                        continue
                                    M_ = slice(64 * m, 64 * m + 64)
                                    C_ = slice(64 * h, 64 * h + 64)
                                    kb.op("pe", lambda e, M_=M_, C_=C_, j=j, R_=R_, cc=cc, PO=PO: e.matmul(
                                        PO[R_, C_], qgT[M_, j, cc], Sb[M_, 64 * j:64 * j + 64], start=True, stop=True),
                                        reads=[qgT, Sb], writes=[PO])
                            yield
                            PO2 = bank()
                            for h in range(H):
                                C_ = slice(64 * h, 64 * h + 64)
                                kb.op("pe", lambda e, C_=C_, R_=R_, PO2=PO2: e.matmul(PO2[R_, C_], At[R_, C_], vn[R_, C_], start=True, stop=True),
                                      reads=[At, vn], writes=[PO2])
                            PS_ = bank()
                            for h in range(H):
                                j, m = h // 2, h % 2
                                C_ = slice(64 * h, 64 * h + 64)
                                kb.op("pe", lambda e, j=j, m=m, R_=R_, C_=C_, PS_=PS_: e.matmul(
                                    PS_[64 * m:64 * m + 64, 64 * j:64 * j + 64], kdec[R_, C_], vn[R_, C_], start=True, stop=True),
                                    reads=[kdec, vn], writes=[PS_])
                            kb.op("act", lambda e, R_=R_, PO=PO: e.copy(o_sb[R_, :], PO[R_, :]), reads=[PO], writes=[o_sb])
                            kb.op("dve", lambda e, R_=R_, PO2=PO2: e.tensor_tensor(o_sb[R_, :], o_sb[R_, :], PO2[R_, :], ALU.add),
                                  reads=[o_sb, PO2], writes=[o_sb])
                            kb.op("dve", lambda e, nb=nb: e.tensor_tensor(v3(S[:, :]), v3(S[:, :]),
                                                                          egl[:, :, nb:nb + 1].to_broadcast([128, 4, 64]), ALU.mult),
                                  reads=[S, egl], writes=[S])
                            kb.op("dve", lambda e, PS_=PS_: e.tensor_tensor(S[:, :], S[:, :], PS_[:, 0:256], ALU.add), reads=[S, PS_], writes=[S])
                            kb.op("act", lambda e: e.copy(Sb[:, :], S[:, :]), reads=[S], writes=[Sb])
                            yield
                        self.dma("sp", odram[rows, :], o_sb[TP, :], [o_sb], [odram], o_sb)

            gens = [pass_gen(0), pass_gen(1)]
            while gens:
                for g in list(gens):
                    try:
                        next(g)
                    except StopIteration:
                        gens.remove(g)

            ofs = [kb.sb(st, f"of_t{i}", [128, 512], F32) for i in range(2)]
            obs = [kb.sb(st, f"ob_t{i}", [128, 512], F32) for i in range(2)]
            zts = [kb.sb(st, f"z_t{i}", [128, 512], F32) for i in range(2)]
            osum = kb.sb(st, "osum", [128, 512], F32)
            osq = kb.sb(st, "osq", [128, 512], F32)
            ssum = kb.sb(st, "ssum", [128, 16], F32)
            gout = kb.sb(st, "gout", [128, 512], BF16)
            gTs = [kb.sb(st, f"gT{i}", [128, 4, 128], BF16) for i in range(2)]
            for t_ in ofs + obs + zts:
                kb.op("pool", lambda e, t_=t_: e.memset(t_[:], 0.0), writes=[t_])
            kb.op("pool", lambda e: e.memset(gout[:], 0.0), writes=[gout])
            ntp = (Lp + 127) // 128
            for t in range(ntp):
                nt = min(128, Lp - 128 * t)
                TP = slice(0, nt)
                rows = slice(128 * t, 128 * t + nt)
                of_t, ob_t, z_t, gT = ofs[t % 2], obs[t % 2], zts[t % 2], gTs[t % 2]
                self.dma("sp", of_t[TP, :], self.OF[rows, :], [self.OF], [of_t], of_t)
                self.dma("sp", ob_t[TP, :], self.OB[rows, :], [self.OB], [ob_t], ob_t)
                self.dma("sp", z_t[TP, :], self.Z[rows, :], [self.Z], [z_t], z_t)
                kb.op("pool", lambda e: e.tensor_tensor(osum[TP, :], of_t[TP, :], ob_t[TP, :], ALU.add), reads=[of_t, ob_t], writes=[osum])
                kb.op("act", lambda e: e.activation(out=osq[TP, :], in_=osum[TP, :], func=AF.Square), reads=[osum], writes=[osq])
                kb.op("dve", lambda e: e.tensor_reduce(ssum[TP, 0:8], v3(osq[TP, :]), AX.X, ALU.add), reads=[osq], writes=[ssum])
                kb.op("act", lambda e: e.activation(out=ssum[TP, 8:16], in_=ssum[TP, 0:8], func=AF.Ln, bias=self.cst[TP, 1:2],
                                                    scale=1.0 / 64.0), reads=[ssum, self.cst], writes=[ssum])
                kb.op("act", lambda e: e.activation(out=ssum[TP, 8:16], in_=ssum[TP, 8:16], func=AF.Exp, scale=-0.5),
                      reads=[ssum], writes=[ssum])
                kb.op("dve", lambda e: e.tensor_tensor(v3(osum[TP, :]), v3(osum[TP, :]),
                                                       ssum[TP, 8:16].unsqueeze(2).to_broadcast([nt, 8, 64]), ALU.mult),
                      reads=[osum, ssum], writes=[osum])
                kb.op("pool", lambda e: e.tensor_tensor(v3(osum[TP, :]), v3(osum[TP, :]),
                                                        gnorm[TP, :].unsqueeze(1).to_broadcast([nt, 8, 64]), ALU.mult),
                      reads=[osum, gnorm], writes=[osum])
                kb.op("dve", lambda e: e.tensor_tensor(gout[TP, :], osum[TP, :], z_t[TP, :], ALU.mult), reads=[osum, z_t], writes=[gout])
                for j in range(4):
                    kb.op("pe", lambda e, j=j: e.transpose(pT[:, j, 0:nt], gout[TP, 128 * j:128 * j + 128], self.ident_b[TP, TP]),
                          reads=[gout, self.ident_b], writes=[pT])
                kb.op("act", lambda e: e.copy(gT[:, :, 0:nt], pT[:, :, 0:nt]), reads=[pT], writes=[gT])
                c0 = 64 if t == 0 else 0
                x0 = 128 * t + c0 - 64
                if nt > c0:
                    self.dma("sp", self.CATT[4:8, :, x0:x0 + nt - c0].rearrange("c p t -> p c t"), gT[:, :, c0:nt],
                             [gT], [self.CATT], gT)
        kb.barrier()

    def phase3(self, si):
        kb = self.kb
        Lx = self.seq_lens[si]
        Lp = Lx + 64
        nkc = (Lp + 127) // 128
        with ExitStack() as st:
            kt = kb.sb(st, "kt", [128, 8, nkc * 128], BF16)
            va = kb.sb(st, "va", [128, nkc, 8, 128], BF16)
            kb.op("pool", lambda e: e.memset(va[:, :, :, 64:128], 1.0), writes=[va])
            kb.op("pool", lambda e: e.memset(va[:, :, :, 0:64], 0.0), writes=[va])
            self.dma("sp", kt[0:96, :, 0:Lp], self.KT[:, :, 0:Lp].rearrange("h r t -> r h t"), [self.KT], [kt], kt)
            for kc in range(nkc):
                nk = min(128, Lp - kc * 128)
                self.dma("sp", va[0:nk, kc, :, 0:64], self.V[kc * 128:kc * 128 + nk, :].rearrange("t (h e) -> t h e", h=8),
                         [self.V], [va], va)
            kb.op("pool", lambda e: e.memset(va[0:32, 0, :, :], 0.0), writes=[va])
            kb.op("pool", lambda e: e.memset(va[32:48, 0, :, :], 0.0), writes=[va])
            qts = [kb.sb(st, f"qt{i}", [128, 8, 512], BF16) for i in range(2)]
            pts = [kb.sb(st, f"pt{i}", [128, 512], BF16) for i in range(5)]
            rec = kb.sb(st, "rec", [128, 512], F32)
            mo = kb.sb(st, "mo", [128, 4, 512], BF16)
            pss = [kb.ps(st, f"pss{i}", [128, 512], F32) for i in range(5)]
            self._p3cnt = 0
            pos = [kb.ps(st, f"pos{i}", [128, 512], F32) for i in range(2)]
            nqb = Lx // 512 if Lx % 512 == 0 else (Lx + 511) // 512
            cnt = 0
            for qb in range(nqb):
                nq = min(512, Lx - qb * 512)
                Q = slice(0, nq)
                qt = qts[qb % 2]
                self.dma("sp", qt[0:96, :, Q], self.QT[:, :, qb * 512:qb * 512 + nq].rearrange("h r t -> r h t"),
                         [self.QT], [qt], qt)
                for h in range(H):
                    po = pos[h % 2]
                    LOOK = 3
                    slots = {}

                    def emit_s(kc, h=h):
                        nk = min(128, Lp - kc * 128)
                        ps_ = pss[self._p3cnt % 5]
                        pt = pts[self._p3cnt % 5]
                        self._p3cnt += 1
                        slots[kc] = (ps_, pt, nk)
                        kb.op("pe", lambda e: e.matmul(ps_[0:nk, Q], kt[0:96, h, kc * 128:kc * 128 + nk], qt[0:96, h, Q], start=True, stop=True),
                              reads=[kt, qt], writes=[ps_])

                    for kc in range(min(LOOK, nkc)):
                        emit_s(kc)
                    for kc in range(nkc):
                        ps_, pt, nk = slots.pop(kc)
                        kb.op("act", lambda e, ps_=ps_, pt=pt, nk=nk: e.activation(out=pt[0:nk, Q], in_=ps_[0:nk, Q], func=AF.Exp),
                              reads=[ps_], writes=[pt])
                        if kc + LOOK < nkc:
                            emit_s(kc + LOOK)
                        kb.op("pe", lambda e, po=po, pt=pt, kc=kc, nk=nk, h=h: e.matmul(
                            po[:, Q], va[0:nk, kc, h, :], pt[0:nk, Q], start=(kc == 0), stop=(kc == nkc - 1)),
                            reads=[va, pt], writes=[po])
                    kb.op("dve", lambda e, po=po: e.reciprocal(rec[0:64, Q], po[64:128, Q]), reads=[po], writes=[rec])
                    kb.op("dve", lambda e, po=po, h=h: e.tensor_tensor(mo[64 * (h % 2):64 * (h % 2) + 64, h // 2, Q], po[0:64, Q],
                                                                       rec[0:64, Q], ALU.mult), reads=[po, rec], writes=[mo])
                self.dma("sp", self.CATT[0:4, :, qb * 512:qb * 512 + nq].rearrange("c p t -> p c t"), mo[:, :, Q],
                         [mo], [self.CATT], mo)
        kb.barrier()

    def phase4a(self, si):
        kb = self.kb
        Lx = self.seq_lens[si]
        W = self.w
        with ExitStack() as st:
            w_out = kb.sb(st, "w_out", [128, 8, D], BF16)
            for kc in range(8):
                self.dma("sp", w_out[:, kc, :], self.Wb["w_out"][kc * 128:kc * 128 + 128, :], [self.Wb["w_out"]], [w_out], w_out)
            lng = self.load_bcast_vec(st, "lng", W["ln_in_g"], D)
            lnb = self.load_bcast_vec(st, "lnb", W["ln_in_b"], D)
            l1g = self.load_bcast_vec(st, "l1g", W["ln1_g"], D)
            l1b = self.load_bcast_vec(st, "l1b", W["ln1_b"], D)
            xts = [kb.sb(st, f"xt{i}", [128, D], F32) for i in range(3)]
            cats = [kb.sb(st, f"cat{i}", [128, 8, 128], BF16) for i in range(3)]
            hress = [kb.sb(st, f"hres{i}", [128, D], F32) for i in range(3)]
            r1s = [kb.sb(st, f"r1{i}", [128, D], F32) for i in range(3)]
            h1s = [kb.sb(st, f"h1{i}", [128, D], F32) for i in range(2)]
            h1bs = [kb.sb(st, f"h1b{i}", [128, D], BF16) for i in range(2)]
            h1ts = [kb.sb(st, f"h1t{i}", [128, 8, 128], BF16) for i in range(2)]
            statss = [kb.sb(st, f"stats{i}", [128, 32], F32) for i in range(5)]
            pT = kb.ps(st, "pT", [128, 8, 128], BF16)
            pb = [kb.ps(st, f"pb{i}", [128, 512], F32) for i in range(6)]
            ntile = Lx // 128

            def stage_a(k):
                xt, cat = xts[k % 3], cats[k % 3]
                rows = slice(128 * k, 128 * k + 128)
                self.dma("sp", xt[:, :], self.x_in[si][rows, :], [self.xin_reg], [xt], xt)
                self.dma("sp", cat[:, :, :], self.CATT[:, :, rows].rearrange("c p t -> p c t"), [self.CATT], [cat], cat)
                return self.layer_norm_gen(xt, xt[:, :], lng, lnb, hress[k % 3], hress[k % 3][:, :], xt, statss[k % 3])

            def stage_b_mm(k):
                cat = cats[k % 3]
                for nh in range(2):
                    bk = pb[(2 * k + nh) % 6]
                    F = slice(nh * 512, nh * 512 + 512)
                    for kc in range(8):
                        kb.op("pe", lambda e, kc=kc, bk=bk, F=F, cat=cat: e.matmul(bk[:, :], cat[:, kc, :], w_out[:, kc, F],
                                                                                   start=(kc == 0), stop=(kc == 7)),
                              reads=[cat, w_out], writes=[bk])

            def stage_b_r1(k):
                hres, r1 = hress[k % 3], r1s[k % 3]
                for nh in range(2):
                    bk = pb[(2 * k + nh) % 6]
                    F = slice(nh * 512, nh * 512 + 512)
                    kb.op("dve", lambda e, bk=bk, F=F: e.scalar_tensor_tensor(r1[:, F], hres[:, F], DN_ALPHA, bk[:, :],
                                                                             ALU.mult, ALU.add), reads=[hres, bk], writes=[r1])

            def stage_c_ln(k):
                r1, h1 = r1s[k % 3], h1s[k % 2]
                return self.layer_norm_gen(r1, r1[:, :], l1g, l1b, h1, h1[:, :], r1, statss[3 + k % 2])

            def stage_c_out(k):
                h1, h1b, h1t = h1s[k % 2], h1bs[k % 2], h1ts[k % 2]
                rows = slice(128 * k, 128 * k + 128)
                self.dma("sp", self.H1[rows, :], h1[:, :], [h1], [self.H1], h1)
                kb.op("act", lambda e: e.copy(h1b[:, :], h1[:, :]), reads=[h1], writes=[h1b])
                self.transpose_tile(h1b, lambda kk: h1b[:, kk * 128:(kk + 1) * 128], pT, h1t,
                                    lambda k0, k1: h1t[:, k0:k1, :])
                self.dma("sp", self.H1T[:, :, rows].rearrange("c p t -> p c t"), h1t[:, :, :], [h1t], [self.H1T], h1t)

            self.run_gens(stage_a(0), stage_a(1) if ntile > 1 else None, stage_a(2) if ntile > 2 else None)
            for k0 in range(min(2, ntile)):
                stage_b_mm(k0)
                stage_b_r1(k0)
            for k in range(ntile):
                if k + 2 < ntile:
                    stage_b_mm(k + 2)
                ga = stage_a(k + 3) if k + 3 < ntile else None
                self.run_gens(ga, stage_c_ln(k))
                if k + 2 < ntile:
                    stage_b_r1(k + 2)
                stage_c_out(k)
        kb.barrier()

    def phase4b(self, si):
        kb = self.kb
        Lx = self.seq_lens[si]
        W = self.w
        with ExitStack() as st:
            w1 = kb.sb(st, "w_ff1", [128, 8, DFF], BF16)
            w2 = kb.sb(st, "w_ff2", [128, 32, D], BF16)
            for kc in range(8):
                self.dma("sp", w1[:, kc, :], self.Wb["w_ff1"][kc * 128:kc * 128 + 128, :], [self.Wb["w_ff1"]], [w1], w1)
            for kc in range(32):
                self.dma("sp", w2[:, kc, :], self.Wb["w_ff2"][kc * 128:kc * 128 + 128, :], [self.Wb["w_ff2"]], [w2], w2)
            l2g = self.load_bcast_vec(st, "l2g", W["ln2_g"], D)
            l2b = self.load_bcast_vec(st, "l2b", W["ln2_b"], D)
            aT = kb.sb(st, "aT", [128, 32, 512], BF16)
            h1T = kb.sb(st, "h1T", [128, 8, 512], BF16)
            rls = [kb.sb(st, f"rl{i}", [128, 512], BF16) for i in range(2)]
            h1 = kb.sb(st, "h1", [128, D], F32)
            r2 = kb.sb(st, "r2", [128, D], F32)
            yo = kb.sb(st, "yo", [128, D], F32)
            lntmp = kb.sb(st, "lntmp", [128, D], F32)
            stats = kb.sb(st, "stats", [128, 32], F32)
            pb = [kb.ps(st, f"pb{i}", [128, 512], F32) for i in range(6)]
            bi = 0
            for b in range((Lx + 511) // 512):
                n = min(512, Lx - 512 * b)
                N = slice(0, n)
                cols = slice(512 * b, 512 * b + n)
                self.dma("sp", h1T[:, :, N], self.H1T[:, :, cols].rearrange("c p t -> p c t"), [self.H1T], [h1T], h1T)
                for mc in range(32):
                    bk = pb[bi % 6]
                    bi += 1
                    rl = rls[mc % 2]
                    for kc in range(8):
                        kb.op("pe", lambda e, kc=kc, mc=mc, bk=bk: e.matmul(bk[:, N], w1[:, kc, mc * 128:mc * 128 + 128], h1T[:, kc, N],
                                                                            start=(kc == 0), stop=(kc == 7)),
                              reads=[w1, h1T], writes=[bk])
                    kb.op("act", lambda e, bk=bk, rl=rl: e.activation(out=rl[:, N], in_=bk[:, N], func=AF.Relu), reads=[bk], writes=[rl])
                    kb.op("pool", lambda e, rl=rl, mc=mc: e.tensor_tensor(aT[:, mc, N], rl[:, N], rl[:, N], ALU.mult),
                          reads=[rl], writes=[aT])
                for tl in range(n // 128):
                    rows = slice(512 * b + 128 * tl, 512 * b + 128 * tl + 128)
                    self.dma("sp", h1[:, :], self.H1[rows, :], [self.H1], [h1], h1)
                    for nh in range(2):
                        bk = pb[bi % 6]
                        bi += 1
                        F = slice(nh * 512, nh * 512 + 512)
                        for mc in range(32):
                            kb.op("pe", lambda e, mc=mc, bk=bk, F=F, tl=tl: e.matmul(bk[:, :], aT[:, mc, tl * 128:tl * 128 + 128],
                                                                                     w2[:, mc, F], start=(mc == 0), stop=(mc == 31)),
                                  reads=[aT, w2], writes=[bk])
                        kb.op("dve", lambda e, bk=bk, F=F: e.scalar_tensor_tensor(r2[:, F], h1[:, F], DN_ALPHA, bk[:, :],
                                                                                 ALU.mult, ALU.add), reads=[h1, bk], writes=[r2])
                    self.layer_norm(r2, r2[:, :], l2g, l2b, yo, yo[:, :], r2, stats)
                    self.dma("sp", self.y_out[si][rows, :], yo[:, :], [yo], [self.yout_reg], yo)
        kb.barrier()


WEIGHT_NAMES = ["meta_tokens", "ln_in_g", "ln_in_b", "w_in", "g_cq", "g_ckv", "w_uq", "w_uk", "w_uv", "conv_w",
                "a_log_f", "a_log_b", "dt_bias_f", "dt_bias_b", "gdn_norm_g", "w_out", "ln1_g", "ln1_b",
                "w_ff1", "w_ff2", "ln2_g", "ln2_b"]


def build_prog(seq_lens, debug=False, phases=None):
    p = Prog(seq_lens, debug=debug)
    p.declare()
    p.setup_consts()
    for si in range(len(seq_lens)):
        for name in ["phase1", "phase2a", "phase2b", "phase3", "phase4a", "phase4b"]:
            if phases is not None and name not in phases:
                continue
            if not hasattr(p, name):
                continue
            getattr(p, name)(si)
    p.finish()
    return p


SEQ_LENS = [2048, 2048, 2048, 2048, 4096]
_PROG_CACHE = {}


def kernel(**inputs):
    n = 8
    x_prompt = np.asarray(inputs["x_prompt"], dtype=np.float32)
    x_sample = np.asarray(inputs["x_sample"], dtype=np.float32)
    wmap = {}
    for k in WEIGHT_NAMES:
        a = np.asarray(inputs[k], dtype=np.float32)
        if k not in ("meta_tokens", "ln_in_g", "ln_in_b"):
            a = a[0]
        wmap[k] = np.ascontiguousarray(a)
    prog = build_prog(SEQ_LENS, debug=False)
    in_maps = []
    for c in range(n):
        m = dict(wmap)
        for i in range(4):
            m[f"x{i}"] = np.ascontiguousarray(x_sample[4 * c + i])
        m["x4"] = np.ascontiguousarray(x_prompt[c // 2])
        in_maps.append(m)
    res = run_bass_kernel_spmd(prog.nc, in_maps, core_ids=list(range(n)))
    y_sample = np.empty_like(x_sample)
    y_prompt = np.empty_like(x_prompt)
    for c in range(n):
        r = res.results[c]
        for i in range(4):
            y_sample[4 * c + i] = np.asarray(r[f"y{i}"], dtype=np.float32)
        if c % 2 == 0:
            y_prompt[c // 2] = np.asarray(r["y4"], dtype=np.float32)
    return (y_prompt, y_sample)
```
